# Optimizing a Trainium2 kernel written in Bass

```python
import math
import jax, jax.numpy as jnp
from jax import lax
import numpy as np

D_MODEL = 1024
BATCH = 8
SEQ = 8192
DEPTH = 2

N_MIXERS = 2
N_ATTN_LAYERS = (DEPTH + 1) // 2
N_GLA_LAYERS = DEPTH // 2

A_HEADS = 16
A_KV_HEADS = 4
A_HEAD_DIM = D_MODEL // A_HEADS
A_GROUP = A_HEADS // A_KV_HEADS
WINDOW = 128
A_BLOCK = 128
ROPE_THETA = 10000.0
A_IN_COLS = (A_HEADS + 2 * A_KV_HEADS) * A_HEAD_DIM

B_HEADS = 4
B_KEY_DIM = (D_MODEL // 2) // B_HEADS
B_VAL_DIM = D_MODEL // B_HEADS
B_GATE_RANK = 16
B_GATE_TAU = 16.0
B_CHUNK = 64
B_QK_COLS = B_HEADS * B_KEY_DIM
B_V_COLS = B_HEADS * B_VAL_DIM
B_IN_COLS = 2 * B_QK_COLS + 2 * B_V_COLS + 2 * B_GATE_RANK

D_FF = 4 * D_MODEL
DN_ALPHA = float((2 * DEPTH) ** 0.25)
DN_BETA = float((8 * DEPTH) ** -0.25)
LN_EPS = 1e-5
HEAD_NORM_EPS = 1e-6

kernel_name = 'hybrid_swa_gla_deepnorm_encoder'


def _layer_norm(x, g, b):
    xf = x.astype(jnp.float32)
    mu = jnp.mean(xf, axis=-1, keepdims=True)
    var = jnp.mean(jnp.square(xf - mu), axis=-1, keepdims=True)
    y = (xf - mu) * lax.rsqrt(var + LN_EPS)
    return (y * g.astype(jnp.float32) + b.astype(jnp.float32)).astype(x.dtype)


def _rope_tables(positions):
    inv_freq = ROPE_THETA ** (-jnp.arange(0, A_HEAD_DIM, 2, dtype=jnp.float32) / A_HEAD_DIM)
    ang = positions.astype(jnp.float32)[..., None] * inv_freq
    return jnp.cos(ang)[:, :, None, :], jnp.sin(ang)[:, :, None, :]


def _rope(t, cos, sin):
    tf = t.astype(jnp.float32)
    t1, t2 = jnp.split(tf, 2, axis=-1)
    return jnp.concatenate([t1 * cos - t2 * sin, t2 * cos + t1 * sin], axis=-1).astype(t.dtype)


def _window_gqa(x, w_in, sink, w_out, cos, sin):
    B, S, _ = x.shape
    hd = A_HEAD_DIM
    h = x @ w_in
    q, k, v = jnp.split(h, [A_HEADS * hd, (A_HEADS + A_KV_HEADS) * hd], axis=-1)
    q = _rope(q.reshape(B, S, A_HEADS, hd), cos, sin)
    k = _rope(k.reshape(B, S, A_KV_HEADS, hd), cos, sin)
    v = v.reshape(B, S, A_KV_HEADS, hd)
    nblk = S // A_BLOCK
    span = A_BLOCK + 2 * WINDOW
    pad = ((0, 0), (WINDOW, WINDOW), (0, 0), (0, 0))
    kp = jnp.pad(k, pad)
    vp = jnp.pad(v, pad)
    qb = q.reshape(B, nblk, A_BLOCK, A_KV_HEADS, A_GROUP, hd).transpose(1, 0, 2, 3, 4, 5)
    rel = jnp.arange(A_BLOCK)[:, None] + WINDOW - jnp.arange(span)[None, :]
    band = jnp.abs(rel) <= WINDOW
    sink_l = sink.astype(jnp.float32).reshape(1, A_KV_HEADS, A_GROUP, 1, 1)
    scale = hd ** -0.5

    def block(args):
        qi, i = args
        start = i * A_BLOCK
        ks = lax.dynamic_slice_in_dim(kp, start, span, axis=1)
        vs = lax.dynamic_slice_in_dim(vp, start, span, axis=1)
        kpos = start - WINDOW + jnp.arange(span)
        valid = band & ((kpos >= 0) & (kpos < S))[None, :]
        s = jnp.einsum('bqkgd,bskd->bkgqs', qi, ks, preferred_element_type=jnp.float32) * scale
        s = jnp.where(valid, s, -jnp.inf)
        sink_col = jnp.broadcast_to(sink_l, s.shape[:-1] + (1,))
        p = jax.nn.softmax(jnp.concatenate([s, sink_col], axis=-1), axis=-1)[..., :span]
        return jnp.einsum('bkgqs,bskd->bqkgd', p.astype(vs.dtype), vs)

    o = lax.map(block, (qb, jnp.arange(nblk)))
    o = o.transpose(1, 0, 2, 3, 4, 5).reshape(B, S, A_HEADS * hd)
    return o @ w_out


def _gla_chunked(q, k, v, g):
    B, S, H, dk = q.shape
    dv = v.shape[-1]
    n = S // B_CHUNK

    def to_chunks(t):
        return t.reshape((B, n, B_CHUNK) + t.shape[2:]).swapaxes(0, 1)

    tril = jnp.tril(jnp.ones((B_CHUNK, B_CHUNK), dtype=bool))

    def step(state, inp):
        qc, kc, vc, gc = inp
        b = jnp.cumsum(gc, axis=1)
        qf = qc.astype(jnp.float32)
        kf = kc.astype(jnp.float32)
        vf = vc.astype(jnp.float32)
        qe = qf * jnp.exp(b)
        ke = kf * jnp.exp(-b)
        att = jnp.where(tril, jnp.einsum('bthd,bshd->bhts', qe, ke), 0.0)
        o = jnp.einsum('bhts,bshv->bthv', att, vf) + jnp.einsum('bthd,bhdv->bthv', qe, state)
        b_last = b[:, -1]
        kd = kf * jnp.exp(b_last[:, None] - b)
        state = jnp.exp(b_last)[..., None] * state + jnp.einsum('bshd,bshv->bhdv', kd, vf)
        return state, o.astype(vc.dtype)

    s0 = jnp.zeros((B, H, dk, dv), jnp.float32)
    _, o = lax.scan(step, s0, (to_chunks(q), to_chunks(k), to_chunks(v), to_chunks(g)))
    return o.swapaxes(0, 1).reshape(B, S, H, dv)


def _bidir_gla(x, w_in, gw2_f, gb_f, gw2_b, gb_b, norm_g, w_out):
    B, S, _ = x.shape
    h = x @ w_in
    c1 = B_QK_COLS
    c2 = 2 * B_QK_COLS
    c3 = c2 + B_V_COLS
    c4 = c3 + B_V_COLS
    c5 = c4 + B_GATE_RANK
    q, k, v, r, lr_f, lr_b = jnp.split(h, [c1, c2, c3, c4, c5], axis=-1)
    q = q.reshape(B, S, B_HEADS, B_KEY_DIM) * (B_KEY_DIM ** -0.5)
    k = k.reshape(B, S, B_HEADS, B_KEY_DIM)
    v = v.reshape(B, S, B_HEADS, B_VAL_DIM)

    def log_gate(lr, w2, bias):
        z = (lr @ w2 + bias).astype(jnp.float32)
        return (jax.nn.log_sigmoid(z) / B_GATE_TAU).reshape(B, S, B_HEADS, B_KEY_DIM)

    g_f = log_gate(lr_f, gw2_f, gb_f)
    g_b = log_gate(lr_b, gw2_b, gb_b)
    o_f = _gla_chunked(q, k, v, g_f)
    flip = lambda t: jnp.flip(t, axis=1)
    o_b = flip(_gla_chunked(flip(q), flip(k), flip(v), flip(g_b)))
    of = o_f.astype(jnp.float32) + o_b.astype(jnp.float32)
    of = of * lax.rsqrt(jnp.mean(jnp.square(of), axis=-1, keepdims=True) + HEAD_NORM_EPS)
    of = of * norm_g.astype(jnp.float32)
    o = of.astype(x.dtype).reshape(B, S, B_V_COLS) * jax.nn.silu(r)
    return o @ w_out


def _sqrelu_mlp(x, w1, w2):
    return jnp.square(jax.nn.relu(x @ w1)) @ w2


def setup_inputs(seed: int = 0) -> dict:
    key = jax.random.key(seed)
    ks = jax.random.split(key, 20)
    f32 = jnp.float32
    nrm = lambda kk, shape, s: jax.random.normal(kk, shape, f32) * s
    x = jax.random.normal(ks[0], (BATCH, SEQ, D_MODEL), f32)
    offs = jax.random.randint(ks[1], (BATCH, 1), 0, 4096, dtype=jnp.int32)
    positions = (jnp.arange(SEQ, dtype=jnp.int32)[None, :] + offs).astype(jnp.int32)
    d_inv = D_MODEL ** -0.5
    return {
        'x': x,
        'positions': positions,
        'attn_w_in': nrm(ks[2], (N_ATTN_LAYERS, D_MODEL, A_IN_COLS), d_inv),
        'attn_sink': nrm(ks[3], (N_ATTN_LAYERS, A_HEADS), 0.5),
        'attn_w_out': nrm(ks[4], (N_ATTN_LAYERS, A_HEADS * A_HEAD_DIM, D_MODEL), d_inv * DN_BETA),
        'gla_w_in': nrm(ks[5], (N_GLA_LAYERS, D_MODEL, B_IN_COLS), d_inv),
        'gla_gate_w2_fwd': nrm(ks[6], (N_GLA_LAYERS, B_GATE_RANK, B_QK_COLS), B_GATE_RANK ** -0.5),
        'gla_gate_b_fwd': nrm(ks[7], (N_GLA_LAYERS, B_QK_COLS), 0.02),
        'gla_gate_w2_bwd': nrm(ks[8], (N_GLA_LAYERS, B_GATE_RANK, B_QK_COLS), B_GATE_RANK ** -0.5),
        'gla_gate_b_bwd': nrm(ks[9], (N_GLA_LAYERS, B_QK_COLS), 0.02),
        'gla_norm_g': 1.0 + nrm(ks[10], (N_GLA_LAYERS, B_VAL_DIM), 0.02),
        'gla_w_out': nrm(ks[11], (N_GLA_LAYERS, B_V_COLS, D_MODEL), (B_V_COLS ** -0.5) * DN_BETA),
        'mix_ln_g': 1.0 + nrm(ks[12], (DEPTH, D_MODEL), 0.02),
        'mix_ln_b': nrm(ks[13], (DEPTH, D_MODEL), 0.02),
        'mlp_w1': nrm(ks[14], (DEPTH, D_MODEL, D_FF), d_inv),
        'mlp_w2': nrm(ks[15], (DEPTH, D_FF, D_MODEL), (D_FF ** -0.5) * DN_BETA),
        'mlp_ln_g': 1.0 + nrm(ks[16], (DEPTH, D_MODEL), 0.02),
        'mlp_ln_b': nrm(ks[17], (DEPTH, D_MODEL), 0.02),
    }


def reference(x, positions, attn_w_in, attn_sink, attn_w_out, gla_w_in, gla_gate_w2_fwd,
              gla_gate_b_fwd, gla_gate_w2_bwd, gla_gate_b_bwd, gla_norm_g, gla_w_out,
              mix_ln_g, mix_ln_b, mlp_w1, mlp_w2, mlp_ln_g, mlp_ln_b):
    cos, sin = _rope_tables(positions)
    for i in range(DEPTH):
        j = i // N_MIXERS
        if i % N_MIXERS == 0:
            y = _window_gqa(x, attn_w_in[j], attn_sink[j], attn_w_out[j], cos, sin)
        else:
            y = _bidir_gla(x, gla_w_in[j], gla_gate_w2_fwd[j], gla_gate_b_fwd[j],
                           gla_gate_w2_bwd[j], gla_gate_b_bwd[j], gla_norm_g[j], gla_w_out[j])
        x = _layer_norm(DN_ALPHA * x + y, mix_ln_g[i], mix_ln_b[i])
        x = _layer_norm(DN_ALPHA * x + _sqrelu_mlp(x, mlp_w1[i], mlp_w2[i]), mlp_ln_g[i], mlp_ln_b[i])
    return x
```

```python
import os
import numpy as np
from contextlib import ExitStack
import ml_dtypes
import concourse.bass as bass
import concourse.mybir as mybir
from concourse.bass_utils import run_bass_kernel_spmd

F32 = mybir.dt.float32
BF16 = mybir.dt.bfloat16
I32 = mybir.dt.int32
AF = mybir.ActivationFunctionType
ALU = mybir.AluOpType

P = 128
D = 1024
DFF = 4096
SEQ = 8192
NCORES = 8
DEPTH = 2
DN_ALPHA = float((2 * DEPTH) ** 0.25)
LN_EPS = 1e-5

SEM_EPOCH = 30000


class Buf:
    __slots__ = ("name", "lw", "rd", "rd_dma", "excl")

    def __init__(self, name, excl=False):
        self.name = name
        self.excl = excl
        self.lw = None
        self.rd = {}
        self.rd_dma = []


class Sched:
    def __init__(self, nc, ctx):
        self.nc = nc
        self.ctx = ctx
        self.E = {"pe": nc.tensor, "act": nc.scalar, "dve": nc.vector,
                  "pool": nc.gpsimd, "sp": nc.sync}
        self.cnt = {e: 0 for e in self.E}
        self.sems = {e: [] for e in self.E}
        self.waited = {e: {} for e in self.E}
        self.streams = {}
        self.nwaits = 0

    def buf(self, name):
        return Buf(name)

    def bufs(self, name, n):
        return [Buf(f"{name}{i}") for i in range(n)]

    def pbufs(self, name, n):
        return [Buf(f"{name}{i}", excl=True) for i in range(n)]

    def _eng_sem(self, eng, k):
        ep = (k - 1) // SEM_EPOCH
        while len(self.sems[eng]) <= ep:
            self.sems[eng].append(
                self.ctx.enter_context(self.nc.semaphore(f"s_{eng}{len(self.sems[eng])}")))
        return self.sems[eng][ep], (k - 1) % SEM_EPOCH + 1

    def _stream(self, name):
        st = self.streams.get(name)
        if st is None:
            st = {"n": 0, "sem": self.ctx.enter_context(self.nc.semaphore(f"d_{name}"))}
            self.streams[name] = st
        return st

    def _wait(self, eng, tok):
        w = self.waited[eng]
        if tok[0] == "c":
            _, peng, k = tok
            if peng == "pe" and eng == "pe":
                return
            if w.get(peng, 0) >= k:
                return
            w[peng] = k
            sem, val = self._eng_sem(peng, k)
        else:
            _, sname, n = tok
            key = "d:" + sname
            if w.get(key, 0) >= n:
                return
            w[key] = n
            sem, val = self.streams[sname]["sem"], 16 * n
        self.E[eng].wait_ge(sem, val)
        self.nwaits += 1

    def _deps(self, reads, writes, eng=None):
        deps = []
        for b in reads:
            if b.lw is not None:
                deps.append(b.lw)
            if b.excl:
                deps.extend(tok for e2, tok in b.rd.items() if e2 != eng)
        for b in writes:
            if b.lw is not None:
                deps.append(b.lw)
            deps.extend(b.rd.values())
            deps.extend(b.rd_dma)
        return deps

    def op(self, eng, fn, reads=(), writes=()):
        for tok in self._deps(reads, writes, eng):
            self._wait(eng, tok)
        ins = fn(self.E[eng])
        self.cnt[eng] += 1
        k = self.cnt[eng]
        sem, _ = self._eng_sem(eng, k)
        ins.then_inc(sem, 1)
        tok = ("c", eng, k)
        for b in writes:
            b.lw = tok
            b.rd = {}
            b.rd_dma = []
        for b in reads:
            if b.lw is not tok:
                b.rd[eng] = tok
        return tok

    def dma(self, q, out, in_, reads=(), writes=(), stream=None, **kw):
        st = self._stream(stream)
        deps = self._deps(reads, writes)
        if st["n"] > 0:
            deps.append(("d", stream, st["n"]))
        for tok in deps:
            self._wait(q, tok)
        ins = self.E[q].dma_start(out=out, in_=in_, **kw)
        ins.then_inc(st["sem"], 16)
        st["n"] += 1
        tok = ("d", stream, st["n"])
        for b in writes:
            b.lw = tok
            b.rd = {}
            b.rd_dma = []
        for b in reads:
            b.rd_dma.append(tok)
        return tok

    def barrier(self):
        for eng in self.E:
            for peng in self.E:
                if peng != eng and self.cnt[peng] > 0:
                    self._wait(eng, ("c", peng, self.cnt[peng]))
            for sname, st in self.streams.items():
                if st["n"] > 0:
                    self._wait(eng, ("d", sname, st["n"]))

    def finish(self):
        for sname, st in self.streams.items():
            if st["n"] > 0:
                self._wait("sp", ("d", sname, st["n"]))


def load_rowvec_bcast(S, q, dst, src_1d, n, stream, wbuf):
    S.dma(q, dst, src_1d.partition_broadcast(P), writes=[wbuf], stream=stream)


def ln_epilogue(S, sb, z, zb, g_t, b_t, gb_buf, out_t, out_b, eps=LN_EPS):
    stats, stats_b, mv, mv_b, sc, sc_b = sb
    S.op("dve", lambda e: (e.bn_stats(out=stats[:, 0, :], in_=z[:, 0:512]),
                           e.bn_stats(out=stats[:, 1, :], in_=z[:, 512:1024]))[-1],
         reads=[zb], writes=[stats_b])
    S.op("dve", lambda e: e.bn_aggr(out=mv[:, :], in_=stats[:, :, :].rearrange("p a b -> p (a b)")),
         reads=[stats_b], writes=[mv_b])
    S.op("act", lambda e: e.activation(out=sc[:, 0:1], in_=mv[:, 1:2], func=AF.Ln, bias=eps, scale=1.0),
         reads=[mv_b], writes=[sc_b])
    S.op("act", lambda e: e.activation(out=sc[:, 0:1], in_=sc[:, 0:1], func=AF.Exp, scale=-0.5),
         reads=[sc_b], writes=[sc_b])
    S.op("dve", lambda e: e.scalar_tensor_tensor(out=sc[:, 1:2], in0=mv[:, 0:1], scalar=-1.0,
                                                 in1=sc[:, 0:1], op0=ALU.mult, op1=ALU.mult),
         reads=[mv_b, sc_b], writes=[sc_b])
    S.op("act", lambda e: e.activation(out=out_t[:, :], in_=z[:, :], func=AF.Identity,
                                       bias=sc[:, 1:2], scale=sc[:, 0:1]),
         reads=[zb, sc_b], writes=[out_b])
    S.op("pool", lambda e: e.tensor_tensor(out=out_t[:, :], in0=out_t[:, :], in1=g_t[:, :], op=ALU.mult),
         reads=[out_b, gb_buf], writes=[out_b])
    S.op("pool", lambda e: e.tensor_tensor(out=out_t[:, :], in0=out_t[:, :], in1=b_t[:, :], op=ALU.add),
         reads=[out_b, gb_buf], writes=[out_b])


def mlp_phase(S, nc, NT, xin, xout, w1, w2, g, b, ident_d, xin_buf, xout_buf, tag):
    ST = 2
    NS = NT // ST
    TOK = ST * P
    with ExitStack() as ctx:
        sbt = lambda name, shape, dt: ctx.enter_context(nc.sbuf_tensor(f"{tag}_{name}", shape, dt))
        pst = lambda name, shape, dt: ctx.enter_context(nc.psum_tensor(f"{tag}_{name}", shape, dt))
        w1b = sbt("w1b", [P, 8, DFF], BF16)
        w2b = sbt("w2b", [P, 32, D], BF16)
        identf = sbt("identf", [P, P], F32)
        g_t = sbt("g", [P, D], F32)
        b_t = sbt("b", [P, D], F32)
        NXS = 4
        xs = [sbt(f"x{i}", [P, D], F32) for i in range(NXS)]
        xT = [sbt(f"xT{i}", [P, 8, TOK], BF16) for i in range(2)]
        hT = sbt("hT", [P, 32, TOK], BF16)
        rt = [sbt(f"rt{i}", [P, 512], F32) for i in range(2)]
        ot = [sbt(f"ot{i}", [P, D], F32) for i in range(2)]
        stats = [sbt(f"st{i}", [P, 2, 6], F32) for i in range(2)]
        mv = [sbt(f"mv{i}", [P, 2], F32) for i in range(2)]
        sc = [sbt(f"sc{i}", [P, 2], F32) for i in range(2)]
        ptr = [pst(f"ptr{i}", [P, 4, P], F32) for i in range(2)]
        pm1 = [pst(f"pm1{i}", [P, 512], F32) for i in range(2)]
        pm2 = [pst(f"pm2{i}", [P, 2, 512], F32) for i in range(2)]

        B = S.buf
        w1_b = [B(f"w1b{c}") for c in range(8)]
        w2_b = [B(f"w2b{c}") for c in range(8)]
        ident_b, gb_b = B("ident"), B("gb")
        xs_b = S.bufs("xs", NXS)
        xT_b = S.bufs("xT", 2)
        hT_b = [B(f"hT{j}") for j in range(16)]
        rt_b = S.bufs("rt", 2)
        ot_b = S.bufs("ot", 2)
        st_b, mv_b, sc_b = S.bufs("st", 2), S.bufs("mv", 2), S.bufs("sc", 2)
        ptr_b, pm1_b, pm2_b = S.pbufs("ptr", 2), S.pbufs("pm1", 2), S.pbufs("pm2", 2)

        S.dma("sp", identf[:, :], ident_d[:, :], writes=[ident_b], stream=f"{tag}c0")
        S.dma("sp", g_t[:, :], g.partition_broadcast(P), writes=[gb_b], stream=f"{tag}c1")
        S.dma("sp", b_t[:, :], b.partition_broadcast(P), writes=[gb_b], stream=f"{tag}c2")
        w1v = w1.rearrange("(c p) f -> p c f", p=P)
        for c in range(8):
            for hh in range(2):
                S.dma("pool", w1b[:, c, hh * 2048:(hh + 1) * 2048], w1v[:, c, hh * 2048:(hh + 1) * 2048],
                      writes=[w1_b[c]], stream=f"{tag}w1_{c % 2}")
        w2v = w2.rearrange("(c p) f -> p c f", p=P)
        for c in range(32):
            S.dma("pool", w2b[:, c, :], w2v[:, c, :], writes=[w2_b[c // 4]], stream=f"{tag}w2_{c % 2}")

        def load_x(t):
            sl = t % NXS
            S.dma("sp", xs[sl][:, :], xin[t * P:(t + 1) * P, :], reads=[xin_buf], writes=[xs_b[sl]],
                  stream=f"{tag}x{sl}")

        for t in range(min(NXS, NT)):
            load_x(t)

        for s in range(NS):
            xTs, xTs_b = xT[s % 2], xT_b[s % 2]
            for m in range(ST):
                t = s * ST + m
                sl = t % NXS
                for hh in range(2):
                    pt, pt_b = ptr[hh], ptr_b[hh]

                    def f_tr(e, sl=sl, hh=hh, pt=pt):
                        for c in range(4):
                            ins = e.transpose(out=pt[:, c, :], in_=xs[sl][:, (hh * 4 + c) * P:(hh * 4 + c + 1) * P],
                                              identity=identf[:, :])
                        return ins
                    S.op("pe", f_tr, reads=[xs_b[sl], ident_b], writes=[pt_b])
                    S.op("act" if hh == 0 else "dve",
                         (lambda e, pt=pt, hh=hh, m=m, xTs=xTs: e.copy(out=xTs[:, hh * 4:(hh + 1) * 4, m * P:(m + 1) * P], in_=pt[:, :, :]))
                         if hh == 0 else
                         (lambda e, pt=pt, hh=hh, m=m, xTs=xTs: e.tensor_copy(out=xTs[:, hh * 4:(hh + 1) * 4, m * P:(m + 1) * P], in_=pt[:, :, :])),
                         reads=[pt_b], writes=[xTs_b])
            for jj in range(16):
                pm, pm_b = pm1[jj % 2], pm1_b[jj % 2]

                def f_mm1(e, jj=jj, pm=pm, xTs=xTs):
                    for u in range(2):
                        j = jj * 2 + u
                        for c in range(8):
                            ins = e.matmul(pm[:, u * TOK:(u + 1) * TOK], lhsT=w1b[:, c, j * P:(j + 1) * P],
                                           rhs=xTs[:, c, :], start=(c == 0), stop=(c == 7))
                    return ins
                S.op("pe", f_mm1, reads=[xTs_b] + w1_b, writes=[pm_b])
                r, r_b = rt[jj % 2], rt_b[jj % 2]
                S.op("act", lambda e, pm=pm, r=r: e.activation(out=r[:, :], in_=pm[:, :], func=AF.Relu),
                     reads=[pm_b], writes=[r_b])
                S.op("dve", lambda e, r=r, jj=jj: e.tensor_tensor(
                    out=hT[:, 2 * jj:2 * jj + 2, :].rearrange("p a t -> p (a t)"), in0=r[:, :], in1=r[:, :], op=ALU.mult),
                     reads=[r_b], writes=[hT_b[jj]])
            for m in range(ST):
                t = s * ST + m
                sl = t % NXS
                k2 = t % 2
                py, py_b = pm2[k2], pm2_b[k2]

                def f_mm2(e, m=m, py=py):
                    for n in range(2):
                        for j in range(32):
                            ins = e.matmul(py[:, n, :], lhsT=hT[:, j, m * P:(m + 1) * P],
                                           rhs=w2b[:, j, n * 512:(n + 1) * 512], start=(j == 0), stop=(j == 31))
                    return ins
                S.op("pe", f_mm2, reads=hT_b + w2_b, writes=[py_b])
                S.op("dve", lambda e, sl=sl, py=py: e.scalar_tensor_tensor(
                    out=xs[sl][:, :], in0=xs[sl][:, :], scalar=DN_ALPHA,
                    in1=py[:, :, :].rearrange("p a b -> p (a b)"), op0=ALU.mult, op1=ALU.add),
                     reads=[xs_b[sl], py_b], writes=[xs_b[sl]])
                ln_epilogue(S, (stats[k2], st_b[k2], mv[k2], mv_b[k2], sc[k2], sc_b[k2]),
                            xs[sl], xs_b[sl], g_t, b_t, gb_b, ot[k2], ot_b[k2])
                S.dma("sp", xout[t * P:(t + 1) * P, :], ot[k2][:, :], reads=[ot_b[k2]], writes=[xout_buf],
                      stream=f"{tag}o{k2}")
                if t + NXS < NT:
                    load_x(t + NXS)
        S.barrier()


def build_test_mlp(NT):
    nc = bass.Bass("TRN2", target_bir_lowering=False)
    x = nc.dram_tensor("x", [NT * P, D], F32, kind="ExternalInput").ap()
    w1 = nc.dram_tensor("w1", [D, DFF], F32, kind="ExternalInput").ap()
    w2 = nc.dram_tensor("w2", [DFF, D], F32, kind="ExternalInput").ap()
    g = nc.dram_tensor("g", [D], F32, kind="ExternalInput").ap()
    b = nc.dram_tensor("b", [D], F32, kind="ExternalInput").ap()
    ident = nc.dram_tensor("ident", [P, P], F32, kind="ExternalInput").ap()
    y = nc.dram_tensor("y", [NT * P, D], F32, kind="ExternalOutput").ap()
    with ExitStack() as ctx:
        S = Sched(nc, ctx)
        mlp_phase(S, nc, NT, x, y, w1, w2, g, b, ident, S.buf("xin"), S.buf("xout"), "m0")
        S.finish()
    return nc


A_HEADS = 16
A_KV = 4
HD = 64
A_IN = 1536
TWO_PI = 2.0 * np.pi
CW1 = 6.28125
CW2 = float(TWO_PI - 6.28125)
PI_LO = 3.1415925
Q_HEAD_ORDER = [0, 4, 1, 5, 2, 6, 3, 7, 8, 12, 9, 13, 10, 14, 11, 15]


def bcast(ap, shape):
    return ap.broadcast_to(list(shape))


def attn_phase(S, nc, NT, xin, xout, pos, w_in, sink, w_out, g, b, consts, xin_buf, xout_buf, tag):
    ident_d, maskp_d, maskn_d, invf_d = consts
    scale = HD ** -0.5
    with ExitStack() as ctx:
        sbt = lambda name, shape, dt: ctx.enter_context(nc.sbuf_tensor(f"{tag}_{name}", shape, dt))
        pst = lambda name, shape, dt: ctx.enter_context(nc.psum_tensor(f"{tag}_{name}", shape, dt))
        B = S.buf
        w_inb = sbt("w_inb", [P, 8, A_IN], BF16)
        w_outb = sbt("w_outb", [P, 8, D], BF16)
        identf = sbt("identf", [P, P], F32)
        identb = sbt("identb", [P, P], BF16)
        maskp = sbt("maskp", [P, 512], BF16)
        maskn = sbt("maskn", [P, 512], BF16)
        g_t = sbt("g", [P, D], F32)
        b_t = sbt("b", [P, D], F32)
        esink = sbt("esink", [P, 16], F32)
        cos_t = sbt("cos", [P, NT, 32], F32)
        sin_t = sbt("sin", [P, NT, 32], F32)
        w_in_b, w_out_b = B("w_in"), B("w_out")
        ident_b, identb_b, mask_b, gb_b, esink_b, cs_b = B("id"), B("idb"), B("mask"), B("gb"), B("esink"), B("cs")

        pmm = [pst(f"pmm{i}", [P, 512], F32) for i in range(2)]
        ptb = pst("ptb", [P, 8, P], BF16)
        psc = [pst(f"psc{i}", [P, 512], F32) for i in range(2)]
        pov = [pst(f"pov{i}", [P, 512], F32) for i in range(3)]
        pmm_b, psc_b, pov_b, ptb_b = S.pbufs("pmm", 2), S.pbufs("psc", 2), S.pbufs("pov", 3), S.pbufs("ptb", 1)[0]

        S.dma("sp", identf[:, :], ident_d[:, :], writes=[ident_b], stream=f"{tag}c0")
        S.dma("sp", g_t[:, :], g.partition_broadcast(P), writes=[gb_b], stream=f"{tag}c1")
        S.dma("sp", b_t[:, :], b.partition_broadcast(P), writes=[gb_b], stream=f"{tag}c2")
        S.dma("sp", esink[:, :], sink.partition_broadcast(P), writes=[esink_b], stream=f"{tag}c3")
        S.dma("pool", maskp[:, :], maskp_d[:, :], writes=[mask_b], stream=f"{tag}c4")
        S.dma("pool", maskn[:, :], maskn_d[:, :], writes=[mask_b], stream=f"{tag}c5")
        w_inv = w_in.rearrange("(c p) f -> p c f", p=P)
        for c in range(8):
            S.dma("pool", w_inb[:, c, :], w_inv[:, c, :], writes=[w_in_b], stream=f"{tag}w{c % 2}")
        w_outv = w_out.rearrange("(c p) f -> p c f", p=P)
        for c in range(8):
            S.dma("pool", w_outb[:, c, :], w_outv[:, c, :], writes=[w_out_b], stream=f"{tag}w{c % 2}")
        S.op("dve", lambda e: e.tensor_copy(out=identb[:, :], in_=identf[:, :]), reads=[ident_b], writes=[identb_b])
        S.op("act", lambda e: e.activation(out=esink[:, :], in_=esink[:, :], func=AF.Exp),
             reads=[esink_b], writes=[esink_b])

        with ExitStack() as c2:
            sb2 = lambda name, shape, dt: c2.enter_context(nc.sbuf_tensor(f"{tag}_{name}", shape, dt))
            posi = sb2("posi", [NT, P], I32)
            posf = sb2("posf", [NT, P], F32)
            posT = sb2("posT", [P, NT], F32)
            invf = sb2("invf", [P, 32], F32)
            ang = sb2("ang", [P, NT, 32], F32)
            u = sb2("u", [P, NT, 32], F32)
            ki = sb2("ki", [P, NT, 32], I32)
            kf = sb2("kf", [P, NT, 32], F32)
            posi_b, posf_b, posT_b, invf_b, ang_b, u_b, ki_b, kf_b = S.bufs("rp", 8)
            S.dma("sp", posi[:, :], pos.rearrange("(t p) -> t p", p=P), writes=[posi_b], stream=f"{tag}c6")
            S.dma("sp", invf[:, :], invf_d.partition_broadcast(P), writes=[invf_b], stream=f"{tag}c7")
            S.op("dve", lambda e: e.tensor_copy(out=posf[:, :], in_=posi[:, :]), reads=[posi_b], writes=[posf_b])
            S.op("pe", lambda e: e.transpose(out=pmm[0][:, 0:NT], in_=posf[:, :], identity=identf[0:NT, 0:NT]),
                 reads=[posf_b, ident_b], writes=[pmm_b[0]])
            S.op("dve", lambda e: e.tensor_copy(out=posT[:, :], in_=pmm[0][:, 0:NT]), reads=[pmm_b[0]], writes=[posT_b])
            S.op("dve", lambda e: e.tensor_tensor(out=ang[:, :, :], in0=bcast(posT[:, :].unsqueeze(2), [P, NT, 32]),
                                                  in1=bcast(invf[:, :].unsqueeze(1), [P, NT, 32]), op=ALU.mult),
                 reads=[posT_b, invf_b], writes=[ang_b])
            for which, off, dst in (("sin", 0.0, sin_t), ("cos", 0.25, cos_t)):
                S.op("dve", lambda e, off=off: e.tensor_scalar(out=u[:, :, :], in0=ang[:, :, :], scalar1=float(1.0 / TWO_PI),
                                                               scalar2=off, op0=ALU.mult, op1=ALU.add),
                     reads=[ang_b], writes=[u_b])
                S.op("dve", lambda e: e.tensor_copy(out=ki[:, :, :], in_=u[:, :, :]), reads=[u_b], writes=[ki_b])
                S.op("dve", lambda e: e.tensor_copy(out=kf[:, :, :], in_=ki[:, :, :]), reads=[ki_b], writes=[kf_b])
                S.op("dve", lambda e: e.scalar_tensor_tensor(out=u[:, :, :], in0=kf[:, :, :], scalar=-CW1, in1=ang[:, :, :],
                                                             op0=ALU.mult, op1=ALU.add),
                     reads=[kf_b, ang_b], writes=[u_b])
                S.op("dve", lambda e: e.scalar_tensor_tensor(out=u[:, :, :], in0=kf[:, :, :], scalar=-CW2, in1=u[:, :, :],
                                                             op0=ALU.mult, op1=ALU.add),
                     reads=[kf_b, u_b], writes=[u_b])
                S.op("dve", lambda e, off=off: e.tensor_scalar(out=u[:, :, :], in0=u[:, :, :], scalar1=float(off * TWO_PI),
                                                               scalar2=PI_LO, op0=ALU.add, op1=ALU.min),
                     reads=[u_b], writes=[u_b])
                S.op("dve", lambda e: e.tensor_scalar(out=u[:, :, :], in0=u[:, :, :], scalar1=-PI_LO, scalar2=None, op0=ALU.max),
                     reads=[u_b], writes=[u_b])
                S.op("act", lambda e, dst=dst: e.activation(out=dst[:, :, :], in_=u[:, :, :], func=AF.Sin),
                     reads=[u_b], writes=[cs_b])
            S.barrier()

        NXS = 4
        xs = [sbt(f"x{i}", [P, D], F32) for i in range(NXS)]
        xT = [sbt(f"xT{i}", [P, 8, P], BF16) for i in range(2)]
        ra = [sbt(f"ra{i}", [P, 8, 2, 32], F32) for i in range(2)]
        rb = [sbt(f"rb{i}", [P, 8, 2, 32], F32) for i in range(2)]
        qr = [sbt(f"qr{i}", [P, 16, 2, 32], BF16) for i in range(2)]
        kr = [sbt(f"kr{i}", [P, 4, 2, 32], BF16) for i in range(2)]
        qT = [sbt(f"qT{i}", [P, 8, P], BF16) for i in range(2)]
        kTw = [sbt(f"kT{i}", [P, 2, P], BF16) for i in range(4)]
        Vw = [sbt(f"V{i}", [P, 4, 65], BF16) for i in range(4)]
        pT = [sbt(f"pT{i}", [P, 512], BF16) for i in range(3)]
        den = [sbt(f"den{i}", [P, 16], F32) for i in range(2)]
        on = [sbt(f"on{i}", [P, 16, 64], BF16) for i in range(2)]
        oT = [sbt(f"oT{i}", [P, 8, P], BF16) for i in range(2)]
        ot = [sbt(f"ot{i}", [P, D], F32) for i in range(2)]
        stats = [sbt(f"st{i}", [P, 2, 6], F32) for i in range(2)]
        mv = [sbt(f"mv{i}", [P, 2], F32) for i in range(2)]
        sc = [sbt(f"sc{i}", [P, 2], F32) for i in range(2)]
        xs_b, xT_b = S.bufs("xs", NXS), S.bufs("xT", 2)
        ra_b, rb_b, qr_b, kr_b, qT_b = S.bufs("ra", 2), S.bufs("rb", 2), S.bufs("qr", 2), S.bufs("kr", 2), S.bufs("qT", 2)
        kT_b, V_b, pT_b = S.bufs("kT", 4), S.bufs("V", 4), S.bufs("pT", 3)
        den_b, on_b, oT_b, ot_b = S.bufs("den", 2), S.bufs("on", 2), S.bufs("oT", 2), S.bufs("ot", 2)
        st_b, mv_b, sc_b = S.bufs("st", 2), S.bufs("mv", 2), S.bufs("sc", 2)
        for i in range(4):
            S.op("pool", lambda e, i=i: e.memset(Vw[i][:, :, :], 1.0), writes=[V_b[i]])

        cnt = {"mm": 0, "sc": 0, "pt": 0, "rr": 0}

        def nxt(key, n):
            v = cnt[key] % n
            cnt[key] += 1
            return v

        def load_x(t):
            sl = t % NXS
            S.dma("sp", xs[sl][:, :], xin[t * P:(t + 1) * P, :], reads=[xin_buf], writes=[xs_b[sl]],
                  stream=f"{tag}x{sl}")

        def rope(pm, pm_bf, nh, t, dst, dst_bf, h0):
            r = nxt("rr", 2)
            pm4 = pm[:, 0:nh * 64].rearrange("p (h two d) -> p h two d", two=2, d=32)
            cb = bcast(cos_t[:, t:t + 1, :].unsqueeze(1), [P, nh, 2, 32])
            sb_ = bcast(sin_t[:, t:t + 1, :].unsqueeze(1), [P, nh, 2, 32])
            S.op("dve", lambda e: e.tensor_tensor(out=ra[r][:, 0:nh, :, :], in0=pm4, in1=cb, op=ALU.mult),
                 reads=[pm_bf, cs_b], writes=[ra_b[r]])
            S.op("dve", lambda e: e.tensor_tensor(out=rb[r][:, 0:nh, :, :], in0=pm4, in1=sb_, op=ALU.mult),
                 reads=[pm_bf, cs_b], writes=[rb_b[r]])
            S.op("pool", lambda e: e.tensor_tensor(out=dst[:, h0:h0 + nh, 0, :], in0=ra[r][:, 0:nh, 0, :],
                                                   in1=rb[r][:, 0:nh, 1, :], op=ALU.subtract),
                 reads=[ra_b[r], rb_b[r]], writes=[dst_bf])
            S.op("pool", lambda e: e.tensor_tensor(out=dst[:, h0:h0 + nh, 1, :], in0=ra[r][:, 0:nh, 1, :],
                                                   in1=rb[r][:, 0:nh, 0, :], op=ALU.add),
                 reads=[ra_b[r], rb_b[r]], writes=[dst_bf])

        def stage1(t):
            sl, p2, p4 = t % NXS, t % 2, t % 4
            for hh in range(2):
                m = nxt("mm", 2)

                def f_tr(e, hh=hh, m=m):
                    for c in range(4):
                        ins = e.transpose(out=pmm[m][:, c * P:(c + 1) * P],
                                          in_=xs[sl][:, (hh * 4 + c) * P:(hh * 4 + c + 1) * P], identity=identf[:, :])
                    return ins
                S.op("pe", f_tr, reads=[xs_b[sl], ident_b], writes=[pmm_b[m]])
                src = pmm[m][:, :].rearrange("p (c t) -> p c t", c=4)
                if hh == 0:
                    S.op("act", lambda e, src=src: e.copy(out=xT[p2][:, 0:4, :], in_=src), reads=[pmm_b[m]], writes=[xT_b[p2]])
                else:
                    S.op("dve", lambda e, src=src: e.tensor_copy(out=xT[p2][:, 4:8, :], in_=src), reads=[pmm_b[m]], writes=[xT_b[p2]])
            for n in range(3):
                m = nxt("mm", 2)

                def f_in(e, n=n, m=m):
                    for c in range(8):
                        ins = e.matmul(pmm[m][:, :], lhsT=xT[p2][:, c, :], rhs=w_inb[:, c, n * 512:(n + 1) * 512],
                                       start=(c == 0), stop=(c == 7))
                    return ins
                S.op("pe", f_in, reads=[xT_b[p2], w_in_b], writes=[pmm_b[m]])
                if n < 2:
                    rope(pmm[m], pmm_b[m], 8, t, qr[p2], qr_b[p2], n * 8)
                else:
                    rope(pmm[m], pmm_b[m], 4, t, kr[p2], kr_b[p2], 0)
                    S.op("act", lambda e, m=m: e.copy(out=Vw[p4][:, :, 0:64],
                                                      in_=pmm[m][:, 256:512].rearrange("p (h d) -> p h d", d=64)),
                         reads=[pmm_b[m]], writes=[V_b[p4]])
            def f_qt(e):
                qflat = qr[p2][:, :, :, :].rearrange("p h two d -> p (h two d)")
                for s_ in range(8):
                    ins = e.transpose(out=ptb[:, s_, :], in_=qflat[:, s_ * P:(s_ + 1) * P], identity=identb[:, :])
                return ins
            S.op("pe", f_qt, reads=[qr_b[p2], identb_b], writes=[ptb_b])
            S.op("act", lambda e: e.copy(out=qT[p2][:, :, :], in_=ptb[:, :, :]), reads=[ptb_b], writes=[qT_b[p2]])

            def f_kt(e):
                kflat = kr[p2][:, :, :, :].rearrange("p h two d -> p (h two d)")
                for s_ in range(2):
                    ins = e.transpose(out=ptb[:, s_, :], in_=kflat[:, s_ * P:(s_ + 1) * P], identity=identb[:, :])
                return ins
            S.op("pe", f_kt, reads=[kr_b[p2], identb_b], writes=[ptb_b])
            S.op("dve", lambda e: e.tensor_copy(out=kTw[p4][:, :, :], in_=ptb[:, 0:2, :]), reads=[ptb_b], writes=[kT_b[p4]])

        def stage2(i):
            sl, p2 = i % NXS, i % 2
            kts = [kt for kt in (i - 1, i, i + 1) if 0 <= kt < NT]
            bank_started = [False, False, False]
            last_in_bank = {0: 5, 1: 11, 2: 15}
            for kv in range(4):
                base, ch = 64 * (kv % 2), kv // 2
                for kt in kts:
                    s_ = nxt("sc", 2)

                    def f_sc(e, kt=kt, s_=s_):
                        if kt != i:
                            e.matmul(psc[s_][:, :], lhsT=identb[:, :], rhs=(maskp if kt < i else maskn)[:, :],
                                     start=True, stop=False)
                        return e.matmul(psc[s_][:, :], lhsT=kTw[kt % 4][base:base + 64, ch, :],
                                        rhs=qT[p2][base:base + 64, 4 * ch:4 * ch + 4, :],
                                        start=(kt == i), stop=True)
                    S.op("pe", f_sc, reads=[kT_b[kt % 4], qT_b[p2], identb_b, mask_b], writes=[psc_b[s_]])
                    pi_ = nxt("pt", 3)
                    S.op("act", lambda e, s_=s_, pi_=pi_: e.activation(out=pT[pi_][:, :], in_=psc[s_][:, :], func=AF.Exp,
                                                                      scale=float(scale)),
                         reads=[psc_b[s_]], writes=[pT_b[pi_]])

                    def f_pv(e, kt=kt, pi_=pi_):
                        for g_ in range(4):
                            h = kv * 4 + g_
                            bk, col = h // 6, (h % 6) * 65
                            st_ = not bank_started[bk]
                            bank_started[bk] = True
                            ins = e.matmul(pov[bk][:, col:col + 65], lhsT=pT[pi_][:, g_ * P:(g_ + 1) * P],
                                           rhs=Vw[kt % 4][:, kv, :], start=st_,
                                           stop=(h == last_in_bank[bk] and kt == kts[-1]), skip_group_check=True)
                        return ins
                    banks = sorted({(kv * 4 + g_) // 6 for g_ in range(4)})
                    S.op("pe", f_pv, reads=[pT_b[pi_], V_b[kt % 4]], writes=[pov_b[bk] for bk in banks])
            for bk in range(3):
                nh = 6 if bk < 2 else 4
                h0 = bk * 6
                pv3 = pov[bk][:, 0:nh * 65].rearrange("p (h d) -> p h d", d=65)
                S.op("dve", lambda e, pv3=pv3, h0=h0, nh=nh: e.tensor_tensor(
                    out=den[p2][:, h0:h0 + nh].unsqueeze(2), in0=pv3[:, :, 64:65],
                    in1=esink[:, h0:h0 + nh].unsqueeze(2), op=ALU.add),
                     reads=[pov_b[bk], esink_b], writes=[den_b[p2]])
                S.op("dve", lambda e, h0=h0, nh=nh: e.reciprocal(out=den[p2][:, h0:h0 + nh], in_=den[p2][:, h0:h0 + nh]),
                     reads=[den_b[p2]], writes=[den_b[p2]])
                S.op("dve", lambda e, pv3=pv3, h0=h0, nh=nh: e.tensor_tensor(
                    out=on[p2][:, h0:h0 + nh, :], in0=pv3[:, :, 0:64],
                    in1=bcast(den[p2][:, h0:h0 + nh].unsqueeze(2), [P, nh, 64]), op=ALU.mult),
                     reads=[pov_b[bk], den_b[p2]], writes=[on_b[p2]])

            def f_ot(e):
                oflat = on[p2][:, :, :].rearrange("p h d -> p (h d)")
                for s_ in range(8):
                    ins = e.transpose(out=ptb[:, s_, :], in_=oflat[:, s_ * P:(s_ + 1) * P], identity=identb[:, :])
                return ins
            S.op("pe", f_ot, reads=[on_b[p2], identb_b], writes=[ptb_b])
            S.op("act", lambda e: e.copy(out=oT[p2][:, :, :], in_=ptb[:, :, :]), reads=[ptb_b], writes=[oT_b[p2]])
            for n in range(2):
                m = nxt("mm", 2)

                def f_op(e, n=n, m=m):
                    for c in range(8):
                        ins = e.matmul(pmm[m][:, :], lhsT=oT[p2][:, c, :], rhs=w_outb[:, c, n * 512:(n + 1) * 512],
                                       start=(c == 0), stop=(c == 7))
                    return ins
                S.op("pe", f_op, reads=[oT_b[p2], w_out_b], writes=[pmm_b[m]])
                S.op("dve", lambda e, n=n, m=m: e.scalar_tensor_tensor(
                    out=xs[sl][:, n * 512:(n + 1) * 512], in0=xs[sl][:, n * 512:(n + 1) * 512], scalar=DN_ALPHA,
                    in1=pmm[m][:, :], op0=ALU.mult, op1=ALU.add),
                     reads=[xs_b[sl], pmm_b[m]], writes=[xs_b[sl]])
            ln_epilogue(S, (stats[p2], st_b[p2], mv[p2], mv_b[p2], sc[p2], sc_b[p2]),
                        xs[sl], xs_b[sl], g_t, b_t, gb_b, ot[p2], ot_b[p2])
            S.dma("sp", xout[i * P:(i + 1) * P, :], ot[p2][:, :], reads=[ot_b[p2]], writes=[xout_buf],
                  stream=f"{tag}o{p2}")
            if i + NXS < NT:
                load_x(i + NXS)

        lvl = int(os.environ.get("ATT_DBG", "9"))
        for t in range(min(NXS, NT)):
            load_x(t)
        for t in range(NT + 1):
            if t < NT and lvl >= 2:
                stage1(t)
            if t >= 1 and lvl >= 3:
                stage2(t - 1)
        S.barrier()


def attn_consts_host():
    j = np.arange(P)[:, None]
    i = np.arange(P)[None, :]
    mp = np.where(j >= i, 0.0, -30000.0).astype(np.float32)
    mn = np.where(j <= i, 0.0, -30000.0).astype(np.float32)
    invf = (10000.0 ** (-np.arange(0, HD, 2, dtype=np.float64) / HD)).astype(np.float32)
    return {"c_ident": np.eye(P, dtype=np.float32), "c_maskp": np.tile(mp, (1, 4)), "c_maskn": np.tile(mn, (1, 4)),
            "c_invf": invf}


def permute_attn_w_in(w_in):
    q = w_in[:, :1024].reshape(1024, 16, 64)[:, Q_HEAD_ORDER, :].reshape(1024, 1024)
    return np.ascontiguousarray(np.concatenate([q, w_in[:, 1024:]], axis=1))


def build_test_attn(NT):
    nc = bass.Bass("TRN2", target_bir_lowering=False)
    dt = lambda name, shape, dtp=F32, kind="ExternalInput": nc.dram_tensor(name, shape, dtp, kind=kind).ap()
    x = dt("x", [NT * P, D])
    pos = dt("pos", [NT * P], I32)
    w_in = dt("w_in", [D, A_IN])
    sink = dt("sink", [16])
    w_out = dt("w_out", [D, D])
    g = dt("g", [D])
    b = dt("b", [D])
    consts = (dt("c_ident", [P, P]), dt("c_maskp", [P, 512]), dt("c_maskn", [P, 512]), dt("c_invf", [32]))
    y = dt("y", [NT * P, D], kind="ExternalOutput")
    with ExitStack() as ctx:
        S = Sched(nc, ctx)
        attn_phase(S, nc, NT, x, y, pos, w_in, sink, w_out, g, b, consts, S.buf("xin"), S.buf("xout"), "a0")
        S.finish()
    return nc


G_H = 4
G_DK = 128
G_DV = 256
G_IN = 3104
G_TAU = 16.0


class PsumPool:
    def __init__(self, S, nc, ctx, tag, n=8):
        self.t = [ctx.enter_context(nc.psum_tensor(f"{tag}_pp{i}", [P, 512], F32)) for i in range(n)]
        self.b = S.pbufs(f"{tag}pp", n)
        self.i = 0
        self.n = n

    def get(self):
        k = self.i % self.n
        self.i += 1
        return self.t[k], self.b[k]


def gla_sweep1(S, nc, NT, xin, w_in, gw2f, gbf, gw2b, gbb, consts, scr, xin_buf, scr_buf, tag):
    ident_d, tri_d, negcol_d, gmask_d = consts
    with ExitStack() as ctx:
        sbt = lambda name, shape, dt: ctx.enter_context(nc.sbuf_tensor(f"{tag}_{name}", shape, dt))
        B = S.buf
        pp = PsumPool(S, nc, ctx, tag)
        w_inb = sbt("w_inb", [P, 8, 2080], BF16)
        w2a = sbt("w2a", [17, 2, 512], BF16)
        tri = sbt("tri", [P, 4, P], F32)
        negcol = sbt("negcol", [P, 1], F32)
        gmask = sbt("gmask", [P, 2, P], F32)
        identf = sbt("identf", [P, P], F32)
        identb = sbt("identb", [P, P], BF16)
        S32 = sbt("S32", [P, 4, G_DV], F32)
        Sb = sbt("Sb", [P, 4, G_DV], BF16)
        w_in_b, w2a_b, cst_b, identb_b, S32_b, Sb_b = B("w_in"), B("w2a"), B("cst"), B("idb"), B("S32"), B("Sb")
        S.dma("sp", identf[:, :], ident_d[:, :], writes=[cst_b], stream=f"{tag}c0")
        S.dma("sp", tri[:, :, :], tri_d.rearrange("k p t -> p k t"), writes=[cst_b], stream=f"{tag}c1")
        S.dma("sp", negcol[:, :], negcol_d[:, :], writes=[cst_b], stream=f"{tag}c2")
        S.dma("sp", gmask[:, :, :], gmask_d.rearrange("k p t -> p k t"), writes=[cst_b], stream=f"{tag}c3")
        w_inv = w_in.rearrange("(c p) f -> p c f", p=P)
        for c in range(8):
            S.dma("pool", w_inb[:, c, 0:2048], w_inv[:, c, 0:2048], writes=[w_in_b], stream=f"{tag}w{c % 2}")
        for c in range(8):
            S.dma("pool", w_inb[:, c, 2048:2080], w_inv[:, c, 3072:3104], writes=[w_in_b], stream=f"{tag}w{c % 2}")
        for d_, (w2, gb_) in enumerate(((gw2f, gbf), (gw2b, gbb))):
            S.dma("pool", w2a[0:16, d_, :], w2[:, :], writes=[w2a_b], stream=f"{tag}w0")
            S.dma("pool", w2a[16:17, d_, :], gb_.rearrange("(o f) -> o f", o=1), writes=[w2a_b], stream=f"{tag}w1")
        S.op("dve", lambda e: e.tensor_copy(out=identb[:, :], in_=identf[:, :]), reads=[cst_b], writes=[identb_b])
        S.op("pool", lambda e: e.memset(S32[:, :, :], 0.0), writes=[S32_b])
        S.op("pool", lambda e: e.memset(Sb[:, :, :], 0.0), writes=[Sb_b])

        NXS = 3
        xs = [sbt(f"x{i}", [P, D], F32) for i in range(NXS)]
        xT = [sbt(f"xT{i}", [P, 8, P], BF16) for i in range(2)]
        q_sb = [sbt(f"q{i}", [P, 512], F32) for i in range(2)]
        k_sb = [sbt(f"k{i}", [P, 512], F32) for i in range(2)]
        v_sb = [sbt(f"v{i}", [P, 1024], BF16) for i in range(2)]
        lrT = [sbt(f"lrT{i}", [17, 2, P], BF16) for i in range(2)]
        sp = [sbt(f"sp{i}", [P, 512], F32) for i in range(2)]
        dec = [[sbt(f"dec{i}_{d_}", [P, 4], F32) for d_ in range(2)] for i in range(2)]
        tb = [[sbt(f"tb{d_}_{j}", [P, 512], F32) for j in range(3)] for d_ in range(2)]
        qe = [[sbt(f"qe{i}_{d_}", [P, 512], BF16) for d_ in range(2)] for i in range(2)]
        ke = [[sbt(f"ke{i}_{d_}", [P, 512], BF16) for d_ in range(2)] for i in range(2)]
        kd = [[sbt(f"kd{i}_{d_}", [P, 512], BF16) for d_ in range(2)] for i in range(2)]
        qkT = [[sbt(f"qkT{i}_{d_}", [P, 8, P], BF16) for d_ in range(2)] for i in range(2)]
        attm = [[sbt(f"attm{i}_{d_}", [P, 4, P], BF16) for d_ in range(2)] for i in range(2)]
        opart = [sbt(f"op{i}", [P, 1024], F32) for i in range(2)]
        xs_b, xT_b, q_b, k_b, v_b, lrT_b = (S.bufs(n_, k_) for n_, k_ in
                                           (("xs", NXS), ("xT", 2), ("q", 2), ("k", 2), ("v", 2), ("lrT", 2)))
        sp_b = S.bufs("sp", 2)
        dec_b = [S.bufs(f"dec{i}", 2) for i in range(2)]
        tb_b = [S.bufs(f"tb{d_}", 3) for d_ in range(2)]
        qe_b, ke_b, kd_b, qkT_b, attm_b = ([S.bufs(f"{n_}{i}", 2) for i in range(2)] for n_ in ("qe", "ke", "kd", "qkT", "attm"))
        op_b = S.bufs("op", 2)
        for i in range(2):
            S.op("pool", lambda e, i=i: e.memset(lrT[i][:, :, :], 1.0), writes=[lrT_b[i]])

        def load_x(t):
            sl = t % NXS
            S.dma("sp", xs[sl][:, :], xin[t * P:(t + 1) * P, :], reads=[xin_buf], writes=[xs_b[sl]], stream=f"{tag}x{sl}")

        def mm8(out_ap, p2, c0, c1, wr_b):
            def f(e):
                for c in range(8):
                    ins = e.matmul(out_ap, lhsT=xT[p2][:, c, :], rhs=w_inb[:, c, c0:c1], start=(c == 0), stop=(c == 7))
                return ins
            S.op("pe", f, reads=[xT_b[p2], w_in_b], writes=[wr_b])

        for t in range(min(NXS, NT)):
            load_x(t)
        for t in range(NT):
            sl, p2 = t % NXS, t % 2
            for hh in range(2):
                pm, pm_b = pp.get()

                def f_tr(e, hh=hh, pm=pm):
                    for c in range(4):
                        ins = e.transpose(out=pm[:, c * P:(c + 1) * P], in_=xs[sl][:, (hh * 4 + c) * P:(hh * 4 + c + 1) * P],
                                          identity=identf[:, :])
                    return ins
                S.op("pe", f_tr, reads=[xs_b[sl], cst_b], writes=[pm_b])
                src = pm[:, :].rearrange("p (c t) -> p c t", c=4)
                if hh == 0:
                    S.op("act", lambda e, src=src: e.copy(out=xT[p2][:, 0:4, :], in_=src), reads=[pm_b], writes=[xT_b[p2]])
                else:
                    S.op("dve", lambda e, src=src: e.tensor_copy(out=xT[p2][:, 4:8, :], in_=src), reads=[pm_b], writes=[xT_b[p2]])
            if t + NXS < NT:
                pass
            pm, pm_b = pp.get()

            def f_lr(e, pm=pm):
                for d_ in range(2):
                    for c in range(8):
                        ins = e.matmul(pm[0:16, d_ * P:(d_ + 1) * P], lhsT=w_inb[:, c, 2048 + 16 * d_:2064 + 16 * d_],
                                       rhs=xT[p2][:, c, :], start=(c == 0), stop=(c == 7))
                return ins
            S.op("pe", f_lr, reads=[xT_b[p2], w_in_b], writes=[pm_b])
            S.op("dve", lambda e, pm=pm: e.tensor_copy(out=lrT[p2][0:16, :, :], in_=pm[0:16, 0:256].rearrange("p (a t) -> p a t", a=2)),
                 reads=[pm_b], writes=[lrT_b[p2]])
            pm, pm_b = pp.get()
            mm8(pm[:, :], p2, 0, 512, pm_b)
            S.op("act", lambda e, pm=pm: e.activation(out=q_sb[p2][:, :], in_=pm[:, :], func=AF.Copy, scale=float(G_DK ** -0.5)),
                 reads=[pm_b], writes=[q_b[p2]])
            pm, pm_b = pp.get()
            mm8(pm[:, :], p2, 512, 1024, pm_b)
            S.op("dve", lambda e, pm=pm: e.tensor_copy(out=k_sb[p2][:, :], in_=pm[:, :]), reads=[pm_b], writes=[k_b[p2]])
            for n in range(2):
                pm, pm_b = pp.get()
                mm8(pm[:, :], p2, 1024 + n * 512, 1536 + n * 512, pm_b)
                S.op("act", lambda e, pm=pm, n=n: e.copy(out=v_sb[p2][:, n * 512:(n + 1) * 512], in_=pm[:, :]),
                     reads=[pm_b], writes=[v_b[p2]])
            for d_ in range(2):
                pm, pm_b = pp.get()
                S.op("pe", lambda e, pm=pm, d_=d_: e.matmul(pm[:, :], lhsT=lrT[p2][0:17, d_, :], rhs=w2a[0:17, d_, :],
                                                           start=True, stop=True),
                     reads=[lrT_b[p2], w2a_b], writes=[pm_b])
                S.op("act", lambda e, pm=pm, d_=d_: e.activation(out=sp[d_][:, :], in_=pm[:, :], func=AF.Exp, scale=-1.0),
                     reads=[pm_b], writes=[sp_b[d_]])
                S.op("act", lambda e, d_=d_: e.activation(out=sp[d_][:, :], in_=sp[d_][:, :], func=AF.Ln, bias=1.0, scale=1.0),
                     reads=[sp_b[d_]], writes=[sp_b[d_]])
                pm, pm_b = pp.get()

                def f_dec(e, pm=pm, d_=d_):
                    for h in range(4):
                        ins = e.matmul(pm[:, 2 * h:2 * h + 1], lhsT=sp[d_][:, h * P:(h + 1) * P], rhs=negcol[:, 0:1],
                                       start=True, stop=True)
                    return ins
                S.op("pe", f_dec, reads=[sp_b[d_], cst_b], writes=[pm_b])
                S.op("act", lambda e, pm=pm, d_=d_: e.activation(
                    out=dec[p2][d_][:, :].unsqueeze(2), in_=pm[:, 0:8].rearrange("p (h two) -> p h two", two=2)[:, :, 0:1],
                    func=AF.Exp), reads=[pm_b], writes=[dec_b[p2][d_]])
                pmb, pmb_b = pp.get()
                S.op("pe", lambda e, pmb=pmb, d_=d_: e.matmul(pmb[:, :], lhsT=tri[:, 2 * d_, :], rhs=sp[d_][:, :], start=True, stop=True),
                     reads=[sp_b[d_], cst_b], writes=[pmb_b])
                pmc, pmc_b = pp.get()
                S.op("pe", lambda e, pmc=pmc, d_=d_: e.matmul(pmc[:, :], lhsT=tri[:, 2 * d_ + 1, :], rhs=sp[d_][:, :], start=True, stop=True),
                     reads=[sp_b[d_], cst_b], writes=[pmc_b])
                S.op("act", lambda e, pmb=pmb, d_=d_: e.activation(out=tb[d_][0][:, :], in_=pmb[:, :], func=AF.Exp),
                     reads=[pmb_b], writes=[tb_b[d_][0]])
                S.op("act", lambda e, pmb=pmb, d_=d_: e.activation(out=tb[d_][1][:, :], in_=pmb[:, :], func=AF.Exp, scale=-1.0),
                     reads=[pmb_b], writes=[tb_b[d_][1]])
                S.op("act", lambda e, pmc=pmc, d_=d_: e.activation(out=tb[d_][2][:, :], in_=pmc[:, :], func=AF.Exp),
                     reads=[pmc_b], writes=[tb_b[d_][2]])
                S.op("dve", lambda e, d_=d_: e.tensor_tensor(out=qe[p2][d_][:, :], in0=q_sb[p2][:, :], in1=tb[d_][0][:, :], op=ALU.mult),
                     reads=[q_b[p2], tb_b[d_][0]], writes=[qe_b[p2][d_]])
                S.op("pool", lambda e, d_=d_: e.tensor_tensor(out=ke[p2][d_][:, :], in0=k_sb[p2][:, :], in1=tb[d_][1][:, :], op=ALU.mult),
                     reads=[k_b[p2], tb_b[d_][1]], writes=[ke_b[p2][d_]])
                S.op("pool", lambda e, d_=d_: e.tensor_tensor(out=kd[p2][d_][:, :], in0=k_sb[p2][:, :], in1=tb[d_][2][:, :], op=ALU.mult),
                     reads=[k_b[p2], tb_b[d_][2]], writes=[kd_b[p2][d_]])
                pm, pm_b = pp.get()
                pmv = pm[:, :].bitcast(BF16).rearrange("p (c t) -> p c t", c=8)

                def f_t(e, pmv=pmv, d_=d_):
                    for j, src in enumerate((qe[p2][d_], ke[p2][d_])):
                        for h in range(4):
                            ins = e.transpose(out=pmv[:, j * 4 + h, :], in_=src[:, h * P:(h + 1) * P], identity=identb[:, :])
                    return ins
                S.op("pe", f_t, reads=[qe_b[p2][d_], ke_b[p2][d_], identb_b], writes=[pm_b])
                S.op("act" if d_ == 0 else "dve",
                     (lambda e, pmv=pmv, d_=d_: e.copy(out=qkT[p2][d_][:, :, :], in_=pmv)) if d_ == 0 else
                     (lambda e, pmv=pmv, d_=d_: e.tensor_copy(out=qkT[p2][d_][:, :, :], in_=pmv)),
                     reads=[pm_b], writes=[qkT_b[p2][d_]])
                pm, pm_b = pp.get()

                def f_att(e, pm=pm, d_=d_):
                    for h in range(4):
                        ins = e.matmul(pm[:, h * P:(h + 1) * P], lhsT=qkT[p2][d_][:, 4 + h, :], rhs=qkT[p2][d_][:, h, :],
                                       start=True, stop=True)
                    return ins
                S.op("pe", f_att, reads=[qkT_b[p2][d_]], writes=[pm_b])
                S.op("dve", lambda e, pm=pm, d_=d_: e.tensor_tensor(
                    out=attm[p2][d_][:, :, :], in0=pm[:, :].rearrange("p (h t) -> p h t", h=4),
                    in1=bcast(gmask[:, d_:d_ + 1, :], [P, 4, P]), op=ALU.mult),
                     reads=[pm_b, cst_b], writes=[attm_b[p2][d_]])
            for hp in range(2):
                pm, pm_b = pp.get()

                def f_o(e, pm=pm, hp=hp):
                    for hh in range(2):
                        h = 2 * hp + hh
                        o_ap = pm[:, hh * G_DV:(hh + 1) * G_DV]
                        vv = v_sb[p2][:, h * G_DV:(h + 1) * G_DV]
                        e.matmul(o_ap, lhsT=attm[p2][0][:, h, :], rhs=vv, start=True, stop=False)
                        e.matmul(o_ap, lhsT=attm[p2][1][:, h, :], rhs=vv, start=False, stop=False)
                        ins = e.matmul(o_ap, lhsT=qkT[p2][0][:, h, :], rhs=Sb[:, h, :], start=False, stop=True)
                    return ins
                S.op("pe", f_o, reads=[attm_b[p2][0], attm_b[p2][1], v_b[p2], qkT_b[p2][0], Sb_b], writes=[pm_b])
                S.op("act", lambda e, pm=pm, hp=hp: e.copy(out=opart[p2][:, hp * 512:(hp + 1) * 512], in_=pm[:, :]),
                     reads=[pm_b], writes=[op_b[p2]])
            for hp in range(2):
                pm, pm_b = pp.get()

                def f_s(e, pm=pm, hp=hp):
                    for hh in range(2):
                        h = 2 * hp + hh
                        ins = e.matmul(pm[:, hh * G_DV:(hh + 1) * G_DV], lhsT=kd[p2][0][:, h * P:(h + 1) * P],
                                       rhs=v_sb[p2][:, h * G_DV:(h + 1) * G_DV], start=True, stop=True)
                    return ins
                S.op("pe", f_s, reads=[kd_b[p2][0], v_b[p2]], writes=[pm_b])
                for hh in range(2):
                    h = 2 * hp + hh
                    S.op("dve", lambda e, pm=pm, h=h, hh=hh: e.scalar_tensor_tensor(
                        out=S32[:, h, :], in0=S32[:, h, :], scalar=dec[p2][0][:, h:h + 1], in1=pm[:, hh * G_DV:(hh + 1) * G_DV],
                        op0=ALU.mult, op1=ALU.add), reads=[S32_b, dec_b[p2][0], pm_b], writes=[S32_b])
            S.op("pool", lambda e: e.tensor_copy(out=Sb[:, :, :], in_=S32[:, :, :]), reads=[S32_b], writes=[Sb_b])
            S.dma("sp", scr["opart"][t * P:(t + 1) * P, :], opart[p2][:, :], reads=[op_b[p2]], writes=[scr_buf], stream=f"{tag}so{p2}")
            S.dma("sp", scr["qeTb"][t], qkT[p2][1][:, 0:4, :].rearrange("p h t -> p (h t)"), reads=[qkT_b[p2][1]],
                  writes=[scr_buf], stream=f"{tag}sq{p2}")
            S.dma("sp", scr["kdb"][t * P:(t + 1) * P, :], kd[p2][1][:, :], reads=[kd_b[p2][1]], writes=[scr_buf], stream=f"{tag}sk{p2}")
            S.dma("sp", scr["vsb"][t * P:(t + 1) * P, :], v_sb[p2][:, :], reads=[v_b[p2]], writes=[scr_buf], stream=f"{tag}sv{p2}")
            S.dma("sp", scr["decb"][t], dec[p2][1][:, :], reads=[dec_b[p2][1]], writes=[scr_buf], stream=f"{tag}sd{p2}")
            if t + NXS < NT:
                load_x(t + NXS)
        S.barrier()


def gla_sweep2(S, nc, NT, xin, xout, w_in, norm_g, w_out, g, b, consts, scr, xin_buf, scr_buf, xout_buf, tag):
    ident_d = consts[0]
    with ExitStack() as ctx:
        sbt = lambda name, shape, dt: ctx.enter_context(nc.sbuf_tensor(f"{tag}_{name}", shape, dt))
        B = S.buf
        pp = PsumPool(S, nc, ctx, tag)
        w_rb = sbt("w_rb", [P, 8, 1024], BF16)
        w_outb = sbt("w_outb", [P, 8, D], BF16)
        identf = sbt("identf", [P, P], F32)
        identb = sbt("identb", [P, P], BF16)
        ng_t = sbt("ng", [P, G_DV], F32)
        g_t = sbt("g", [P, D], F32)
        b_t = sbt("b", [P, D], F32)
        S32 = sbt("S32", [P, 4, G_DV], F32)
        Sb = sbt("Sb", [P, 4, G_DV], BF16)
        w_b, cst_b, identb_b, gb_b, S32_b, Sb_b = B("w"), B("cst"), B("idb"), B("gb"), B("S32"), B("Sb")
        S.dma("sp", identf[:, :], ident_d[:, :], writes=[cst_b], stream=f"{tag}c0")
        S.dma("sp", ng_t[:, :], norm_g.partition_broadcast(P), writes=[gb_b], stream=f"{tag}c1")
        S.dma("sp", g_t[:, :], g.partition_broadcast(P), writes=[gb_b], stream=f"{tag}c2")
        S.dma("sp", b_t[:, :], b.partition_broadcast(P), writes=[gb_b], stream=f"{tag}c3")
        w_inv = w_in.rearrange("(c p) f -> p c f", p=P)
        w_outv = w_out.rearrange("(c p) f -> p c f", p=P)
        for c in range(8):
            S.dma("pool", w_rb[:, c, :], w_inv[:, c, 2048:3072], writes=[w_b], stream=f"{tag}w{c % 2}")
        for c in range(8):
            S.dma("pool", w_outb[:, c, :], w_outv[:, c, :], writes=[w_b], stream=f"{tag}w{c % 2}")
        S.op("dve", lambda e: e.tensor_copy(out=identb[:, :], in_=identf[:, :]), reads=[cst_b], writes=[identb_b])
        S.op("pool", lambda e: e.memset(S32[:, :, :], 0.0), writes=[S32_b])
        S.op("pool", lambda e: e.memset(Sb[:, :, :], 0.0), writes=[Sb_b])

        NXS = 3
        xs = [sbt(f"x{i}", [P, D], F32) for i in range(NXS)]
        opa = [sbt(f"opa{i}", [P, D], F32) for i in range(NXS)]
        qeT = [sbt(f"qeT{i}", [P, 4, P], BF16) for i in range(NXS)]
        kdb = [sbt(f"kdb{i}", [P, 512], BF16) for i in range(NXS)]
        vsb = [sbt(f"vsb{i}", [P, 1024], BF16) for i in range(NXS)]
        decb = [sbt(f"decb{i}", [P, 4], F32) for i in range(NXS)]
        xT = [sbt(f"xT{i}", [P, 8, P], BF16) for i in range(2)]
        er = [sbt(f"er{i}", [P, 1024], F32) for i in range(2)]
        sr = [sbt(f"sr{i}", [P, 4, G_DV], F32) for i in range(2)]
        of = [sbt(f"of{i}", [P, 4, G_DV], F32) for i in range(2)]
        sq = sbt("sq", [P, G_DV], F32)
        ss = [sbt(f"ss{i}", [P, 4], F32) for i in range(2)]
        og = [sbt(f"og{i}", [P, 1024], BF16) for i in range(2)]
        oT = [sbt(f"oT{i}", [P, 8, P], BF16) for i in range(2)]
        ot = [sbt(f"ot{i}", [P, D], F32) for i in range(2)]
        stats = [sbt(f"st{i}", [P, 2, 6], F32) for i in range(2)]
        mv = [sbt(f"mv{i}", [P, 2], F32) for i in range(2)]
        sc = [sbt(f"sc{i}", [P, 2], F32) for i in range(2)]
        xs_b, opa_b, qeT_b, kdb_b, vsb_b, decb_b = (S.bufs(n_, NXS) for n_ in ("xs", "opa", "qeT", "kdb", "vsb", "decb"))
        xT_b, er_b, sr_b, of_b, ss_b, og_b, oT_b, ot_b, st_b, mv_b, sc_b = (
            S.bufs(n_, 2) for n_ in ("xT", "er", "sr", "of", "ss", "og", "oT", "ot", "st", "mv", "sc"))
        sq_b = B("sq")

        order = list(range(NT - 1, -1, -1))

        def load(idx):
            t = order[idx]
            sl = idx % NXS
            S.dma("sp", xs[sl][:, :], xin[t * P:(t + 1) * P, :], reads=[xin_buf], writes=[xs_b[sl]], stream=f"{tag}x{sl}")
            S.dma("sp", opa[sl][:, :], scr["opart"][t * P:(t + 1) * P, :], reads=[scr_buf], writes=[opa_b[sl]], stream=f"{tag}lo{sl}")
            S.dma("sp", qeT[sl][:, :, :].rearrange("p h t -> p (h t)"), scr["qeTb"][t], reads=[scr_buf], writes=[qeT_b[sl]],
                  stream=f"{tag}lq{sl}")
            S.dma("sp", kdb[sl][:, :], scr["kdb"][t * P:(t + 1) * P, :], reads=[scr_buf], writes=[kdb_b[sl]], stream=f"{tag}lk{sl}")
            S.dma("sp", vsb[sl][:, :], scr["vsb"][t * P:(t + 1) * P, :], reads=[scr_buf], writes=[vsb_b[sl]], stream=f"{tag}lv{sl}")
            S.dma("sp", decb[sl][:, :], scr["decb"][t], reads=[scr_buf], writes=[decb_b[sl]], stream=f"{tag}ld{sl}")

        for idx in range(min(NXS, NT)):
            load(idx)
        for idx in range(NT):
            t = order[idx]
            sl, p2 = idx % NXS, idx % 2
            for hh in range(2):
                pm, pm_b = pp.get()

                def f_tr(e, hh=hh, pm=pm):
                    for c in range(4):
                        ins = e.transpose(out=pm[:, c * P:(c + 1) * P], in_=xs[sl][:, (hh * 4 + c) * P:(hh * 4 + c + 1) * P],
                                          identity=identf[:, :])
                    return ins
                S.op("pe", f_tr, reads=[xs_b[sl], cst_b], writes=[pm_b])
                src = pm[:, :].rearrange("p (c t) -> p c t", c=4)
                if hh == 0:
                    S.op("act", lambda e, src=src: e.copy(out=xT[p2][:, 0:4, :], in_=src), reads=[pm_b], writes=[xT_b[p2]])
                else:
                    S.op("dve", lambda e, src=src: e.tensor_copy(out=xT[p2][:, 4:8, :], in_=src), reads=[pm_b], writes=[xT_b[p2]])
            for n in range(2):
                pm, pm_b = pp.get()

                def f_r(e, pm=pm, n=n):
                    for c in range(8):
                        ins = e.matmul(pm[:, :], lhsT=xT[p2][:, c, :], rhs=w_rb[:, c, n * 512:(n + 1) * 512], start=(c == 0), stop=(c == 7))
                    return ins
                S.op("pe", f_r, reads=[xT_b[p2], w_b], writes=[pm_b])
                ern = er[p2][:, n * 512:(n + 1) * 512]
                srn = sr[p2][:, :, :].rearrange("p h d -> p (h d)")[:, n * 512:(n + 1) * 512]
                S.op("act", lambda e, pm=pm, ern=ern: e.activation(out=ern, in_=pm[:, :], func=AF.Exp, scale=-1.0),
                     reads=[pm_b], writes=[er_b[p2]])
                S.op("pool", lambda e, ern=ern: e.tensor_scalar_add(out=ern, in0=ern, scalar1=1.0), reads=[er_b[p2]], writes=[er_b[p2]])
                S.op("dve", lambda e, ern=ern: e.reciprocal(out=ern, in_=ern), reads=[er_b[p2]], writes=[er_b[p2]])
                S.op("dve", lambda e, pm=pm, ern=ern, srn=srn: e.tensor_tensor(out=srn, in0=pm[:, :], in1=ern, op=ALU.mult),
                     reads=[pm_b, er_b[p2]], writes=[sr_b[p2]])
            for hp in range(2):
                pm, pm_b = pp.get()

                def f_o(e, pm=pm, hp=hp):
                    for hh in range(2):
                        h = 2 * hp + hh
                        ins = e.matmul(pm[:, hh * G_DV:(hh + 1) * G_DV], lhsT=qeT[sl][:, h, :], rhs=Sb[:, h, :], start=True, stop=True)
                    return ins
                S.op("pe", f_o, reads=[qeT_b[sl], Sb_b], writes=[pm_b])
                S.op("dve", lambda e, pm=pm, hp=hp: e.tensor_tensor(
                    out=of[p2][:, 2 * hp:2 * hp + 2, :].rearrange("p h d -> p (h d)"), in0=pm[:, :],
                    in1=opa[sl][:, hp * 512:(hp + 1) * 512], op=ALU.add),
                     reads=[pm_b, opa_b[sl]], writes=[of_b[p2]])
            for hp in range(2):
                pm, pm_b = pp.get()

                def f_s(e, pm=pm, hp=hp):
                    for hh in range(2):
                        h = 2 * hp + hh
                        ins = e.matmul(pm[:, hh * G_DV:(hh + 1) * G_DV], lhsT=kdb[sl][:, h * P:(h + 1) * P],
                                       rhs=vsb[sl][:, h * G_DV:(h + 1) * G_DV], start=True, stop=True)
                    return ins
                S.op("pe", f_s, reads=[kdb_b[sl], vsb_b[sl]], writes=[pm_b])
                for hh in range(2):
                    h = 2 * hp + hh
                    S.op("dve", lambda e, pm=pm, h=h, hh=hh: e.scalar_tensor_tensor(
                        out=S32[:, h, :], in0=S32[:, h, :], scalar=decb[sl][:, h:h + 1], in1=pm[:, hh * G_DV:(hh + 1) * G_DV],
                        op0=ALU.mult, op1=ALU.add), reads=[S32_b, decb_b[sl], pm_b], writes=[S32_b])
            S.op("pool", lambda e: e.tensor_copy(out=Sb[:, :, :], in_=S32[:, :, :]), reads=[S32_b], writes=[Sb_b])
            for h in range(4):
                S.op("act", lambda e, h=h: e.activation(out=sq[:, :], in_=of[p2][:, h, :], func=AF.Square,
                                                        accum_out=ss[p2][:, h:h + 1]),
                     reads=[of_b[p2]], writes=[sq_b, ss_b[p2]])
            S.op("act", lambda e: e.activation(out=ss[p2][:, :], in_=ss[p2][:, :], func=AF.Ln, bias=1e-6, scale=1.0 / G_DV),
                 reads=[ss_b[p2]], writes=[ss_b[p2]])
            S.op("act", lambda e: e.activation(out=ss[p2][:, :], in_=ss[p2][:, :], func=AF.Exp, scale=-0.5),
                 reads=[ss_b[p2]], writes=[ss_b[p2]])
            S.op("dve", lambda e: e.tensor_tensor(out=of[p2][:, :, :], in0=of[p2][:, :, :],
                                                  in1=bcast(ss[p2][:, :].unsqueeze(2), [P, 4, G_DV]), op=ALU.mult),
                 reads=[of_b[p2], ss_b[p2]], writes=[of_b[p2]])
            S.op("pool", lambda e: e.tensor_tensor(out=of[p2][:, :, :], in0=of[p2][:, :, :],
                                                   in1=bcast(ng_t[:, :].unsqueeze(1), [P, 4, G_DV]), op=ALU.mult),
                 reads=[of_b[p2], gb_b], writes=[of_b[p2]])
            S.op("pool", lambda e: e.tensor_tensor(out=og[p2][:, :], in0=of[p2][:, :, :].rearrange("p h d -> p (h d)"),
                                                   in1=sr[p2][:, :, :].rearrange("p h d -> p (h d)"), op=ALU.mult),
                 reads=[of_b[p2], sr_b[p2]], writes=[og_b[p2]])
            pm, pm_b = pp.get()
            pmv = pm[:, :].bitcast(BF16).rearrange("p (c t) -> p c t", c=8)

            def f_ot(e, pmv=pmv):
                for s_ in range(8):
                    ins = e.transpose(out=pmv[:, s_, :], in_=og[p2][:, s_ * P:(s_ + 1) * P], identity=identb[:, :])
                return ins
            S.op("pe", f_ot, reads=[og_b[p2], identb_b], writes=[pm_b])
            S.op("act", lambda e, pmv=pmv: e.copy(out=oT[p2][:, :, :], in_=pmv), reads=[pm_b], writes=[oT_b[p2]])
            for n in range(2):
                pm, pm_b = pp.get()

                def f_op(e, pm=pm, n=n):
                    for c in range(8):
                        ins = e.matmul(pm[:, :], lhsT=oT[p2][:, c, :], rhs=w_outb[:, c, n * 512:(n + 1) * 512], start=(c == 0), stop=(c == 7))
                    return ins
                S.op("pe", f_op, reads=[oT_b[p2], w_b], writes=[pm_b])
                S.op("dve", lambda e, pm=pm, n=n: e.scalar_tensor_tensor(
                    out=xs[sl][:, n * 512:(n + 1) * 512], in0=xs[sl][:, n * 512:(n + 1) * 512], scalar=DN_ALPHA,
                    in1=pm[:, :], op0=ALU.mult, op1=ALU.add), reads=[xs_b[sl], pm_b], writes=[xs_b[sl]])
            ln_epilogue(S, (stats[p2], st_b[p2], mv[p2], mv_b[p2], sc[p2], sc_b[p2]),
                        xs[sl], xs_b[sl], g_t, b_t, gb_b, ot[p2], ot_b[p2])
            S.dma("sp", xout[t * P:(t + 1) * P, :], ot[p2][:, :], reads=[ot_b[p2]], writes=[xout_buf], stream=f"{tag}o{p2}")
            if idx + NXS < NT:
                load(idx + NXS)
        S.barrier()


def gla_consts_host():
    u = np.arange(P)[:, None]
    t = np.arange(P)[None, :]
    c = np.float32(-1.0 / G_TAU)
    tri = np.stack([np.where(u <= t, c, 0), np.where(u > t, c, 0),
                    np.where(u >= t, c, 0), np.where(u < t, c, 0)]).astype(np.float32)
    gmask = np.stack([np.where(t >= u, 1.0, 0.0), np.where(t <= u, 1.0, 0.0)]).astype(np.float32)
    return {"c_tri": tri, "c_negcol": np.full((P, 1), c, np.float32), "c_gmask": gmask}


def gla_scratch(nc, NT, tag):
    mk = lambda name, shape, dtp: nc.dram_tensor(f"{tag}_{name}", shape, dtp, kind="Internal").ap()
    return {"opart": mk("opart", [NT * P, 1024], F32), "qeTb": mk("qeTb", [NT, P, 512], BF16),
            "kdb": mk("kdb", [NT * P, 512], BF16), "vsb": mk("vsb", [NT * P, 1024], BF16),
            "decb": mk("decb", [NT, P, 4], F32)}


def build_test_gla(NT):
    nc = bass.Bass("TRN2", target_bir_lowering=False)
    dt = lambda name, shape, dtp=F32, kind="ExternalInput": nc.dram_tensor(name, shape, dtp, kind=kind).ap()
    x = dt("x", [NT * P, D])
    w_in = dt("w_in", [D, G_IN])
    gw2f, gbf, gw2b, gbb = dt("gw2f", [16, 512]), dt("gbf", [512]), dt("gw2b", [16, 512]), dt("gbb", [512])
    norm_g = dt("norm_g", [G_DV])
    w_out = dt("w_out", [D, D])
    g, b = dt("g", [D]), dt("b", [D])
    consts = (dt("c_ident", [P, P]), dt("c_tri", [4, P, P]), dt("c_negcol", [P, 1]), dt("c_gmask", [2, P, P]))
    y = dt("y", [NT * P, D], kind="ExternalOutput")
    scr = gla_scratch(nc, NT, "g1")
    with ExitStack() as ctx:
        S = Sched(nc, ctx)
        xin_buf, scr_buf, xout_buf = S.buf("xin"), S.buf("scr"), S.buf("xout")
        gla_sweep1(S, nc, NT, x, w_in, gw2f, gbf, gw2b, gbb, consts, scr, xin_buf, scr_buf, "g1a")
        gla_sweep2(S, nc, NT, x, y, w_in, norm_g, w_out, g, b, consts, scr, xin_buf, scr_buf, xout_buf, "g1b")
        S.finish()
    return nc


def build_full(NT):
    nc = bass.Bass("TRN2", target_bir_lowering=False)
    dt = lambda name, shape, dtp=F32, kind="ExternalInput": nc.dram_tensor(name, shape, dtp, kind=kind).ap()
    N = NT * P
    x = dt("x", [N, D])
    pos = dt("pos", [N], I32)
    a_w_in, a_sink, a_w_out = dt("a_w_in", [D, A_IN]), dt("a_sink", [16]), dt("a_w_out", [D, D])
    g_w_in = dt("g_w_in", [D, G_IN])
    gw2f, gbf, gw2b, gbb = dt("gw2f", [16, 512]), dt("gbf", [512]), dt("gw2b", [16, 512]), dt("gbb", [512])
    norm_g = dt("norm_g", [G_DV])
    g_w_out = dt("g_w_out", [D, D])
    mix_g, mix_b = [dt(f"mix_g{i}", [D]) for i in range(2)], [dt(f"mix_b{i}", [D]) for i in range(2)]
    w1 = [dt(f"w1_{i}", [D, DFF]) for i in range(2)]
    w2 = [dt(f"w2_{i}", [DFF, D]) for i in range(2)]
    mlp_g, mlp_b = [dt(f"mlp_g{i}", [D]) for i in range(2)], [dt(f"mlp_b{i}", [D]) for i in range(2)]
    c_ident = dt("c_ident", [P, P])
    a_consts = (c_ident, dt("c_maskp", [P, 512]), dt("c_maskn", [P, 512]), dt("c_invf", [32]))
    g_consts = (c_ident, dt("c_tri", [4, P, P]), dt("c_negcol", [P, 1]), dt("c_gmask", [2, P, P]))
    out = dt("out", [N, D], kind="ExternalOutput")
    s = [nc.dram_tensor(f"act{i}", [N, D], F32, kind="Internal").ap() for i in range(3)]
    scr = gla_scratch(nc, NT, "g1")
    with ExitStack() as ctx:
        S = Sched(nc, ctx)
        bx, bout, bscr = S.buf("x"), S.buf("out"), S.buf("scr")
        bs = S.bufs("act", 3)
        attn_phase(S, nc, NT, x, s[0], pos, a_w_in, a_sink, a_w_out, mix_g[0], mix_b[0], a_consts, bx, bs[0], "a0")
        mlp_phase(S, nc, NT, s[0], s[1], w1[0], w2[0], mlp_g[0], mlp_b[0], c_ident, bs[0], bs[1], "m0")
        gla_sweep1(S, nc, NT, s[1], g_w_in, gw2f, gbf, gw2b, gbb, g_consts, scr, bs[1], bscr, "ga")
        gla_sweep2(S, nc, NT, s[1], s[2], g_w_in, norm_g, g_w_out, mix_g[1], mix_b[1], g_consts, scr, bs[1], bscr, bs[2], "gb")
        mlp_phase(S, nc, NT, s[2], out, w1[1], w2[1], mlp_g[1], mlp_b[1], c_ident, bs[2], bout, "m1")
        S.finish()
    return nc


def host_inputs(inp, core, NT=SEQ // P):
    f = lambda a: np.ascontiguousarray(np.asarray(a, dtype=np.float32))
    N = NT * P
    m = {
        "x": f(inp["x"][core, :N]),
        "pos": np.ascontiguousarray(np.asarray(inp["positions"][core, :N], dtype=np.int32)),
        "a_w_in": permute_attn_w_in(f(inp["attn_w_in"][0])),
        "a_sink": f(inp["attn_sink"][0]),
        "a_w_out": f(inp["attn_w_out"][0]),
        "g_w_in": f(inp["gla_w_in"][0]),
        "gw2f": f(inp["gla_gate_w2_fwd"][0]), "gbf": f(inp["gla_gate_b_fwd"][0]),
        "gw2b": f(inp["gla_gate_w2_bwd"][0]), "gbb": f(inp["gla_gate_b_bwd"][0]),
        "norm_g": f(inp["gla_norm_g"][0]),
        "g_w_out": f(inp["gla_w_out"][0]),
    }
    for i in range(2):
        m[f"mix_g{i}"] = f(inp["mix_ln_g"][i])
        m[f"mix_b{i}"] = f(inp["mix_ln_b"][i])
        m[f"w1_{i}"] = f(inp["mlp_w1"][i])
        m[f"w2_{i}"] = f(inp["mlp_w2"][i])
        m[f"mlp_g{i}"] = f(inp["mlp_ln_g"][i])
        m[f"mlp_b{i}"] = f(inp["mlp_ln_b"][i])
    m.update(attn_consts_host())
    m.update(gla_consts_host())
    return m


def kernel(**inputs):
    NT = SEQ // P
    nc = build_full(NT)
    shared = host_inputs(inputs, 0)
    in_maps = []
    for c in range(NCORES):
        m = dict(shared)
        m["x"] = np.ascontiguousarray(np.asarray(inputs["x"][c], dtype=np.float32))
        m["pos"] = np.ascontiguousarray(np.asarray(inputs["positions"][c], dtype=np.int32))
        in_maps.append(m)
    res = run_bass_kernel_spmd(nc, in_maps, core_ids=list(range(NCORES)))
    return np.stack([np.asarray(r["out"], dtype=np.float32) for r in res.results], axis=0)
```

```python
import os
import numpy as np
from contextlib import ExitStack
import ml_dtypes
import concourse.bass as bass
import concourse.mybir as mybir
from concourse.bass_utils import run_bass_kernel_spmd

F32 = mybir.dt.float32
BF16 = mybir.dt.bfloat16
I32 = mybir.dt.int32
AF = mybir.ActivationFunctionType
ALU = mybir.AluOpType

P = 128
D = 1024
DFF = 4096
SEQ = 8192
NCORES = 8
DEPTH = 2
DN_ALPHA = float((2 * DEPTH) ** 0.25)
LN_EPS = 1e-5

SEM_EPOCH = 30000


class Buf:
    __slots__ = ("name", "lw", "rd", "rd_dma", "excl")

    def __init__(self, name, excl=False):
        self.name = name
        self.excl = excl
        self.lw = None
        self.rd = {}
        self.rd_dma = []


class Sched:
    def __init__(self, nc, ctx):
        self.nc = nc
        self.ctx = ctx
        self.E = {"pe": nc.tensor, "act": nc.scalar, "dve": nc.vector,
                  "pool": nc.gpsimd, "sp": nc.sync}
        self.cnt = {e: 0 for e in self.E}
        self.sems = {e: [] for e in self.E}
        self.waited = {e: {} for e in self.E}
        self.streams = {}
        self.nwaits = 0

    def buf(self, name):
        return Buf(name)

    def bufs(self, name, n):
        return [Buf(f"{name}{i}") for i in range(n)]

    def pbufs(self, name, n):
        return [Buf(f"{name}{i}", excl=True) for i in range(n)]

    def _eng_sem(self, eng, k):
        ep = (k - 1) // SEM_EPOCH
        while len(self.sems[eng]) <= ep:
            self.sems[eng].append(
                self.ctx.enter_context(self.nc.semaphore(f"s_{eng}{len(self.sems[eng])}")))
        return self.sems[eng][ep], (k - 1) % SEM_EPOCH + 1

    def _stream(self, name):
        st = self.streams.get(name)
        if st is None:
            st = {"n": 0, "sem": self.ctx.enter_context(self.nc.semaphore(f"d_{name}"))}
            self.streams[name] = st
        return st

    def _wait(self, eng, tok):
        w = self.waited[eng]
        if tok[0] == "c":
            _, peng, k = tok
            if peng == "pe" and eng == "pe":
                return
            if w.get(peng, 0) >= k:
                return
            w[peng] = k
            sem, val = self._eng_sem(peng, k)
        else:
            _, sname, n = tok
            key = "d:" + sname
            if w.get(key, 0) >= n:
                return
            w[key] = n
            sem, val = self.streams[sname]["sem"], 16 * n
        self.E[eng].wait_ge(sem, val)
        self.nwaits += 1

    def _deps(self, reads, writes, eng=None):
        deps = []
        for b in reads:
            if b.lw is not None:
                deps.append(b.lw)
            if b.excl:
                deps.extend(tok for e2, tok in b.rd.items() if e2 != eng)
        for b in writes:
            if b.lw is not None:
                deps.append(b.lw)
            deps.extend(b.rd.values())
            deps.extend(b.rd_dma)
        return deps

    def op(self, eng, fn, reads=(), writes=()):
        for tok in self._deps(reads, writes, eng):
            self._wait(eng, tok)
        ins = fn(self.E[eng])
        self.cnt[eng] += 1
        k = self.cnt[eng]
        sem, _ = self._eng_sem(eng, k)
        ins.then_inc(sem, 1)
        tok = ("c", eng, k)
        for b in writes:
            b.lw = tok
            b.rd = {}
            b.rd_dma = []
        for b in reads:
            if b.lw is not tok:
                b.rd[eng] = tok
        return tok

    def dma(self, q, out, in_, reads=(), writes=(), stream=None, **kw):
        st = self._stream(stream)
        deps = self._deps(reads, writes)
        if st["n"] > 0:
            deps.append(("d", stream, st["n"]))
        for tok in deps:
            self._wait(q, tok)
        ins = self.E[q].dma_start(out=out, in_=in_, **kw)
        ins.then_inc(st["sem"], 16)
        st["n"] += 1
        tok = ("d", stream, st["n"])
        for b in writes:
            b.lw = tok
            b.rd = {}
            b.rd_dma = []
        for b in reads:
            b.rd_dma.append(tok)
        return tok

    def barrier(self):
        for eng in self.E:
            for peng in self.E:
                if peng != eng and self.cnt[peng] > 0:
                    self._wait(eng, ("c", peng, self.cnt[peng]))
            for sname, st in self.streams.items():
                if st["n"] > 0:
                    self._wait(eng, ("d", sname, st["n"]))

    def finish(self):
        for sname, st in self.streams.items():
            if st["n"] > 0:
                self._wait("sp", ("d", sname, st["n"]))


def run_interleaved(gens, depth=2, skew=1):
    it = iter(gens)
    pending = next(it, None)
    active = []
    while active or pending is not None:
        if pending is not None and len(active) < depth and (not active or active[-1][1] >= skew):
            active.append([pending, 0])
            pending = next(it, None)
        for a in list(active):
            try:
                next(a[0])
                a[1] += 1
            except StopIteration:
                active.remove(a)


def load_rowvec_bcast(S, q, dst, src_1d, n, stream, wbuf):
    S.dma(q, dst, src_1d.partition_broadcast(P), writes=[wbuf], stream=stream)


def ln_epilogue(S, sb, z, zb, g_t, b_t, gb_buf, out_t, out_b, eps=LN_EPS):
    stats, stats_b, mv, mv_b, sc, sc_b = sb
    S.op("dve", lambda e: (e.bn_stats(out=stats[:, 0, :], in_=z[:, 0:512]),
                           e.bn_stats(out=stats[:, 1, :], in_=z[:, 512:1024]))[-1],
         reads=[zb], writes=[stats_b])
    S.op("dve", lambda e: e.bn_aggr(out=mv[:, :], in_=stats[:, :, :].rearrange("p a b -> p (a b)")),
         reads=[stats_b], writes=[mv_b])
    S.op("act", lambda e: e.activation(out=sc[:, 0:1], in_=mv[:, 1:2], func=AF.Ln, bias=eps, scale=1.0),
         reads=[mv_b], writes=[sc_b])
    S.op("act", lambda e: e.activation(out=sc[:, 0:1], in_=sc[:, 0:1], func=AF.Exp, scale=-0.5),
         reads=[sc_b], writes=[sc_b])
    S.op("dve", lambda e: e.scalar_tensor_tensor(out=sc[:, 1:2], in0=mv[:, 0:1], scalar=-1.0,
                                                 in1=sc[:, 0:1], op0=ALU.mult, op1=ALU.mult),
         reads=[mv_b, sc_b], writes=[sc_b])
    S.op("act", lambda e: e.activation(out=out_t[:, :], in_=z[:, :], func=AF.Identity,
                                       bias=sc[:, 1:2], scale=sc[:, 0:1]),
         reads=[zb, sc_b], writes=[out_b])
    S.op("dve", lambda e: e.tensor_tensor(out=out_t[:, :], in0=out_t[:, :], in1=g_t[:, :], op=ALU.mult),
         reads=[out_b, gb_buf], writes=[out_b])
    S.op("dve", lambda e: e.tensor_tensor(out=out_t[:, :], in0=out_t[:, :], in1=b_t[:, :], op=ALU.add),
         reads=[out_b, gb_buf], writes=[out_b])


def mlp_phase(S, nc, NT, xin, xout, w1, w2, g, b, ident_d, xin_buf, xout_buf, tag):
    ST = 2
    NS = NT // ST
    TOK = ST * P
    with ExitStack() as ctx:
        sbt = lambda name, shape, dt: ctx.enter_context(nc.sbuf_tensor(f"{tag}_{name}", shape, dt))
        pst = lambda name, shape, dt: ctx.enter_context(nc.psum_tensor(f"{tag}_{name}", shape, dt))
        w1b = sbt("w1b", [P, 8, DFF], BF16)
        w2b = sbt("w2b", [P, 32, D], BF16)
        identf = sbt("identf", [P, P], F32)
        g_t = sbt("g", [P, D], F32)
        b_t = sbt("b", [P, D], F32)
        NXS = 4
        xs = [sbt(f"x{i}", [P, D], F32) for i in range(NXS)]
        xT = [sbt(f"xT{i}", [P, 8, TOK], BF16) for i in range(2)]
        hT = sbt("hT", [P, 32, TOK], BF16)
        rt = [sbt(f"rt{i}", [P, 512], F32) for i in range(2)]
        ot = [sbt(f"ot{i}", [P, D], F32) for i in range(2)]
        stats = [sbt(f"st{i}", [P, 2, 6], F32) for i in range(2)]
        mv = [sbt(f"mv{i}", [P, 2], F32) for i in range(2)]
        sc = [sbt(f"sc{i}", [P, 2], F32) for i in range(2)]
        ptr = [pst(f"ptr{i}", [P, 4, P], F32) for i in range(2)]
        pm1 = [pst(f"pm1{i}", [P, 512], F32) for i in range(2)]
        pm2 = [pst(f"pm2{i}", [P, 2, 512], F32) for i in range(2)]

        B = S.buf
        w1_b = [B(f"w1b{c}") for c in range(8)]
        w2_b = [B(f"w2b{c}") for c in range(8)]
        ident_b, gb_b = B("ident"), B("gb")
        xs_b = S.bufs("xs", NXS)
        xT_b = S.bufs("xT", 2)
        hT_b = [B(f"hT{j}") for j in range(16)]
        rt_b = S.bufs("rt", 2)
        ot_b = S.bufs("ot", 2)
        st_b, mv_b, sc_b = S.bufs("st", 2), S.bufs("mv", 2), S.bufs("sc", 2)
        ptr_b, pm1_b, pm2_b = S.pbufs("ptr", 2), S.pbufs("pm1", 2), S.pbufs("pm2", 2)

        S.dma("sp", identf[:, :], ident_d[:, :], writes=[ident_b], stream=f"{tag}c0")
        S.dma("sp", g_t[:, :], g.partition_broadcast(P), writes=[gb_b], stream=f"{tag}c1")
        S.dma("sp", b_t[:, :], b.partition_broadcast(P), writes=[gb_b], stream=f"{tag}c2")
        w1v = w1.rearrange("(c p) f -> p c f", p=P)
        for c in range(8):
            for hh in range(2):
                S.dma("pool", w1b[:, c, hh * 2048:(hh + 1) * 2048], w1v[:, c, hh * 2048:(hh + 1) * 2048],
                      writes=[w1_b[c]], stream=f"{tag}w1_{c % 2}")
        w2v = w2.rearrange("(c p) f -> p c f", p=P)
        for c in range(32):
            S.dma("pool", w2b[:, c, :], w2v[:, c, :], writes=[w2_b[c // 4]], stream=f"{tag}w2_{c % 2}")

        def load_x(t):
            sl = t % NXS
            S.dma("sp", xs[sl][:, :], xin[t * P:(t + 1) * P, :], reads=[xin_buf], writes=[xs_b[sl]],
                  stream=f"{tag}x{sl}")

        for t in range(min(NXS, NT)):
            load_x(t)

        def transposes(s):
            xTs, xTs_b = xT[s % 2], xT_b[s % 2]
            for m in range(ST):
                t = s * ST + m
                sl = t % NXS
                for hh in range(2):
                    pt, pt_b = ptr[hh], ptr_b[hh]

                    def f_tr(e, sl=sl, hh=hh, pt=pt):
                        for c in range(4):
                            ins = e.transpose(out=pt[:, c, :], in_=xs[sl][:, (hh * 4 + c) * P:(hh * 4 + c + 1) * P],
                                              identity=identf[:, :])
                        return ins
                    S.op("pe", f_tr, reads=[xs_b[sl], ident_b], writes=[pt_b])
                    S.op("act" if hh == 0 else "dve",
                         (lambda e, pt=pt, hh=hh, m=m, xTs=xTs: e.copy(out=xTs[:, hh * 4:(hh + 1) * 4, m * P:(m + 1) * P], in_=pt[:, :, :]))
                         if hh == 0 else
                         (lambda e, pt=pt, hh=hh, m=m, xTs=xTs: e.tensor_copy(out=xTs[:, hh * 4:(hh + 1) * 4, m * P:(m + 1) * P], in_=pt[:, :, :])),
                         reads=[pt_b], writes=[xTs_b])
        transposes(0)
        for s in range(NS):
            xTs, xTs_b = xT[s % 2], xT_b[s % 2]
            for jj in range(16):
                pm, pm_b = pm1[jj % 2], pm1_b[jj % 2]

                def f_mm1(e, jj=jj, pm=pm, xTs=xTs):
                    for u in range(2):
                        j = jj * 2 + u
                        for c in range(8):
                            ins = e.matmul(pm[:, u * TOK:(u + 1) * TOK], lhsT=w1b[:, c, j * P:(j + 1) * P],
                                           rhs=xTs[:, c, :], start=(c == 0), stop=(c == 7))
                    return ins
                S.op("pe", f_mm1, reads=[xTs_b] + w1_b, writes=[pm_b])
                r, r_b = rt[jj % 2], rt_b[jj % 2]
                S.op("act", lambda e, pm=pm, r=r: e.activation(out=r[:, :], in_=pm[:, :], func=AF.Relu),
                     reads=[pm_b], writes=[r_b])
                S.op("dve", lambda e, r=r, jj=jj: e.tensor_tensor(
                    out=hT[:, 2 * jj:2 * jj + 2, :].rearrange("p a t -> p (a t)"), in0=r[:, :], in1=r[:, :], op=ALU.mult),
                     reads=[r_b], writes=[hT_b[jj]])
            if s + 1 < NS:
                transposes(s + 1)
            for m in range(ST):
                t = s * ST + m
                sl = t % NXS
                k2 = t % 2
                py, py_b = pm2[k2], pm2_b[k2]

                def f_mm2(e, m=m, py=py):
                    for n in range(2):
                        for j in range(32):
                            ins = e.matmul(py[:, n, :], lhsT=hT[:, j, m * P:(m + 1) * P],
                                           rhs=w2b[:, j, n * 512:(n + 1) * 512], start=(j == 0), stop=(j == 31))
                    return ins
                S.op("pe", f_mm2, reads=hT_b + w2_b, writes=[py_b])
                S.op("dve", lambda e, sl=sl, py=py: e.scalar_tensor_tensor(
                    out=xs[sl][:, :], in0=xs[sl][:, :], scalar=DN_ALPHA,
                    in1=py[:, :, :].rearrange("p a b -> p (a b)"), op0=ALU.mult, op1=ALU.add),
                     reads=[xs_b[sl], py_b], writes=[xs_b[sl]])
                ln_epilogue(S, (stats[k2], st_b[k2], mv[k2], mv_b[k2], sc[k2], sc_b[k2]),
                            xs[sl], xs_b[sl], g_t, b_t, gb_b, ot[k2], ot_b[k2])
                S.dma("sp", xout[t * P:(t + 1) * P, :], ot[k2][:, :], reads=[ot_b[k2]], writes=[xout_buf],
                      stream=f"{tag}o{k2}")
                if t + NXS < NT:
                    load_x(t + NXS)
        S.barrier()


def build_test_mlp(NT):
    nc = bass.Bass("TRN2", target_bir_lowering=False)
    x = nc.dram_tensor("x", [NT * P, D], F32, kind="ExternalInput").ap()
    w1 = nc.dram_tensor("w1", [D, DFF], F32, kind="ExternalInput").ap()
    w2 = nc.dram_tensor("w2", [DFF, D], F32, kind="ExternalInput").ap()
    g = nc.dram_tensor("g", [D], F32, kind="ExternalInput").ap()
    b = nc.dram_tensor("b", [D], F32, kind="ExternalInput").ap()
    ident = nc.dram_tensor("ident", [P, P], F32, kind="ExternalInput").ap()
    y = nc.dram_tensor("y", [NT * P, D], F32, kind="ExternalOutput").ap()
    with ExitStack() as ctx:
        S = Sched(nc, ctx)
        mlp_phase(S, nc, NT, x, y, w1, w2, g, b, ident, S.buf("xin"), S.buf("xout"), "m0")
        S.finish()
    return nc


A_HEADS = 16
A_KV = 4
HD = 64
A_IN = 1536
TWO_PI = 2.0 * np.pi
CW1 = 6.28125
CW2 = float(TWO_PI - 6.28125)
PI_LO = 3.1415925
Q_HEAD_ORDER = [0, 4, 1, 5, 2, 6, 3, 7, 8, 12, 9, 13, 10, 14, 11, 15]


def bcast(ap, shape):
    return ap.broadcast_to(list(shape))


def attn_phase(S, nc, NT, xin, xout, pos, w_in, sink, w_out, g, b, consts, xin_buf, xout_buf, tag):
    ident_d, maskp_d, maskn_d, invf_d = consts
    scale = HD ** -0.5
    with ExitStack() as ctx:
        sbt = lambda name, shape, dt: ctx.enter_context(nc.sbuf_tensor(f"{tag}_{name}", shape, dt))
        pst = lambda name, shape, dt: ctx.enter_context(nc.psum_tensor(f"{tag}_{name}", shape, dt))
        B = S.buf
        w_inb = sbt("w_inb", [P, 8, A_IN], BF16)
        w_outb = sbt("w_outb", [P, 8, D], BF16)
        identf = sbt("identf", [P, P], F32)
        identb = sbt("identb", [P, P], BF16)
        maskp = sbt("maskp", [P, 512], BF16)
        maskn = sbt("maskn", [P, 512], BF16)
        g_t = sbt("g", [P, D], F32)
        b_t = sbt("b", [P, D], F32)
        esink = sbt("esink", [P, 16], F32)
        cos_t = sbt("cos", [P, NT, 32], F32)
        sin_t = sbt("sin", [P, NT, 32], F32)
        w_in_b, w_out_b = B("w_in"), B("w_out")
        ident_b, identb_b, mask_b, gb_b, esink_b, cs_b = B("id"), B("idb"), B("mask"), B("gb"), B("esink"), B("cs")

        pmm = [pst(f"pmm{i}", [P, 512], F32) for i in range(2)]
        ptb = pst("ptb", [P, 8, P], BF16)
        psc = [pst(f"psc{i}", [P, 512], F32) for i in range(2)]
        pov = [pst(f"pov{i}", [P, 512], F32) for i in range(3)]
        pmm_b, psc_b, pov_b, ptb_b = S.pbufs("pmm", 2), S.pbufs("psc", 2), S.pbufs("pov", 3), S.pbufs("ptb", 1)[0]

        S.dma("sp", identf[:, :], ident_d[:, :], writes=[ident_b], stream=f"{tag}c0")
        S.dma("sp", g_t[:, :], g.partition_broadcast(P), writes=[gb_b], stream=f"{tag}c1")
        S.dma("sp", b_t[:, :], b.partition_broadcast(P), writes=[gb_b], stream=f"{tag}c2")
        S.dma("sp", esink[:, :], sink.partition_broadcast(P), writes=[esink_b], stream=f"{tag}c3")
        S.dma("pool", maskp[:, :], maskp_d[:, :], writes=[mask_b], stream=f"{tag}c4")
        S.dma("pool", maskn[:, :], maskn_d[:, :], writes=[mask_b], stream=f"{tag}c5")
        w_inv = w_in.rearrange("(c p) f -> p c f", p=P)
        for c in range(8):
            S.dma("pool", w_inb[:, c, :], w_inv[:, c, :], writes=[w_in_b], stream=f"{tag}w{c % 2}")
        w_outv = w_out.rearrange("(c p) f -> p c f", p=P)
        for c in range(8):
            S.dma("pool", w_outb[:, c, :], w_outv[:, c, :], writes=[w_out_b], stream=f"{tag}w{c % 2}")
        S.op("dve", lambda e: e.tensor_copy(out=identb[:, :], in_=identf[:, :]), reads=[ident_b], writes=[identb_b])
        S.op("act", lambda e: e.activation(out=esink[:, :], in_=esink[:, :], func=AF.Exp),
             reads=[esink_b], writes=[esink_b])

        with ExitStack() as c2:
            sb2 = lambda name, shape, dt: c2.enter_context(nc.sbuf_tensor(f"{tag}_{name}", shape, dt))
            posi = sb2("posi", [NT, P], I32)
            posf = sb2("posf", [NT, P], F32)
            posT = sb2("posT", [P, NT], F32)
            invf = sb2("invf", [P, 32], F32)
            ang = sb2("ang", [P, NT, 32], F32)
            u = sb2("u", [P, NT, 32], F32)
            ki = sb2("ki", [P, NT, 32], I32)
            kf = sb2("kf", [P, NT, 32], F32)
            posi_b, posf_b, posT_b, invf_b, ang_b, u_b, ki_b, kf_b = S.bufs("rp", 8)
            S.dma("sp", posi[:, :], pos.rearrange("(t p) -> t p", p=P), writes=[posi_b], stream=f"{tag}c6")
            S.dma("sp", invf[:, :], invf_d.partition_broadcast(P), writes=[invf_b], stream=f"{tag}c7")
            S.op("dve", lambda e: e.tensor_copy(out=posf[:, :], in_=posi[:, :]), reads=[posi_b], writes=[posf_b])
            S.op("pe", lambda e: e.transpose(out=pmm[0][:, 0:NT], in_=posf[:, :], identity=identf[0:NT, 0:NT]),
                 reads=[posf_b, ident_b], writes=[pmm_b[0]])
            S.op("dve", lambda e: e.tensor_copy(out=posT[:, :], in_=pmm[0][:, 0:NT]), reads=[pmm_b[0]], writes=[posT_b])
            S.op("dve", lambda e: e.tensor_tensor(out=ang[:, :, :], in0=bcast(posT[:, :].unsqueeze(2), [P, NT, 32]),
                                                  in1=bcast(invf[:, :].unsqueeze(1), [P, NT, 32]), op=ALU.mult),
                 reads=[posT_b, invf_b], writes=[ang_b])
            for which, off, dst in (("sin", 0.0, sin_t), ("cos", 0.25, cos_t)):
                S.op("dve", lambda e, off=off: e.tensor_scalar(out=u[:, :, :], in0=ang[:, :, :], scalar1=float(1.0 / TWO_PI),
                                                               scalar2=off, op0=ALU.mult, op1=ALU.add),
                     reads=[ang_b], writes=[u_b])
                S.op("dve", lambda e: e.tensor_copy(out=ki[:, :, :], in_=u[:, :, :]), reads=[u_b], writes=[ki_b])
                S.op("dve", lambda e: e.tensor_copy(out=kf[:, :, :], in_=ki[:, :, :]), reads=[ki_b], writes=[kf_b])
                S.op("dve", lambda e: e.scalar_tensor_tensor(out=u[:, :, :], in0=kf[:, :, :], scalar=-CW1, in1=ang[:, :, :],
                                                             op0=ALU.mult, op1=ALU.add),
                     reads=[kf_b, ang_b], writes=[u_b])
                S.op("dve", lambda e: e.scalar_tensor_tensor(out=u[:, :, :], in0=kf[:, :, :], scalar=-CW2, in1=u[:, :, :],
                                                             op0=ALU.mult, op1=ALU.add),
                     reads=[kf_b, u_b], writes=[u_b])
                S.op("dve", lambda e, off=off: e.tensor_scalar(out=u[:, :, :], in0=u[:, :, :], scalar1=float(off * TWO_PI),
                                                               scalar2=PI_LO, op0=ALU.add, op1=ALU.min),
                     reads=[u_b], writes=[u_b])
                S.op("dve", lambda e: e.tensor_scalar(out=u[:, :, :], in0=u[:, :, :], scalar1=-PI_LO, scalar2=None, op0=ALU.max),
                     reads=[u_b], writes=[u_b])
                S.op("act", lambda e, dst=dst: e.activation(out=dst[:, :, :], in_=u[:, :, :], func=AF.Sin),
                     reads=[u_b], writes=[cs_b])
            S.barrier()

        NXS = 4
        xs = [sbt(f"x{i}", [P, D], F32) for i in range(NXS)]
        xT = [sbt(f"xT{i}", [P, 8, P], BF16) for i in range(2)]
        ra = [sbt(f"ra{i}", [P, 8, 2, 32], F32) for i in range(2)]
        rb = [sbt(f"rb{i}", [P, 8, 2, 32], F32) for i in range(2)]
        qr = [sbt(f"qr{i}", [P, 16, 2, 32], BF16) for i in range(2)]
        kr = [sbt(f"kr{i}", [P, 4, 2, 32], BF16) for i in range(2)]
        qT = [sbt(f"qT{i}", [P, 8, P], BF16) for i in range(2)]
        kTw = [sbt(f"kT{i}", [P, 2, P], BF16) for i in range(4)]
        Vw = [sbt(f"V{i}", [P, 4, 65], BF16) for i in range(4)]
        pT = [sbt(f"pT{i}", [P, 512], BF16) for i in range(3)]
        den = [sbt(f"den{i}", [P, 16], F32) for i in range(2)]
        on = [sbt(f"on{i}", [P, 16, 64], BF16) for i in range(2)]
        oT = [sbt(f"oT{i}", [P, 8, P], BF16) for i in range(2)]
        ot = [sbt(f"ot{i}", [P, D], F32) for i in range(2)]
        stats = [sbt(f"st{i}", [P, 2, 6], F32) for i in range(2)]
        mv = [sbt(f"mv{i}", [P, 2], F32) for i in range(2)]
        sc = [sbt(f"sc{i}", [P, 2], F32) for i in range(2)]
        xs_b, xT_b = S.bufs("xs", NXS), S.bufs("xT", 2)
        ra_b, rb_b, qr_b, kr_b, qT_b = S.bufs("ra", 2), S.bufs("rb", 2), S.bufs("qr", 2), S.bufs("kr", 2), S.bufs("qT", 2)
        kT_b, V_b, pT_b = S.bufs("kT", 4), S.bufs("V", 4), S.bufs("pT", 3)
        den_b, on_b, oT_b, ot_b = S.bufs("den", 2), S.bufs("on", 2), S.bufs("oT", 2), S.bufs("ot", 2)
        st_b, mv_b, sc_b = S.bufs("st", 2), S.bufs("mv", 2), S.bufs("sc", 2)
        for i in range(4):
            S.op("pool", lambda e, i=i: e.memset(Vw[i][:, :, :], 1.0), writes=[V_b[i]])

        cnt = {"mm": 0, "sc": 0, "pt": 0, "rr": 0}

        def nxt(key, n):
            v = cnt[key] % n
            cnt[key] += 1
            return v

        def load_x(t):
            sl = t % NXS
            S.dma("sp", xs[sl][:, :], xin[t * P:(t + 1) * P, :], reads=[xin_buf], writes=[xs_b[sl]],
                  stream=f"{tag}x{sl}")

        def rope(pm, pm_bf, nh, t, dst, dst_bf, h0):
            r = nxt("rr", 2)
            pm4 = pm[:, 0:nh * 64].rearrange("p (h two d) -> p h two d", two=2, d=32)
            cb = bcast(cos_t[:, t:t + 1, :].unsqueeze(1), [P, nh, 2, 32])
            sb_ = bcast(sin_t[:, t:t + 1, :].unsqueeze(1), [P, nh, 2, 32])
            S.op("dve", lambda e: e.tensor_tensor(out=ra[r][:, 0:nh, :, :], in0=pm4, in1=cb, op=ALU.mult),
                 reads=[pm_bf, cs_b], writes=[ra_b[r]])
            S.op("dve", lambda e: e.tensor_tensor(out=rb[r][:, 0:nh, :, :], in0=pm4, in1=sb_, op=ALU.mult),
                 reads=[pm_bf, cs_b], writes=[rb_b[r]])
            S.op("pool", lambda e: e.tensor_tensor(out=dst[:, h0:h0 + nh, 0, :], in0=ra[r][:, 0:nh, 0, :],
                                                   in1=rb[r][:, 0:nh, 1, :], op=ALU.subtract),
                 reads=[ra_b[r], rb_b[r]], writes=[dst_bf])
            S.op("pool", lambda e: e.tensor_tensor(out=dst[:, h0:h0 + nh, 1, :], in0=ra[r][:, 0:nh, 1, :],
                                                   in1=rb[r][:, 0:nh, 0, :], op=ALU.add),
                 reads=[ra_b[r], rb_b[r]], writes=[dst_bf])

        def stage1(t):
            sl, p2, p4 = t % NXS, t % 2, t % 4
            for hh in range(2):
                m = nxt("mm", 2)

                def f_tr(e, hh=hh, m=m):
                    for c in range(4):
                        ins = e.transpose(out=pmm[m][:, c * P:(c + 1) * P],
                                          in_=xs[sl][:, (hh * 4 + c) * P:(hh * 4 + c + 1) * P], identity=identf[:, :])
                    return ins
                S.op("pe", f_tr, reads=[xs_b[sl], ident_b], writes=[pmm_b[m]])
                src = pmm[m][:, :].rearrange("p (c t) -> p c t", c=4)
                if hh == 0:
                    S.op("act", lambda e, src=src: e.copy(out=xT[p2][:, 0:4, :], in_=src), reads=[pmm_b[m]], writes=[xT_b[p2]])
                else:
                    S.op("dve", lambda e, src=src: e.tensor_copy(out=xT[p2][:, 4:8, :], in_=src), reads=[pmm_b[m]], writes=[xT_b[p2]])
            for n in range(3):
                m = nxt("mm", 2)

                def f_in(e, n=n, m=m):
                    for c in range(8):
                        ins = e.matmul(pmm[m][:, :], lhsT=xT[p2][:, c, :], rhs=w_inb[:, c, n * 512:(n + 1) * 512],
                                       start=(c == 0), stop=(c == 7))
                    return ins
                S.op("pe", f_in, reads=[xT_b[p2], w_in_b], writes=[pmm_b[m]])
                if n < 2:
                    rope(pmm[m], pmm_b[m], 8, t, qr[p2], qr_b[p2], n * 8)
                else:
                    rope(pmm[m], pmm_b[m], 4, t, kr[p2], kr_b[p2], 0)
                    S.op("act", lambda e, m=m: e.copy(out=Vw[p4][:, :, 0:64],
                                                      in_=pmm[m][:, 256:512].rearrange("p (h d) -> p h d", d=64)),
                         reads=[pmm_b[m]], writes=[V_b[p4]])
            def f_qt(e):
                qflat = qr[p2][:, :, :, :].rearrange("p h two d -> p (h two d)")
                for s_ in range(8):
                    ins = e.transpose(out=ptb[:, s_, :], in_=qflat[:, s_ * P:(s_ + 1) * P], identity=identb[:, :])
                return ins
            S.op("pe", f_qt, reads=[qr_b[p2], identb_b], writes=[ptb_b])
            S.op("act", lambda e: e.copy(out=qT[p2][:, :, :], in_=ptb[:, :, :]), reads=[ptb_b], writes=[qT_b[p2]])

            def f_kt(e):
                kflat = kr[p2][:, :, :, :].rearrange("p h two d -> p (h two d)")
                for s_ in range(2):
                    ins = e.transpose(out=ptb[:, s_, :], in_=kflat[:, s_ * P:(s_ + 1) * P], identity=identb[:, :])
                return ins
            S.op("pe", f_kt, reads=[kr_b[p2], identb_b], writes=[ptb_b])
            S.op("dve", lambda e: e.tensor_copy(out=kTw[p4][:, :, :], in_=ptb[:, 0:2, :]), reads=[ptb_b], writes=[kT_b[p4]])

        def stage2(i):
            sl, p2 = i % NXS, i % 2
            kts = [kt for kt in (i - 1, i, i + 1) if 0 <= kt < NT]
            bank_started = [False, False, False]
            last_in_bank = {0: 5, 1: 11, 2: 15}
            for kv in range(4):
                base, ch = 64 * (kv % 2), kv // 2
                for kt in kts:
                    s_ = nxt("sc", 2)

                    def f_sc(e, kt=kt, s_=s_):
                        if kt != i:
                            e.matmul(psc[s_][:, :], lhsT=identb[:, :], rhs=(maskp if kt < i else maskn)[:, :],
                                     start=True, stop=False)
                        return e.matmul(psc[s_][:, :], lhsT=kTw[kt % 4][base:base + 64, ch, :],
                                        rhs=qT[p2][base:base + 64, 4 * ch:4 * ch + 4, :],
                                        start=(kt == i), stop=True)
                    S.op("pe", f_sc, reads=[kT_b[kt % 4], qT_b[p2], identb_b, mask_b], writes=[psc_b[s_]])
                    pi_ = nxt("pt", 3)
                    S.op("act", lambda e, s_=s_, pi_=pi_: e.activation(out=pT[pi_][:, :], in_=psc[s_][:, :], func=AF.Exp,
                                                                      scale=float(scale)),
                         reads=[psc_b[s_]], writes=[pT_b[pi_]])

                    def f_pv(e, kt=kt, pi_=pi_):
                        for g_ in range(4):
                            h = kv * 4 + g_
                            bk, col = h // 6, (h % 6) * 65
                            st_ = not bank_started[bk]
                            bank_started[bk] = True
                            ins = e.matmul(pov[bk][:, col:col + 65], lhsT=pT[pi_][:, g_ * P:(g_ + 1) * P],
                                           rhs=Vw[kt % 4][:, kv, :], start=st_,
                                           stop=(h == last_in_bank[bk] and kt == kts[-1]), skip_group_check=True)
                        return ins
                    banks = sorted({(kv * 4 + g_) // 6 for g_ in range(4)})
                    S.op("pe", f_pv, reads=[pT_b[pi_], V_b[kt % 4]], writes=[pov_b[bk] for bk in banks])
            for bk in range(3):
                nh = 6 if bk < 2 else 4
                h0 = bk * 6
                pv3 = pov[bk][:, 0:nh * 65].rearrange("p (h d) -> p h d", d=65)
                S.op("dve", lambda e, pv3=pv3, h0=h0, nh=nh: e.tensor_tensor(
                    out=den[p2][:, h0:h0 + nh].unsqueeze(2), in0=pv3[:, :, 64:65],
                    in1=esink[:, h0:h0 + nh].unsqueeze(2), op=ALU.add),
                     reads=[pov_b[bk], esink_b], writes=[den_b[p2]])
                S.op("dve", lambda e, h0=h0, nh=nh: e.reciprocal(out=den[p2][:, h0:h0 + nh], in_=den[p2][:, h0:h0 + nh]),
                     reads=[den_b[p2]], writes=[den_b[p2]])
                S.op("dve", lambda e, pv3=pv3, h0=h0, nh=nh: e.tensor_tensor(
                    out=on[p2][:, h0:h0 + nh, :], in0=pv3[:, :, 0:64],
                    in1=bcast(den[p2][:, h0:h0 + nh].unsqueeze(2), [P, nh, 64]), op=ALU.mult),
                     reads=[pov_b[bk], den_b[p2]], writes=[on_b[p2]])

            def f_ot(e):
                oflat = on[p2][:, :, :].rearrange("p h d -> p (h d)")
                for s_ in range(8):
                    ins = e.transpose(out=ptb[:, s_, :], in_=oflat[:, s_ * P:(s_ + 1) * P], identity=identb[:, :])
                return ins
            S.op("pe", f_ot, reads=[on_b[p2], identb_b], writes=[ptb_b])
            S.op("act", lambda e: e.copy(out=oT[p2][:, :, :], in_=ptb[:, :, :]), reads=[ptb_b], writes=[oT_b[p2]])
            for n in range(2):
                m = nxt("mm", 2)

                def f_op(e, n=n, m=m):
                    for c in range(8):
                        ins = e.matmul(pmm[m][:, :], lhsT=oT[p2][:, c, :], rhs=w_outb[:, c, n * 512:(n + 1) * 512],
                                       start=(c == 0), stop=(c == 7))
                    return ins
                S.op("pe", f_op, reads=[oT_b[p2], w_out_b], writes=[pmm_b[m]])
                S.op("dve", lambda e, n=n, m=m: e.scalar_tensor_tensor(
                    out=xs[sl][:, n * 512:(n + 1) * 512], in0=xs[sl][:, n * 512:(n + 1) * 512], scalar=DN_ALPHA,
                    in1=pmm[m][:, :], op0=ALU.mult, op1=ALU.add),
                     reads=[xs_b[sl], pmm_b[m]], writes=[xs_b[sl]])
            ln_epilogue(S, (stats[p2], st_b[p2], mv[p2], mv_b[p2], sc[p2], sc_b[p2]),
                        xs[sl], xs_b[sl], g_t, b_t, gb_b, ot[p2], ot_b[p2])
            S.dma("sp", xout[i * P:(i + 1) * P, :], ot[p2][:, :], reads=[ot_b[p2]], writes=[xout_buf],
                  stream=f"{tag}o{p2}")
            if i + NXS < NT:
                load_x(i + NXS)

        lvl = int(os.environ.get("ATT_DBG", "9"))
        for t in range(min(NXS, NT)):
            load_x(t)
        for t in range(NT + 1):
            if t < NT and lvl >= 2:
                stage1(t)
            if t >= 1 and lvl >= 3:
                stage2(t - 1)
        S.barrier()


def attn_consts_host():
    j = np.arange(P)[:, None]
    i = np.arange(P)[None, :]
    mp = np.where(j >= i, 0.0, -30000.0).astype(np.float32)
    mn = np.where(j <= i, 0.0, -30000.0).astype(np.float32)
    invf = (10000.0 ** (-np.arange(0, HD, 2, dtype=np.float64) / HD)).astype(np.float32)
    return {"c_ident": np.eye(P, dtype=np.float32), "c_maskp": np.tile(mp, (1, 4)), "c_maskn": np.tile(mn, (1, 4)),
            "c_invf": invf}


def permute_attn_w_in(w_in):
    q = w_in[:, :1024].reshape(1024, 16, 64)[:, Q_HEAD_ORDER, :].reshape(1024, 1024)
    return np.ascontiguousarray(np.concatenate([q, w_in[:, 1024:]], axis=1))


def build_test_attn(NT):
    nc = bass.Bass("TRN2", target_bir_lowering=False)
    dt = lambda name, shape, dtp=F32, kind="ExternalInput": nc.dram_tensor(name, shape, dtp, kind=kind).ap()
    x = dt("x", [NT * P, D])
    pos = dt("pos", [NT * P], I32)
    w_in = dt("w_in", [D, A_IN])
    sink = dt("sink", [16])
    w_out = dt("w_out", [D, D])
    g = dt("g", [D])
    b = dt("b", [D])
    consts = (dt("c_ident", [P, P]), dt("c_maskp", [P, 512]), dt("c_maskn", [P, 512]), dt("c_invf", [32]))
    y = dt("y", [NT * P, D], kind="ExternalOutput")
    with ExitStack() as ctx:
        S = Sched(nc, ctx)
        attn_phase(S, nc, NT, x, y, pos, w_in, sink, w_out, g, b, consts, S.buf("xin"), S.buf("xout"), "a0")
        S.finish()
    return nc


G_H = 4
G_DK = 128
G_DV = 256
G_IN = 3104
G_TAU = 16.0


class PsumPool:
    def __init__(self, S, nc, ctx, tag, n=8):
        self.t = [ctx.enter_context(nc.psum_tensor(f"{tag}_pp{i}", [P, 512], F32)) for i in range(n)]
        self.b = S.pbufs(f"{tag}pp", n)
        self.i = 0
        self.n = n

    def get(self):
        k = self.i % self.n
        self.i += 1
        return self.t[k], self.b[k]


def gla_sweep1(S, nc, NT, xin, w_in, gw2f, gbf, gw2b, gbb, consts, scr, xin_buf, scr_buf, tag):
    ident_d, tri_d, negcol_d, gmask_d = consts
    with ExitStack() as ctx:
        sbt = lambda name, shape, dt: ctx.enter_context(nc.sbuf_tensor(f"{tag}_{name}", shape, dt))
        B = S.buf
        pp = PsumPool(S, nc, ctx, tag)
        w_inb = sbt("w_inb", [P, 8, 2080], BF16)
        w2a = sbt("w2a", [17, 2, 512], BF16)
        tri = sbt("tri", [P, 4, P], F32)
        negcol = sbt("negcol", [P, 1], F32)
        gmask = sbt("gmask", [P, 2, P], F32)
        identf = sbt("identf", [P, P], F32)
        identb = sbt("identb", [P, P], BF16)
        S32 = sbt("S32", [P, 4, G_DV], F32)
        Sb = sbt("Sb", [P, 4, G_DV], BF16)
        w_in_b, w2a_b, cst_b, identb_b, S32_b, Sb_b = B("w_in"), B("w2a"), B("cst"), B("idb"), B("S32"), B("Sb")
        S.dma("sp", identf[:, :], ident_d[:, :], writes=[cst_b], stream=f"{tag}c0")
        S.dma("sp", tri[:, :, :], tri_d.rearrange("k p t -> p k t"), writes=[cst_b], stream=f"{tag}c1")
        S.dma("sp", negcol[:, :], negcol_d[:, :], writes=[cst_b], stream=f"{tag}c2")
        S.dma("sp", gmask[:, :, :], gmask_d.rearrange("k p t -> p k t"), writes=[cst_b], stream=f"{tag}c3")
        w_inv = w_in.rearrange("(c p) f -> p c f", p=P)
        for c in range(8):
            S.dma("pool", w_inb[:, c, 0:2048], w_inv[:, c, 0:2048], writes=[w_in_b], stream=f"{tag}w{c % 2}")
        for c in range(8):
            S.dma("pool", w_inb[:, c, 2048:2080], w_inv[:, c, 3072:3104], writes=[w_in_b], stream=f"{tag}w{c % 2}")
        for d_, (w2, gb_) in enumerate(((gw2f, gbf), (gw2b, gbb))):
            S.dma("pool", w2a[0:16, d_, :], w2[:, :], writes=[w2a_b], stream=f"{tag}w0")
            S.dma("pool", w2a[16:17, d_, :], gb_.rearrange("(o f) -> o f", o=1), writes=[w2a_b], stream=f"{tag}w1")
        S.op("dve", lambda e: e.tensor_copy(out=identb[:, :], in_=identf[:, :]), reads=[cst_b], writes=[identb_b])
        S.op("pool", lambda e: e.memset(S32[:, :, :], 0.0), writes=[S32_b])
        S.op("pool", lambda e: e.memset(Sb[:, :, :], 0.0), writes=[Sb_b])

        NXS = 3
        xs = [sbt(f"x{i}", [P, D], F32) for i in range(NXS)]
        xT = [sbt(f"xT{i}", [P, 8, P], BF16) for i in range(2)]
        q_sb = [sbt(f"q{i}", [P, 512], F32) for i in range(2)]
        k_sb = [sbt(f"k{i}", [P, 512], F32) for i in range(2)]
        v_sb = [sbt(f"v{i}", [P, 1024], BF16) for i in range(2)]
        lrT = [sbt(f"lrT{i}", [17, 2, P], BF16) for i in range(2)]
        sp = [sbt(f"sp{i}", [P, 512], F32) for i in range(2)]
        dec = [[sbt(f"dec{i}_{d_}", [P, 4], F32) for d_ in range(2)] for i in range(2)]
        tb = [[sbt(f"tb{d_}_{j}", [P, 512], F32) for j in range(3)] for d_ in range(2)]
        qe = [[sbt(f"qe{i}_{d_}", [P, 512], BF16) for d_ in range(2)] for i in range(2)]
        ke = [[sbt(f"ke{i}_{d_}", [P, 512], BF16) for d_ in range(2)] for i in range(2)]
        kd = [[sbt(f"kd{i}_{d_}", [P, 512], BF16) for d_ in range(2)] for i in range(2)]
        qkT = [[sbt(f"qkT{i}_{d_}", [P, 8, P], BF16) for d_ in range(2)] for i in range(2)]
        attm = [[sbt(f"attm{i}_{d_}", [P, 4, P], BF16) for d_ in range(2)] for i in range(2)]
        opart = [sbt(f"op{i}", [P, 1024], F32) for i in range(2)]
        xs_b, xT_b, q_b, k_b, v_b, lrT_b = (S.bufs(n_, k_) for n_, k_ in
                                           (("xs", NXS), ("xT", 2), ("q", 2), ("k", 2), ("v", 2), ("lrT", 2)))
        sp_b = S.bufs("sp", 2)
        dec_b = [S.bufs(f"dec{i}", 2) for i in range(2)]
        tb_b = [S.bufs(f"tb{d_}", 3) for d_ in range(2)]
        qe_b, ke_b, kd_b, qkT_b, attm_b = ([S.bufs(f"{n_}{i}", 2) for i in range(2)] for n_ in ("qe", "ke", "kd", "qkT", "attm"))
        op_b = S.bufs("op", 2)
        for i in range(2):
            S.op("pool", lambda e, i=i: e.memset(lrT[i][:, :, :], 1.0), writes=[lrT_b[i]])

        def load_x(t):
            sl = t % NXS
            S.dma("sp", xs[sl][:, :], xin[t * P:(t + 1) * P, :], reads=[xin_buf], writes=[xs_b[sl]], stream=f"{tag}x{sl}")

        def mm8(out_ap, p2, c0, c1, wr_b):
            def f(e):
                for c in range(8):
                    ins = e.matmul(out_ap, lhsT=xT[p2][:, c, :], rhs=w_inb[:, c, c0:c1], start=(c == 0), stop=(c == 7))
                return ins
            S.op("pe", f, reads=[xT_b[p2], w_in_b], writes=[wr_b])

        for t in range(min(NXS, NT)):
            load_x(t)
        for t in range(NT):
            sl, p2 = t % NXS, t % 2
            for hh in range(2):
                pm, pm_b = pp.get()

                def f_tr(e, hh=hh, pm=pm):
                    for c in range(4):
                        ins = e.transpose(out=pm[:, c * P:(c + 1) * P], in_=xs[sl][:, (hh * 4 + c) * P:(hh * 4 + c + 1) * P],
                                          identity=identf[:, :])
                    return ins
                S.op("pe", f_tr, reads=[xs_b[sl], cst_b], writes=[pm_b])
                src = pm[:, :].rearrange("p (c t) -> p c t", c=4)
                if hh == 0:
                    S.op("act", lambda e, src=src: e.copy(out=xT[p2][:, 0:4, :], in_=src), reads=[pm_b], writes=[xT_b[p2]])
                else:
                    S.op("dve", lambda e, src=src: e.tensor_copy(out=xT[p2][:, 4:8, :], in_=src), reads=[pm_b], writes=[xT_b[p2]])
            if t + NXS < NT:
                pass
            pm, pm_b = pp.get()

            def f_lr(e, pm=pm):
                for d_ in range(2):
                    for c in range(8):
                        ins = e.matmul(pm[0:16, d_ * P:(d_ + 1) * P], lhsT=w_inb[:, c, 2048 + 16 * d_:2064 + 16 * d_],
                                       rhs=xT[p2][:, c, :], start=(c == 0), stop=(c == 7))
                return ins
            S.op("pe", f_lr, reads=[xT_b[p2], w_in_b], writes=[pm_b])
            S.op("dve", lambda e, pm=pm: e.tensor_copy(out=lrT[p2][0:16, :, :], in_=pm[0:16, 0:256].rearrange("p (a t) -> p a t", a=2)),
                 reads=[pm_b], writes=[lrT_b[p2]])
            pm, pm_b = pp.get()
            mm8(pm[:, :], p2, 0, 512, pm_b)
            S.op("act", lambda e, pm=pm: e.activation(out=q_sb[p2][:, :], in_=pm[:, :], func=AF.Copy, scale=float(G_DK ** -0.5)),
                 reads=[pm_b], writes=[q_b[p2]])
            pm, pm_b = pp.get()
            mm8(pm[:, :], p2, 512, 1024, pm_b)
            S.op("dve", lambda e, pm=pm: e.tensor_copy(out=k_sb[p2][:, :], in_=pm[:, :]), reads=[pm_b], writes=[k_b[p2]])
            for n in range(2):
                pm, pm_b = pp.get()
                mm8(pm[:, :], p2, 1024 + n * 512, 1536 + n * 512, pm_b)
                S.op("act", lambda e, pm=pm, n=n: e.copy(out=v_sb[p2][:, n * 512:(n + 1) * 512], in_=pm[:, :]),
                     reads=[pm_b], writes=[v_b[p2]])
            for d_ in range(2):
                pm, pm_b = pp.get()
                S.op("pe", lambda e, pm=pm, d_=d_: e.matmul(pm[:, :], lhsT=lrT[p2][0:17, d_, :], rhs=w2a[0:17, d_, :],
                                                           start=True, stop=True),
                     reads=[lrT_b[p2], w2a_b], writes=[pm_b])
                S.op("act", lambda e, pm=pm, d_=d_: e.activation(out=sp[d_][:, :], in_=pm[:, :], func=AF.Exp, scale=-1.0),
                     reads=[pm_b], writes=[sp_b[d_]])
                S.op("act", lambda e, d_=d_: e.activation(out=sp[d_][:, :], in_=sp[d_][:, :], func=AF.Ln, bias=1.0, scale=1.0),
                     reads=[sp_b[d_]], writes=[sp_b[d_]])
                pm, pm_b = pp.get()

                def f_dec(e, pm=pm, d_=d_):
                    for h in range(4):
                        ins = e.matmul(pm[:, 2 * h:2 * h + 1], lhsT=sp[d_][:, h * P:(h + 1) * P], rhs=negcol[:, 0:1],
                                       start=True, stop=True)
                    return ins
                S.op("pe", f_dec, reads=[sp_b[d_], cst_b], writes=[pm_b])
                S.op("act", lambda e, pm=pm, d_=d_: e.activation(
                    out=dec[p2][d_][:, :].unsqueeze(2), in_=pm[:, 0:8].rearrange("p (h two) -> p h two", two=2)[:, :, 0:1],
                    func=AF.Exp), reads=[pm_b], writes=[dec_b[p2][d_]])
                pmb, pmb_b = pp.get()
                S.op("pe", lambda e, pmb=pmb, d_=d_: e.matmul(pmb[:, :], lhsT=tri[:, 2 * d_, :], rhs=sp[d_][:, :], start=True, stop=True),
                     reads=[sp_b[d_], cst_b], writes=[pmb_b])
                pmc, pmc_b = pp.get()
                S.op("pe", lambda e, pmc=pmc, d_=d_: e.matmul(pmc[:, :], lhsT=tri[:, 2 * d_ + 1, :], rhs=sp[d_][:, :], start=True, stop=True),
                     reads=[sp_b[d_], cst_b], writes=[pmc_b])
                S.op("act", lambda e, pmb=pmb, d_=d_: e.activation(out=tb[d_][0][:, :], in_=pmb[:, :], func=AF.Exp),
                     reads=[pmb_b], writes=[tb_b[d_][0]])
                S.op("act", lambda e, pmb=pmb, d_=d_: e.activation(out=tb[d_][1][:, :], in_=pmb[:, :], func=AF.Exp, scale=-1.0),
                     reads=[pmb_b], writes=[tb_b[d_][1]])
                S.op("act", lambda e, pmc=pmc, d_=d_: e.activation(out=tb[d_][2][:, :], in_=pmc[:, :], func=AF.Exp),
                     reads=[pmc_b], writes=[tb_b[d_][2]])
                S.op("dve", lambda e, d_=d_: e.tensor_tensor(out=qe[p2][d_][:, :], in0=q_sb[p2][:, :], in1=tb[d_][0][:, :], op=ALU.mult),
                     reads=[q_b[p2], tb_b[d_][0]], writes=[qe_b[p2][d_]])
                S.op("pool", lambda e, d_=d_: e.tensor_tensor(out=ke[p2][d_][:, :], in0=k_sb[p2][:, :], in1=tb[d_][1][:, :], op=ALU.mult),
                     reads=[k_b[p2], tb_b[d_][1]], writes=[ke_b[p2][d_]])
                S.op("pool", lambda e, d_=d_: e.tensor_tensor(out=kd[p2][d_][:, :], in0=k_sb[p2][:, :], in1=tb[d_][2][:, :], op=ALU.mult),
                     reads=[k_b[p2], tb_b[d_][2]], writes=[kd_b[p2][d_]])
                pm, pm_b = pp.get()
                pmv = pm[:, :].bitcast(BF16).rearrange("p (c t) -> p c t", c=8)

                def f_t(e, pmv=pmv, d_=d_):
                    for j, src in enumerate((qe[p2][d_], ke[p2][d_])):
                        for h in range(4):
                            ins = e.transpose(out=pmv[:, j * 4 + h, :], in_=src[:, h * P:(h + 1) * P], identity=identb[:, :])
                    return ins
                S.op("pe", f_t, reads=[qe_b[p2][d_], ke_b[p2][d_], identb_b], writes=[pm_b])
                S.op("act" if d_ == 0 else "dve",
                     (lambda e, pmv=pmv, d_=d_: e.copy(out=qkT[p2][d_][:, :, :], in_=pmv)) if d_ == 0 else
                     (lambda e, pmv=pmv, d_=d_: e.tensor_copy(out=qkT[p2][d_][:, :, :], in_=pmv)),
                     reads=[pm_b], writes=[qkT_b[p2][d_]])
                pm, pm_b = pp.get()

                def f_att(e, pm=pm, d_=d_):
                    for h in range(4):
                        ins = e.matmul(pm[:, h * P:(h + 1) * P], lhsT=qkT[p2][d_][:, 4 + h, :], rhs=qkT[p2][d_][:, h, :],
                                       start=True, stop=True)
                    return ins
                S.op("pe", f_att, reads=[qkT_b[p2][d_]], writes=[pm_b])
                S.op("dve", lambda e, pm=pm, d_=d_: e.tensor_tensor(
                    out=attm[p2][d_][:, :, :], in0=pm[:, :].rearrange("p (h t) -> p h t", h=4),
                    in1=bcast(gmask[:, d_:d_ + 1, :], [P, 4, P]), op=ALU.mult),
                     reads=[pm_b, cst_b], writes=[attm_b[p2][d_]])
            for hp in range(2):
                pm, pm_b = pp.get()

                def f_o(e, pm=pm, hp=hp):
                    for hh in range(2):
                        h = 2 * hp + hh
                        o_ap = pm[:, hh * G_DV:(hh + 1) * G_DV]
                        vv = v_sb[p2][:, h * G_DV:(h + 1) * G_DV]
                        e.matmul(o_ap, lhsT=attm[p2][0][:, h, :], rhs=vv, start=True, stop=False)
                        e.matmul(o_ap, lhsT=attm[p2][1][:, h, :], rhs=vv, start=False, stop=False)
                        ins = e.matmul(o_ap, lhsT=qkT[p2][0][:, h, :], rhs=Sb[:, h, :], start=False, stop=True)
                    return ins
                S.op("pe", f_o, reads=[attm_b[p2][0], attm_b[p2][1], v_b[p2], qkT_b[p2][0], Sb_b], writes=[pm_b])
                S.op("act", lambda e, pm=pm, hp=hp: e.copy(out=opart[p2][:, hp * 512:(hp + 1) * 512], in_=pm[:, :]),
                     reads=[pm_b], writes=[op_b[p2]])
            for hp in range(2):
                pm, pm_b = pp.get()

                def f_s(e, pm=pm, hp=hp):
                    for hh in range(2):
                        h = 2 * hp + hh
                        ins = e.matmul(pm[:, hh * G_DV:(hh + 1) * G_DV], lhsT=kd[p2][0][:, h * P:(h + 1) * P],
                                       rhs=v_sb[p2][:, h * G_DV:(h + 1) * G_DV], start=True, stop=True)
                    return ins
                S.op("pe", f_s, reads=[kd_b[p2][0], v_b[p2]], writes=[pm_b])
                for hh in range(2):
                    h = 2 * hp + hh
                    S.op("dve", lambda e, pm=pm, h=h, hh=hh: e.scalar_tensor_tensor(
                        out=S32[:, h, :], in0=S32[:, h, :], scalar=dec[p2][0][:, h:h + 1], in1=pm[:, hh * G_DV:(hh + 1) * G_DV],
                        op0=ALU.mult, op1=ALU.add), reads=[S32_b, dec_b[p2][0], pm_b], writes=[S32_b])
            S.op("pool", lambda e: e.tensor_copy(out=Sb[:, :, :], in_=S32[:, :, :]), reads=[S32_b], writes=[Sb_b])
            S.dma("sp", scr["opart"][t * P:(t + 1) * P, :], opart[p2][:, :], reads=[op_b[p2]], writes=[scr_buf], stream=f"{tag}so{p2}")
            S.dma("sp", scr["qeTb"][t], qkT[p2][1][:, 0:4, :].rearrange("p h t -> p (h t)"), reads=[qkT_b[p2][1]],
                  writes=[scr_buf], stream=f"{tag}sq{p2}")
            S.dma("sp", scr["kdb"][t * P:(t + 1) * P, :], kd[p2][1][:, :], reads=[kd_b[p2][1]], writes=[scr_buf], stream=f"{tag}sk{p2}")
            S.dma("sp", scr["vsb"][t * P:(t + 1) * P, :], v_sb[p2][:, :], reads=[v_b[p2]], writes=[scr_buf], stream=f"{tag}sv{p2}")
            S.dma("sp", scr["decb"][t], dec[p2][1][:, :], reads=[dec_b[p2][1]], writes=[scr_buf], stream=f"{tag}sd{p2}")
            if t + NXS < NT:
                load_x(t + NXS)
        S.barrier()


def gla_sweep2(S, nc, NT, xin, xout, w_in, norm_g, w_out, g, b, consts, scr, xin_buf, scr_buf, xout_buf, tag):
    ident_d = consts[0]
    with ExitStack() as ctx:
        sbt = lambda name, shape, dt: ctx.enter_context(nc.sbuf_tensor(f"{tag}_{name}", shape, dt))
        B = S.buf
        pp = PsumPool(S, nc, ctx, tag)
        w_rb = sbt("w_rb", [P, 8, 1024], BF16)
        w_outb = sbt("w_outb", [P, 8, D], BF16)
        identf = sbt("identf", [P, P], F32)
        identb = sbt("identb", [P, P], BF16)
        ng_t = sbt("ng", [P, G_DV], F32)
        g_t = sbt("g", [P, D], F32)
        b_t = sbt("b", [P, D], F32)
        S32 = sbt("S32", [P, 4, G_DV], F32)
        Sb = sbt("Sb", [P, 4, G_DV], BF16)
        w_b, cst_b, identb_b, gb_b, S32_b, Sb_b = B("w"), B("cst"), B("idb"), B("gb"), B("S32"), B("Sb")
        S.dma("sp", identf[:, :], ident_d[:, :], writes=[cst_b], stream=f"{tag}c0")
        S.dma("sp", ng_t[:, :], norm_g.partition_broadcast(P), writes=[gb_b], stream=f"{tag}c1")
        S.dma("sp", g_t[:, :], g.partition_broadcast(P), writes=[gb_b], stream=f"{tag}c2")
        S.dma("sp", b_t[:, :], b.partition_broadcast(P), writes=[gb_b], stream=f"{tag}c3")
        w_inv = w_in.rearrange("(c p) f -> p c f", p=P)
        w_outv = w_out.rearrange("(c p) f -> p c f", p=P)
        for c in range(8):
            S.dma("pool", w_rb[:, c, :], w_inv[:, c, 2048:3072], writes=[w_b], stream=f"{tag}w{c % 2}")
        for c in range(8):
            S.dma("pool", w_outb[:, c, :], w_outv[:, c, :], writes=[w_b], stream=f"{tag}w{c % 2}")
        S.op("dve", lambda e: e.tensor_copy(out=identb[:, :], in_=identf[:, :]), reads=[cst_b], writes=[identb_b])
        S.op("pool", lambda e: e.memset(S32[:, :, :], 0.0), writes=[S32_b])
        S.op("pool", lambda e: e.memset(Sb[:, :, :], 0.0), writes=[Sb_b])

        NXS = 3
        xs = [sbt(f"x{i}", [P, D], F32) for i in range(NXS)]
        opa = [sbt(f"opa{i}", [P, D], F32) for i in range(NXS)]
        qeT = [sbt(f"qeT{i}", [P, 4, P], BF16) for i in range(NXS)]
        kdb = [sbt(f"kdb{i}", [P, 512], BF16) for i in range(NXS)]
        vsb = [sbt(f"vsb{i}", [P, 1024], BF16) for i in range(NXS)]
        decb = [sbt(f"decb{i}", [P, 4], F32) for i in range(NXS)]
        xT = [sbt(f"xT{i}", [P, 8, P], BF16) for i in range(2)]
        er = [sbt(f"er{i}", [P, 1024], F32) for i in range(2)]
        sr = [sbt(f"sr{i}", [P, 4, G_DV], F32) for i in range(2)]
        of = [sbt(f"of{i}", [P, 4, G_DV], F32) for i in range(2)]
        sq = sbt("sq", [P, G_DV], F32)
        ss = [sbt(f"ss{i}", [P, 4], F32) for i in range(2)]
        og = [sbt(f"og{i}", [P, 1024], BF16) for i in range(2)]
        oT = [sbt(f"oT{i}", [P, 8, P], BF16) for i in range(2)]
        ot = [sbt(f"ot{i}", [P, D], F32) for i in range(2)]
        stats = [sbt(f"st{i}", [P, 2, 6], F32) for i in range(2)]
        mv = [sbt(f"mv{i}", [P, 2], F32) for i in range(2)]
        sc = [sbt(f"sc{i}", [P, 2], F32) for i in range(2)]
        xs_b, opa_b, qeT_b, kdb_b, vsb_b, decb_b = (S.bufs(n_, NXS) for n_ in ("xs", "opa", "qeT", "kdb", "vsb", "decb"))
        xT_b, er_b, sr_b, of_b, ss_b, og_b, oT_b, ot_b, st_b, mv_b, sc_b = (
            S.bufs(n_, 2) for n_ in ("xT", "er", "sr", "of", "ss", "og", "oT", "ot", "st", "mv", "sc"))
        sq_b = B("sq")

        order = list(range(NT - 1, -1, -1))

        def load(idx):
            t = order[idx]
            sl = idx % NXS
            S.dma("sp", xs[sl][:, :], xin[t * P:(t + 1) * P, :], reads=[xin_buf], writes=[xs_b[sl]], stream=f"{tag}x{sl}")
            S.dma("sp", opa[sl][:, :], scr["opart"][t * P:(t + 1) * P, :], reads=[scr_buf], writes=[opa_b[sl]], stream=f"{tag}lo{sl}")
            S.dma("sp", qeT[sl][:, :, :].rearrange("p h t -> p (h t)"), scr["qeTb"][t], reads=[scr_buf], writes=[qeT_b[sl]],
                  stream=f"{tag}lq{sl}")
            S.dma("sp", kdb[sl][:, :], scr["kdb"][t * P:(t + 1) * P, :], reads=[scr_buf], writes=[kdb_b[sl]], stream=f"{tag}lk{sl}")
            S.dma("sp", vsb[sl][:, :], scr["vsb"][t * P:(t + 1) * P, :], reads=[scr_buf], writes=[vsb_b[sl]], stream=f"{tag}lv{sl}")
            S.dma("sp", decb[sl][:, :], scr["decb"][t], reads=[scr_buf], writes=[decb_b[sl]], stream=f"{tag}ld{sl}")

        state_ver = [0]

        def tile_gen(idx):
            t = order[idx]
            sl, p2 = idx % NXS, idx % 2
            for hh in range(2):
                pm, pm_b = pp.get()

                def f_tr(e, hh=hh, pm=pm):
                    for c in range(4):
                        ins = e.transpose(out=pm[:, c * P:(c + 1) * P], in_=xs[sl][:, (hh * 4 + c) * P:(hh * 4 + c + 1) * P],
                                          identity=identf[:, :])
                    return ins
                S.op("pe", f_tr, reads=[xs_b[sl], cst_b], writes=[pm_b])
                src = pm[:, :].rearrange("p (c t) -> p c t", c=4)
                if hh == 0:
                    S.op("act", lambda e, src=src: e.copy(out=xT[p2][:, 0:4, :], in_=src), reads=[pm_b], writes=[xT_b[p2]])
                else:
                    S.op("dve", lambda e, src=src: e.tensor_copy(out=xT[p2][:, 4:8, :], in_=src), reads=[pm_b], writes=[xT_b[p2]])
            yield
            for n in range(2):
                pm, pm_b = pp.get()

                def f_r(e, pm=pm, n=n):
                    for c in range(8):
                        ins = e.matmul(pm[:, :], lhsT=xT[p2][:, c, :], rhs=w_rb[:, c, n * 512:(n + 1) * 512], start=(c == 0), stop=(c == 7))
                    return ins
                S.op("pe", f_r, reads=[xT_b[p2], w_b], writes=[pm_b])
                ern = er[p2][:, n * 512:(n + 1) * 512]
                srn = sr[p2][:, :, :].rearrange("p h d -> p (h d)")[:, n * 512:(n + 1) * 512]
                S.op("act", lambda e, pm=pm, ern=ern: e.activation(out=ern, in_=pm[:, :], func=AF.Exp, scale=-1.0),
                     reads=[pm_b], writes=[er_b[p2]])
                S.op("act", lambda e, ern=ern: e.activation(out=ern, in_=ern, func=AF.Ln, bias=1.0, scale=1.0),
                     reads=[er_b[p2]], writes=[er_b[p2]])
                S.op("act", lambda e, ern=ern: e.activation(out=ern, in_=ern, func=AF.Exp, scale=-1.0),
                     reads=[er_b[p2]], writes=[er_b[p2]])
                S.op("dve", lambda e, pm=pm, ern=ern, srn=srn: e.tensor_tensor(out=srn, in0=pm[:, :], in1=ern, op=ALU.mult),
                     reads=[pm_b, er_b[p2]], writes=[sr_b[p2]])
                S.op("dve", lambda e, srn=srn, n=n: e.tensor_tensor(
                    out=srn.rearrange("p (h d) -> p h d", d=G_DV), in0=srn.rearrange("p (h d) -> p h d", d=G_DV),
                    in1=bcast(ng_t[:, :].unsqueeze(1), [P, 2, G_DV]), op=ALU.mult),
                     reads=[sr_b[p2], gb_b], writes=[sr_b[p2]])
                yield
            assert state_ver[0] == idx
            for hp in range(2):
                pm, pm_b = pp.get()

                def f_o(e, pm=pm, hp=hp):
                    for hh in range(2):
                        h = 2 * hp + hh
                        ins = e.matmul(pm[:, hh * G_DV:(hh + 1) * G_DV], lhsT=qeT[sl][:, h, :], rhs=Sb[:, h, :], start=True, stop=True)
                    return ins
                S.op("pe", f_o, reads=[qeT_b[sl], Sb_b], writes=[pm_b])
                S.op("dve", lambda e, pm=pm, hp=hp: e.tensor_tensor(
                    out=of[p2][:, 2 * hp:2 * hp + 2, :].rearrange("p h d -> p (h d)"), in0=pm[:, :],
                    in1=opa[sl][:, hp * 512:(hp + 1) * 512], op=ALU.add),
                     reads=[pm_b, opa_b[sl]], writes=[of_b[p2]])
            yield
            for hp in range(2):
                pm, pm_b = pp.get()

                def f_s(e, pm=pm, hp=hp):
                    for hh in range(2):
                        h = 2 * hp + hh
                        ins = e.matmul(pm[:, hh * G_DV:(hh + 1) * G_DV], lhsT=kdb[sl][:, h * P:(h + 1) * P],
                                       rhs=vsb[sl][:, h * G_DV:(h + 1) * G_DV], start=True, stop=True)
                    return ins
                S.op("pe", f_s, reads=[kdb_b[sl], vsb_b[sl]], writes=[pm_b])
                for hh in range(2):
                    h = 2 * hp + hh
                    S.op("dve", lambda e, pm=pm, h=h, hh=hh: e.scalar_tensor_tensor(
                        out=S32[:, h, :], in0=S32[:, h, :], scalar=decb[sl][:, h:h + 1], in1=pm[:, hh * G_DV:(hh + 1) * G_DV],
                        op0=ALU.mult, op1=ALU.add), reads=[S32_b, decb_b[sl], pm_b], writes=[S32_b])
            S.op("act", lambda e: e.copy(out=Sb[:, :, :], in_=S32[:, :, :]), reads=[S32_b], writes=[Sb_b])
            state_ver[0] = idx + 1
            yield
            for h in range(4):
                S.op("act", lambda e, h=h: e.activation(out=sq[:, :], in_=of[p2][:, h, :], func=AF.Square,
                                                        accum_out=ss[p2][:, h:h + 1]),
                     reads=[of_b[p2]], writes=[ss_b[p2]])
            S.op("act", lambda e: e.activation(out=ss[p2][:, :], in_=ss[p2][:, :], func=AF.Ln, bias=1e-6, scale=1.0 / G_DV),
                 reads=[ss_b[p2]], writes=[ss_b[p2]])
            S.op("act", lambda e: e.activation(out=ss[p2][:, :], in_=ss[p2][:, :], func=AF.Exp, scale=-0.5),
                 reads=[ss_b[p2]], writes=[ss_b[p2]])
            def f_hn(e):
                for h in range(4):
                    ins = e.activation(out=of[p2][:, h, :], in_=of[p2][:, h, :], func=AF.Copy, scale=ss[p2][:, h:h + 1])
                return ins
            S.op("act", f_hn, reads=[of_b[p2], ss_b[p2]], writes=[of_b[p2]])
            S.op("dve", lambda e: e.tensor_tensor(out=og[p2][:, :], in0=of[p2][:, :, :].rearrange("p h d -> p (h d)"),
                                                  in1=sr[p2][:, :, :].rearrange("p h d -> p (h d)"), op=ALU.mult),
                 reads=[of_b[p2], sr_b[p2]], writes=[og_b[p2]])
            yield
            pm, pm_b = pp.get()
            pmv = pm[:, :].bitcast(BF16).rearrange("p (c t) -> p c t", c=8)

            def f_ot(e, pmv=pmv):
                for s_ in range(8):
                    ins = e.transpose(out=pmv[:, s_, :], in_=og[p2][:, s_ * P:(s_ + 1) * P], identity=identb[:, :])
                return ins
            S.op("pe", f_ot, reads=[og_b[p2], identb_b], writes=[pm_b])
            S.op("act", lambda e, pmv=pmv: e.copy(out=oT[p2][:, :, :], in_=pmv), reads=[pm_b], writes=[oT_b[p2]])
            for n in range(2):
                pm, pm_b = pp.get()

                def f_op(e, pm=pm, n=n):
                    for c in range(8):
                        ins = e.matmul(pm[:, :], lhsT=oT[p2][:, c, :], rhs=w_outb[:, c, n * 512:(n + 1) * 512], start=(c == 0), stop=(c == 7))
                    return ins
                S.op("pe", f_op, reads=[oT_b[p2], w_b], writes=[pm_b])
                S.op("dve", lambda e, pm=pm, n=n: e.scalar_tensor_tensor(
                    out=xs[sl][:, n * 512:(n + 1) * 512], in0=xs[sl][:, n * 512:(n + 1) * 512], scalar=DN_ALPHA,
                    in1=pm[:, :], op0=ALU.mult, op1=ALU.add), reads=[xs_b[sl], pm_b], writes=[xs_b[sl]])
            yield
            ln_epilogue(S, (stats[p2], st_b[p2], mv[p2], mv_b[p2], sc[p2], sc_b[p2]),
                        xs[sl], xs_b[sl], g_t, b_t, gb_b, ot[p2], ot_b[p2])
            S.dma("sp", xout[t * P:(t + 1) * P, :], ot[p2][:, :], reads=[ot_b[p2]], writes=[xout_buf], stream=f"{tag}o{p2}")
            if idx + NXS < NT:
                load(idx + NXS)

        for idx in range(min(NXS, NT)):
            load(idx)
        run_interleaved((tile_gen(i) for i in range(NT)), depth=2, skew=5)
        S.barrier()


def gla_consts_host():
    u = np.arange(P)[:, None]
    t = np.arange(P)[None, :]
    c = np.float32(-1.0 / G_TAU)
    tri = np.stack([np.where(u <= t, c, 0), np.where(u > t, c, 0),
                    np.where(u >= t, c, 0), np.where(u < t, c, 0)]).astype(np.float32)
    gmask = np.stack([np.where(t >= u, 1.0, 0.0), np.where(t <= u, 1.0, 0.0)]).astype(np.float32)
    return {"c_tri": tri, "c_negcol": np.full((P, 1), c, np.float32), "c_gmask": gmask}


def gla_scratch(nc, NT, tag):
    mk = lambda name, shape, dtp: nc.dram_tensor(f"{tag}_{name}", shape, dtp, kind="Internal").ap()
    return {"opart": mk("opart", [NT * P, 1024], F32), "qeTb": mk("qeTb", [NT, P, 512], BF16),
            "kdb": mk("kdb", [NT * P, 512], BF16), "vsb": mk("vsb", [NT * P, 1024], BF16),
            "decb": mk("decb", [NT, P, 4], F32)}


def build_test_gla(NT):
    nc = bass.Bass("TRN2", target_bir_lowering=False)
    dt = lambda name, shape, dtp=F32, kind="ExternalInput": nc.dram_tensor(name, shape, dtp, kind=kind).ap()
    x = dt("x", [NT * P, D])
    w_in = dt("w_in", [D, G_IN])
    gw2f, gbf, gw2b, gbb = dt("gw2f", [16, 512]), dt("gbf", [512]), dt("gw2b", [16, 512]), dt("gbb", [512])
    norm_g = dt("norm_g", [G_DV])
    w_out = dt("w_out", [D, D])
    g, b = dt("g", [D]), dt("b", [D])
    consts = (dt("c_ident", [P, P]), dt("c_tri", [4, P, P]), dt("c_negcol", [P, 1]), dt("c_gmask", [2, P, P]))
    y = dt("y", [NT * P, D], kind="ExternalOutput")
    scr = gla_scratch(nc, NT, "g1")
    with ExitStack() as ctx:
        S = Sched(nc, ctx)
        xin_buf, scr_buf, xout_buf = S.buf("xin"), S.buf("scr"), S.buf("xout")
        gla_sweep1(S, nc, NT, x, w_in, gw2f, gbf, gw2b, gbb, consts, scr, xin_buf, scr_buf, "g1a")
        gla_sweep2(S, nc, NT, x, y, w_in, norm_g, w_out, g, b, consts, scr, xin_buf, scr_buf, xout_buf, "g1b")
        S.finish()
    return nc


def build_full(NT):
    nc = bass.Bass("TRN2", target_bir_lowering=False)
    dt = lambda name, shape, dtp=F32, kind="ExternalInput": nc.dram_tensor(name, shape, dtp, kind=kind).ap()
    N = NT * P
    x = dt("x", [N, D])
    pos = dt("pos", [N], I32)
    a_w_in, a_sink, a_w_out = dt("a_w_in", [D, A_IN]), dt("a_sink", [16]), dt("a_w_out", [D, D])
    g_w_in = dt("g_w_in", [D, G_IN])
    gw2f, gbf, gw2b, gbb = dt("gw2f", [16, 512]), dt("gbf", [512]), dt("gw2b", [16, 512]), dt("gbb", [512])
    norm_g = dt("norm_g", [G_DV])
    g_w_out = dt("g_w_out", [D, D])
    mix_g, mix_b = [dt(f"mix_g{i}", [D]) for i in range(2)], [dt(f"mix_b{i}", [D]) for i in range(2)]
    w1 = [dt(f"w1_{i}", [D, DFF]) for i in range(2)]
    w2 = [dt(f"w2_{i}", [DFF, D]) for i in range(2)]
    mlp_g, mlp_b = [dt(f"mlp_g{i}", [D]) for i in range(2)], [dt(f"mlp_b{i}", [D]) for i in range(2)]
    c_ident = dt("c_ident", [P, P])
    a_consts = (c_ident, dt("c_maskp", [P, 512]), dt("c_maskn", [P, 512]), dt("c_invf", [32]))
    g_consts = (c_ident, dt("c_tri", [4, P, P]), dt("c_negcol", [P, 1]), dt("c_gmask", [2, P, P]))
    out = dt("out", [N, D], kind="ExternalOutput")
    s = [nc.dram_tensor(f"act{i}", [N, D], F32, kind="Internal").ap() for i in range(3)]
    scr = gla_scratch(nc, NT, "g1")
    with ExitStack() as ctx:
        S = Sched(nc, ctx)
        bx, bout, bscr = S.buf("x"), S.buf("out"), S.buf("scr")
        bs = S.bufs("act", 3)
        attn_phase(S, nc, NT, x, s[0], pos, a_w_in, a_sink, a_w_out, mix_g[0], mix_b[0], a_consts, bx, bs[0], "a0")
        mlp_phase(S, nc, NT, s[0], s[1], w1[0], w2[0], mlp_g[0], mlp_b[0], c_ident, bs[0], bs[1], "m0")
        gla_sweep1(S, nc, NT, s[1], g_w_in, gw2f, gbf, gw2b, gbb, g_consts, scr, bs[1], bscr, "ga")
        gla_sweep2(S, nc, NT, s[1], s[2], g_w_in, norm_g, g_w_out, mix_g[1], mix_b[1], g_consts, scr, bs[1], bscr, bs[2], "gb")
        mlp_phase(S, nc, NT, s[2], out, w1[1], w2[1], mlp_g[1], mlp_b[1], c_ident, bs[2], bout, "m1")
        S.finish()
    return nc


def host_inputs(inp, core, NT=SEQ // P):
    f = lambda a: np.ascontiguousarray(np.asarray(a, dtype=np.float32))
    N = NT * P
    m = {
        "x": f(inp["x"][core, :N]),
        "pos": np.ascontiguousarray(np.asarray(inp["positions"][core, :N], dtype=np.int32)),
        "a_w_in": permute_attn_w_in(f(inp["attn_w_in"][0])),
        "a_sink": f(inp["attn_sink"][0]),
        "a_w_out": f(inp["attn_w_out"][0]),
        "g_w_in": f(inp["gla_w_in"][0]),
        "gw2f": f(inp["gla_gate_w2_fwd"][0]), "gbf": f(inp["gla_gate_b_fwd"][0]),
        "gw2b": f(inp["gla_gate_w2_bwd"][0]), "gbb": f(inp["gla_gate_b_bwd"][0]),
        "norm_g": f(inp["gla_norm_g"][0]),
        "g_w_out": f(inp["gla_w_out"][0]),
    }
    for i in range(2):
        m[f"mix_g{i}"] = f(inp["mix_ln_g"][i])
        m[f"mix_b{i}"] = f(inp["mix_ln_b"][i])
        m[f"w1_{i}"] = f(inp["mlp_w1"][i])
        m[f"w2_{i}"] = f(inp["mlp_w2"][i])
        m[f"mlp_g{i}"] = f(inp["mlp_ln_g"][i])
        m[f"mlp_b{i}"] = f(inp["mlp_ln_b"][i])
    m.update(attn_consts_host())
    m.update(gla_consts_host())
    return m


def kernel(**inputs):
    NT = SEQ // P
    nc = build_full(NT)
    shared = host_inputs(inputs, 0)
    in_maps = []
    for c in range(NCORES):
        m = dict(shared)
        m["x"] = np.ascontiguousarray(np.asarray(inputs["x"][c], dtype=np.float32))
        m["pos"] = np.ascontiguousarray(np.asarray(inputs["positions"][c], dtype=np.int32))
        in_maps.append(m)
    res = run_bass_kernel_spmd(nc, in_maps, core_ids=list(range(NCORES)))
    return np.stack([np.asarray(r["out"], dtype=np.float32) for r in res.results], axis=0)
```

```python
import os
import numpy as np
from contextlib import ExitStack
import ml_dtypes
import concourse.bass as bass
import concourse.mybir as mybir
from concourse.bass_utils import run_bass_kernel_spmd

F32 = mybir.dt.float32
BF16 = mybir.dt.bfloat16
I32 = mybir.dt.int32
AF = mybir.ActivationFunctionType
ALU = mybir.AluOpType

P = 128
D = 1024
DFF = 4096
SEQ = 8192
NCORES = 8
DEPTH = 2
DN_ALPHA = float((2 * DEPTH) ** 0.25)
LN_EPS = 1e-5

SEM_EPOCH = 30000


class Buf:
    __slots__ = ("name", "lw", "rd", "rd_dma", "excl")

    def __init__(self, name, excl=False):
        self.name = name
        self.excl = excl
        self.lw = None
        self.rd = {}
        self.rd_dma = []


class Sched:
    def __init__(self, nc, ctx):
        self.nc = nc
        self.ctx = ctx
        self.E = {"pe": nc.tensor, "act": nc.scalar, "dve": nc.vector,
                  "pool": nc.gpsimd, "sp": nc.sync}
        self.cnt = {e: 0 for e in self.E}
        self.sems = {e: [] for e in self.E}
        self.waited = {e: {} for e in self.E}
        self.streams = {}
        self.nwaits = 0

    def buf(self, name):
        return Buf(name)

    def bufs(self, name, n):
        return [Buf(f"{name}{i}") for i in range(n)]

    def pbufs(self, name, n):
        return [Buf(f"{name}{i}", excl=True) for i in range(n)]

    def _eng_sem(self, eng, k):
        ep = (k - 1) // SEM_EPOCH
        while len(self.sems[eng]) <= ep:
            self.sems[eng].append(
                self.ctx.enter_context(self.nc.semaphore(f"s_{eng}{len(self.sems[eng])}")))
        return self.sems[eng][ep], (k - 1) % SEM_EPOCH + 1

    def _stream(self, name):
        st = self.streams.get(name)
        if st is None:
            st = {"n": 0, "sem": self.ctx.enter_context(self.nc.semaphore(f"d_{name}"))}
            self.streams[name] = st
        return st

    def _wait(self, eng, tok):
        w = self.waited[eng]
        if tok[0] == "c":
            _, peng, k = tok
            if peng == "pe" and eng == "pe":
                return
            if w.get(peng, 0) >= k:
                return
            w[peng] = k
            sem, val = self._eng_sem(peng, k)
        else:
            _, sname, n = tok
            key = "d:" + sname
            if w.get(key, 0) >= n:
                return
            w[key] = n
            sem, val = self.streams[sname]["sem"], 16 * n
        self.E[eng].wait_ge(sem, val)
        self.nwaits += 1

    def _deps(self, reads, writes, eng=None):
        deps = []
        for b in reads:
            if b.lw is not None:
                deps.append(b.lw)
            if b.excl:
                deps.extend(tok for e2, tok in b.rd.items() if e2 != eng)
        for b in writes:
            if b.lw is not None:
                deps.append(b.lw)
            deps.extend(b.rd.values())
            deps.extend(b.rd_dma)
        return deps

    def op(self, eng, fn, reads=(), writes=()):
        for tok in self._deps(reads, writes, eng):
            self._wait(eng, tok)
        ins = fn(self.E[eng])
        self.cnt[eng] += 1
        k = self.cnt[eng]
        sem, _ = self._eng_sem(eng, k)
        ins.then_inc(sem, 1)
        tok = ("c", eng, k)
        for b in writes:
            b.lw = tok
            b.rd = {}
            b.rd_dma = []
        for b in reads:
            if b.lw is not tok:
                b.rd[eng] = tok
        return tok

    def dma(self, q, out, in_, reads=(), writes=(), stream=None, **kw):
        stream = f"{q}_{stream}"
        st = self._stream(stream)
        deps = self._deps(reads, writes)
        if st["n"] > 0:
            deps.append(("d", stream, st["n"]))
        for tok in deps:
            self._wait(q, tok)
        ins = self.E[q].dma_start(out=out, in_=in_, **kw)
        ins.then_inc(st["sem"], 16)
        st["n"] += 1
        tok = ("d", stream, st["n"])
        for b in writes:
            b.lw = tok
            b.rd = {}
            b.rd_dma = []
        for b in reads:
            b.rd_dma.append(tok)
        return tok

    def barrier(self):
        for eng in self.E:
            for peng in self.E:
                if peng != eng and self.cnt[peng] > 0:
                    self._wait(eng, ("c", peng, self.cnt[peng]))
            for sname, st in self.streams.items():
                if st["n"] > 0:
                    self._wait(eng, ("d", sname, st["n"]))

    def finish(self):
        for sname, st in self.streams.items():
            if st["n"] > 0:
                self._wait("sp", ("d", sname, st["n"]))


def run_interleaved(gens, depth=2, skew=1):
    it = iter(gens)
    pending = next(it, None)
    active = []
    while active or pending is not None:
        if pending is not None and len(active) < depth and (not active or active[-1][1] >= skew):
            active.append([pending, 0])
            pending = next(it, None)
        for a in list(active):
            try:
                next(a[0])
                a[1] += 1
            except StopIteration:
                active.remove(a)


def load_rowvec_bcast(S, q, dst, src_1d, n, stream, wbuf):
    S.dma(q, dst, src_1d.partition_broadcast(P), writes=[wbuf], stream=stream)


def ln_epilogue(S, sb, z, zb, g_t, b_t, gb_buf, out_t, out_b, eps=LN_EPS):
    stats, stats_b, mv, mv_b, sc, sc_b = sb
    S.op("dve", lambda e: (e.bn_stats(out=stats[:, 0, :], in_=z[:, 0:512]),
                           e.bn_stats(out=stats[:, 1, :], in_=z[:, 512:1024]))[-1],
         reads=[zb], writes=[stats_b])
    S.op("dve", lambda e: e.bn_aggr(out=mv[:, :], in_=stats[:, :, :].rearrange("p a b -> p (a b)")),
         reads=[stats_b], writes=[mv_b])
    S.op("act", lambda e: e.activation(out=sc[:, 0:1], in_=mv[:, 1:2], func=AF.Ln, bias=eps, scale=1.0),
         reads=[mv_b], writes=[sc_b])
    S.op("act", lambda e: e.activation(out=sc[:, 0:1], in_=sc[:, 0:1], func=AF.Exp, scale=-0.5),
         reads=[sc_b], writes=[sc_b])
    S.op("dve", lambda e: e.scalar_tensor_tensor(out=sc[:, 1:2], in0=mv[:, 0:1], scalar=-1.0,
                                                 in1=sc[:, 0:1], op0=ALU.mult, op1=ALU.mult),
         reads=[mv_b, sc_b], writes=[sc_b])
    S.op("act", lambda e: e.activation(out=out_t[:, :], in_=z[:, :], func=AF.Identity,
                                       bias=sc[:, 1:2], scale=sc[:, 0:1]),
         reads=[zb, sc_b], writes=[out_b])
    S.op("dve", lambda e: e.tensor_tensor(out=out_t[:, :], in0=out_t[:, :], in1=g_t[:, :], op=ALU.mult),
         reads=[out_b, gb_buf], writes=[out_b])
    S.op("dve", lambda e: e.tensor_tensor(out=out_t[:, :], in0=out_t[:, :], in1=b_t[:, :], op=ALU.add),
         reads=[out_b, gb_buf], writes=[out_b])


def ln_epilogue_gen(S, sb, z, zb, g_t, b_t, gb_buf, out_t, out_b, eps=LN_EPS):
    stats, stats_b, mv, mv_b, sc, sc_b = sb
    S.op("dve", lambda e: (e.bn_stats(out=stats[:, 0, :], in_=z[:, 0:512]),
                           e.bn_stats(out=stats[:, 1, :], in_=z[:, 512:1024]))[-1],
         reads=[zb], writes=[stats_b])
    S.op("dve", lambda e: e.bn_aggr(out=mv[:, :], in_=stats[:, :, :].rearrange("p a b -> p (a b)")),
         reads=[stats_b], writes=[mv_b])
    yield
    S.op("act", lambda e: e.activation(out=sc[:, 0:1], in_=mv[:, 1:2], func=AF.Ln, bias=eps, scale=1.0),
         reads=[mv_b], writes=[sc_b])
    S.op("act", lambda e: e.activation(out=sc[:, 0:1], in_=sc[:, 0:1], func=AF.Exp, scale=-0.5),
         reads=[sc_b], writes=[sc_b])
    yield
    S.op("dve", lambda e: e.scalar_tensor_tensor(out=sc[:, 1:2], in0=mv[:, 0:1], scalar=-1.0,
                                                 in1=sc[:, 0:1], op0=ALU.mult, op1=ALU.mult),
         reads=[mv_b, sc_b], writes=[sc_b])
    yield
    S.op("act", lambda e: e.activation(out=out_t[:, :], in_=z[:, :], func=AF.Identity,
                                       bias=sc[:, 1:2], scale=sc[:, 0:1]),
         reads=[zb, sc_b], writes=[out_b])
    yield
    S.op("dve", lambda e: e.tensor_tensor(out=out_t[:, :], in0=out_t[:, :], in1=g_t[:, :], op=ALU.mult),
         reads=[out_b, gb_buf], writes=[out_b])
    S.op("dve", lambda e: e.tensor_tensor(out=out_t[:, :], in0=out_t[:, :], in1=b_t[:, :], op=ALU.add),
         reads=[out_b, gb_buf], writes=[out_b])
    yield


def mlp_phase(S, nc, NT, xin, xout, w1, w2, g, b, ident_d, xin_buf, xout_buf, tag):
    ST = 2
    NS = NT // ST
    TOK = ST * P
    with ExitStack() as ctx:
        sbt = lambda name, shape, dt: ctx.enter_context(nc.sbuf_tensor(f"{tag}_{name}", shape, dt))
        pst = lambda name, shape, dt: ctx.enter_context(nc.psum_tensor(f"{tag}_{name}", shape, dt))
        w1b = sbt("w1b", [P, 8, DFF], BF16)
        w2b = sbt("w2b", [P, 32, D], BF16)
        identf = sbt("identf", [P, P], F32)
        g_t = sbt("g", [P, D], F32)
        b_t = sbt("b", [P, D], F32)
        NXS = 4
        xs = [sbt(f"x{i}", [P, D], F32) for i in range(NXS)]
        xT = [sbt(f"xT{i}", [P, 8, TOK], BF16) for i in range(2)]
        hT = sbt("hT", [P, 32, TOK], BF16)
        rt = [sbt(f"rt{i}", [P, 512], F32) for i in range(2)]
        ot = [sbt(f"ot{i}", [P, D], F32) for i in range(2)]
        stats = [sbt(f"st{i}", [P, 2, 6], F32) for i in range(2)]
        mv = [sbt(f"mv{i}", [P, 2], F32) for i in range(2)]
        sc = [sbt(f"sc{i}", [P, 2], F32) for i in range(2)]
        ptr = [pst(f"ptr{i}", [P, 4, P], F32) for i in range(2)]
        pm1 = [pst(f"pm1{i}", [P, 512], F32) for i in range(2)]
        pm2 = [pst(f"pm2{i}", [P, 2, 512], F32) for i in range(2)]

        B = S.buf
        w1_b = [B(f"w1b{c}") for c in range(8)]
        w2_b = [B(f"w2b{c}") for c in range(8)]
        ident_b, gb_b = B("ident"), B("gb")
        xs_b = S.bufs("xs", NXS)
        xT_b = S.bufs("xT", 2)
        hT_b = [B(f"hT{j}") for j in range(16)]
        rt_b = S.bufs("rt", 2)
        ot_b = S.bufs("ot", 2)
        st_b, mv_b, sc_b = S.bufs("st", 2), S.bufs("mv", 2), S.bufs("sc", 2)
        ptr_b, pm1_b, pm2_b = S.pbufs("ptr", 2), S.pbufs("pm1", 2), S.pbufs("pm2", 2)

        S.dma("sp", identf[:, :], ident_d[:, :], writes=[ident_b], stream=f"c0")
        S.dma("sp", g_t[:, :], g.partition_broadcast(P), writes=[gb_b], stream=f"c1")
        S.dma("sp", b_t[:, :], b.partition_broadcast(P), writes=[gb_b], stream=f"c2")
        w1v = w1.rearrange("(c p) f -> p c f", p=P)
        for c in range(8):
            for hh in range(2):
                S.dma("pool", w1b[:, c, hh * 2048:(hh + 1) * 2048], w1v[:, c, hh * 2048:(hh + 1) * 2048],
                      writes=[w1_b[c]], stream=f"w1_{c % 2}")
        w2v = w2.rearrange("(c p) f -> p c f", p=P)
        for c in range(32):
            S.dma("pool", w2b[:, c, :], w2v[:, c, :], writes=[w2_b[c // 4]], stream=f"w2_{c % 2}")

        def load_x(t):
            sl = t % NXS
            S.dma("sp", xs[sl][:, :], xin[t * P:(t + 1) * P, :], reads=[xin_buf], writes=[xs_b[sl]],
                  stream=f"x{sl}")

        for t in range(min(NXS, NT)):
            load_x(t)

        def transposes(s):
            xTs, xTs_b = xT[s % 2], xT_b[s % 2]
            for m in range(ST):
                t = s * ST + m
                sl = t % NXS
                for hh in range(2):
                    pt, pt_b = ptr[hh], ptr_b[hh]

                    def f_tr(e, sl=sl, hh=hh, pt=pt):
                        for c in range(4):
                            ins = e.transpose(out=pt[:, c, :], in_=xs[sl][:, (hh * 4 + c) * P:(hh * 4 + c + 1) * P],
                                              identity=identf[:, :])
                        return ins
                    S.op("pe", f_tr, reads=[xs_b[sl], ident_b], writes=[pt_b])
                    S.op("act" if hh == 0 else "dve",
                         (lambda e, pt=pt, hh=hh, m=m, xTs=xTs: e.copy(out=xTs[:, hh * 4:(hh + 1) * 4, m * P:(m + 1) * P], in_=pt[:, :, :]))
                         if hh == 0 else
                         (lambda e, pt=pt, hh=hh, m=m, xTs=xTs: e.tensor_copy(out=xTs[:, hh * 4:(hh + 1) * 4, m * P:(m + 1) * P], in_=pt[:, :, :])),
                         reads=[pt_b], writes=[xTs_b])
        transposes(0)
        for s in range(NS):
            xTs, xTs_b = xT[s % 2], xT_b[s % 2]
            for jj in range(16):
                pm, pm_b = pm1[jj % 2], pm1_b[jj % 2]

                def f_mm1(e, jj=jj, pm=pm, xTs=xTs):
                    for u in range(2):
                        j = jj * 2 + u
                        for c in range(8):
                            ins = e.matmul(pm[:, u * TOK:(u + 1) * TOK], lhsT=w1b[:, c, j * P:(j + 1) * P],
                                           rhs=xTs[:, c, :], start=(c == 0), stop=(c == 7))
                    return ins
                S.op("pe", f_mm1, reads=[xTs_b] + w1_b, writes=[pm_b])
                r, r_b = rt[jj % 2], rt_b[jj % 2]
                S.op("act", lambda e, pm=pm, r=r: e.activation(out=r[:, :], in_=pm[:, :], func=AF.Relu),
                     reads=[pm_b], writes=[r_b])
                S.op("dve", lambda e, r=r, jj=jj: e.tensor_tensor(
                    out=hT[:, 2 * jj:2 * jj + 2, :].rearrange("p a t -> p (a t)"), in0=r[:, :], in1=r[:, :], op=ALU.mult),
                     reads=[r_b], writes=[hT_b[jj]])
            if s + 1 < NS:
                transposes(s + 1)
            for m in range(ST):
                t = s * ST + m
                sl = t % NXS
                k2 = t % 2
                py, py_b = pm2[k2], pm2_b[k2]

                def f_mm2(e, m=m, py=py):
                    for n in range(2):
                        for j in range(32):
                            ins = e.matmul(py[:, n, :], lhsT=hT[:, j, m * P:(m + 1) * P],
                                           rhs=w2b[:, j, n * 512:(n + 1) * 512], start=(j == 0), stop=(j == 31))
                    return ins
                S.op("pe", f_mm2, reads=hT_b + w2_b, writes=[py_b])
                S.op("dve", lambda e, sl=sl, py=py: e.scalar_tensor_tensor(
                    out=xs[sl][:, :], in0=xs[sl][:, :], scalar=DN_ALPHA,
                    in1=py[:, :, :].rearrange("p a b -> p (a b)"), op0=ALU.mult, op1=ALU.add),
                     reads=[xs_b[sl], py_b], writes=[xs_b[sl]])
                ln_epilogue(S, (stats[k2], st_b[k2], mv[k2], mv_b[k2], sc[k2], sc_b[k2]),
                            xs[sl], xs_b[sl], g_t, b_t, gb_b, ot[k2], ot_b[k2])
                S.dma("sp", xout[t * P:(t + 1) * P, :], ot[k2][:, :], reads=[ot_b[k2]], writes=[xout_buf],
                      stream=f"o{k2}")
                if t + NXS < NT:
                    load_x(t + NXS)
        S.barrier()


def build_test_mlp(NT):
    nc = bass.Bass("TRN2", target_bir_lowering=False)
    x = nc.dram_tensor("x", [NT * P, D], F32, kind="ExternalInput").ap()
    w1 = nc.dram_tensor("w1", [D, DFF], F32, kind="ExternalInput").ap()
    w2 = nc.dram_tensor("w2", [DFF, D], F32, kind="ExternalInput").ap()
    g = nc.dram_tensor("g", [D], F32, kind="ExternalInput").ap()
    b = nc.dram_tensor("b", [D], F32, kind="ExternalInput").ap()
    ident = nc.dram_tensor("ident", [P, P], F32, kind="ExternalInput").ap()
    y = nc.dram_tensor("y", [NT * P, D], F32, kind="ExternalOutput").ap()
    with ExitStack() as ctx:
        S = Sched(nc, ctx)
        mlp_phase(S, nc, NT, x, y, w1, w2, g, b, ident, S.buf("xin"), S.buf("xout"), "m0")
        S.finish()
    return nc


A_HEADS = 16
A_KV = 4
HD = 64
A_IN = 1536
TWO_PI = 2.0 * np.pi
CW1 = 6.28125
CW2 = float(TWO_PI - 6.28125)
PI_LO = 3.1415925
Q_HEAD_ORDER = [0, 4, 1, 5, 2, 6, 3, 7, 8, 12, 9, 13, 10, 14, 11, 15]


def bcast(ap, shape):
    return ap.broadcast_to(list(shape))


def attn_phase(S, nc, NT, xin, xout, pos, w_in, sink, w_out, g, b, consts, xin_buf, xout_buf, tag):
    ident_d, maskp_d, maskn_d, invf_d = consts
    scale = HD ** -0.5
    with ExitStack() as ctx:
        sbt = lambda name, shape, dt: ctx.enter_context(nc.sbuf_tensor(f"{tag}_{name}", shape, dt))
        pst = lambda name, shape, dt: ctx.enter_context(nc.psum_tensor(f"{tag}_{name}", shape, dt))
        B = S.buf
        w_inb = sbt("w_inb", [P, 8, A_IN], BF16)
        w_outb = sbt("w_outb", [P, 8, D], BF16)
        identf = sbt("identf", [P, P], F32)
        identb = sbt("identb", [P, P], BF16)
        maskp = sbt("maskp", [P, 512], BF16)
        maskn = sbt("maskn", [P, 512], BF16)
        g_t = sbt("g", [P, D], F32)
        b_t = sbt("b", [P, D], F32)
        esink = sbt("esink", [P, 16], F32)
        cos_t = sbt("cos", [P, NT, 32], F32)
        sin_t = sbt("sin", [P, NT, 32], F32)
        w_in_b, w_out_b = B("w_in"), B("w_out")
        ident_b, identb_b, mask_b, gb_b, esink_b, cs_b = B("id"), B("idb"), B("mask"), B("gb"), B("esink"), B("cs")

        pmm = [pst(f"pmm{i}", [P, 512], F32) for i in range(2)]
        ptb = pst("ptb", [P, 8, P], BF16)
        psc = [pst(f"psc{i}", [P, 512], F32) for i in range(2)]
        pov = [pst(f"pov{i}", [P, 512], F32) for i in range(3)]
        pmm_b, psc_b, pov_b, ptb_b = S.pbufs("pmm", 2), S.pbufs("psc", 2), S.pbufs("pov", 3), S.pbufs("ptb", 1)[0]

        S.dma("sp", identf[:, :], ident_d[:, :], writes=[ident_b], stream=f"c0")
        S.dma("sp", g_t[:, :], g.partition_broadcast(P), writes=[gb_b], stream=f"c1")
        S.dma("sp", b_t[:, :], b.partition_broadcast(P), writes=[gb_b], stream=f"c2")
        S.dma("sp", esink[:, :], sink.partition_broadcast(P), writes=[esink_b], stream=f"c3")
        S.dma("pool", maskp[:, :], maskp_d[:, :], writes=[mask_b], stream=f"c4")
        S.dma("pool", maskn[:, :], maskn_d[:, :], writes=[mask_b], stream=f"c5")
        w_inv = w_in.rearrange("(c p) f -> p c f", p=P)
        for c in range(8):
            S.dma("pool", w_inb[:, c, :], w_inv[:, c, :], writes=[w_in_b], stream=f"w{c % 2}")
        w_outv = w_out.rearrange("(c p) f -> p c f", p=P)
        for c in range(8):
            S.dma("pool", w_outb[:, c, :], w_outv[:, c, :], writes=[w_out_b], stream=f"w{c % 2}")
        S.op("dve", lambda e: e.tensor_copy(out=identb[:, :], in_=identf[:, :]), reads=[ident_b], writes=[identb_b])
        S.op("act", lambda e: e.activation(out=esink[:, :], in_=esink[:, :], func=AF.Exp),
             reads=[esink_b], writes=[esink_b])

        with ExitStack() as c2:
            sb2 = lambda name, shape, dt: c2.enter_context(nc.sbuf_tensor(f"{tag}_{name}", shape, dt))
            posi = sb2("posi", [NT, P], I32)
            posf = sb2("posf", [NT, P], F32)
            posT = sb2("posT", [P, NT], F32)
            invf = sb2("invf", [P, 32], F32)
            ang = sb2("ang", [P, NT, 32], F32)
            u = sb2("u", [P, NT, 32], F32)
            ki = sb2("ki", [P, NT, 32], I32)
            kf = sb2("kf", [P, NT, 32], F32)
            posi_b, posf_b, posT_b, invf_b, ang_b, u_b, ki_b, kf_b = S.bufs("rp", 8)
            S.dma("sp", posi[:, :], pos.rearrange("(t p) -> t p", p=P), writes=[posi_b], stream=f"c6")
            S.dma("sp", invf[:, :], invf_d.partition_broadcast(P), writes=[invf_b], stream=f"c7")
            S.op("dve", lambda e: e.tensor_copy(out=posf[:, :], in_=posi[:, :]), reads=[posi_b], writes=[posf_b])
            S.op("pe", lambda e: e.transpose(out=pmm[0][:, 0:NT], in_=posf[:, :], identity=identf[0:NT, 0:NT]),
                 reads=[posf_b, ident_b], writes=[pmm_b[0]])
            S.op("dve", lambda e: e.tensor_copy(out=posT[:, :], in_=pmm[0][:, 0:NT]), reads=[pmm_b[0]], writes=[posT_b])
            S.op("dve", lambda e: e.tensor_tensor(out=ang[:, :, :], in0=bcast(posT[:, :].unsqueeze(2), [P, NT, 32]),
                                                  in1=bcast(invf[:, :].unsqueeze(1), [P, NT, 32]), op=ALU.mult),
                 reads=[posT_b, invf_b], writes=[ang_b])
            for which, off, dst in (("sin", 0.0, sin_t), ("cos", 0.25, cos_t)):
                S.op("dve", lambda e, off=off: e.tensor_scalar(out=u[:, :, :], in0=ang[:, :, :], scalar1=float(1.0 / TWO_PI),
                                                               scalar2=off, op0=ALU.mult, op1=ALU.add),
                     reads=[ang_b], writes=[u_b])
                S.op("dve", lambda e: e.tensor_copy(out=ki[:, :, :], in_=u[:, :, :]), reads=[u_b], writes=[ki_b])
                S.op("dve", lambda e: e.tensor_copy(out=kf[:, :, :], in_=ki[:, :, :]), reads=[ki_b], writes=[kf_b])
                S.op("dve", lambda e: e.scalar_tensor_tensor(out=u[:, :, :], in0=kf[:, :, :], scalar=-CW1, in1=ang[:, :, :],
                                                             op0=ALU.mult, op1=ALU.add),
                     reads=[kf_b, ang_b], writes=[u_b])
                S.op("dve", lambda e: e.scalar_tensor_tensor(out=u[:, :, :], in0=kf[:, :, :], scalar=-CW2, in1=u[:, :, :],
                                                             op0=ALU.mult, op1=ALU.add),
                     reads=[kf_b, u_b], writes=[u_b])
                S.op("dve", lambda e, off=off: e.tensor_scalar(out=u[:, :, :], in0=u[:, :, :], scalar1=float(off * TWO_PI),
                                                               scalar2=PI_LO, op0=ALU.add, op1=ALU.min),
                     reads=[u_b], writes=[u_b])
                S.op("dve", lambda e: e.tensor_scalar(out=u[:, :, :], in0=u[:, :, :], scalar1=-PI_LO, scalar2=None, op0=ALU.max),
                     reads=[u_b], writes=[u_b])
                S.op("act", lambda e, dst=dst: e.activation(out=dst[:, :, :], in_=u[:, :, :], func=AF.Sin),
                     reads=[u_b], writes=[cs_b])
            S.barrier()

        NXS = 6
        NW = 3
        NQ = 4
        NK = 6
        xs = [sbt(f"x{i}", [P, D], F32) for i in range(NXS)]
        xT = [sbt(f"xT{i}", [P, 8, P], BF16) for i in range(NW)]
        ra = [sbt(f"ra{i}", [P, 8, 2, 32], F32) for i in range(2)]
        rb = [sbt(f"rb{i}", [P, 8, 2, 32], F32) for i in range(2)]
        qr = [sbt(f"qr{i}", [P, 16, 2, 32], BF16) for i in range(NW)]
        kr = [sbt(f"kr{i}", [P, 4, 2, 32], BF16) for i in range(NW)]
        qT = [sbt(f"qT{i}", [P, 8, P], BF16) for i in range(NQ)]
        kTw = [sbt(f"kT{i}", [P, 2, P], BF16) for i in range(NK)]
        Vw = [sbt(f"V{i}", [P, 4, 65], BF16) for i in range(NK)]
        pT = [sbt(f"pT{i}", [P, 512], BF16) for i in range(3)]
        den = [sbt(f"den{i}", [P, 16], F32) for i in range(NW)]
        on = [sbt(f"on{i}", [P, 16, 64], BF16) for i in range(NW)]
        oT = [sbt(f"oT{i}", [P, 8, P], BF16) for i in range(NW)]
        ot = [sbt(f"ot{i}", [P, D], F32) for i in range(NW)]
        stats = [sbt(f"st{i}", [P, 2, 6], F32) for i in range(NW)]
        mv = [sbt(f"mv{i}", [P, 2], F32) for i in range(NW)]
        sc = [sbt(f"sc{i}", [P, 2], F32) for i in range(NW)]
        xs_b, xT_b = S.bufs("xs", NXS), S.bufs("xT", NW)
        ra_b, rb_b, qr_b, kr_b, qT_b = S.bufs("ra", 2), S.bufs("rb", 2), S.bufs("qr", NW), S.bufs("kr", NW), S.bufs("qT", NQ)
        kT_b, V_b, pT_b = S.bufs("kT", NK), S.bufs("V", NK), S.bufs("pT", 3)
        den_b, on_b, oT_b, ot_b = S.bufs("den", NW), S.bufs("on", NW), S.bufs("oT", NW), S.bufs("ot", NW)
        st_b, mv_b, sc_b = S.bufs("st", NW), S.bufs("mv", NW), S.bufs("sc", NW)
        for i in range(NK):
            S.op("pool", lambda e, i=i: e.memset(Vw[i][:, :, :], 1.0), writes=[V_b[i]])

        cnt = {"mm": 0, "sc": 0, "pt": 0, "rr": 0}
        pov_free = [True]

        def nxt(key, n):
            v = cnt[key] % n
            cnt[key] += 1
            return v

        def load_x(t):
            sl = t % NXS
            S.dma("sp", xs[sl][:, :], xin[t * P:(t + 1) * P, :], reads=[xin_buf], writes=[xs_b[sl]],
                  stream=f"x{sl}")

        def rope(pm, pm_bf, nh, t, dst, dst_bf, h0):
            r = nxt("rr", 2)
            pm4 = pm[:, 0:nh * 64].rearrange("p (h two d) -> p h two d", two=2, d=32)
            cb = bcast(cos_t[:, t:t + 1, :].unsqueeze(1), [P, nh, 2, 32])
            sb_ = bcast(sin_t[:, t:t + 1, :].unsqueeze(1), [P, nh, 2, 32])
            S.op("dve", lambda e: e.tensor_tensor(out=ra[r][:, 0:nh, :, :], in0=pm4, in1=cb, op=ALU.mult),
                 reads=[pm_bf, cs_b], writes=[ra_b[r]])
            S.op("dve", lambda e: e.tensor_tensor(out=rb[r][:, 0:nh, :, :], in0=pm4, in1=sb_, op=ALU.mult),
                 reads=[pm_bf, cs_b], writes=[rb_b[r]])
            S.op("pool", lambda e: e.tensor_tensor(out=dst[:, h0:h0 + nh, 0, :], in0=ra[r][:, 0:nh, 0, :],
                                                   in1=rb[r][:, 0:nh, 1, :], op=ALU.subtract),
                 reads=[ra_b[r], rb_b[r]], writes=[dst_bf])
            S.op("pool", lambda e: e.tensor_tensor(out=dst[:, h0:h0 + nh, 1, :], in0=ra[r][:, 0:nh, 1, :],
                                                   in1=rb[r][:, 0:nh, 0, :], op=ALU.add),
                 reads=[ra_b[r], rb_b[r]], writes=[dst_bf])

        def stage1(t):
            sl, p2, p4, pq = t % NXS, t % NW, t % NK, t % NQ
            for hh in range(2):
                m = nxt("mm", 2)

                def f_tr(e, hh=hh, m=m):
                    for c in range(4):
                        ins = e.transpose(out=pmm[m][:, c * P:(c + 1) * P],
                                          in_=xs[sl][:, (hh * 4 + c) * P:(hh * 4 + c + 1) * P], identity=identf[:, :])
                    return ins
                S.op("pe", f_tr, reads=[xs_b[sl], ident_b], writes=[pmm_b[m]])
                src = pmm[m][:, :].rearrange("p (c t) -> p c t", c=4)
                if hh == 0:
                    S.op("act", lambda e, src=src: e.copy(out=xT[p2][:, 0:4, :], in_=src), reads=[pmm_b[m]], writes=[xT_b[p2]])
                else:
                    S.op("dve", lambda e, src=src: e.tensor_copy(out=xT[p2][:, 4:8, :], in_=src), reads=[pmm_b[m]], writes=[xT_b[p2]])
                yield
            for n in range(3):
                m = nxt("mm", 2)

                def f_in(e, n=n, m=m):
                    for c in range(8):
                        ins = e.matmul(pmm[m][:, :], lhsT=xT[p2][:, c, :], rhs=w_inb[:, c, n * 512:(n + 1) * 512],
                                       start=(c == 0), stop=(c == 7))
                    return ins
                S.op("pe", f_in, reads=[xT_b[p2], w_in_b], writes=[pmm_b[m]])
                if n < 2:
                    rope(pmm[m], pmm_b[m], 8, t, qr[p2], qr_b[p2], n * 8)
                else:
                    rope(pmm[m], pmm_b[m], 4, t, kr[p2], kr_b[p2], 0)
                    S.op("act", lambda e, m=m: e.copy(out=Vw[p4][:, :, 0:64],
                                                      in_=pmm[m][:, 256:512].rearrange("p (h d) -> p h d", d=64)),
                         reads=[pmm_b[m]], writes=[V_b[p4]])
                yield

            def f_qt(e):
                qflat = qr[p2][:, :, :, :].rearrange("p h two d -> p (h two d)")
                for s_ in range(8):
                    ins = e.transpose(out=ptb[:, s_, :], in_=qflat[:, s_ * P:(s_ + 1) * P], identity=identb[:, :])
                return ins
            S.op("pe", f_qt, reads=[qr_b[p2], identb_b], writes=[ptb_b])
            S.op("act", lambda e: e.copy(out=qT[pq][:, :, :], in_=ptb[:, :, :]), reads=[ptb_b], writes=[qT_b[pq]])
            yield

            def f_kt(e):
                kflat = kr[p2][:, :, :, :].rearrange("p h two d -> p (h two d)")
                for s_ in range(2):
                    ins = e.transpose(out=ptb[:, s_, :], in_=kflat[:, s_ * P:(s_ + 1) * P], identity=identb[:, :])
                return ins
            S.op("pe", f_kt, reads=[kr_b[p2], identb_b], writes=[ptb_b])
            S.op("dve", lambda e: e.tensor_copy(out=kTw[p4][:, :, :], in_=ptb[:, 0:2, :]), reads=[ptb_b], writes=[kT_b[p4]])
            yield

        def stage2(i):
            sl, p2, pq = i % NXS, i % NW, i % NQ
            kts = [kt for kt in (i - 1, i, i + 1) if 0 <= kt < NT]
            bank_started = [False, False, False]
            last_in_bank = {0: 5, 1: 11, 2: 15}
            assert pov_free[0]
            pov_free[0] = False
            work = [(kv, kt) for kv in range(4) for kt in kts]

            def emit_scores(kv, kt):
                base, ch = 64 * (kv % 2), kv // 2
                s_ = nxt("sc", 2)

                def f_sc(e):
                    if kt != i:
                        e.matmul(psc[s_][:, :], lhsT=identb[:, :], rhs=(maskp if kt < i else maskn)[:, :],
                                 start=True, stop=False)
                    return e.matmul(psc[s_][:, :], lhsT=kTw[kt % NK][base:base + 64, ch, :],
                                    rhs=qT[pq][base:base + 64, 4 * ch:4 * ch + 4, :],
                                    start=(kt == i), stop=True)
                S.op("pe", f_sc, reads=[kT_b[kt % NK], qT_b[pq], identb_b, mask_b], writes=[psc_b[s_]])
                return s_

            def emit_exp_pv(kv, kt, s_):
                pi_ = nxt("pt", 3)
                S.op("act", lambda e: e.activation(out=pT[pi_][:, :], in_=psc[s_][:, :], func=AF.Exp, scale=float(scale)),
                     reads=[psc_b[s_]], writes=[pT_b[pi_]])

                def f_pv(e):
                    for g_ in range(4):
                        h = kv * 4 + g_
                        bk, col = h // 6, (h % 6) * 65
                        st_ = not bank_started[bk]
                        bank_started[bk] = True
                        ins = e.matmul(pov[bk][:, col:col + 65], lhsT=pT[pi_][:, g_ * P:(g_ + 1) * P],
                                       rhs=Vw[kt % NK][:, kv, :], start=st_,
                                       stop=(h == last_in_bank[bk] and kt == kts[-1]), skip_group_check=True)
                    return ins
                banks = sorted({(kv * 4 + g_) // 6 for g_ in range(4)})
                S.op("pe", f_pv, reads=[pT_b[pi_], V_b[kt % NK]], writes=[pov_b[bk] for bk in banks])

            prev = None
            for (kv, kt) in work:
                s_ = emit_scores(kv, kt)
                if prev is not None:
                    emit_exp_pv(*prev)
                prev = (kv, kt, s_)
            emit_exp_pv(*prev)
            yield
            for bk in range(3):
                nh = 6 if bk < 2 else 4
                h0 = bk * 6
                pv3 = pov[bk][:, 0:nh * 65].rearrange("p (h d) -> p h d", d=65)
                S.op("dve", lambda e, pv3=pv3, h0=h0, nh=nh: e.tensor_tensor(
                    out=den[p2][:, h0:h0 + nh].unsqueeze(2), in0=pv3[:, :, 64:65],
                    in1=esink[:, h0:h0 + nh].unsqueeze(2), op=ALU.add),
                     reads=[pov_b[bk], esink_b], writes=[den_b[p2]])
                S.op("dve", lambda e, h0=h0, nh=nh: e.reciprocal(out=den[p2][:, h0:h0 + nh], in_=den[p2][:, h0:h0 + nh]),
                     reads=[den_b[p2]], writes=[den_b[p2]])
                S.op("dve", lambda e, pv3=pv3, h0=h0, nh=nh: e.tensor_tensor(
                    out=on[p2][:, h0:h0 + nh, :], in0=pv3[:, :, 0:64],
                    in1=bcast(den[p2][:, h0:h0 + nh].unsqueeze(2), [P, nh, 64]), op=ALU.mult),
                     reads=[pov_b[bk], den_b[p2]], writes=[on_b[p2]])
            pov_free[0] = True
            yield

            def f_ot(e):
                oflat = on[p2][:, :, :].rearrange("p h d -> p (h d)")
                for s_ in range(8):
                    ins = e.transpose(out=ptb[:, s_, :], in_=oflat[:, s_ * P:(s_ + 1) * P], identity=identb[:, :])
                return ins
            S.op("pe", f_ot, reads=[on_b[p2], identb_b], writes=[ptb_b])
            S.op("act", lambda e: e.copy(out=oT[p2][:, :, :], in_=ptb[:, :, :]), reads=[ptb_b], writes=[oT_b[p2]])
            yield
            for n in range(2):
                m = nxt("mm", 2)

                def f_op(e, n=n, m=m):
                    for c in range(8):
                        ins = e.matmul(pmm[m][:, :], lhsT=oT[p2][:, c, :], rhs=w_outb[:, c, n * 512:(n + 1) * 512],
                                       start=(c == 0), stop=(c == 7))
                    return ins
                S.op("pe", f_op, reads=[oT_b[p2], w_out_b], writes=[pmm_b[m]])
                S.op("dve", lambda e, n=n, m=m: e.scalar_tensor_tensor(
                    out=xs[sl][:, n * 512:(n + 1) * 512], in0=xs[sl][:, n * 512:(n + 1) * 512], scalar=DN_ALPHA,
                    in1=pmm[m][:, :], op0=ALU.mult, op1=ALU.add),
                     reads=[xs_b[sl], pmm_b[m]], writes=[xs_b[sl]])
                yield
            yield from ln_epilogue_gen(S, (stats[p2], st_b[p2], mv[p2], mv_b[p2], sc[p2], sc_b[p2]),
                                       xs[sl], xs_b[sl], g_t, b_t, gb_b, ot[p2], ot_b[p2])
            S.dma("sp", xout[i * P:(i + 1) * P, :], ot[p2][:, :], reads=[ot_b[p2]], writes=[xout_buf],
                  stream=f"o{p2}")
            if i + NXS < NT:
                load_x(i + NXS)

        def tile_gen(t):
            if t < NT:
                yield from stage1(t)
            else:
                for _ in range(7):
                    yield
            if t >= 1:
                yield from stage2(t - 1)

        for t in range(min(NXS, NT)):
            load_x(t)
        run_interleaved((tile_gen(t) for t in range(NT + 1)), depth=3, skew=6)
        S.barrier()


def attn_consts_host():
    j = np.arange(P)[:, None]
    i = np.arange(P)[None, :]
    mp = np.where(j >= i, 0.0, -30000.0).astype(np.float32)
    mn = np.where(j <= i, 0.0, -30000.0).astype(np.float32)
    invf = (10000.0 ** (-np.arange(0, HD, 2, dtype=np.float64) / HD)).astype(np.float32)
    return {"c_ident": np.eye(P, dtype=np.float32), "c_maskp": np.tile(mp, (1, 4)), "c_maskn": np.tile(mn, (1, 4)),
            "c_invf": invf}


def permute_attn_w_in(w_in):
    q = w_in[:, :1024].reshape(1024, 16, 64)[:, Q_HEAD_ORDER, :].reshape(1024, 1024)
    return np.ascontiguousarray(np.concatenate([q, w_in[:, 1024:]], axis=1))


def build_test_attn(NT):
    nc = bass.Bass("TRN2", target_bir_lowering=False)
    dt = lambda name, shape, dtp=F32, kind="ExternalInput": nc.dram_tensor(name, shape, dtp, kind=kind).ap()
    x = dt("x", [NT * P, D])
    pos = dt("pos", [NT * P], I32)
    w_in = dt("w_in", [D, A_IN])
    sink = dt("sink", [16])
    w_out = dt("w_out", [D, D])
    g = dt("g", [D])
    b = dt("b", [D])
    consts = (dt("c_ident", [P, P]), dt("c_maskp", [P, 512]), dt("c_maskn", [P, 512]), dt("c_invf", [32]))
    y = dt("y", [NT * P, D], kind="ExternalOutput")
    with ExitStack() as ctx:
        S = Sched(nc, ctx)
        attn_phase(S, nc, NT, x, y, pos, w_in, sink, w_out, g, b, consts, S.buf("xin"), S.buf("xout"), "a0")
        S.finish()
    return nc


G_H = 4
G_DK = 128
G_DV = 256
G_IN = 3104
G_TAU = 16.0


class PsumPool:
    def __init__(self, S, nc, ctx, tag, n=8):
        self.t = [ctx.enter_context(nc.psum_tensor(f"{tag}_pp{i}", [P, 512], F32)) for i in range(n)]
        self.b = S.pbufs(f"{tag}pp", n)
        self.i = 0
        self.n = n

    def get(self):
        k = self.i % self.n
        self.i += 1
        return self.t[k], self.b[k]


def gla_sweep1(S, nc, NT, xin, w_in, gw2f, gbf, gw2b, gbb, consts, scr, xin_buf, scr_buf, tag):
    ident_d, tri_d, negcol_d, gmask_d = consts
    with ExitStack() as ctx:
        sbt = lambda name, shape, dt: ctx.enter_context(nc.sbuf_tensor(f"{tag}_{name}", shape, dt))
        B = S.buf
        pp = PsumPool(S, nc, ctx, tag)
        w_inb = sbt("w_inb", [P, 8, 2080], BF16)
        w2a = sbt("w2a", [17, 2, 512], BF16)
        tri = sbt("tri", [P, 4, P], F32)
        negcol = sbt("negcol", [P, 1], F32)
        gmask = sbt("gmask", [P, 2, P], F32)
        identf = sbt("identf", [P, P], F32)
        identb = sbt("identb", [P, P], BF16)
        S32 = sbt("S32", [P, 4, G_DV], F32)
        Sb = sbt("Sb", [P, 4, G_DV], BF16)
        w_in_b, w2a_b, cst_b, identb_b, S32_b, Sb_b = B("w_in"), B("w2a"), B("cst"), B("idb"), B("S32"), B("Sb")
        S.dma("sp", identf[:, :], ident_d[:, :], writes=[cst_b], stream=f"c0")
        S.dma("sp", tri[:, :, :], tri_d.rearrange("k p t -> p k t"), writes=[cst_b], stream=f"c1")
        S.dma("sp", negcol[:, :], negcol_d[:, :], writes=[cst_b], stream=f"c2")
        S.dma("sp", gmask[:, :, :], gmask_d.rearrange("k p t -> p k t"), writes=[cst_b], stream=f"c3")
        w_inv = w_in.rearrange("(c p) f -> p c f", p=P)
        for c in range(8):
            S.dma("pool", w_inb[:, c, 0:2048], w_inv[:, c, 0:2048], writes=[w_in_b], stream=f"w{c % 2}")
        for c in range(8):
            S.dma("pool", w_inb[:, c, 2048:2080], w_inv[:, c, 3072:3104], writes=[w_in_b], stream=f"w{c % 2}")
        for d_, (w2, gb_) in enumerate(((gw2f, gbf), (gw2b, gbb))):
            S.dma("pool", w2a[0:16, d_, :], w2[:, :], writes=[w2a_b], stream=f"w0")
            S.dma("pool", w2a[16:17, d_, :], gb_.rearrange("(o f) -> o f", o=1), writes=[w2a_b], stream=f"w1")
        S.op("dve", lambda e: e.tensor_copy(out=identb[:, :], in_=identf[:, :]), reads=[cst_b], writes=[identb_b])
        S.op("pool", lambda e: e.memset(S32[:, :, :], 0.0), writes=[S32_b])
        S.op("pool", lambda e: e.memset(Sb[:, :, :], 0.0), writes=[Sb_b])

        NXS = 4
        NW = 3
        xs = [sbt(f"x{i}", [P, D], F32) for i in range(NXS)]
        xT = [sbt(f"xT{i}", [P, 8, P], BF16) for i in range(NW)]
        q_sb = [sbt(f"q{i}", [P, 512], F32) for i in range(NW)]
        k_sb = [sbt(f"k{i}", [P, 512], F32) for i in range(NW)]
        v_sb = [sbt(f"v{i}", [P, 1024], BF16) for i in range(NW)]
        lrT = [sbt(f"lrT{i}", [17, 2, P], BF16) for i in range(NW)]
        sp = [[sbt(f"sp{i}_{d_}", [P, 512], F32) for d_ in range(2)] for i in range(NW)]
        dec = [[sbt(f"dec{i}_{d_}", [P, 4], F32) for d_ in range(2)] for i in range(NW)]
        tb = [[[sbt(f"tb{i}_{d_}_{j}", [P, 512], F32) for j in range(3)] for d_ in range(2)] for i in range(NW)]
        qe = [[sbt(f"qe{i}_{d_}", [P, 512], BF16) for d_ in range(2)] for i in range(NW)]
        ke = [[sbt(f"ke{i}_{d_}", [P, 512], BF16) for d_ in range(2)] for i in range(NW)]
        kd = [[sbt(f"kd{i}_{d_}", [P, 512], BF16) for d_ in range(2)] for i in range(NW)]
        qkT = [[sbt(f"qkT{i}_{d_}", [P, 8, P], BF16) for d_ in range(2)] for i in range(NW)]
        attm = [[sbt(f"attm{i}_{d_}", [P, 4, P], BF16) for d_ in range(2)] for i in range(NW)]
        opart = [sbt(f"op{i}", [P, 1024], F32) for i in range(NW)]
        xs_b = S.bufs("xs", NXS)
        xT_b, q_b, k_b, v_b, lrT_b, op_b = (S.bufs(n_, NW) for n_ in ("xT", "q", "k", "v", "lrT", "op"))
        sp_b, dec_b, qe_b, ke_b, kd_b, qkT_b, attm_b = ([S.bufs(f"{n_}{i}", 2) for i in range(NW)]
                                                        for n_ in ("sp", "dec", "qe", "ke", "kd", "qkT", "attm"))
        tb_b = [[S.bufs(f"tb{i}_{d_}", 3) for d_ in range(2)] for i in range(NW)]
        for i in range(NW):
            S.op("pool", lambda e, i=i: e.memset(lrT[i][:, :, :], 1.0), writes=[lrT_b[i]])

        def load_x(t):
            sl = t % NXS
            S.dma("sp", xs[sl][:, :], xin[t * P:(t + 1) * P, :], reads=[xin_buf], writes=[xs_b[sl]], stream=f"x{sl}")

        state_ver = [0]

        def tile_gen(t):
            sl, p2 = t % NXS, t % NW

            def mm8(out_ap, c0, c1, wr_b):
                def f(e):
                    for c in range(8):
                        ins = e.matmul(out_ap, lhsT=xT[p2][:, c, :], rhs=w_inb[:, c, c0:c1], start=(c == 0), stop=(c == 7))
                    return ins
                S.op("pe", f, reads=[xT_b[p2], w_in_b], writes=[wr_b])

            for hh in range(2):
                pm, pm_b = pp.get()

                def f_tr(e, hh=hh, pm=pm):
                    for c in range(4):
                        ins = e.transpose(out=pm[:, c * P:(c + 1) * P], in_=xs[sl][:, (hh * 4 + c) * P:(hh * 4 + c + 1) * P],
                                          identity=identf[:, :])
                    return ins
                S.op("pe", f_tr, reads=[xs_b[sl], cst_b], writes=[pm_b])
                src = pm[:, :].rearrange("p (c t) -> p c t", c=4)
                if hh == 0:
                    S.op("act", lambda e, src=src: e.copy(out=xT[p2][:, 0:4, :], in_=src), reads=[pm_b], writes=[xT_b[p2]])
                else:
                    S.op("dve", lambda e, src=src: e.tensor_copy(out=xT[p2][:, 4:8, :], in_=src), reads=[pm_b], writes=[xT_b[p2]])
                yield
            pm, pm_b = pp.get()

            def f_lr(e, pm=pm):
                for d_ in range(2):
                    for c in range(8):
                        ins = e.matmul(pm[0:16, d_ * P:(d_ + 1) * P], lhsT=w_inb[:, c, 2048 + 16 * d_:2064 + 16 * d_],
                                       rhs=xT[p2][:, c, :], start=(c == 0), stop=(c == 7))
                return ins
            S.op("pe", f_lr, reads=[xT_b[p2], w_in_b], writes=[pm_b])
            S.op("dve", lambda e, pm=pm: e.tensor_copy(out=lrT[p2][0:16, :, :], in_=pm[0:16, 0:256].rearrange("p (a t) -> p a t", a=2)),
                 reads=[pm_b], writes=[lrT_b[p2]])
            yield
            pm, pm_b = pp.get()
            mm8(pm[:, :], 0, 512, pm_b)
            S.op("act", lambda e, pm=pm: e.activation(out=q_sb[p2][:, :], in_=pm[:, :], func=AF.Copy, scale=float(G_DK ** -0.5)),
                 reads=[pm_b], writes=[q_b[p2]])
            yield
            pm, pm_b = pp.get()
            mm8(pm[:, :], 512, 1024, pm_b)
            S.op("dve", lambda e, pm=pm: e.tensor_copy(out=k_sb[p2][:, :], in_=pm[:, :]), reads=[pm_b], writes=[k_b[p2]])
            yield
            for n in range(2):
                pm, pm_b = pp.get()
                mm8(pm[:, :], 1024 + n * 512, 1536 + n * 512, pm_b)
                S.op("act", lambda e, pm=pm, n=n: e.copy(out=v_sb[p2][:, n * 512:(n + 1) * 512], in_=pm[:, :]),
                     reads=[pm_b], writes=[v_b[p2]])
                yield
            for d_ in range(2):
                spd, spd_b, tbd, tbd_b = sp[p2][d_], sp_b[p2][d_], tb[p2][d_], tb_b[p2][d_]
                pm, pm_b = pp.get()
                S.op("pe", lambda e, pm=pm, d_=d_: e.matmul(pm[:, :], lhsT=lrT[p2][0:17, d_, :], rhs=w2a[0:17, d_, :],
                                                           start=True, stop=True),
                     reads=[lrT_b[p2], w2a_b], writes=[pm_b])
                S.op("act", lambda e, pm=pm, spd=spd: e.activation(out=spd[:, :], in_=pm[:, :], func=AF.Exp, scale=-1.0),
                     reads=[pm_b], writes=[spd_b])
                S.op("act", lambda e, spd=spd: e.activation(out=spd[:, :], in_=spd[:, :], func=AF.Ln, bias=1.0, scale=1.0),
                     reads=[spd_b], writes=[spd_b])
                yield
                pm, pm_b = pp.get()

                def f_dec(e, pm=pm, spd=spd):
                    for h in range(4):
                        ins = e.matmul(pm[:, 2 * h:2 * h + 1], lhsT=spd[:, h * P:(h + 1) * P], rhs=negcol[:, 0:1],
                                       start=True, stop=True)
                    return ins
                S.op("pe", f_dec, reads=[spd_b, cst_b], writes=[pm_b])
                S.op("act", lambda e, pm=pm, d_=d_: e.activation(
                    out=dec[p2][d_][:, :].unsqueeze(2), in_=pm[:, 0:8].rearrange("p (h two) -> p h two", two=2)[:, :, 0:1],
                    func=AF.Exp), reads=[pm_b], writes=[dec_b[p2][d_]])
                pmb, pmb_b = pp.get()
                S.op("pe", lambda e, pmb=pmb, d_=d_, spd=spd: e.matmul(pmb[:, :], lhsT=tri[:, 2 * d_, :], rhs=spd[:, :], start=True, stop=True),
                     reads=[spd_b, cst_b], writes=[pmb_b])
                pmc, pmc_b = pp.get()
                S.op("pe", lambda e, pmc=pmc, d_=d_, spd=spd: e.matmul(pmc[:, :], lhsT=tri[:, 2 * d_ + 1, :], rhs=spd[:, :], start=True, stop=True),
                     reads=[spd_b, cst_b], writes=[pmc_b])
                yield
                S.op("act", lambda e, pmb=pmb, tbd=tbd: e.activation(out=tbd[0][:, :], in_=pmb[:, :], func=AF.Exp),
                     reads=[pmb_b], writes=[tbd_b[0]])
                S.op("act", lambda e, pmb=pmb, tbd=tbd: e.activation(out=tbd[1][:, :], in_=pmb[:, :], func=AF.Exp, scale=-1.0),
                     reads=[pmb_b], writes=[tbd_b[1]])
                S.op("act", lambda e, pmc=pmc, tbd=tbd: e.activation(out=tbd[2][:, :], in_=pmc[:, :], func=AF.Exp),
                     reads=[pmc_b], writes=[tbd_b[2]])
                yield
                S.op("dve", lambda e, d_=d_, tbd=tbd: e.tensor_tensor(out=qe[p2][d_][:, :], in0=q_sb[p2][:, :], in1=tbd[0][:, :], op=ALU.mult),
                     reads=[q_b[p2], tbd_b[0]], writes=[qe_b[p2][d_]])
                S.op("dve", lambda e, d_=d_, tbd=tbd: e.tensor_tensor(out=ke[p2][d_][:, :], in0=k_sb[p2][:, :], in1=tbd[1][:, :], op=ALU.mult),
                     reads=[k_b[p2], tbd_b[1]], writes=[ke_b[p2][d_]])
                S.op("pool", lambda e, d_=d_, tbd=tbd: e.tensor_tensor(out=kd[p2][d_][:, :], in0=k_sb[p2][:, :], in1=tbd[2][:, :], op=ALU.mult),
                     reads=[k_b[p2], tbd_b[2]], writes=[kd_b[p2][d_]])
                yield
                pm, pm_b = pp.get()
                pmv = pm[:, :].bitcast(BF16).rearrange("p (c t) -> p c t", c=8)

                def f_t(e, pmv=pmv, d_=d_):
                    for j, src in enumerate((qe[p2][d_], ke[p2][d_])):
                        for h in range(4):
                            ins = e.transpose(out=pmv[:, j * 4 + h, :], in_=src[:, h * P:(h + 1) * P], identity=identb[:, :])
                    return ins
                S.op("pe", f_t, reads=[qe_b[p2][d_], ke_b[p2][d_], identb_b], writes=[pm_b])
                S.op("act" if d_ == 0 else "dve",
                     (lambda e, pmv=pmv, d_=d_: e.copy(out=qkT[p2][d_][:, :, :], in_=pmv)) if d_ == 0 else
                     (lambda e, pmv=pmv, d_=d_: e.tensor_copy(out=qkT[p2][d_][:, :, :], in_=pmv)),
                     reads=[pm_b], writes=[qkT_b[p2][d_]])
                yield
                pm, pm_b = pp.get()

                def f_att(e, pm=pm, d_=d_):
                    for h in range(4):
                        ins = e.matmul(pm[:, h * P:(h + 1) * P], lhsT=qkT[p2][d_][:, 4 + h, :], rhs=qkT[p2][d_][:, h, :],
                                       start=True, stop=True)
                    return ins
                S.op("pe", f_att, reads=[qkT_b[p2][d_]], writes=[pm_b])
                S.op("dve", lambda e, pm=pm, d_=d_: e.tensor_tensor(
                    out=attm[p2][d_][:, :, :], in0=pm[:, :].rearrange("p (h t) -> p h t", h=4),
                    in1=bcast(gmask[:, d_:d_ + 1, :], [P, 4, P]), op=ALU.mult),
                     reads=[pm_b, cst_b], writes=[attm_b[p2][d_]])
                yield
            assert state_ver[0] == t
            for hp in range(2):
                pm, pm_b = pp.get()

                def f_o(e, pm=pm, hp=hp):
                    for hh in range(2):
                        h = 2 * hp + hh
                        o_ap = pm[:, hh * G_DV:(hh + 1) * G_DV]
                        vv = v_sb[p2][:, h * G_DV:(h + 1) * G_DV]
                        e.matmul(o_ap, lhsT=attm[p2][0][:, h, :], rhs=vv, start=True, stop=False)
                        e.matmul(o_ap, lhsT=attm[p2][1][:, h, :], rhs=vv, start=False, stop=False)
                        ins = e.matmul(o_ap, lhsT=qkT[p2][0][:, h, :], rhs=Sb[:, h, :], start=False, stop=True)
                    return ins
                S.op("pe", f_o, reads=[attm_b[p2][0], attm_b[p2][1], v_b[p2], qkT_b[p2][0], Sb_b], writes=[pm_b])
                S.op("act", lambda e, pm=pm, hp=hp: e.copy(out=opart[p2][:, hp * 512:(hp + 1) * 512], in_=pm[:, :]),
                     reads=[pm_b], writes=[op_b[p2]])
            for hp in range(2):
                pm, pm_b = pp.get()

                def f_s(e, pm=pm, hp=hp):
                    for hh in range(2):
                        h = 2 * hp + hh
                        ins = e.matmul(pm[:, hh * G_DV:(hh + 1) * G_DV], lhsT=kd[p2][0][:, h * P:(h + 1) * P],
                                       rhs=v_sb[p2][:, h * G_DV:(h + 1) * G_DV], start=True, stop=True)
                    return ins
                S.op("pe", f_s, reads=[kd_b[p2][0], v_b[p2]], writes=[pm_b])
                for hh in range(2):
                    h = 2 * hp + hh
                    S.op("dve", lambda e, pm=pm, h=h, hh=hh: e.scalar_tensor_tensor(
                        out=S32[:, h, :], in0=S32[:, h, :], scalar=dec[p2][0][:, h:h + 1], in1=pm[:, hh * G_DV:(hh + 1) * G_DV],
                        op0=ALU.mult, op1=ALU.add), reads=[S32_b, dec_b[p2][0], pm_b], writes=[S32_b])
            S.op("act", lambda e: e.copy(out=Sb[:, :, :], in_=S32[:, :, :]), reads=[S32_b], writes=[Sb_b])
            state_ver[0] = t + 1
            yield
            S.dma("sp", scr["opart"][t * P:(t + 1) * P, :], opart[p2][:, :], reads=[op_b[p2]], writes=[scr_buf], stream=f"so{p2}")
            S.dma("sp", scr["qeTb"][t], qkT[p2][1][:, 0:4, :].rearrange("p h t -> p (h t)"), reads=[qkT_b[p2][1]],
                  writes=[scr_buf], stream=f"sq{p2}")
            S.dma("sp", scr["kdb"][t * P:(t + 1) * P, :], kd[p2][1][:, :], reads=[kd_b[p2][1]], writes=[scr_buf], stream=f"sk{p2}")
            S.dma("sp", scr["vsb"][t * P:(t + 1) * P, :], v_sb[p2][:, :], reads=[v_b[p2]], writes=[scr_buf], stream=f"sv{p2}")
            S.dma("sp", scr["decb"][t], dec[p2][1][:, :], reads=[dec_b[p2][1]], writes=[scr_buf], stream=f"sd{p2}")
            if t + NXS < NT:
                load_x(t + NXS)

        for t in range(min(NXS, NT)):
            load_x(t)
        run_interleaved((tile_gen(t) for t in range(NT)), depth=3, skew=7)
        S.barrier()


def gla_sweep2(S, nc, NT, xin, xout, w_in, norm_g, w_out, g, b, consts, scr, xin_buf, scr_buf, xout_buf, tag):
    ident_d = consts[0]
    with ExitStack() as ctx:
        sbt = lambda name, shape, dt: ctx.enter_context(nc.sbuf_tensor(f"{tag}_{name}", shape, dt))
        B = S.buf
        pp = PsumPool(S, nc, ctx, tag)
        w_rb = sbt("w_rb", [P, 8, 1024], BF16)
        w_outb = sbt("w_outb", [P, 8, D], BF16)
        identf = sbt("identf", [P, P], F32)
        identb = sbt("identb", [P, P], BF16)
        ng_t = sbt("ng", [P, G_DV], F32)
        g_t = sbt("g", [P, D], F32)
        b_t = sbt("b", [P, D], F32)
        S32 = sbt("S32", [P, 4, G_DV], F32)
        Sb = sbt("Sb", [P, 4, G_DV], BF16)
        w_b, cst_b, identb_b, gb_b, S32_b, Sb_b = B("w"), B("cst"), B("idb"), B("gb"), B("S32"), B("Sb")
        S.dma("sp", identf[:, :], ident_d[:, :], writes=[cst_b], stream=f"c0")
        S.dma("sp", ng_t[:, :], norm_g.partition_broadcast(P), writes=[gb_b], stream=f"c1")
        S.dma("sp", g_t[:, :], g.partition_broadcast(P), writes=[gb_b], stream=f"c2")
        S.dma("sp", b_t[:, :], b.partition_broadcast(P), writes=[gb_b], stream=f"c3")
        w_inv = w_in.rearrange("(c p) f -> p c f", p=P)
        w_outv = w_out.rearrange("(c p) f -> p c f", p=P)
        for c in range(8):
            S.dma("pool", w_rb[:, c, :], w_inv[:, c, 2048:3072], writes=[w_b], stream=f"w{c % 2}")
        for c in range(8):
            S.dma("pool", w_outb[:, c, :], w_outv[:, c, :], writes=[w_b], stream=f"w{c % 2}")
        S.op("dve", lambda e: e.tensor_copy(out=identb[:, :], in_=identf[:, :]), reads=[cst_b], writes=[identb_b])
        S.op("pool", lambda e: e.memset(S32[:, :, :], 0.0), writes=[S32_b])
        S.op("pool", lambda e: e.memset(Sb[:, :, :], 0.0), writes=[Sb_b])

        NXS = 4
        NW = 3
        xs = [sbt(f"x{i}", [P, D], F32) for i in range(NXS)]
        opa = [sbt(f"opa{i}", [P, D], F32) for i in range(NXS)]
        qeT = [sbt(f"qeT{i}", [P, 4, P], BF16) for i in range(NXS)]
        kdb = [sbt(f"kdb{i}", [P, 512], BF16) for i in range(NXS)]
        vsb = [sbt(f"vsb{i}", [P, 1024], BF16) for i in range(NXS)]
        decb = [sbt(f"decb{i}", [P, 4], F32) for i in range(NXS)]
        xT = [sbt(f"xT{i}", [P, 8, P], BF16) for i in range(NW)]
        er = [sbt(f"er{i}", [P, 1024], F32) for i in range(NW)]
        sr = [sbt(f"sr{i}", [P, 4, G_DV], F32) for i in range(NW)]
        of = [sbt(f"of{i}", [P, 4, G_DV], F32) for i in range(NW)]
        sq = sbt("sq", [P, G_DV], F32)
        ss = [sbt(f"ss{i}", [P, 4], F32) for i in range(NW)]
        og = [sbt(f"og{i}", [P, 1024], BF16) for i in range(NW)]
        oT = [sbt(f"oT{i}", [P, 8, P], BF16) for i in range(NW)]
        ot = [sbt(f"ot{i}", [P, D], F32) for i in range(NW)]
        stats = [sbt(f"st{i}", [P, 2, 6], F32) for i in range(NW)]
        mv = [sbt(f"mv{i}", [P, 2], F32) for i in range(NW)]
        sc = [sbt(f"sc{i}", [P, 2], F32) for i in range(NW)]
        xs_b, opa_b, qeT_b, kdb_b, vsb_b, decb_b = (S.bufs(n_, NXS) for n_ in ("xs", "opa", "qeT", "kdb", "vsb", "decb"))
        xT_b, er_b, sr_b, of_b, ss_b, og_b, oT_b, ot_b, st_b, mv_b, sc_b = (
            S.bufs(n_, NW) for n_ in ("xT", "er", "sr", "of", "ss", "og", "oT", "ot", "st", "mv", "sc"))

        order = list(range(NT - 1, -1, -1))

        def load(idx):
            t = order[idx]
            sl = idx % NXS
            S.dma("sp", qeT[sl][:, :, :].rearrange("p h t -> p (h t)"), scr["qeTb"][t], reads=[scr_buf], writes=[qeT_b[sl]],
                  stream=f"lq{sl}")
            S.dma("sp", kdb[sl][:, :], scr["kdb"][t * P:(t + 1) * P, :], reads=[scr_buf], writes=[kdb_b[sl]], stream=f"lk{sl}")
            S.dma("sp", vsb[sl][:, :], scr["vsb"][t * P:(t + 1) * P, :], reads=[scr_buf], writes=[vsb_b[sl]], stream=f"lv{sl}")
            S.dma("sp", decb[sl][:, :], scr["decb"][t], reads=[scr_buf], writes=[decb_b[sl]], stream=f"ld{sl}")
            S.dma("sp", opa[sl][:, :], scr["opart"][t * P:(t + 1) * P, :], reads=[scr_buf], writes=[opa_b[sl]], stream=f"lo{sl}")
            S.dma("sp", xs[sl][:, :], xin[t * P:(t + 1) * P, :], reads=[xin_buf], writes=[xs_b[sl]], stream=f"x{sl}")

        state_ver = [0]

        def tile_gen(idx):
            t = order[idx]
            sl, p2 = idx % NXS, idx % NW
            assert state_ver[0] == idx
            pmo = []
            for hp in range(2):
                pm, pm_b = pp.get()
                pmo.append((pm, pm_b))

                def f_o(e, pm=pm, hp=hp):
                    for hh in range(2):
                        h = 2 * hp + hh
                        ins = e.matmul(pm[:, hh * G_DV:(hh + 1) * G_DV], lhsT=qeT[sl][:, h, :], rhs=Sb[:, h, :], start=True, stop=True)
                    return ins
                S.op("pe", f_o, reads=[qeT_b[sl], Sb_b], writes=[pm_b])
            pms = []
            for hp in range(2):
                pm, pm_b = pp.get()
                pms.append((pm, pm_b))

                def f_s(e, pm=pm, hp=hp):
                    for hh in range(2):
                        h = 2 * hp + hh
                        ins = e.matmul(pm[:, hh * G_DV:(hh + 1) * G_DV], lhsT=kdb[sl][:, h * P:(h + 1) * P],
                                       rhs=vsb[sl][:, h * G_DV:(h + 1) * G_DV], start=True, stop=True)
                    return ins
                S.op("pe", f_s, reads=[kdb_b[sl], vsb_b[sl]], writes=[pm_b])
            for hp in range(2):
                pm, pm_b = pms[hp]
                for hh in range(2):
                    h = 2 * hp + hh
                    S.op("dve", lambda e, pm=pm, h=h, hh=hh: e.scalar_tensor_tensor(
                        out=S32[:, h, :], in0=S32[:, h, :], scalar=decb[sl][:, h:h + 1], in1=pm[:, hh * G_DV:(hh + 1) * G_DV],
                        op0=ALU.mult, op1=ALU.add), reads=[S32_b, decb_b[sl], pm_b], writes=[S32_b])
            S.op("act", lambda e: e.copy(out=Sb[:, :, :], in_=S32[:, :, :]), reads=[S32_b], writes=[Sb_b])
            state_ver[0] = idx + 1
            for hp in range(2):
                pm, pm_b = pmo[hp]
                S.op("dve", lambda e, pm=pm, hp=hp: e.tensor_tensor(
                    out=of[p2][:, 2 * hp:2 * hp + 2, :].rearrange("p h d -> p (h d)"), in0=pm[:, :],
                    in1=opa[sl][:, hp * 512:(hp + 1) * 512], op=ALU.add),
                     reads=[pm_b, opa_b[sl]], writes=[of_b[p2]])
            yield
            def f_sq(e):
                for h in range(4):
                    ins = e.activation(out=sq[:, :], in_=of[p2][:, h, :], func=AF.Square, accum_out=ss[p2][:, h:h + 1])
                return ins
            S.op("act", f_sq, reads=[of_b[p2]], writes=[ss_b[p2]])
            S.op("act", lambda e: e.activation(out=ss[p2][:, :], in_=ss[p2][:, :], func=AF.Ln, bias=1e-6, scale=1.0 / G_DV),
                 reads=[ss_b[p2]], writes=[ss_b[p2]])
            S.op("act", lambda e: e.activation(out=ss[p2][:, :], in_=ss[p2][:, :], func=AF.Exp, scale=-0.5),
                 reads=[ss_b[p2]], writes=[ss_b[p2]])

            def f_hn(e):
                for h in range(4):
                    ins = e.activation(out=of[p2][:, h, :], in_=of[p2][:, h, :], func=AF.Copy, scale=ss[p2][:, h:h + 1])
                return ins
            S.op("act", f_hn, reads=[of_b[p2], ss_b[p2]], writes=[of_b[p2]])
            yield
            for hh in range(2):
                pm, pm_b = pp.get()

                def f_tr(e, hh=hh, pm=pm):
                    for c in range(4):
                        ins = e.transpose(out=pm[:, c * P:(c + 1) * P], in_=xs[sl][:, (hh * 4 + c) * P:(hh * 4 + c + 1) * P],
                                          identity=identf[:, :])
                    return ins
                S.op("pe", f_tr, reads=[xs_b[sl], cst_b], writes=[pm_b])
                src = pm[:, :].rearrange("p (c t) -> p c t", c=4)
                if hh == 0:
                    S.op("act", lambda e, src=src: e.copy(out=xT[p2][:, 0:4, :], in_=src), reads=[pm_b], writes=[xT_b[p2]])
                else:
                    S.op("dve", lambda e, src=src: e.tensor_copy(out=xT[p2][:, 4:8, :], in_=src), reads=[pm_b], writes=[xT_b[p2]])
                yield
            for n in range(2):
                pm, pm_b = pp.get()

                def f_r(e, pm=pm, n=n):
                    for c in range(8):
                        ins = e.matmul(pm[:, :], lhsT=xT[p2][:, c, :], rhs=w_rb[:, c, n * 512:(n + 1) * 512], start=(c == 0), stop=(c == 7))
                    return ins
                S.op("pe", f_r, reads=[xT_b[p2], w_b], writes=[pm_b])
                ern = er[p2][:, n * 512:(n + 1) * 512]
                srn = sr[p2][:, :, :].rearrange("p h d -> p (h d)")[:, n * 512:(n + 1) * 512]
                S.op("act", lambda e, pm=pm, ern=ern: e.activation(out=ern, in_=pm[:, :], func=AF.Exp, scale=-1.0),
                     reads=[pm_b], writes=[er_b[p2]])
                S.op("act", lambda e, ern=ern: e.activation(out=ern, in_=ern, func=AF.Ln, bias=1.0, scale=1.0),
                     reads=[er_b[p2]], writes=[er_b[p2]])
                S.op("act", lambda e, ern=ern: e.activation(out=ern, in_=ern, func=AF.Exp, scale=-1.0),
                     reads=[er_b[p2]], writes=[er_b[p2]])
                yield
                S.op("dve", lambda e, pm=pm, ern=ern, srn=srn: e.tensor_tensor(out=srn, in0=pm[:, :], in1=ern, op=ALU.mult),
                     reads=[pm_b, er_b[p2]], writes=[sr_b[p2]])
                S.op("dve", lambda e, srn=srn, n=n: e.tensor_tensor(
                    out=srn.rearrange("p (h d) -> p h d", d=G_DV), in0=srn.rearrange("p (h d) -> p h d", d=G_DV),
                    in1=bcast(ng_t[:, :].unsqueeze(1), [P, 2, G_DV]), op=ALU.mult),
                     reads=[sr_b[p2], gb_b], writes=[sr_b[p2]])
                yield
            S.op("dve", lambda e: e.tensor_tensor(out=og[p2][:, :], in0=of[p2][:, :, :].rearrange("p h d -> p (h d)"),
                                                  in1=sr[p2][:, :, :].rearrange("p h d -> p (h d)"), op=ALU.mult),
                 reads=[of_b[p2], sr_b[p2]], writes=[og_b[p2]])
            yield
            pm, pm_b = pp.get()
            pmv = pm[:, :].bitcast(BF16).rearrange("p (c t) -> p c t", c=8)

            def f_ot(e, pmv=pmv):
                for s_ in range(8):
                    ins = e.transpose(out=pmv[:, s_, :], in_=og[p2][:, s_ * P:(s_ + 1) * P], identity=identb[:, :])
                return ins
            S.op("pe", f_ot, reads=[og_b[p2], identb_b], writes=[pm_b])
            S.op("act", lambda e, pmv=pmv: e.copy(out=oT[p2][:, :, :], in_=pmv), reads=[pm_b], writes=[oT_b[p2]])
            yield
            for n in range(2):
                pm, pm_b = pp.get()

                def f_op(e, pm=pm, n=n):
                    for c in range(8):
                        ins = e.matmul(pm[:, :], lhsT=oT[p2][:, c, :], rhs=w_outb[:, c, n * 512:(n + 1) * 512], start=(c == 0), stop=(c == 7))
                    return ins
                S.op("pe", f_op, reads=[oT_b[p2], w_b], writes=[pm_b])
                S.op("dve", lambda e, pm=pm, n=n: e.scalar_tensor_tensor(
                    out=xs[sl][:, n * 512:(n + 1) * 512], in0=xs[sl][:, n * 512:(n + 1) * 512], scalar=DN_ALPHA,
                    in1=pm[:, :], op0=ALU.mult, op1=ALU.add), reads=[xs_b[sl], pm_b], writes=[xs_b[sl]])
                yield
            yield from ln_epilogue_gen(S, (stats[p2], st_b[p2], mv[p2], mv_b[p2], sc[p2], sc_b[p2]),
                                       xs[sl], xs_b[sl], g_t, b_t, gb_b, ot[p2], ot_b[p2])
            S.dma("sp", xout[t * P:(t + 1) * P, :], ot[p2][:, :], reads=[ot_b[p2]], writes=[xout_buf], stream=f"o{p2}")
            if idx + NXS < NT:
                load(idx + NXS)

        for idx in range(min(NXS, NT)):
            load(idx)
        run_interleaved((tile_gen(i) for i in range(NT)), depth=3, skew=6)
        S.barrier()


def gla_consts_host():
    u = np.arange(P)[:, None]
    t = np.arange(P)[None, :]
    c = np.float32(-1.0 / G_TAU)
    tri = np.stack([np.where(u <= t, c, 0), np.where(u > t, c, 0),
                    np.where(u >= t, c, 0), np.where(u < t, c, 0)]).astype(np.float32)
    gmask = np.stack([np.where(t >= u, 1.0, 0.0), np.where(t <= u, 1.0, 0.0)]).astype(np.float32)
    return {"c_tri": tri, "c_negcol": np.full((P, 1), c, np.float32), "c_gmask": gmask}


def gla_scratch(nc, NT, tag):
    mk = lambda name, shape, dtp: nc.dram_tensor(f"{tag}_{name}", shape, dtp, kind="Internal").ap()
    return {"opart": mk("opart", [NT * P, 1024], F32), "qeTb": mk("qeTb", [NT, P, 512], BF16),
            "kdb": mk("kdb", [NT * P, 512], BF16), "vsb": mk("vsb", [NT * P, 1024], BF16),
            "decb": mk("decb", [NT, P, 4], F32)}


def build_test_gla(NT):
    nc = bass.Bass("TRN2", target_bir_lowering=False)
    dt = lambda name, shape, dtp=F32, kind="ExternalInput": nc.dram_tensor(name, shape, dtp, kind=kind).ap()
    x = dt("x", [NT * P, D])
    w_in = dt("w_in", [D, G_IN])
    gw2f, gbf, gw2b, gbb = dt("gw2f", [16, 512]), dt("gbf", [512]), dt("gw2b", [16, 512]), dt("gbb", [512])
    norm_g = dt("norm_g", [G_DV])
    w_out = dt("w_out", [D, D])
    g, b = dt("g", [D]), dt("b", [D])
    consts = (dt("c_ident", [P, P]), dt("c_tri", [4, P, P]), dt("c_negcol", [P, 1]), dt("c_gmask", [2, P, P]))
    y = dt("y", [NT * P, D], kind="ExternalOutput")
    scr = gla_scratch(nc, NT, "g1")
    with ExitStack() as ctx:
        S = Sched(nc, ctx)
        xin_buf, scr_buf, xout_buf = S.buf("xin"), S.buf("scr"), S.buf("xout")
        gla_sweep1(S, nc, NT, x, w_in, gw2f, gbf, gw2b, gbb, consts, scr, xin_buf, scr_buf, "g1a")
        gla_sweep2(S, nc, NT, x, y, w_in, norm_g, w_out, g, b, consts, scr, xin_buf, scr_buf, xout_buf, "g1b")
        S.finish()
    return nc


def build_full(NT):
    nc = bass.Bass("TRN2", target_bir_lowering=False)
    dt = lambda name, shape, dtp=F32, kind="ExternalInput": nc.dram_tensor(name, shape, dtp, kind=kind).ap()
    N = NT * P
    x = dt("x", [N, D])
    pos = dt("pos", [N], I32)
    a_w_in, a_sink, a_w_out = dt("a_w_in", [D, A_IN]), dt("a_sink", [16]), dt("a_w_out", [D, D])
    g_w_in = dt("g_w_in", [D, G_IN])
    gw2f, gbf, gw2b, gbb = dt("gw2f", [16, 512]), dt("gbf", [512]), dt("gw2b", [16, 512]), dt("gbb", [512])
    norm_g = dt("norm_g", [G_DV])
    g_w_out = dt("g_w_out", [D, D])
    mix_g, mix_b = [dt(f"mix_g{i}", [D]) for i in range(2)], [dt(f"mix_b{i}", [D]) for i in range(2)]
    w1 = [dt(f"w1_{i}", [D, DFF]) for i in range(2)]
    w2 = [dt(f"w2_{i}", [DFF, D]) for i in range(2)]
    mlp_g, mlp_b = [dt(f"mlp_g{i}", [D]) for i in range(2)], [dt(f"mlp_b{i}", [D]) for i in range(2)]
    c_ident = dt("c_ident", [P, P])
    a_consts = (c_ident, dt("c_maskp", [P, 512]), dt("c_maskn", [P, 512]), dt("c_invf", [32]))
    g_consts = (c_ident, dt("c_tri", [4, P, P]), dt("c_negcol", [P, 1]), dt("c_gmask", [2, P, P]))
    out = dt("out", [N, D], kind="ExternalOutput")
    s = [nc.dram_tensor(f"act{i}", [N, D], F32, kind="Internal").ap() for i in range(3)]
    scr = gla_scratch(nc, NT, "g1")
    with ExitStack() as ctx:
        S = Sched(nc, ctx)
        bx, bout, bscr = S.buf("x"), S.buf("out"), S.buf("scr")
        bs = S.bufs("act", 3)
        attn_phase(S, nc, NT, x, s[0], pos, a_w_in, a_sink, a_w_out, mix_g[0], mix_b[0], a_consts, bx, bs[0], "a0")
        mlp_phase(S, nc, NT, s[0], s[1], w1[0], w2[0], mlp_g[0], mlp_b[0], c_ident, bs[0], bs[1], "m0")
        gla_sweep1(S, nc, NT, s[1], g_w_in, gw2f, gbf, gw2b, gbb, g_consts, scr, bs[1], bscr, "ga")
        gla_sweep2(S, nc, NT, s[1], s[2], g_w_in, norm_g, g_w_out, mix_g[1], mix_b[1], g_consts, scr, bs[1], bscr, bs[2], "gb")
        mlp_phase(S, nc, NT, s[2], out, w1[1], w2[1], mlp_g[1], mlp_b[1], c_ident, bs[2], bout, "m1")
        S.finish()
    return nc


def host_inputs(inp, core, NT=SEQ // P):
    f = lambda a: np.ascontiguousarray(np.asarray(a, dtype=np.float32))
    N = NT * P
    m = {
        "x": f(inp["x"][core, :N]),
        "pos": np.ascontiguousarray(np.asarray(inp["positions"][core, :N], dtype=np.int32)),
        "a_w_in": permute_attn_w_in(f(inp["attn_w_in"][0])),
        "a_sink": f(inp["attn_sink"][0]),
        "a_w_out": f(inp["attn_w_out"][0]),
        "g_w_in": f(inp["gla_w_in"][0]),
        "gw2f": f(inp["gla_gate_w2_fwd"][0]), "gbf": f(inp["gla_gate_b_fwd"][0]),
        "gw2b": f(inp["gla_gate_w2_bwd"][0]), "gbb": f(inp["gla_gate_b_bwd"][0]),
        "norm_g": f(inp["gla_norm_g"][0]),
        "g_w_out": f(inp["gla_w_out"][0]),
    }
    for i in range(2):
        m[f"mix_g{i}"] = f(inp["mix_ln_g"][i])
        m[f"mix_b{i}"] = f(inp["mix_ln_b"][i])
        m[f"w1_{i}"] = f(inp["mlp_w1"][i])
        m[f"w2_{i}"] = f(inp["mlp_w2"][i])
        m[f"mlp_g{i}"] = f(inp["mlp_ln_g"][i])
        m[f"mlp_b{i}"] = f(inp["mlp_ln_b"][i])
    m.update(attn_consts_host())
    m.update(gla_consts_host())
    return m


def kernel(**inputs):
    NT = SEQ // P
    nc = build_full(NT)
    shared = host_inputs(inputs, 0)
    in_maps = []
    for c in range(NCORES):
        m = dict(shared)
        m["x"] = np.ascontiguousarray(np.asarray(inputs["x"][c], dtype=np.float32))
        m["pos"] = np.ascontiguousarray(np.asarray(inputs["positions"][c], dtype=np.int32))
        in_maps.append(m)
    res = run_bass_kernel_spmd(nc, in_maps, core_ids=list(range(NCORES)))
    return np.stack([np.asarray(r["out"], dtype=np.float32) for r in res.results], axis=0)
```

```python
import os
import numpy as np
from contextlib import ExitStack
import ml_dtypes
import concourse.bass as bass
import concourse.mybir as mybir
from concourse.bass_utils import run_bass_kernel_spmd

F32 = mybir.dt.float32
BF16 = mybir.dt.bfloat16
I32 = mybir.dt.int32
AF = mybir.ActivationFunctionType
ALU = mybir.AluOpType

P = 128
D = 1024
DFF = 4096
SEQ = 8192
NCORES = 8
DEPTH = 2
DN_ALPHA = float((2 * DEPTH) ** 0.25)
LN_EPS = 1e-5

SEM_EPOCH = 30000


class Buf:
    __slots__ = ("name", "lw", "rd", "rd_dma", "excl")

    def __init__(self, name, excl=False):
        self.name = name
        self.excl = excl
        self.lw = None
        self.rd = {}
        self.rd_dma = []


class Sched:
    def __init__(self, nc, ctx):
        self.nc = nc
        self.ctx = ctx
        self.E = {"pe": nc.tensor, "act": nc.scalar, "dve": nc.vector,
                  "pool": nc.gpsimd, "sp": nc.sync}
        self.cnt = {e: 0 for e in self.E}
        self.sems = {e: [] for e in self.E}
        self.waited = {e: {} for e in self.E}
        self.streams = {}
        self.nwaits = 0

    def buf(self, name):
        return Buf(name)

    def bufs(self, name, n):
        return [Buf(f"{name}{i}") for i in range(n)]

    def pbufs(self, name, n):
        return [Buf(f"{name}{i}", excl=True) for i in range(n)]

    def _eng_sem(self, eng, k):
        ep = (k - 1) // SEM_EPOCH
        while len(self.sems[eng]) <= ep:
            self.sems[eng].append(
                self.ctx.enter_context(self.nc.semaphore(f"s_{eng}{len(self.sems[eng])}")))
        return self.sems[eng][ep], (k - 1) % SEM_EPOCH + 1

    def _stream(self, name):
        st = self.streams.get(name)
        if st is None:
            st = {"n": 0, "sem": self.ctx.enter_context(self.nc.semaphore(f"d_{name}"))}
            self.streams[name] = st
        return st

    def _wait(self, eng, tok):
        w = self.waited[eng]
        if tok[0] == "c":
            _, peng, k = tok
            if peng == "pe" and eng == "pe":
                return
            if w.get(peng, 0) >= k:
                return
            w[peng] = k
            sem, val = self._eng_sem(peng, k)
        else:
            _, sname, n = tok
            key = "d:" + sname
            if w.get(key, 0) >= n:
                return
            w[key] = n
            sem, val = self.streams[sname]["sem"], 16 * n
        self.E[eng].wait_ge(sem, val)
        self.nwaits += 1

    def _deps(self, reads, writes, eng=None):
        deps = []
        for b in reads:
            if b.lw is not None:
                deps.append(b.lw)
            if b.excl:
                deps.extend(tok for e2, tok in b.rd.items() if e2 != eng)
        for b in writes:
            if b.lw is not None:
                deps.append(b.lw)
            deps.extend(b.rd.values())
            deps.extend(b.rd_dma)
        return deps

    def op(self, eng, fn, reads=(), writes=()):
        for tok in self._deps(reads, writes, eng):
            self._wait(eng, tok)
        ins = fn(self.E[eng])
        self.cnt[eng] += 1
        k = self.cnt[eng]
        sem, _ = self._eng_sem(eng, k)
        ins.then_inc(sem, 1)
        tok = ("c", eng, k)
        for b in writes:
            b.lw = tok
            b.rd = {}
            b.rd_dma = []
        for b in reads:
            if b.lw is not tok:
                b.rd[eng] = tok
        return tok

    def dma(self, q, out, in_, reads=(), writes=(), stream=None, **kw):
        stream = f"{q}_{stream}"
        st = self._stream(stream)
        deps = self._deps(reads, writes)
        if st["n"] > 0:
            deps.append(("d", stream, st["n"]))
        for tok in deps:
            self._wait(q, tok)
        ins = self.E[q].dma_start(out=out, in_=in_, **kw)
        ins.then_inc(st["sem"], 16)
        st["n"] += 1
        tok = ("d", stream, st["n"])
        for b in writes:
            b.lw = tok
            b.rd = {}
            b.rd_dma = []
        for b in reads:
            b.rd_dma.append(tok)
        return tok

    def barrier(self):
        for eng in self.E:
            for peng in self.E:
                if peng != eng and self.cnt[peng] > 0:
                    self._wait(eng, ("c", peng, self.cnt[peng]))
            for sname, st in self.streams.items():
                if st["n"] > 0:
                    self._wait(eng, ("d", sname, st["n"]))

    def finish(self):
        for sname, st in self.streams.items():
            if st["n"] > 0:
                self._wait("sp", ("d", sname, st["n"]))


def run_interleaved(gens, depth=2, skew=1):
    it = iter(gens)
    pending = next(it, None)
    active = []
    while active or pending is not None:
        if pending is not None and len(active) < depth and (not active or active[-1][1] >= skew):
            active.append([pending, 0])
            pending = next(it, None)
        for a in list(active):
            try:
                next(a[0])
                a[1] += 1
            except StopIteration:
                active.remove(a)


def load_rowvec_bcast(S, q, dst, src_1d, n, stream, wbuf):
    S.dma(q, dst, src_1d.partition_broadcast(P), writes=[wbuf], stream=stream)


def ln_epilogue(S, sb, z, zb, g_t, b_t, gb_buf, out_t, out_b, eps=LN_EPS):
    stats, stats_b, mv, mv_b, sc, sc_b = sb
    S.op("dve", lambda e: (e.bn_stats(out=stats[:, 0, :], in_=z[:, 0:512]),
                           e.bn_stats(out=stats[:, 1, :], in_=z[:, 512:1024]))[-1],
         reads=[zb], writes=[stats_b])
    S.op("dve", lambda e: e.bn_aggr(out=mv[:, :], in_=stats[:, :, :].rearrange("p a b -> p (a b)")),
         reads=[stats_b], writes=[mv_b])
    S.op("act", lambda e: e.activation(out=sc[:, 0:1], in_=mv[:, 1:2], func=AF.Ln, bias=eps, scale=1.0),
         reads=[mv_b], writes=[sc_b])
    S.op("act", lambda e: e.activation(out=sc[:, 0:1], in_=sc[:, 0:1], func=AF.Exp, scale=-0.5),
         reads=[sc_b], writes=[sc_b])
    S.op("dve", lambda e: e.scalar_tensor_tensor(out=sc[:, 1:2], in0=mv[:, 0:1], scalar=-1.0,
                                                 in1=sc[:, 0:1], op0=ALU.mult, op1=ALU.mult),
         reads=[mv_b, sc_b], writes=[sc_b])
    S.op("act", lambda e: e.activation(out=out_t[:, :], in_=z[:, :], func=AF.Identity,
                                       bias=sc[:, 1:2], scale=sc[:, 0:1]),
         reads=[zb, sc_b], writes=[out_b])
    S.op("dve", lambda e: e.tensor_tensor(out=out_t[:, :], in0=out_t[:, :], in1=g_t[:, :], op=ALU.mult),
         reads=[out_b, gb_buf], writes=[out_b])
    S.op("dve", lambda e: e.tensor_tensor(out=out_t[:, :], in0=out_t[:, :], in1=b_t[:, :], op=ALU.add),
         reads=[out_b, gb_buf], writes=[out_b])


class CastLoader:
    def __init__(self, S, nc, ctx, tag, width=1024, n=4):
        self.S = S
        self.stg = [ctx.enter_context(nc.sbuf_tensor(f"{tag}_stg{i}", [P, width], F32)) for i in range(n)]
        self.stg_b = S.bufs("stg", n)
        self.k = 0

    def load(self, dst, src, dbuf):
        S = self.S
        i = self.k % len(self.stg)
        self.k += 1
        w = dst.shape[-1]
        st = self.stg[i]
        S.dma("sp", st[:, 0:w], src, writes=[self.stg_b[i]], stream=f"stg{i}")
        if self.k % 2:
            S.op("act", lambda e: e.copy(out=dst, in_=st[:, 0:w]), reads=[self.stg_b[i]], writes=[dbuf])
        else:
            S.op("dve", lambda e: e.tensor_copy(out=dst, in_=st[:, 0:w]), reads=[self.stg_b[i]], writes=[dbuf])


def ln_epilogue_gen(S, sb, z, zb, g_t, b_t, gb_buf, out_t, out_b, eps=LN_EPS):
    stats, stats_b, mv, mv_b, sc, sc_b = sb
    S.op("dve", lambda e: (e.bn_stats(out=stats[:, 0, :], in_=z[:, 0:512]),
                           e.bn_stats(out=stats[:, 1, :], in_=z[:, 512:1024]))[-1],
         reads=[zb], writes=[stats_b])
    S.op("dve", lambda e: e.bn_aggr(out=mv[:, :], in_=stats[:, :, :].rearrange("p a b -> p (a b)")),
         reads=[stats_b], writes=[mv_b])
    yield
    S.op("act", lambda e: e.activation(out=sc[:, 0:1], in_=mv[:, 1:2], func=AF.Ln, bias=eps, scale=1.0),
         reads=[mv_b], writes=[sc_b])
    S.op("act", lambda e: e.activation(out=sc[:, 0:1], in_=sc[:, 0:1], func=AF.Exp, scale=-0.5),
         reads=[sc_b], writes=[sc_b])
    yield
    S.op("dve", lambda e: e.scalar_tensor_tensor(out=sc[:, 1:2], in0=mv[:, 0:1], scalar=-1.0,
                                                 in1=sc[:, 0:1], op0=ALU.mult, op1=ALU.mult),
         reads=[mv_b, sc_b], writes=[sc_b])
    yield
    S.op("act", lambda e: e.activation(out=out_t[:, :], in_=z[:, :], func=AF.Identity,
                                       bias=sc[:, 1:2], scale=sc[:, 0:1]),
         reads=[zb, sc_b], writes=[out_b])
    yield
    S.op("dve", lambda e: e.tensor_tensor(out=out_t[:, :], in0=out_t[:, :], in1=g_t[:, :], op=ALU.mult),
         reads=[out_b, gb_buf], writes=[out_b])
    S.op("dve", lambda e: e.tensor_tensor(out=out_t[:, :], in0=out_t[:, :], in1=b_t[:, :], op=ALU.add),
         reads=[out_b, gb_buf], writes=[out_b])
    yield


def mlp_phase(S, nc, NT, xin, xout, w1, w2, g, b, ident_d, xin_buf, xout_buf, tag):
    ST = 2
    NS = NT // ST
    TOK = ST * P
    with ExitStack() as ctx:
        sbt = lambda name, shape, dt: ctx.enter_context(nc.sbuf_tensor(f"{tag}_{name}", shape, dt))
        pst = lambda name, shape, dt: ctx.enter_context(nc.psum_tensor(f"{tag}_{name}", shape, dt))
        w1b = sbt("w1b", [P, 8, DFF], BF16)
        w2b = sbt("w2b", [P, 32, D], BF16)
        identf = sbt("identf", [P, P], F32)
        g_t = sbt("g", [P, D], F32)
        b_t = sbt("b", [P, D], F32)
        NXS = 4
        xs = [sbt(f"x{i}", [P, D], F32) for i in range(NXS)]
        xT = [sbt(f"xT{i}", [P, 8, TOK], BF16) for i in range(2)]
        hT = sbt("hT", [P, 32, TOK], BF16)
        rt = [sbt(f"rt{i}", [P, 512], F32) for i in range(2)]
        ot = [sbt(f"ot{i}", [P, D], F32) for i in range(2)]
        stats = [sbt(f"st{i}", [P, 2, 6], F32) for i in range(2)]
        mv = [sbt(f"mv{i}", [P, 2], F32) for i in range(2)]
        sc = [sbt(f"sc{i}", [P, 2], F32) for i in range(2)]
        ptr = [pst(f"ptr{i}", [P, 4, P], F32) for i in range(2)]
        pm1 = [pst(f"pm1{i}", [P, 512], F32) for i in range(2)]
        pm2 = [pst(f"pm2{i}", [P, 2, 512], F32) for i in range(2)]

        B = S.buf
        w1_b = [B(f"w1b{c}") for c in range(8)]
        w2_b = [B(f"w2b{c}") for c in range(8)]
        ident_b, gb_b = B("ident"), B("gb")
        xs_b = S.bufs("xs", NXS)
        xT_b = S.bufs("xT", 2)
        hT_b = [B(f"hT{j}") for j in range(16)]
        rt_b = S.bufs("rt", 2)
        ot_b = S.bufs("ot", 2)
        st_b, mv_b, sc_b = S.bufs("st", 2), S.bufs("mv", 2), S.bufs("sc", 2)
        ptr_b, pm1_b, pm2_b = S.pbufs("ptr", 2), S.pbufs("pm1", 2), S.pbufs("pm2", 2)

        S.dma("sp", identf[:, :], ident_d[:, :], writes=[ident_b], stream=f"c0")
        S.dma("sp", g_t[:, :], g.partition_broadcast(P), writes=[gb_b], stream=f"c1")
        S.dma("sp", b_t[:, :], b.partition_broadcast(P), writes=[gb_b], stream=f"c2")
        cl = CastLoader(S, nc, ctx, tag)
        w1v = w1.rearrange("(c p) f -> p c f", p=P)
        for c in range(8):
            for q4 in range(4):
                cl.load(w1b[:, c, q4 * 1024:(q4 + 1) * 1024], w1v[:, c, q4 * 1024:(q4 + 1) * 1024], w1_b[c])
        w2v = w2.rearrange("(c p) f -> p c f", p=P)
        for c in range(32):
            cl.load(w2b[:, c, :], w2v[:, c, :], w2_b[c // 4])

        def load_x(t):
            sl = t % NXS
            S.dma("sp", xs[sl][:, :], xin[t * P:(t + 1) * P, :], reads=[xin_buf], writes=[xs_b[sl]],
                  stream=f"x{sl}")

        for t in range(min(NXS, NT)):
            load_x(t)

        def transposes(s):
            xTs, xTs_b = xT[s % 2], xT_b[s % 2]
            for m in range(ST):
                t = s * ST + m
                sl = t % NXS
                for hh in range(2):
                    pt, pt_b = ptr[hh], ptr_b[hh]

                    def f_tr(e, sl=sl, hh=hh, pt=pt):
                        for c in range(4):
                            ins = e.transpose(out=pt[:, c, :], in_=xs[sl][:, (hh * 4 + c) * P:(hh * 4 + c + 1) * P],
                                              identity=identf[:, :])
                        return ins
                    S.op("pe", f_tr, reads=[xs_b[sl], ident_b], writes=[pt_b])
                    S.op("act" if hh == 0 else "dve",
                         (lambda e, pt=pt, hh=hh, m=m, xTs=xTs: e.copy(out=xTs[:, hh * 4:(hh + 1) * 4, m * P:(m + 1) * P], in_=pt[:, :, :]))
                         if hh == 0 else
                         (lambda e, pt=pt, hh=hh, m=m, xTs=xTs: e.tensor_copy(out=xTs[:, hh * 4:(hh + 1) * 4, m * P:(m + 1) * P], in_=pt[:, :, :])),
                         reads=[pt_b], writes=[xTs_b])
        transposes(0)
        for s in range(NS):
            xTs, xTs_b = xT[s % 2], xT_b[s % 2]
            for jj in range(16):
                pm, pm_b = pm1[jj % 2], pm1_b[jj % 2]

                def f_mm1(e, jj=jj, pm=pm, xTs=xTs):
                    for u in range(2):
                        j = jj * 2 + u
                        for c in range(8):
                            ins = e.matmul(pm[:, u * TOK:(u + 1) * TOK], lhsT=w1b[:, c, j * P:(j + 1) * P],
                                           rhs=xTs[:, c, :], start=(c == 0), stop=(c == 7))
                    return ins
                S.op("pe", f_mm1, reads=[xTs_b] + w1_b, writes=[pm_b])
                r, r_b = rt[jj % 2], rt_b[jj % 2]
                S.op("act", lambda e, pm=pm, r=r: e.activation(out=r[:, :], in_=pm[:, :], func=AF.Relu),
                     reads=[pm_b], writes=[r_b])
                S.op("dve", lambda e, r=r, jj=jj: e.tensor_tensor(
                    out=hT[:, 2 * jj:2 * jj + 2, :].rearrange("p a t -> p (a t)"), in0=r[:, :], in1=r[:, :], op=ALU.mult),
                     reads=[r_b], writes=[hT_b[jj]])
            if s + 1 < NS:
                transposes(s + 1)
            for m in range(ST):
                t = s * ST + m
                sl = t % NXS
                k2 = t % 2
                py, py_b = pm2[k2], pm2_b[k2]

                def f_mm2(e, m=m, py=py):
                    for n in range(2):
                        for j in range(32):
                            ins = e.matmul(py[:, n, :], lhsT=hT[:, j, m * P:(m + 1) * P],
                                           rhs=w2b[:, j, n * 512:(n + 1) * 512], start=(j == 0), stop=(j == 31))
                    return ins
                S.op("pe", f_mm2, reads=hT_b + w2_b, writes=[py_b])
                S.op("dve", lambda e, sl=sl, py=py: e.scalar_tensor_tensor(
                    out=xs[sl][:, :], in0=xs[sl][:, :], scalar=DN_ALPHA,
                    in1=py[:, :, :].rearrange("p a b -> p (a b)"), op0=ALU.mult, op1=ALU.add),
                     reads=[xs_b[sl], py_b], writes=[xs_b[sl]])
                ln_epilogue(S, (stats[k2], st_b[k2], mv[k2], mv_b[k2], sc[k2], sc_b[k2]),
                            xs[sl], xs_b[sl], g_t, b_t, gb_b, ot[k2], ot_b[k2])
                S.dma("sp", xout[t * P:(t + 1) * P, :], ot[k2][:, :], reads=[ot_b[k2]], writes=[xout_buf],
                      stream=f"o{k2}")
                if t + NXS < NT:
                    load_x(t + NXS)
        S.barrier()


def build_test_mlp(NT):
    nc = bass.Bass("TRN2", target_bir_lowering=False)
    x = nc.dram_tensor("x", [NT * P, D], F32, kind="ExternalInput").ap()
    w1 = nc.dram_tensor("w1", [D, DFF], F32, kind="ExternalInput").ap()
    w2 = nc.dram_tensor("w2", [DFF, D], F32, kind="ExternalInput").ap()
    g = nc.dram_tensor("g", [D], F32, kind="ExternalInput").ap()
    b = nc.dram_tensor("b", [D], F32, kind="ExternalInput").ap()
    ident = nc.dram_tensor("ident", [P, P], F32, kind="ExternalInput").ap()
    y = nc.dram_tensor("y", [NT * P, D], F32, kind="ExternalOutput").ap()
    with ExitStack() as ctx:
        S = Sched(nc, ctx)
        mlp_phase(S, nc, NT, x, y, w1, w2, g, b, ident, S.buf("xin"), S.buf("xout"), "m0")
        S.finish()
    return nc


A_HEADS = 16
A_KV = 4
HD = 64
A_IN = 1536
TWO_PI = 2.0 * np.pi
CW1 = 6.28125
CW2 = float(TWO_PI - 6.28125)
PI_LO = 3.1415925
Q_HEAD_ORDER = [0, 4, 1, 5, 2, 6, 3, 7, 8, 12, 9, 13, 10, 14, 11, 15]


def bcast(ap, shape):
    return ap.broadcast_to(list(shape))


def attn_phase(S, nc, NT, xin, xout, pos, w_in, sink, w_out, g, b, consts, xin_buf, xout_buf, tag):
    ident_d, maskp_d, maskn_d, invf_d = consts
    scale = HD ** -0.5
    with ExitStack() as ctx:
        sbt = lambda name, shape, dt: ctx.enter_context(nc.sbuf_tensor(f"{tag}_{name}", shape, dt))
        pst = lambda name, shape, dt: ctx.enter_context(nc.psum_tensor(f"{tag}_{name}", shape, dt))
        B = S.buf
        w_inb = sbt("w_inb", [P, 8, A_IN], BF16)
        w_outb = sbt("w_outb", [P, 8, D], BF16)
        identf = sbt("identf", [P, P], F32)
        identb = sbt("identb", [P, P], BF16)
        maskp = sbt("maskp", [P, 512], BF16)
        maskn = sbt("maskn", [P, 512], BF16)
        g_t = sbt("g", [P, D], F32)
        b_t = sbt("b", [P, D], F32)
        esink = sbt("esink", [P, 16], F32)
        cos_t = sbt("cos", [P, NT, 32], F32)
        sin_t = sbt("sin", [P, NT, 32], F32)
        w_in_b, w_out_b = B("w_in"), B("w_out")
        ident_b, identb_b, mask_b, gb_b, esink_b, cs_b = B("id"), B("idb"), B("mask"), B("gb"), B("esink"), B("cs")

        pmm = [pst(f"pmm{i}", [P, 512], F32) for i in range(2)]
        ptb = pst("ptb", [P, 8, P], BF16)
        psc = [pst(f"psc{i}", [P, 512], F32) for i in range(2)]
        pov = [pst(f"pov{i}", [P, 512], F32) for i in range(3)]
        pmm_b, psc_b, pov_b, ptb_b = S.pbufs("pmm", 2), S.pbufs("psc", 2), S.pbufs("pov", 3), S.pbufs("ptb", 1)[0]

        S.dma("sp", identf[:, :], ident_d[:, :], writes=[ident_b], stream=f"c0")
        S.dma("sp", g_t[:, :], g.partition_broadcast(P), writes=[gb_b], stream=f"c1")
        S.dma("sp", b_t[:, :], b.partition_broadcast(P), writes=[gb_b], stream=f"c2")
        S.dma("sp", esink[:, :], sink.partition_broadcast(P), writes=[esink_b], stream=f"c3")
        S.dma("pool", maskp[:, :], maskp_d[:, :], writes=[mask_b], stream=f"c4")
        S.dma("pool", maskn[:, :], maskn_d[:, :], writes=[mask_b], stream=f"c5")
        cl = CastLoader(S, nc, ctx, tag)
        w_inv = w_in.rearrange("(c p) f -> p c f", p=P)
        for c in range(8):
            for hh in range(2):
                cl.load(w_inb[:, c, hh * 768:(hh + 1) * 768], w_inv[:, c, hh * 768:(hh + 1) * 768], w_in_b)
        w_outv = w_out.rearrange("(c p) f -> p c f", p=P)
        for c in range(8):
            cl.load(w_outb[:, c, :], w_outv[:, c, :], w_out_b)
        S.op("dve", lambda e: e.tensor_copy(out=identb[:, :], in_=identf[:, :]), reads=[ident_b], writes=[identb_b])
        S.op("act", lambda e: e.activation(out=esink[:, :], in_=esink[:, :], func=AF.Exp),
             reads=[esink_b], writes=[esink_b])

        with ExitStack() as c2:
            sb2 = lambda name, shape, dt: c2.enter_context(nc.sbuf_tensor(f"{tag}_{name}", shape, dt))
            posi = sb2("posi", [NT, P], I32)
            posf = sb2("posf", [NT, P], F32)
            posT = sb2("posT", [P, NT], F32)
            invf = sb2("invf", [P, 32], F32)
            ang = sb2("ang", [P, NT, 32], F32)
            u = sb2("u", [P, NT, 32], F32)
            ki = sb2("ki", [P, NT, 32], I32)
            kf = sb2("kf", [P, NT, 32], F32)
            posi_b, posf_b, posT_b, invf_b, ang_b, u_b, ki_b, kf_b = S.bufs("rp", 8)
            S.dma("sp", posi[:, :], pos.rearrange("(t p) -> t p", p=P), writes=[posi_b], stream=f"c6")
            S.dma("sp", invf[:, :], invf_d.partition_broadcast(P), writes=[invf_b], stream=f"c7")
            S.op("dve", lambda e: e.tensor_copy(out=posf[:, :], in_=posi[:, :]), reads=[posi_b], writes=[posf_b])
            S.op("pe", lambda e: e.transpose(out=pmm[0][:, 0:NT], in_=posf[:, :], identity=identf[0:NT, 0:NT]),
                 reads=[posf_b, ident_b], writes=[pmm_b[0]])
            S.op("dve", lambda e: e.tensor_copy(out=posT[:, :], in_=pmm[0][:, 0:NT]), reads=[pmm_b[0]], writes=[posT_b])
            S.op("dve", lambda e: e.tensor_tensor(out=ang[:, :, :], in0=bcast(posT[:, :].unsqueeze(2), [P, NT, 32]),
                                                  in1=bcast(invf[:, :].unsqueeze(1), [P, NT, 32]), op=ALU.mult),
                 reads=[posT_b, invf_b], writes=[ang_b])
            for which, off, dst in (("sin", 0.0, sin_t), ("cos", 0.25, cos_t)):
                S.op("dve", lambda e, off=off: e.tensor_scalar(out=u[:, :, :], in0=ang[:, :, :], scalar1=float(1.0 / TWO_PI),
                                                               scalar2=off, op0=ALU.mult, op1=ALU.add),
                     reads=[ang_b], writes=[u_b])
                S.op("dve", lambda e: e.tensor_copy(out=ki[:, :, :], in_=u[:, :, :]), reads=[u_b], writes=[ki_b])
                S.op("dve", lambda e: e.tensor_copy(out=kf[:, :, :], in_=ki[:, :, :]), reads=[ki_b], writes=[kf_b])
                S.op("dve", lambda e: e.scalar_tensor_tensor(out=u[:, :, :], in0=kf[:, :, :], scalar=-CW1, in1=ang[:, :, :],
                                                             op0=ALU.mult, op1=ALU.add),
                     reads=[kf_b, ang_b], writes=[u_b])
                S.op("dve", lambda e: e.scalar_tensor_tensor(out=u[:, :, :], in0=kf[:, :, :], scalar=-CW2, in1=u[:, :, :],
                                                             op0=ALU.mult, op1=ALU.add),
                     reads=[kf_b, u_b], writes=[u_b])
                S.op("dve", lambda e, off=off: e.tensor_scalar(out=u[:, :, :], in0=u[:, :, :], scalar1=float(off * TWO_PI),
                                                               scalar2=PI_LO, op0=ALU.add, op1=ALU.min),
                     reads=[u_b], writes=[u_b])
                S.op("dve", lambda e: e.tensor_scalar(out=u[:, :, :], in0=u[:, :, :], scalar1=-PI_LO, scalar2=None, op0=ALU.max),
                     reads=[u_b], writes=[u_b])
                S.op("act", lambda e, dst=dst: e.activation(out=dst[:, :, :], in_=u[:, :, :], func=AF.Sin),
                     reads=[u_b], writes=[cs_b])
            S.barrier()

        NXS = 6
        NW = 3
        NQ = 4
        NK = 6
        xs = [sbt(f"x{i}", [P, D], F32) for i in range(NXS)]
        xT = [sbt(f"xT{i}", [P, 8, P], BF16) for i in range(NW)]
        ra = [sbt(f"ra{i}", [P, 8, 2, 32], F32) for i in range(2)]
        rb = [sbt(f"rb{i}", [P, 8, 2, 32], F32) for i in range(2)]
        qr = [sbt(f"qr{i}", [P, 16, 2, 32], BF16) for i in range(NW)]
        kr = [sbt(f"kr{i}", [P, 4, 2, 32], BF16) for i in range(NW)]
        qT = [sbt(f"qT{i}", [P, 8, P], BF16) for i in range(NQ)]
        kTw = [sbt(f"kT{i}", [P, 2, P], BF16) for i in range(NK)]
        Vw = [sbt(f"V{i}", [P, 4, 65], BF16) for i in range(NK)]
        pT = [sbt(f"pT{i}", [P, 512], BF16) for i in range(3)]
        den = [sbt(f"den{i}", [P, 16], F32) for i in range(NW)]
        on = [sbt(f"on{i}", [P, 16, 64], BF16) for i in range(NW)]
        oT = [sbt(f"oT{i}", [P, 8, P], BF16) for i in range(NW)]
        ot = [sbt(f"ot{i}", [P, D], F32) for i in range(NW)]
        stats = [sbt(f"st{i}", [P, 2, 6], F32) for i in range(NW)]
        mv = [sbt(f"mv{i}", [P, 2], F32) for i in range(NW)]
        sc = [sbt(f"sc{i}", [P, 2], F32) for i in range(NW)]
        xs_b, xT_b = S.bufs("xs", NXS), S.bufs("xT", NW)
        ra_b, rb_b, qr_b, kr_b, qT_b = S.bufs("ra", 2), S.bufs("rb", 2), S.bufs("qr", NW), S.bufs("kr", NW), S.bufs("qT", NQ)
        kT_b, V_b, pT_b = S.bufs("kT", NK), S.bufs("V", NK), S.bufs("pT", 3)
        den_b, on_b, oT_b, ot_b = S.bufs("den", NW), S.bufs("on", NW), S.bufs("oT", NW), S.bufs("ot", NW)
        st_b, mv_b, sc_b = S.bufs("st", NW), S.bufs("mv", NW), S.bufs("sc", NW)
        for i in range(NK):
            S.op("pool", lambda e, i=i: e.memset(Vw[i][:, :, :], 1.0), writes=[V_b[i]])

        cnt = {"mm": 0, "sc": 0, "pt": 0, "rr": 0}
        pov_free = [True]

        def nxt(key, n):
            v = cnt[key] % n
            cnt[key] += 1
            return v

        def load_x(t):
            sl = t % NXS
            S.dma("sp", xs[sl][:, :], xin[t * P:(t + 1) * P, :], reads=[xin_buf], writes=[xs_b[sl]],
                  stream=f"x{sl}")

        def rope(pm, pm_bf, nh, t, dst, dst_bf, h0):
            r = nxt("rr", 2)
            pm4 = pm[:, 0:nh * 64].rearrange("p (h two d) -> p h two d", two=2, d=32)
            cb = bcast(cos_t[:, t:t + 1, :].unsqueeze(1), [P, nh, 2, 32])
            sb_ = bcast(sin_t[:, t:t + 1, :].unsqueeze(1), [P, nh, 2, 32])
            S.op("dve", lambda e: e.tensor_tensor(out=ra[r][:, 0:nh, :, :], in0=pm4, in1=cb, op=ALU.mult),
                 reads=[pm_bf, cs_b], writes=[ra_b[r]])
            S.op("dve", lambda e: e.tensor_tensor(out=rb[r][:, 0:nh, :, :], in0=pm4, in1=sb_, op=ALU.mult),
                 reads=[pm_bf, cs_b], writes=[rb_b[r]])
            S.op("pool", lambda e: e.tensor_tensor(out=dst[:, h0:h0 + nh, 0, :], in0=ra[r][:, 0:nh, 0, :],
                                                   in1=rb[r][:, 0:nh, 1, :], op=ALU.subtract),
                 reads=[ra_b[r], rb_b[r]], writes=[dst_bf])
            S.op("pool", lambda e: e.tensor_tensor(out=dst[:, h0:h0 + nh, 1, :], in0=ra[r][:, 0:nh, 1, :],
                                                   in1=rb[r][:, 0:nh, 0, :], op=ALU.add),
                 reads=[ra_b[r], rb_b[r]], writes=[dst_bf])

        def stage1(t):
            sl, p2, p4, pq = t % NXS, t % NW, t % NK, t % NQ
            for hh in range(2):
                m = nxt("mm", 2)

                def f_tr(e, hh=hh, m=m):
                    for c in range(4):
                        ins = e.transpose(out=pmm[m][:, c * P:(c + 1) * P],
                                          in_=xs[sl][:, (hh * 4 + c) * P:(hh * 4 + c + 1) * P], identity=identf[:, :])
                    return ins
                S.op("pe", f_tr, reads=[xs_b[sl], ident_b], writes=[pmm_b[m]])
                src = pmm[m][:, :].rearrange("p (c t) -> p c t", c=4)
                if hh == 0:
                    S.op("act", lambda e, src=src: e.copy(out=xT[p2][:, 0:4, :], in_=src), reads=[pmm_b[m]], writes=[xT_b[p2]])
                else:
                    S.op("dve", lambda e, src=src: e.tensor_copy(out=xT[p2][:, 4:8, :], in_=src), reads=[pmm_b[m]], writes=[xT_b[p2]])
                yield
            for n in range(3):
                m = nxt("mm", 2)

                def f_in(e, n=n, m=m):
                    for c in range(8):
                        ins = e.matmul(pmm[m][:, :], lhsT=xT[p2][:, c, :], rhs=w_inb[:, c, n * 512:(n + 1) * 512],
                                       start=(c == 0), stop=(c == 7))
                    return ins
                S.op("pe", f_in, reads=[xT_b[p2], w_in_b], writes=[pmm_b[m]])
                if n < 2:
                    rope(pmm[m], pmm_b[m], 8, t, qr[p2], qr_b[p2], n * 8)
                else:
                    rope(pmm[m], pmm_b[m], 4, t, kr[p2], kr_b[p2], 0)
                    S.op("act", lambda e, m=m: e.copy(out=Vw[p4][:, :, 0:64],
                                                      in_=pmm[m][:, 256:512].rearrange("p (h d) -> p h d", d=64)),
                         reads=[pmm_b[m]], writes=[V_b[p4]])
                yield

            def f_qt(e):
                qflat = qr[p2][:, :, :, :].rearrange("p h two d -> p (h two d)")
                for s_ in range(8):
                    ins = e.transpose(out=ptb[:, s_, :], in_=qflat[:, s_ * P:(s_ + 1) * P], identity=identb[:, :])
                return ins
            S.op("pe", f_qt, reads=[qr_b[p2], identb_b], writes=[ptb_b])
            S.op("act", lambda e: e.copy(out=qT[pq][:, :, :], in_=ptb[:, :, :]), reads=[ptb_b], writes=[qT_b[pq]])
            yield

            def f_kt(e):
                kflat = kr[p2][:, :, :, :].rearrange("p h two d -> p (h two d)")
                for s_ in range(2):
                    ins = e.transpose(out=ptb[:, s_, :], in_=kflat[:, s_ * P:(s_ + 1) * P], identity=identb[:, :])
                return ins
            S.op("pe", f_kt, reads=[kr_b[p2], identb_b], writes=[ptb_b])
            S.op("dve", lambda e: e.tensor_copy(out=kTw[p4][:, :, :], in_=ptb[:, 0:2, :]), reads=[ptb_b], writes=[kT_b[p4]])
            yield

        def stage2(i):
            sl, p2, pq = i % NXS, i % NW, i % NQ
            kts = [kt for kt in (i - 1, i, i + 1) if 0 <= kt < NT]
            bank_started = [False, False, False]
            last_in_bank = {0: 5, 1: 11, 2: 15}
            assert pov_free[0]
            pov_free[0] = False
            work = [(kv, kt) for kv in range(4) for kt in kts]

            def emit_scores(kv, kt):
                base, ch = 64 * (kv % 2), kv // 2
                s_ = nxt("sc", 2)

                def f_sc(e):
                    if kt != i:
                        e.matmul(psc[s_][:, :], lhsT=identb[:, :], rhs=(maskp if kt < i else maskn)[:, :],
                                 start=True, stop=False)
                    return e.matmul(psc[s_][:, :], lhsT=kTw[kt % NK][base:base + 64, ch, :],
                                    rhs=qT[pq][base:base + 64, 4 * ch:4 * ch + 4, :],
                                    start=(kt == i), stop=True)
                S.op("pe", f_sc, reads=[kT_b[kt % NK], qT_b[pq], identb_b, mask_b], writes=[psc_b[s_]])
                return s_

            def emit_exp_pv(kv, kt, s_):
                pi_ = nxt("pt", 3)
                S.op("act", lambda e: e.activation(out=pT[pi_][:, :], in_=psc[s_][:, :], func=AF.Exp, scale=float(scale)),
                     reads=[psc_b[s_]], writes=[pT_b[pi_]])

                def f_pv(e):
                    for g_ in range(4):
                        h = kv * 4 + g_
                        bk, col = h // 6, (h % 6) * 65
                        st_ = not bank_started[bk]
                        bank_started[bk] = True
                        ins = e.matmul(pov[bk][:, col:col + 65], lhsT=pT[pi_][:, g_ * P:(g_ + 1) * P],
                                       rhs=Vw[kt % NK][:, kv, :], start=st_,
                                       stop=(h == last_in_bank[bk] and kt == kts[-1]), skip_group_check=True)
                    return ins
                banks = sorted({(kv * 4 + g_) // 6 for g_ in range(4)})
                S.op("pe", f_pv, reads=[pT_b[pi_], V_b[kt % NK]], writes=[pov_b[bk] for bk in banks])

            prev = None
            for (kv, kt) in work:
                s_ = emit_scores(kv, kt)
                if prev is not None:
                    emit_exp_pv(*prev)
                prev = (kv, kt, s_)
            emit_exp_pv(*prev)
            yield
            for bk in range(3):
                nh = 6 if bk < 2 else 4
                h0 = bk * 6
                pv3 = pov[bk][:, 0:nh * 65].rearrange("p (h d) -> p h d", d=65)
                S.op("dve", lambda e, pv3=pv3, h0=h0, nh=nh: e.tensor_tensor(
                    out=den[p2][:, h0:h0 + nh].unsqueeze(2), in0=pv3[:, :, 64:65],
                    in1=esink[:, h0:h0 + nh].unsqueeze(2), op=ALU.add),
                     reads=[pov_b[bk], esink_b], writes=[den_b[p2]])
                S.op("dve", lambda e, h0=h0, nh=nh: e.reciprocal(out=den[p2][:, h0:h0 + nh], in_=den[p2][:, h0:h0 + nh]),
                     reads=[den_b[p2]], writes=[den_b[p2]])
                S.op("dve", lambda e, pv3=pv3, h0=h0, nh=nh: e.tensor_tensor(
                    out=on[p2][:, h0:h0 + nh, :], in0=pv3[:, :, 0:64],
                    in1=bcast(den[p2][:, h0:h0 + nh].unsqueeze(2), [P, nh, 64]), op=ALU.mult),
                     reads=[pov_b[bk], den_b[p2]], writes=[on_b[p2]])
            pov_free[0] = True
            yield

            def f_ot(e):
                oflat = on[p2][:, :, :].rearrange("p h d -> p (h d)")
                for s_ in range(8):
                    ins = e.transpose(out=ptb[:, s_, :], in_=oflat[:, s_ * P:(s_ + 1) * P], identity=identb[:, :])
                return ins
            S.op("pe", f_ot, reads=[on_b[p2], identb_b], writes=[ptb_b])
            S.op("act", lambda e: e.copy(out=oT[p2][:, :, :], in_=ptb[:, :, :]), reads=[ptb_b], writes=[oT_b[p2]])
            yield
            for n in range(2):
                m = nxt("mm", 2)

                def f_op(e, n=n, m=m):
                    for c in range(8):
                        ins = e.matmul(pmm[m][:, :], lhsT=oT[p2][:, c, :], rhs=w_outb[:, c, n * 512:(n + 1) * 512],
                                       start=(c == 0), stop=(c == 7))
                    return ins
                S.op("pe", f_op, reads=[oT_b[p2], w_out_b], writes=[pmm_b[m]])
                S.op("dve", lambda e, n=n, m=m: e.scalar_tensor_tensor(
                    out=xs[sl][:, n * 512:(n + 1) * 512], in0=xs[sl][:, n * 512:(n + 1) * 512], scalar=DN_ALPHA,
                    in1=pmm[m][:, :], op0=ALU.mult, op1=ALU.add),
                     reads=[xs_b[sl], pmm_b[m]], writes=[xs_b[sl]])
                yield
            yield from ln_epilogue_gen(S, (stats[p2], st_b[p2], mv[p2], mv_b[p2], sc[p2], sc_b[p2]),
                                       xs[sl], xs_b[sl], g_t, b_t, gb_b, ot[p2], ot_b[p2])
            S.dma("sp", xout[i * P:(i + 1) * P, :], ot[p2][:, :], reads=[ot_b[p2]], writes=[xout_buf],
                  stream=f"o{p2}")
            if i + NXS < NT:
                load_x(i + NXS)

        def tile_gen(t):
            if t < NT:
                yield from stage1(t)
            else:
                for _ in range(7):
                    yield
            if t >= 1:
                yield from stage2(t - 1)

        for t in range(min(NXS, NT)):
            load_x(t)
        run_interleaved((tile_gen(t) for t in range(NT + 1)), depth=3, skew=6)
        S.barrier()


def attn_consts_host():
    j = np.arange(P)[:, None]
    i = np.arange(P)[None, :]
    mp = np.where(j >= i, 0.0, -30000.0).astype(np.float32)
    mn = np.where(j <= i, 0.0, -30000.0).astype(np.float32)
    invf = (10000.0 ** (-np.arange(0, HD, 2, dtype=np.float64) / HD)).astype(np.float32)
    return {"c_ident": np.eye(P, dtype=np.float32), "c_maskp": np.tile(mp, (1, 4)), "c_maskn": np.tile(mn, (1, 4)),
            "c_invf": invf}


def permute_attn_w_in(w_in):
    q = w_in[:, :1024].reshape(1024, 16, 64)[:, Q_HEAD_ORDER, :].reshape(1024, 1024)
    return np.ascontiguousarray(np.concatenate([q, w_in[:, 1024:]], axis=1))


def build_test_attn(NT):
    nc = bass.Bass("TRN2", target_bir_lowering=False)
    dt = lambda name, shape, dtp=F32, kind="ExternalInput": nc.dram_tensor(name, shape, dtp, kind=kind).ap()
    x = dt("x", [NT * P, D])
    pos = dt("pos", [NT * P], I32)
    w_in = dt("w_in", [D, A_IN])
    sink = dt("sink", [16])
    w_out = dt("w_out", [D, D])
    g = dt("g", [D])
    b = dt("b", [D])
    consts = (dt("c_ident", [P, P]), dt("c_maskp", [P, 512]), dt("c_maskn", [P, 512]), dt("c_invf", [32]))
    y = dt("y", [NT * P, D], kind="ExternalOutput")
    with ExitStack() as ctx:
        S = Sched(nc, ctx)
        attn_phase(S, nc, NT, x, y, pos, w_in, sink, w_out, g, b, consts, S.buf("xin"), S.buf("xout"), "a0")
        S.finish()
    return nc


G_H = 4
G_DK = 128
G_DV = 256
G_IN = 3104
G_TAU = 16.0


class PsumPool:
    def __init__(self, S, nc, ctx, tag, n=8):
        self.t = [ctx.enter_context(nc.psum_tensor(f"{tag}_pp{i}", [P, 512], F32)) for i in range(n)]
        self.b = S.pbufs(f"{tag}pp", n)
        self.i = 0
        self.n = n

    def get(self):
        k = self.i % self.n
        self.i += 1
        return self.t[k], self.b[k]


def gla_sweep1(S, nc, NT, xin, w_in, gw2f, gbf, gw2b, gbb, consts, scr, xin_buf, scr_buf, tag):
    ident_d, tri_d, negcol_d, gmask_d = consts
    with ExitStack() as ctx:
        sbt = lambda name, shape, dt: ctx.enter_context(nc.sbuf_tensor(f"{tag}_{name}", shape, dt))
        B = S.buf
        pp = PsumPool(S, nc, ctx, tag)
        w_inb = sbt("w_inb", [P, 8, 2080], BF16)
        w2a = sbt("w2a", [17, 2, 512], BF16)
        tri = sbt("tri", [P, 4, P], F32)
        negcol = sbt("negcol", [P, 1], F32)
        gmask = sbt("gmask", [P, 2, P], F32)
        identf = sbt("identf", [P, P], F32)
        identb = sbt("identb", [P, P], BF16)
        S32 = sbt("S32", [P, 4, G_DV], F32)
        Sb = sbt("Sb", [P, 4, G_DV], BF16)
        w_in_b, w2a_b, cst_b, identb_b, S32_b, Sb_b = B("w_in"), B("w2a"), B("cst"), B("idb"), B("S32"), B("Sb")
        S.dma("sp", identf[:, :], ident_d[:, :], writes=[cst_b], stream=f"c0")
        S.dma("sp", tri[:, :, :], tri_d.rearrange("k p t -> p k t"), writes=[cst_b], stream=f"c1")
        S.dma("sp", negcol[:, :], negcol_d[:, :], writes=[cst_b], stream=f"c2")
        S.dma("sp", gmask[:, :, :], gmask_d.rearrange("k p t -> p k t"), writes=[cst_b], stream=f"c3")
        w_inv = w_in.rearrange("(c p) f -> p c f", p=P)
        cl = CastLoader(S, nc, ctx, tag)
        for c in range(8):
            for hh in range(2):
                cl.load(w_inb[:, c, hh * 1024:(hh + 1) * 1024], w_inv[:, c, hh * 1024:(hh + 1) * 1024], w_in_b)
        for c in range(8):
            S.dma("pool", w_inb[:, c, 2048:2080], w_inv[:, c, 3072:3104], writes=[w_in_b], stream=f"w{c % 2}")
        for d_, (w2, gb_) in enumerate(((gw2f, gbf), (gw2b, gbb))):
            S.dma("pool", w2a[0:16, d_, :], w2[:, :], writes=[w2a_b], stream=f"w0")
            S.dma("pool", w2a[16:17, d_, :], gb_.rearrange("(o f) -> o f", o=1), writes=[w2a_b], stream=f"w1")
        S.op("dve", lambda e: e.tensor_copy(out=identb[:, :], in_=identf[:, :]), reads=[cst_b], writes=[identb_b])
        S.op("pool", lambda e: e.memset(S32[:, :, :], 0.0), writes=[S32_b])
        S.op("pool", lambda e: e.memset(Sb[:, :, :], 0.0), writes=[Sb_b])

        NXS = 4
        NW = 3
        xs = [sbt(f"x{i}", [P, D], F32) for i in range(NXS)]
        xT = [sbt(f"xT{i}", [P, 8, P], BF16) for i in range(NW)]
        q_sb = [sbt(f"q{i}", [P, 512], F32) for i in range(NW)]
        k_sb = [sbt(f"k{i}", [P, 512], F32) for i in range(NW)]
        v_sb = [sbt(f"v{i}", [P, 1024], BF16) for i in range(NW)]
        lrT = [sbt(f"lrT{i}", [17, 2, P], BF16) for i in range(NW)]
        sp = [[sbt(f"sp{i}_{d_}", [P, 512], F32) for d_ in range(2)] for i in range(NW)]
        dec = [[sbt(f"dec{i}_{d_}", [P, 4], F32) for d_ in range(2)] for i in range(NW)]
        tb = [[[sbt(f"tb{i}_{d_}_{j}", [P, 512], F32) for j in range(3)] for d_ in range(2)] for i in range(NW)]
        qe = [[sbt(f"qe{i}_{d_}", [P, 512], BF16) for d_ in range(2)] for i in range(NW)]
        ke = [[sbt(f"ke{i}_{d_}", [P, 512], BF16) for d_ in range(2)] for i in range(NW)]
        kd = [[sbt(f"kd{i}_{d_}", [P, 512], BF16) for d_ in range(2)] for i in range(NW)]
        qkT = [[sbt(f"qkT{i}_{d_}", [P, 8, P], BF16) for d_ in range(2)] for i in range(NW)]
        attm = [[sbt(f"attm{i}_{d_}", [P, 4, P], BF16) for d_ in range(2)] for i in range(NW)]
        opart = [sbt(f"op{i}", [P, 1024], F32) for i in range(NW)]
        xs_b = S.bufs("xs", NXS)
        xT_b, q_b, k_b, v_b, lrT_b, op_b = (S.bufs(n_, NW) for n_ in ("xT", "q", "k", "v", "lrT", "op"))
        sp_b, dec_b, qe_b, ke_b, kd_b, qkT_b, attm_b = ([S.bufs(f"{n_}{i}", 2) for i in range(NW)]
                                                        for n_ in ("sp", "dec", "qe", "ke", "kd", "qkT", "attm"))
        tb_b = [[S.bufs(f"tb{i}_{d_}", 3) for d_ in range(2)] for i in range(NW)]
        for i in range(NW):
            S.op("pool", lambda e, i=i: e.memset(lrT[i][:, :, :], 1.0), writes=[lrT_b[i]])

        def load_x(t):
            sl = t % NXS
            S.dma("sp", xs[sl][:, :], xin[t * P:(t + 1) * P, :], reads=[xin_buf], writes=[xs_b[sl]], stream=f"x{sl}")

        state_ver = [0]

        def tile_gen(t):
            sl, p2 = t % NXS, t % NW

            def mm8(out_ap, c0, c1, wr_b):
                def f(e):
                    for c in range(8):
                        ins = e.matmul(out_ap, lhsT=xT[p2][:, c, :], rhs=w_inb[:, c, c0:c1], start=(c == 0), stop=(c == 7))
                    return ins
                S.op("pe", f, reads=[xT_b[p2], w_in_b], writes=[wr_b])

            for hh in range(2):
                pm, pm_b = pp.get()

                def f_tr(e, hh=hh, pm=pm):
                    for c in range(4):
                        ins = e.transpose(out=pm[:, c * P:(c + 1) * P], in_=xs[sl][:, (hh * 4 + c) * P:(hh * 4 + c + 1) * P],
                                          identity=identf[:, :])
                    return ins
                S.op("pe", f_tr, reads=[xs_b[sl], cst_b], writes=[pm_b])
                src = pm[:, :].rearrange("p (c t) -> p c t", c=4)
                if hh == 0:
                    S.op("act", lambda e, src=src: e.copy(out=xT[p2][:, 0:4, :], in_=src), reads=[pm_b], writes=[xT_b[p2]])
                else:
                    S.op("dve", lambda e, src=src: e.tensor_copy(out=xT[p2][:, 4:8, :], in_=src), reads=[pm_b], writes=[xT_b[p2]])
                yield
            pm, pm_b = pp.get()

            def f_lr(e, pm=pm):
                for d_ in range(2):
                    for c in range(8):
                        ins = e.matmul(pm[0:16, d_ * P:(d_ + 1) * P], lhsT=w_inb[:, c, 2048 + 16 * d_:2064 + 16 * d_],
                                       rhs=xT[p2][:, c, :], start=(c == 0), stop=(c == 7))
                return ins
            S.op("pe", f_lr, reads=[xT_b[p2], w_in_b], writes=[pm_b])
            S.op("dve", lambda e, pm=pm: e.tensor_copy(out=lrT[p2][0:16, :, :], in_=pm[0:16, 0:256].rearrange("p (a t) -> p a t", a=2)),
                 reads=[pm_b], writes=[lrT_b[p2]])
            yield
            pm, pm_b = pp.get()
            mm8(pm[:, :], 0, 512, pm_b)
            S.op("act", lambda e, pm=pm: e.activation(out=q_sb[p2][:, :], in_=pm[:, :], func=AF.Copy, scale=float(G_DK ** -0.5)),
                 reads=[pm_b], writes=[q_b[p2]])
            yield
            pm, pm_b = pp.get()
            mm8(pm[:, :], 512, 1024, pm_b)
            S.op("dve", lambda e, pm=pm: e.tensor_copy(out=k_sb[p2][:, :], in_=pm[:, :]), reads=[pm_b], writes=[k_b[p2]])
            yield
            for n in range(2):
                pm, pm_b = pp.get()
                mm8(pm[:, :], 1024 + n * 512, 1536 + n * 512, pm_b)
                S.op("act", lambda e, pm=pm, n=n: e.copy(out=v_sb[p2][:, n * 512:(n + 1) * 512], in_=pm[:, :]),
                     reads=[pm_b], writes=[v_b[p2]])
                yield
            for d_ in range(2):
                spd, spd_b, tbd, tbd_b = sp[p2][d_], sp_b[p2][d_], tb[p2][d_], tb_b[p2][d_]
                pm, pm_b = pp.get()
                S.op("pe", lambda e, pm=pm, d_=d_: e.matmul(pm[:, :], lhsT=lrT[p2][0:17, d_, :], rhs=w2a[0:17, d_, :],
                                                           start=True, stop=True),
                     reads=[lrT_b[p2], w2a_b], writes=[pm_b])
                S.op("act", lambda e, pm=pm, spd=spd: e.activation(out=spd[:, :], in_=pm[:, :], func=AF.Exp, scale=-1.0),
                     reads=[pm_b], writes=[spd_b])
                S.op("act", lambda e, spd=spd: e.activation(out=spd[:, :], in_=spd[:, :], func=AF.Ln, bias=1.0, scale=1.0),
                     reads=[spd_b], writes=[spd_b])
                yield
                pm, pm_b = pp.get()

                def f_dec(e, pm=pm, spd=spd):
                    for h in range(4):
                        ins = e.matmul(pm[:, 2 * h:2 * h + 1], lhsT=spd[:, h * P:(h + 1) * P], rhs=negcol[:, 0:1],
                                       start=True, stop=True)
                    return ins
                S.op("pe", f_dec, reads=[spd_b, cst_b], writes=[pm_b])
                S.op("act", lambda e, pm=pm, d_=d_: e.activation(
                    out=dec[p2][d_][:, :].unsqueeze(2), in_=pm[:, 0:8].rearrange("p (h two) -> p h two", two=2)[:, :, 0:1],
                    func=AF.Exp), reads=[pm_b], writes=[dec_b[p2][d_]])
                pmb, pmb_b = pp.get()
                S.op("pe", lambda e, pmb=pmb, d_=d_, spd=spd: e.matmul(pmb[:, :], lhsT=tri[:, 2 * d_, :], rhs=spd[:, :], start=True, stop=True),
                     reads=[spd_b, cst_b], writes=[pmb_b])
                pmc, pmc_b = pp.get()
                S.op("pe", lambda e, pmc=pmc, d_=d_, spd=spd: e.matmul(pmc[:, :], lhsT=tri[:, 2 * d_ + 1, :], rhs=spd[:, :], start=True, stop=True),
                     reads=[spd_b, cst_b], writes=[pmc_b])
                yield
                S.op("act", lambda e, pmb=pmb, tbd=tbd: e.activation(out=tbd[0][:, :], in_=pmb[:, :], func=AF.Exp),
                     reads=[pmb_b], writes=[tbd_b[0]])
                S.op("act", lambda e, pmb=pmb, tbd=tbd: e.activation(out=tbd[1][:, :], in_=pmb[:, :], func=AF.Exp, scale=-1.0),
                     reads=[pmb_b], writes=[tbd_b[1]])
                S.op("act", lambda e, pmc=pmc, tbd=tbd: e.activation(out=tbd[2][:, :], in_=pmc[:, :], func=AF.Exp),
                     reads=[pmc_b], writes=[tbd_b[2]])
                yield
                S.op("dve", lambda e, d_=d_, tbd=tbd: e.tensor_tensor(out=qe[p2][d_][:, :], in0=q_sb[p2][:, :], in1=tbd[0][:, :], op=ALU.mult),
                     reads=[q_b[p2], tbd_b[0]], writes=[qe_b[p2][d_]])
                S.op("dve", lambda e, d_=d_, tbd=tbd: e.tensor_tensor(out=ke[p2][d_][:, :], in0=k_sb[p2][:, :], in1=tbd[1][:, :], op=ALU.mult),
                     reads=[k_b[p2], tbd_b[1]], writes=[ke_b[p2][d_]])
                S.op("pool", lambda e, d_=d_, tbd=tbd: e.tensor_tensor(out=kd[p2][d_][:, :], in0=k_sb[p2][:, :], in1=tbd[2][:, :], op=ALU.mult),
                     reads=[k_b[p2], tbd_b[2]], writes=[kd_b[p2][d_]])
                yield
                pm, pm_b = pp.get()
                pmv = pm[:, :].bitcast(BF16).rearrange("p (c t) -> p c t", c=8)

                def f_t(e, pmv=pmv, d_=d_):
                    for j, src in enumerate((qe[p2][d_], ke[p2][d_])):
                        for h in range(4):
                            ins = e.transpose(out=pmv[:, j * 4 + h, :], in_=src[:, h * P:(h + 1) * P], identity=identb[:, :])
                    return ins
                S.op("pe", f_t, reads=[qe_b[p2][d_], ke_b[p2][d_], identb_b], writes=[pm_b])
                S.op("act" if d_ == 0 else "dve",
                     (lambda e, pmv=pmv, d_=d_: e.copy(out=qkT[p2][d_][:, :, :], in_=pmv)) if d_ == 0 else
                     (lambda e, pmv=pmv, d_=d_: e.tensor_copy(out=qkT[p2][d_][:, :, :], in_=pmv)),
                     reads=[pm_b], writes=[qkT_b[p2][d_]])
                yield
                pm, pm_b = pp.get()

                def f_att(e, pm=pm, d_=d_):
                    for h in range(4):
                        ins = e.matmul(pm[:, h * P:(h + 1) * P], lhsT=qkT[p2][d_][:, 4 + h, :], rhs=qkT[p2][d_][:, h, :],
                                       start=True, stop=True)
                    return ins
                S.op("pe", f_att, reads=[qkT_b[p2][d_]], writes=[pm_b])
                S.op("dve", lambda e, pm=pm, d_=d_: e.tensor_tensor(
                    out=attm[p2][d_][:, :, :], in0=pm[:, :].rearrange("p (h t) -> p h t", h=4),
                    in1=bcast(gmask[:, d_:d_ + 1, :], [P, 4, P]), op=ALU.mult),
                     reads=[pm_b, cst_b], writes=[attm_b[p2][d_]])
                yield
            assert state_ver[0] == t
            for hp in range(2):
                pm, pm_b = pp.get()

                def f_o(e, pm=pm, hp=hp):
                    for hh in range(2):
                        h = 2 * hp + hh
                        o_ap = pm[:, hh * G_DV:(hh + 1) * G_DV]
                        vv = v_sb[p2][:, h * G_DV:(h + 1) * G_DV]
                        e.matmul(o_ap, lhsT=attm[p2][0][:, h, :], rhs=vv, start=True, stop=False)
                        e.matmul(o_ap, lhsT=attm[p2][1][:, h, :], rhs=vv, start=False, stop=False)
                        ins = e.matmul(o_ap, lhsT=qkT[p2][0][:, h, :], rhs=Sb[:, h, :], start=False, stop=True)
                    return ins
                S.op("pe", f_o, reads=[attm_b[p2][0], attm_b[p2][1], v_b[p2], qkT_b[p2][0], Sb_b], writes=[pm_b])
                S.op("act", lambda e, pm=pm, hp=hp: e.copy(out=opart[p2][:, hp * 512:(hp + 1) * 512], in_=pm[:, :]),
                     reads=[pm_b], writes=[op_b[p2]])
            for hp in range(2):
                pm, pm_b = pp.get()

                def f_s(e, pm=pm, hp=hp):
                    for hh in range(2):
                        h = 2 * hp + hh
                        ins = e.matmul(pm[:, hh * G_DV:(hh + 1) * G_DV], lhsT=kd[p2][0][:, h * P:(h + 1) * P],
                                       rhs=v_sb[p2][:, h * G_DV:(h + 1) * G_DV], start=True, stop=True)
                    return ins
                S.op("pe", f_s, reads=[kd_b[p2][0], v_b[p2]], writes=[pm_b])
                for hh in range(2):
                    h = 2 * hp + hh
                    S.op("dve", lambda e, pm=pm, h=h, hh=hh: e.scalar_tensor_tensor(
                        out=S32[:, h, :], in0=S32[:, h, :], scalar=dec[p2][0][:, h:h + 1], in1=pm[:, hh * G_DV:(hh + 1) * G_DV],
                        op0=ALU.mult, op1=ALU.add), reads=[S32_b, dec_b[p2][0], pm_b], writes=[S32_b])
            S.op("act", lambda e: e.copy(out=Sb[:, :, :], in_=S32[:, :, :]), reads=[S32_b], writes=[Sb_b])
            state_ver[0] = t + 1
            yield
            S.dma("sp", scr["opart"][t * P:(t + 1) * P, :], opart[p2][:, :], reads=[op_b[p2]], writes=[scr_buf], stream=f"so{p2}")
            S.dma("sp", scr["qeTb"][t], qkT[p2][1][:, 0:4, :].rearrange("p h t -> p (h t)"), reads=[qkT_b[p2][1]],
                  writes=[scr_buf], stream=f"sq{p2}")
            S.dma("sp", scr["kdb"][t * P:(t + 1) * P, :], kd[p2][1][:, :], reads=[kd_b[p2][1]], writes=[scr_buf], stream=f"sk{p2}")
            S.dma("sp", scr["vsb"][t * P:(t + 1) * P, :], v_sb[p2][:, :], reads=[v_b[p2]], writes=[scr_buf], stream=f"sv{p2}")
            S.dma("sp", scr["decb"][t], dec[p2][1][:, :], reads=[dec_b[p2][1]], writes=[scr_buf], stream=f"sd{p2}")
            if t + NXS < NT:
                load_x(t + NXS)

        for t in range(min(NXS, NT)):
            load_x(t)
        run_interleaved((tile_gen(t) for t in range(NT)), depth=3, skew=7)
        S.barrier()


def gla_sweep2(S, nc, NT, xin, xout, w_in, norm_g, w_out, g, b, consts, scr, xin_buf, scr_buf, xout_buf, tag):
    ident_d = consts[0]
    with ExitStack() as ctx:
        sbt = lambda name, shape, dt: ctx.enter_context(nc.sbuf_tensor(f"{tag}_{name}", shape, dt))
        B = S.buf
        pp = PsumPool(S, nc, ctx, tag)
        w_rb = sbt("w_rb", [P, 8, 1024], BF16)
        w_outb = sbt("w_outb", [P, 8, D], BF16)
        identf = sbt("identf", [P, P], F32)
        identb = sbt("identb", [P, P], BF16)
        ng_t = sbt("ng", [P, G_DV], F32)
        g_t = sbt("g", [P, D], F32)
        b_t = sbt("b", [P, D], F32)
        S32 = sbt("S32", [P, 4, G_DV], F32)
        Sb = sbt("Sb", [P, 4, G_DV], BF16)
        w_b, cst_b, identb_b, gb_b, S32_b, Sb_b = B("w"), B("cst"), B("idb"), B("gb"), B("S32"), B("Sb")
        S.dma("sp", identf[:, :], ident_d[:, :], writes=[cst_b], stream=f"c0")
        S.dma("sp", ng_t[:, :], norm_g.partition_broadcast(P), writes=[gb_b], stream=f"c1")
        S.dma("sp", g_t[:, :], g.partition_broadcast(P), writes=[gb_b], stream=f"c2")
        S.dma("sp", b_t[:, :], b.partition_broadcast(P), writes=[gb_b], stream=f"c3")
        w_inv = w_in.rearrange("(c p) f -> p c f", p=P)
        w_outv = w_out.rearrange("(c p) f -> p c f", p=P)
        cl = CastLoader(S, nc, ctx, tag)
        for c in range(8):
            cl.load(w_rb[:, c, :], w_inv[:, c, 2048:3072], w_b)
        for c in range(8):
            cl.load(w_outb[:, c, :], w_outv[:, c, :], w_b)
        S.op("dve", lambda e: e.tensor_copy(out=identb[:, :], in_=identf[:, :]), reads=[cst_b], writes=[identb_b])
        S.op("pool", lambda e: e.memset(S32[:, :, :], 0.0), writes=[S32_b])
        S.op("pool", lambda e: e.memset(Sb[:, :, :], 0.0), writes=[Sb_b])

        NXS = 4
        NW = 4
        xs = [sbt(f"x{i}", [P, D], F32) for i in range(NXS)]
        opa = [sbt(f"opa{i}", [P, D], F32) for i in range(NXS)]
        qeT = [sbt(f"qeT{i}", [P, 4, P], BF16) for i in range(NXS)]
        kdb = [sbt(f"kdb{i}", [P, 512], BF16) for i in range(NXS)]
        vsb = [sbt(f"vsb{i}", [P, 1024], BF16) for i in range(NXS)]
        decb = [sbt(f"decb{i}", [P, 4], F32) for i in range(NXS)]
        xT = [sbt(f"xT{i}", [P, 8, P], BF16) for i in range(NW)]
        er = [sbt(f"er{i}", [P, 1024], F32) for i in range(NW)]
        sr = [sbt(f"sr{i}", [P, 4, G_DV], F32) for i in range(NW)]
        of = [sbt(f"of{i}", [P, 4, G_DV], F32) for i in range(NW)]
        sq = sbt("sq", [P, G_DV], F32)
        ss = [sbt(f"ss{i}", [P, 4], F32) for i in range(NW)]
        og = [sbt(f"og{i}", [P, 1024], BF16) for i in range(NW)]
        oT = [sbt(f"oT{i}", [P, 8, P], BF16) for i in range(NW)]
        ot = [sbt(f"ot{i}", [P, D], F32) for i in range(NW)]
        stats = [sbt(f"st{i}", [P, 2, 6], F32) for i in range(NW)]
        mv = [sbt(f"mv{i}", [P, 2], F32) for i in range(NW)]
        sc = [sbt(f"sc{i}", [P, 2], F32) for i in range(NW)]
        xs_b, opa_b, qeT_b, kdb_b, vsb_b, decb_b = (S.bufs(n_, NXS) for n_ in ("xs", "opa", "qeT", "kdb", "vsb", "decb"))
        xT_b, er_b, sr_b, of_b, ss_b, og_b, oT_b, ot_b, st_b, mv_b, sc_b = (
            S.bufs(n_, NW) for n_ in ("xT", "er", "sr", "of", "ss", "og", "oT", "ot", "st", "mv", "sc"))

        order = list(range(NT - 1, -1, -1))

        def load(idx):
            t = order[idx]
            sl = idx % NXS
            S.dma("sp", qeT[sl][:, :, :].rearrange("p h t -> p (h t)"), scr["qeTb"][t], reads=[scr_buf], writes=[qeT_b[sl]],
                  stream=f"lq{sl}")
            S.dma("sp", kdb[sl][:, :], scr["kdb"][t * P:(t + 1) * P, :], reads=[scr_buf], writes=[kdb_b[sl]], stream=f"lk{sl}")
            S.dma("sp", vsb[sl][:, :], scr["vsb"][t * P:(t + 1) * P, :], reads=[scr_buf], writes=[vsb_b[sl]], stream=f"lv{sl}")
            S.dma("sp", decb[sl][:, :], scr["decb"][t], reads=[scr_buf], writes=[decb_b[sl]], stream=f"ld{sl}")
            S.dma("sp", opa[sl][:, :], scr["opart"][t * P:(t + 1) * P, :], reads=[scr_buf], writes=[opa_b[sl]], stream=f"lo{sl}")
            S.dma("sp", xs[sl][:, :], xin[t * P:(t + 1) * P, :], reads=[xin_buf], writes=[xs_b[sl]], stream=f"x{sl}")

        state_ver = [0]

        def tile_gen(idx):
            t = order[idx]
            sl, p2 = idx % NXS, idx % NW
            assert state_ver[0] == idx
            pmo = []
            for hp in range(2):
                pm, pm_b = pp.get()
                pmo.append((pm, pm_b))

                def f_o(e, pm=pm, hp=hp):
                    for hh in range(2):
                        h = 2 * hp + hh
                        ins = e.matmul(pm[:, hh * G_DV:(hh + 1) * G_DV], lhsT=qeT[sl][:, h, :], rhs=Sb[:, h, :], start=True, stop=True)
                    return ins
                S.op("pe", f_o, reads=[qeT_b[sl], Sb_b], writes=[pm_b])
            pms = []
            for hp in range(2):
                pm, pm_b = pp.get()
                pms.append((pm, pm_b))

                def f_s(e, pm=pm, hp=hp):
                    for hh in range(2):
                        h = 2 * hp + hh
                        ins = e.matmul(pm[:, hh * G_DV:(hh + 1) * G_DV], lhsT=kdb[sl][:, h * P:(h + 1) * P],
                                       rhs=vsb[sl][:, h * G_DV:(h + 1) * G_DV], start=True, stop=True)
                    return ins
                S.op("pe", f_s, reads=[kdb_b[sl], vsb_b[sl]], writes=[pm_b])
            for hp in range(2):
                pm, pm_b = pms[hp]
                for hh in range(2):
                    h = 2 * hp + hh
                    S.op("dve", lambda e, pm=pm, h=h, hh=hh: e.scalar_tensor_tensor(
                        out=S32[:, h, :], in0=S32[:, h, :], scalar=decb[sl][:, h:h + 1], in1=pm[:, hh * G_DV:(hh + 1) * G_DV],
                        op0=ALU.mult, op1=ALU.add), reads=[S32_b, decb_b[sl], pm_b], writes=[S32_b])
            S.op("act", lambda e: e.copy(out=Sb[:, :, :], in_=S32[:, :, :]), reads=[S32_b], writes=[Sb_b])
            state_ver[0] = idx + 1
            for hp in range(2):
                pm, pm_b = pmo[hp]
                S.op("dve", lambda e, pm=pm, hp=hp: e.tensor_tensor(
                    out=of[p2][:, 2 * hp:2 * hp + 2, :].rearrange("p h d -> p (h d)"), in0=pm[:, :],
                    in1=opa[sl][:, hp * 512:(hp + 1) * 512], op=ALU.add),
                     reads=[pm_b, opa_b[sl]], writes=[of_b[p2]])
            yield
            def f_sq(e):
                for h in range(4):
                    ins = e.activation(out=sq[:, :], in_=of[p2][:, h, :], func=AF.Square, accum_out=ss[p2][:, h:h + 1])
                return ins
            S.op("act", f_sq, reads=[of_b[p2]], writes=[ss_b[p2]])
            S.op("act", lambda e: e.activation(out=ss[p2][:, :], in_=ss[p2][:, :], func=AF.Ln, bias=1e-6, scale=1.0 / G_DV),
                 reads=[ss_b[p2]], writes=[ss_b[p2]])
            S.op("act", lambda e: e.activation(out=ss[p2][:, :], in_=ss[p2][:, :], func=AF.Exp, scale=-0.5),
                 reads=[ss_b[p2]], writes=[ss_b[p2]])

            def f_hn(e):
                for h in range(4):
                    ins = e.activation(out=of[p2][:, h, :], in_=of[p2][:, h, :], func=AF.Copy, scale=ss[p2][:, h:h + 1])
                return ins
            S.op("act", f_hn, reads=[of_b[p2], ss_b[p2]], writes=[of_b[p2]])
            yield
            for hh in range(2):
                pm, pm_b = pp.get()

                def f_tr(e, hh=hh, pm=pm):
                    for c in range(4):
                        ins = e.transpose(out=pm[:, c * P:(c + 1) * P], in_=xs[sl][:, (hh * 4 + c) * P:(hh * 4 + c + 1) * P],
                                          identity=identf[:, :])
                    return ins
                S.op("pe", f_tr, reads=[xs_b[sl], cst_b], writes=[pm_b])
                src = pm[:, :].rearrange("p (c t) -> p c t", c=4)
                if hh == 0:
                    S.op("act", lambda e, src=src: e.copy(out=xT[p2][:, 0:4, :], in_=src), reads=[pm_b], writes=[xT_b[p2]])
                else:
                    S.op("dve", lambda e, src=src: e.tensor_copy(out=xT[p2][:, 4:8, :], in_=src), reads=[pm_b], writes=[xT_b[p2]])
                yield
            for n in range(2):
                pm, pm_b = pp.get()

                def f_r(e, pm=pm, n=n):
                    for c in range(8):
                        ins = e.matmul(pm[:, :], lhsT=xT[p2][:, c, :], rhs=w_rb[:, c, n * 512:(n + 1) * 512], start=(c == 0), stop=(c == 7))
                    return ins
                S.op("pe", f_r, reads=[xT_b[p2], w_b], writes=[pm_b])
                ern = er[p2][:, n * 512:(n + 1) * 512]
                srn = sr[p2][:, :, :].rearrange("p h d -> p (h d)")[:, n * 512:(n + 1) * 512]
                S.op("act", lambda e, pm=pm, ern=ern: e.activation(out=ern, in_=pm[:, :], func=AF.Exp, scale=-1.0),
                     reads=[pm_b], writes=[er_b[p2]])
                S.op("act", lambda e, ern=ern: e.activation(out=ern, in_=ern, func=AF.Ln, bias=1.0, scale=1.0),
                     reads=[er_b[p2]], writes=[er_b[p2]])
                S.op("act", lambda e, ern=ern: e.activation(out=ern, in_=ern, func=AF.Exp, scale=-1.0),
                     reads=[er_b[p2]], writes=[er_b[p2]])
                yield
                S.op("dve", lambda e, pm=pm, ern=ern, srn=srn: e.tensor_tensor(out=srn, in0=pm[:, :], in1=ern, op=ALU.mult),
                     reads=[pm_b, er_b[p2]], writes=[sr_b[p2]])
                S.op("dve", lambda e, srn=srn, n=n: e.tensor_tensor(
                    out=srn.rearrange("p (h d) -> p h d", d=G_DV), in0=srn.rearrange("p (h d) -> p h d", d=G_DV),
                    in1=bcast(ng_t[:, :].unsqueeze(1), [P, 2, G_DV]), op=ALU.mult),
                     reads=[sr_b[p2], gb_b], writes=[sr_b[p2]])
                yield
            S.op("dve", lambda e: e.tensor_tensor(out=og[p2][:, :], in0=of[p2][:, :, :].rearrange("p h d -> p (h d)"),
                                                  in1=sr[p2][:, :, :].rearrange("p h d -> p (h d)"), op=ALU.mult),
                 reads=[of_b[p2], sr_b[p2]], writes=[og_b[p2]])
            yield
            pm, pm_b = pp.get()
            pmv = pm[:, :].bitcast(BF16).rearrange("p (c t) -> p c t", c=8)

            def f_ot(e, pmv=pmv):
                for s_ in range(8):
                    ins = e.transpose(out=pmv[:, s_, :], in_=og[p2][:, s_ * P:(s_ + 1) * P], identity=identb[:, :])
                return ins
            S.op("pe", f_ot, reads=[og_b[p2], identb_b], writes=[pm_b])
            S.op("act", lambda e, pmv=pmv: e.copy(out=oT[p2][:, :, :], in_=pmv), reads=[pm_b], writes=[oT_b[p2]])
            yield
            for n in range(2):
                pm, pm_b = pp.get()

                def f_op(e, pm=pm, n=n):
                    for c in range(8):
                        ins = e.matmul(pm[:, :], lhsT=oT[p2][:, c, :], rhs=w_outb[:, c, n * 512:(n + 1) * 512], start=(c == 0), stop=(c == 7))
                    return ins
                S.op("pe", f_op, reads=[oT_b[p2], w_b], writes=[pm_b])
                S.op("dve", lambda e, pm=pm, n=n: e.scalar_tensor_tensor(
                    out=xs[sl][:, n * 512:(n + 1) * 512], in0=xs[sl][:, n * 512:(n + 1) * 512], scalar=DN_ALPHA,
                    in1=pm[:, :], op0=ALU.mult, op1=ALU.add), reads=[xs_b[sl], pm_b], writes=[xs_b[sl]])
                yield
            yield from ln_epilogue_gen(S, (stats[p2], st_b[p2], mv[p2], mv_b[p2], sc[p2], sc_b[p2]),
                                       xs[sl], xs_b[sl], g_t, b_t, gb_b, ot[p2], ot_b[p2])
            S.dma("sp", xout[t * P:(t + 1) * P, :], ot[p2][:, :], reads=[ot_b[p2]], writes=[xout_buf], stream=f"o{p2}")
            if idx + NXS < NT:
                load(idx + NXS)

        for idx in range(min(NXS, NT)):
            load(idx)
        run_interleaved((tile_gen(i) for i in range(NT)), depth=4, skew=5)
        S.barrier()


def gla_consts_host():
    u = np.arange(P)[:, None]
    t = np.arange(P)[None, :]
    c = np.float32(-1.0 / G_TAU)
    tri = np.stack([np.where(u <= t, c, 0), np.where(u > t, c, 0),
                    np.where(u >= t, c, 0), np.where(u < t, c, 0)]).astype(np.float32)
    gmask = np.stack([np.where(t >= u, 1.0, 0.0), np.where(t <= u, 1.0, 0.0)]).astype(np.float32)
    return {"c_tri": tri, "c_negcol": np.full((P, 1), c, np.float32), "c_gmask": gmask}


def gla_scratch(nc, NT, tag):
    mk = lambda name, shape, dtp: nc.dram_tensor(f"{tag}_{name}", shape, dtp, kind="Internal").ap()
    return {"opart": mk("opart", [NT * P, 1024], F32), "qeTb": mk("qeTb", [NT, P, 512], BF16),
            "kdb": mk("kdb", [NT * P, 512], BF16), "vsb": mk("vsb", [NT * P, 1024], BF16),
            "decb": mk("decb", [NT, P, 4], F32)}


def build_test_gla(NT):
    nc = bass.Bass("TRN2", target_bir_lowering=False)
    dt = lambda name, shape, dtp=F32, kind="ExternalInput": nc.dram_tensor(name, shape, dtp, kind=kind).ap()
    x = dt("x", [NT * P, D])
    w_in = dt("w_in", [D, G_IN])
    gw2f, gbf, gw2b, gbb = dt("gw2f", [16, 512]), dt("gbf", [512]), dt("gw2b", [16, 512]), dt("gbb", [512])
    norm_g = dt("norm_g", [G_DV])
    w_out = dt("w_out", [D, D])
    g, b = dt("g", [D]), dt("b", [D])
    consts = (dt("c_ident", [P, P]), dt("c_tri", [4, P, P]), dt("c_negcol", [P, 1]), dt("c_gmask", [2, P, P]))
    y = dt("y", [NT * P, D], kind="ExternalOutput")
    scr = gla_scratch(nc, NT, "g1")
    with ExitStack() as ctx:
        S = Sched(nc, ctx)
        xin_buf, scr_buf, xout_buf = S.buf("xin"), S.buf("scr"), S.buf("xout")
        gla_sweep1(S, nc, NT, x, w_in, gw2f, gbf, gw2b, gbb, consts, scr, xin_buf, scr_buf, "g1a")
        gla_sweep2(S, nc, NT, x, y, w_in, norm_g, w_out, g, b, consts, scr, xin_buf, scr_buf, xout_buf, "g1b")
        S.finish()
    return nc


def build_full(NT):
    nc = bass.Bass("TRN2", target_bir_lowering=False)
    dt = lambda name, shape, dtp=F32, kind="ExternalInput": nc.dram_tensor(name, shape, dtp, kind=kind).ap()
    N = NT * P
    x = dt("x", [N, D])
    pos = dt("pos", [N], I32)
    a_w_in, a_sink, a_w_out = dt("a_w_in", [D, A_IN]), dt("a_sink", [16]), dt("a_w_out", [D, D])
    g_w_in = dt("g_w_in", [D, G_IN])
    gw2f, gbf, gw2b, gbb = dt("gw2f", [16, 512]), dt("gbf", [512]), dt("gw2b", [16, 512]), dt("gbb", [512])
    norm_g = dt("norm_g", [G_DV])
    g_w_out = dt("g_w_out", [D, D])
    mix_g, mix_b = [dt(f"mix_g{i}", [D]) for i in range(2)], [dt(f"mix_b{i}", [D]) for i in range(2)]
    w1 = [dt(f"w1_{i}", [D, DFF]) for i in range(2)]
    w2 = [dt(f"w2_{i}", [DFF, D]) for i in range(2)]
    mlp_g, mlp_b = [dt(f"mlp_g{i}", [D]) for i in range(2)], [dt(f"mlp_b{i}", [D]) for i in range(2)]
    c_ident = dt("c_ident", [P, P])
    a_consts = (c_ident, dt("c_maskp", [P, 512]), dt("c_maskn", [P, 512]), dt("c_invf", [32]))
    g_consts = (c_ident, dt("c_tri", [4, P, P]), dt("c_negcol", [P, 1]), dt("c_gmask", [2, P, P]))
    out = dt("out", [N, D], kind="ExternalOutput")
    s = [nc.dram_tensor(f"act{i}", [N, D], F32, kind="Internal").ap() for i in range(3)]
    scr = gla_scratch(nc, NT, "g1")
    with ExitStack() as ctx:
        S = Sched(nc, ctx)
        bx, bout, bscr = S.buf("x"), S.buf("out"), S.buf("scr")
        bs = S.bufs("act", 3)
        attn_phase(S, nc, NT, x, s[0], pos, a_w_in, a_sink, a_w_out, mix_g[0], mix_b[0], a_consts, bx, bs[0], "a0")
        mlp_phase(S, nc, NT, s[0], s[1], w1[0], w2[0], mlp_g[0], mlp_b[0], c_ident, bs[0], bs[1], "m0")
        gla_sweep1(S, nc, NT, s[1], g_w_in, gw2f, gbf, gw2b, gbb, g_consts, scr, bs[1], bscr, "ga")
        gla_sweep2(S, nc, NT, s[1], s[2], g_w_in, norm_g, g_w_out, mix_g[1], mix_b[1], g_consts, scr, bs[1], bscr, bs[2], "gb")
        mlp_phase(S, nc, NT, s[2], out, w1[1], w2[1], mlp_g[1], mlp_b[1], c_ident, bs[2], bout, "m1")
        S.finish()
    return nc


def host_inputs(inp, core, NT=SEQ // P):
    f = lambda a: np.ascontiguousarray(np.asarray(a, dtype=np.float32))
    N = NT * P
    m = {
        "x": f(inp["x"][core, :N]),
        "pos": np.ascontiguousarray(np.asarray(inp["positions"][core, :N], dtype=np.int32)),
        "a_w_in": permute_attn_w_in(f(inp["attn_w_in"][0])),
        "a_sink": f(inp["attn_sink"][0]),
        "a_w_out": f(inp["attn_w_out"][0]),
        "g_w_in": f(inp["gla_w_in"][0]),
        "gw2f": f(inp["gla_gate_w2_fwd"][0]), "gbf": f(inp["gla_gate_b_fwd"][0]),
        "gw2b": f(inp["gla_gate_w2_bwd"][0]), "gbb": f(inp["gla_gate_b_bwd"][0]),
        "norm_g": f(inp["gla_norm_g"][0]),
        "g_w_out": f(inp["gla_w_out"][0]),
    }
    for i in range(2):
        m[f"mix_g{i}"] = f(inp["mix_ln_g"][i])
        m[f"mix_b{i}"] = f(inp["mix_ln_b"][i])
        m[f"w1_{i}"] = f(inp["mlp_w1"][i])
        m[f"w2_{i}"] = f(inp["mlp_w2"][i])
        m[f"mlp_g{i}"] = f(inp["mlp_ln_g"][i])
        m[f"mlp_b{i}"] = f(inp["mlp_ln_b"][i])
    m.update(attn_consts_host())
    m.update(gla_consts_host())
    return m


def kernel(**inputs):
    NT = SEQ // P
    nc = build_full(NT)
    shared = host_inputs(inputs, 0)
    in_maps = []
    for c in range(NCORES):
        m = dict(shared)
        m["x"] = np.ascontiguousarray(np.asarray(inputs["x"][c], dtype=np.float32))
        m["pos"] = np.ascontiguousarray(np.asarray(inputs["positions"][c], dtype=np.int32))
        in_maps.append(m)
    res = run_bass_kernel_spmd(nc, in_maps, core_ids=list(range(NCORES)))
    return np.stack([np.asarray(r["out"], dtype=np.float32) for r in res.results], axis=0)
```

```python
import os
import numpy as np
from contextlib import ExitStack
import ml_dtypes
import concourse.bass as bass
import concourse.mybir as mybir
from concourse.bass_utils import run_bass_kernel_spmd

F32 = mybir.dt.float32
BF16 = mybir.dt.bfloat16
I32 = mybir.dt.int32
AF = mybir.ActivationFunctionType
ALU = mybir.AluOpType

P = 128
D = 1024
DFF = 4096
SEQ = 8192
NCORES = 8
DEPTH = 2
DN_ALPHA = float((2 * DEPTH) ** 0.25)
LN_EPS = 1e-5

SEM_EPOCH = 30000


class Buf:
    __slots__ = ("name", "lw", "rd", "rd_dma", "excl")

    def __init__(self, name, excl=False):
        self.name = name
        self.excl = excl
        self.lw = None
        self.rd = {}
        self.rd_dma = []


class Sched:
    def __init__(self, nc, ctx):
        self.nc = nc
        self.ctx = ctx
        self.E = {"pe": nc.tensor, "act": nc.scalar, "dve": nc.vector,
                  "pool": nc.gpsimd, "sp": nc.sync}
        self.cnt = {e: 0 for e in self.E}
        self.sems = {e: [] for e in self.E}
        self.waited = {e: {} for e in self.E}
        self.streams = {}
        self.nwaits = 0

    def buf(self, name):
        return Buf(name)

    def bufs(self, name, n):
        return [Buf(f"{name}{i}") for i in range(n)]

    def pbufs(self, name, n):
        return [Buf(f"{name}{i}", excl=True) for i in range(n)]

    def _eng_sem(self, eng, k):
        ep = (k - 1) // SEM_EPOCH
        while len(self.sems[eng]) <= ep:
            self.sems[eng].append(
                self.ctx.enter_context(self.nc.semaphore(f"s_{eng}{len(self.sems[eng])}")))
        return self.sems[eng][ep], (k - 1) % SEM_EPOCH + 1

    def _stream(self, name):
        st = self.streams.get(name)
        if st is None:
            st = {"n": 0, "sem": self.ctx.enter_context(self.nc.semaphore(f"d_{name}"))}
            self.streams[name] = st
        return st

    def _wait(self, eng, tok):
        w = self.waited[eng]
        if tok[0] == "c":
            _, peng, k = tok
            if peng == "pe" and eng == "pe":
                return
            if w.get(peng, 0) >= k:
                return
            w[peng] = k
            sem, val = self._eng_sem(peng, k)
        else:
            _, sname, n = tok
            key = "d:" + sname
            if w.get(key, 0) >= n:
                return
            w[key] = n
            sem, val = self.streams[sname]["sem"], 16 * n
        self.E[eng].wait_ge(sem, val)
        self.nwaits += 1

    def _deps(self, reads, writes, eng=None):
        deps = []
        for b in reads:
            if b.lw is not None:
                deps.append(b.lw)
            if b.excl:
                deps.extend(tok for e2, tok in b.rd.items() if e2 != eng)
        for b in writes:
            if b.lw is not None:
                deps.append(b.lw)
            deps.extend(b.rd.values())
            deps.extend(b.rd_dma)
        return deps

    def op(self, eng, fn, reads=(), writes=()):
        for tok in self._deps(reads, writes, eng):
            self._wait(eng, tok)
        ins = fn(self.E[eng])
        self.cnt[eng] += 1
        k = self.cnt[eng]
        sem, _ = self._eng_sem(eng, k)
        ins.then_inc(sem, 1)
        tok = ("c", eng, k)
        for b in writes:
            b.lw = tok
            b.rd = {}
            b.rd_dma = []
        for b in reads:
            if b.lw is not tok:
                b.rd[eng] = tok
        return tok

    def dma(self, q, out, in_, reads=(), writes=(), stream=None, **kw):
        stream = f"{q}_{stream}"
        st = self._stream(stream)
        deps = self._deps(reads, writes)
        if st["n"] > 0:
            deps.append(("d", stream, st["n"]))
        for tok in deps:
            self._wait(q, tok)
        ins = self.E[q].dma_start(out=out, in_=in_, **kw)
        ins.then_inc(st["sem"], 16)
        st["n"] += 1
        tok = ("d", stream, st["n"])
        for b in writes:
            b.lw = tok
            b.rd = {}
            b.rd_dma = []
        for b in reads:
            b.rd_dma.append(tok)
        return tok

    def barrier(self):
        for eng in self.E:
            for peng in self.E:
                if peng != eng and self.cnt[peng] > 0:
                    self._wait(eng, ("c", peng, self.cnt[peng]))
            for sname, st in self.streams.items():
                if st["n"] > 0:
                    self._wait(eng, ("d", sname, st["n"]))

    def finish(self):
        for sname, st in self.streams.items():
            if st["n"] > 0:
                self._wait("sp", ("d", sname, st["n"]))


def run_interleaved(gens, depth=2, skew=1):
    it = iter(gens)
    pending = next(it, None)
    active = []
    while active or pending is not None:
        if pending is not None and len(active) < depth and (not active or active[-1][1] >= skew):
            active.append([pending, 0])
            pending = next(it, None)
        for a in list(active):
            try:
                next(a[0])
                a[1] += 1
            except StopIteration:
                active.remove(a)


def load_rowvec_bcast(S, q, dst, src_1d, n, stream, wbuf):
    S.dma(q, dst, src_1d.partition_broadcast(P), writes=[wbuf], stream=stream)


def ln_epilogue(S, sb, z, zb, g_t, b_t, gb_buf, out_t, out_b, eps=LN_EPS):
    stats, stats_b, mv, mv_b, sc, sc_b = sb
    S.op("dve", lambda e: (e.bn_stats(out=stats[:, 0, :], in_=z[:, 0:512]),
                           e.bn_stats(out=stats[:, 1, :], in_=z[:, 512:1024]))[-1],
         reads=[zb], writes=[stats_b])
    S.op("dve", lambda e: e.bn_aggr(out=mv[:, :], in_=stats[:, :, :].rearrange("p a b -> p (a b)")),
         reads=[stats_b], writes=[mv_b])
    S.op("act", lambda e: e.activation(out=sc[:, 0:1], in_=mv[:, 1:2], func=AF.Ln, bias=eps, scale=1.0),
         reads=[mv_b], writes=[sc_b])
    S.op("act", lambda e: e.activation(out=sc[:, 0:1], in_=sc[:, 0:1], func=AF.Exp, scale=-0.5),
         reads=[sc_b], writes=[sc_b])
    S.op("dve", lambda e: e.scalar_tensor_tensor(out=sc[:, 1:2], in0=mv[:, 0:1], scalar=-1.0,
                                                 in1=sc[:, 0:1], op0=ALU.mult, op1=ALU.mult),
         reads=[mv_b, sc_b], writes=[sc_b])
    S.op("act", lambda e: e.activation(out=out_t[:, :], in_=z[:, :], func=AF.Identity,
                                       bias=sc[:, 1:2], scale=sc[:, 0:1]),
         reads=[zb, sc_b], writes=[out_b])
    S.op("dve", lambda e: e.tensor_tensor(out=out_t[:, :], in0=out_t[:, :], in1=g_t[:, :], op=ALU.mult),
         reads=[out_b, gb_buf], writes=[out_b])
    S.op("dve", lambda e: e.tensor_tensor(out=out_t[:, :], in0=out_t[:, :], in1=b_t[:, :], op=ALU.add),
         reads=[out_b, gb_buf], writes=[out_b])


class CastLoader:
    def __init__(self, S, nc, ctx, tag, width=1024, n=4):
        self.S = S
        self.stg = [ctx.enter_context(nc.sbuf_tensor(f"{tag}_stg{i}", [P, width], F32)) for i in range(n)]
        self.stg_b = S.bufs("stg", n)
        self.k = 0

    def load(self, dst, src, dbuf):
        S = self.S
        i = self.k % len(self.stg)
        self.k += 1
        w = dst.shape[-1]
        st = self.stg[i]
        S.dma("sp", st[:, 0:w], src, writes=[self.stg_b[i]], stream=f"stg{i}")
        if self.k % 2:
            S.op("act", lambda e: e.copy(out=dst, in_=st[:, 0:w]), reads=[self.stg_b[i]], writes=[dbuf])
        else:
            S.op("dve", lambda e: e.tensor_copy(out=dst, in_=st[:, 0:w]), reads=[self.stg_b[i]], writes=[dbuf])


def ln_epilogue_gen(S, sb, z, zb, g_t, b_t, gb_buf, out_t, out_b, eps=LN_EPS):
    stats, stats_b, mv, mv_b, sc, sc_b = sb
    S.op("dve", lambda e: (e.bn_stats(out=stats[:, 0, :], in_=z[:, 0:512]),
                           e.bn_stats(out=stats[:, 1, :], in_=z[:, 512:1024]))[-1],
         reads=[zb], writes=[stats_b])
    S.op("dve", lambda e: e.bn_aggr(out=mv[:, :], in_=stats[:, :, :].rearrange("p a b -> p (a b)")),
         reads=[stats_b], writes=[mv_b])
    yield
    S.op("act", lambda e: e.activation(out=sc[:, 0:1], in_=mv[:, 1:2], func=AF.Ln, bias=eps, scale=1.0),
         reads=[mv_b], writes=[sc_b])
    S.op("act", lambda e: e.activation(out=sc[:, 0:1], in_=sc[:, 0:1], func=AF.Exp, scale=-0.5),
         reads=[sc_b], writes=[sc_b])
    yield
    S.op("dve", lambda e: e.scalar_tensor_tensor(out=sc[:, 1:2], in0=mv[:, 0:1], scalar=-1.0,
                                                 in1=sc[:, 0:1], op0=ALU.mult, op1=ALU.mult),
         reads=[mv_b, sc_b], writes=[sc_b])
    yield
    S.op("act", lambda e: e.activation(out=out_t[:, :], in_=z[:, :], func=AF.Identity,
                                       bias=sc[:, 1:2], scale=sc[:, 0:1]),
         reads=[zb, sc_b], writes=[out_b])
    yield
    S.op("dve", lambda e: e.tensor_tensor(out=out_t[:, :], in0=out_t[:, :], in1=g_t[:, :], op=ALU.mult),
         reads=[out_b, gb_buf], writes=[out_b])
    S.op("dve", lambda e: e.tensor_tensor(out=out_t[:, :], in0=out_t[:, :], in1=b_t[:, :], op=ALU.add),
         reads=[out_b, gb_buf], writes=[out_b])
    yield


def mlp_phase(S, nc, NT, xin, xout, w1, w2, g, b, ident_d, xin_buf, xout_buf, tag):
    ST = 2
    NS = NT // ST
    TOK = ST * P
    with ExitStack() as ctx:
        sbt = lambda name, shape, dt: ctx.enter_context(nc.sbuf_tensor(f"{tag}_{name}", shape, dt))
        pst = lambda name, shape, dt: ctx.enter_context(nc.psum_tensor(f"{tag}_{name}", shape, dt))
        w1b = sbt("w1b", [P, 8, DFF], BF16)
        w2b = sbt("w2b", [P, 32, D], BF16)
        identf = sbt("identf", [P, P], F32)
        g_t = sbt("g", [P, D], F32)
        b_t = sbt("b", [P, D], F32)
        NXS = 4
        xs = [sbt(f"x{i}", [P, D], F32) for i in range(NXS)]
        xT = [sbt(f"xT{i}", [P, 8, TOK], BF16) for i in range(2)]
        hT = sbt("hT", [P, 32, TOK], BF16)
        rt = [sbt(f"rt{i}", [P, 512], F32) for i in range(2)]
        ot = [sbt(f"ot{i}", [P, D], F32) for i in range(2)]
        stats = [sbt(f"st{i}", [P, 2, 6], F32) for i in range(2)]
        mv = [sbt(f"mv{i}", [P, 2], F32) for i in range(2)]
        sc = [sbt(f"sc{i}", [P, 2], F32) for i in range(2)]
        ptr = [pst(f"ptr{i}", [P, 4, P], F32) for i in range(2)]
        pm1 = [pst(f"pm1{i}", [P, 512], F32) for i in range(2)]
        pm2 = [pst(f"pm2{i}", [P, 2, 512], F32) for i in range(2)]

        B = S.buf
        w1_b = [B(f"w1b{c}") for c in range(8)]
        w2_b = [B(f"w2b{c}") for c in range(8)]
        ident_b, gb_b = B("ident"), B("gb")
        xs_b = S.bufs("xs", NXS)
        xT_b = S.bufs("xT", 2)
        hT_b = [B(f"hT{j}") for j in range(16)]
        rt_b = S.bufs("rt", 2)
        ot_b = S.bufs("ot", 2)
        st_b, mv_b, sc_b = S.bufs("st", 2), S.bufs("mv", 2), S.bufs("sc", 2)
        ptr_b, pm1_b, pm2_b = S.pbufs("ptr", 2), S.pbufs("pm1", 2), S.pbufs("pm2", 2)

        S.dma("sp", identf[:, :], ident_d[:, :], writes=[ident_b], stream=f"c0")
        S.dma("sp", g_t[:, :], g.partition_broadcast(P), writes=[gb_b], stream=f"c1")
        S.dma("sp", b_t[:, :], b.partition_broadcast(P), writes=[gb_b], stream=f"c2")
        cl = CastLoader(S, nc, ctx, tag)
        w1v = w1.rearrange("(c p) f -> p c f", p=P)
        for c in range(8):
            for q4 in range(4):
                cl.load(w1b[:, c, q4 * 1024:(q4 + 1) * 1024], w1v[:, c, q4 * 1024:(q4 + 1) * 1024], w1_b[c])
        w2v = w2.rearrange("(c p) f -> p c f", p=P)
        for c in range(32):
            cl.load(w2b[:, c, :], w2v[:, c, :], w2_b[c // 4])

        def load_x(t):
            sl = t % NXS
            S.dma("sp", xs[sl][:, :], xin[t * P:(t + 1) * P, :], reads=[xin_buf], writes=[xs_b[sl]],
                  stream=f"x{sl}")

        for t in range(min(NXS, NT)):
            load_x(t)

        def transposes(s):
            xTs, xTs_b = xT[s % 2], xT_b[s % 2]
            for m in range(ST):
                t = s * ST + m
                sl = t % NXS
                for hh in range(2):
                    pt, pt_b = ptr[hh], ptr_b[hh]

                    def f_tr(e, sl=sl, hh=hh, pt=pt):
                        for c in range(4):
                            ins = e.transpose(out=pt[:, c, :], in_=xs[sl][:, (hh * 4 + c) * P:(hh * 4 + c + 1) * P],
                                              identity=identf[:, :])
                        return ins
                    S.op("pe", f_tr, reads=[xs_b[sl], ident_b], writes=[pt_b])
                    S.op("act" if hh == 0 else "dve",
                         (lambda e, pt=pt, hh=hh, m=m, xTs=xTs: e.copy(out=xTs[:, hh * 4:(hh + 1) * 4, m * P:(m + 1) * P], in_=pt[:, :, :]))
                         if hh == 0 else
                         (lambda e, pt=pt, hh=hh, m=m, xTs=xTs: e.tensor_copy(out=xTs[:, hh * 4:(hh + 1) * 4, m * P:(m + 1) * P], in_=pt[:, :, :])),
                         reads=[pt_b], writes=[xTs_b])
        transposes(0)
        for s in range(NS):
            xTs, xTs_b = xT[s % 2], xT_b[s % 2]
            for jj in range(16):
                pm, pm_b = pm1[jj % 2], pm1_b[jj % 2]

                def f_mm1(e, jj=jj, pm=pm, xTs=xTs):
                    for u in range(2):
                        j = jj * 2 + u
                        for c in range(8):
                            ins = e.matmul(pm[:, u * TOK:(u + 1) * TOK], lhsT=w1b[:, c, j * P:(j + 1) * P],
                                           rhs=xTs[:, c, :], start=(c == 0), stop=(c == 7))
                    return ins
                S.op("pe", f_mm1, reads=[xTs_b] + w1_b, writes=[pm_b])
                r, r_b = rt[jj % 2], rt_b[jj % 2]
                S.op("act", lambda e, pm=pm, r=r: e.activation(out=r[:, :], in_=pm[:, :], func=AF.Relu),
                     reads=[pm_b], writes=[r_b])
                S.op("dve", lambda e, r=r, jj=jj: e.tensor_tensor(
                    out=hT[:, 2 * jj:2 * jj + 2, :].rearrange("p a t -> p (a t)"), in0=r[:, :], in1=r[:, :], op=ALU.mult),
                     reads=[r_b], writes=[hT_b[jj]])
            if s + 1 < NS:
                transposes(s + 1)
            for m in range(ST):
                t = s * ST + m
                sl = t % NXS
                k2 = t % 2
                py, py_b = pm2[k2], pm2_b[k2]

                def f_mm2(e, m=m, py=py):
                    for n in range(2):
                        for j in range(32):
                            ins = e.matmul(py[:, n, :], lhsT=hT[:, j, m * P:(m + 1) * P],
                                           rhs=w2b[:, j, n * 512:(n + 1) * 512], start=(j == 0), stop=(j == 31))
                    return ins
                S.op("pe", f_mm2, reads=hT_b + w2_b, writes=[py_b])
                S.op("dve", lambda e, sl=sl, py=py: e.scalar_tensor_tensor(
                    out=xs[sl][:, :], in0=xs[sl][:, :], scalar=DN_ALPHA,
                    in1=py[:, :, :].rearrange("p a b -> p (a b)"), op0=ALU.mult, op1=ALU.add),
                     reads=[xs_b[sl], py_b], writes=[xs_b[sl]])
                ln_epilogue(S, (stats[k2], st_b[k2], mv[k2], mv_b[k2], sc[k2], sc_b[k2]),
                            xs[sl], xs_b[sl], g_t, b_t, gb_b, ot[k2], ot_b[k2])
                S.dma("sp", xout[t * P:(t + 1) * P, :], ot[k2][:, :], reads=[ot_b[k2]], writes=[xout_buf],
                      stream=f"o{k2}")
                if t + NXS < NT:
                    load_x(t + NXS)
        S.barrier()


def build_test_mlp(NT):
    nc = bass.Bass("TRN2", target_bir_lowering=False)
    x = nc.dram_tensor("x", [NT * P, D], F32, kind="ExternalInput").ap()
    w1 = nc.dram_tensor("w1", [D, DFF], F32, kind="ExternalInput").ap()
    w2 = nc.dram_tensor("w2", [DFF, D], F32, kind="ExternalInput").ap()
    g = nc.dram_tensor("g", [D], F32, kind="ExternalInput").ap()
    b = nc.dram_tensor("b", [D], F32, kind="ExternalInput").ap()
    ident = nc.dram_tensor("ident", [P, P], F32, kind="ExternalInput").ap()
    y = nc.dram_tensor("y", [NT * P, D], F32, kind="ExternalOutput").ap()
    with ExitStack() as ctx:
        S = Sched(nc, ctx)
        mlp_phase(S, nc, NT, x, y, w1, w2, g, b, ident, S.buf("xin"), S.buf("xout"), "m0")
        S.finish()
    return nc


A_HEADS = 16
A_KV = 4
HD = 64
A_IN = 1536
TWO_PI = 2.0 * np.pi
CW1 = 6.28125
CW2 = float(TWO_PI - 6.28125)
PI_LO = 3.1415925
Q_HEAD_ORDER = [0, 4, 1, 5, 2, 6, 3, 7, 8, 12, 9, 13, 10, 14, 11, 15]


def bcast(ap, shape):
    return ap.broadcast_to(list(shape))


def attn_phase(S, nc, NT, xin, xout, pos, w_in, sink, w_out, g, b, consts, xin_buf, xout_buf, tag):
    ident_d, maskp_d, maskn_d, invf_d = consts
    scale = HD ** -0.5
    with ExitStack() as ctx:
        sbt = lambda name, shape, dt: ctx.enter_context(nc.sbuf_tensor(f"{tag}_{name}", shape, dt))
        pst = lambda name, shape, dt: ctx.enter_context(nc.psum_tensor(f"{tag}_{name}", shape, dt))
        B = S.buf
        w_inb = sbt("w_inb", [P, 8, A_IN], BF16)
        w_outb = sbt("w_outb", [P, 8, D], BF16)
        identf = sbt("identf", [P, P], F32)
        identb = sbt("identb", [P, P], BF16)
        maskp = sbt("maskp", [P, 512], BF16)
        maskn = sbt("maskn", [P, 512], BF16)
        g_t = sbt("g", [P, D], F32)
        b_t = sbt("b", [P, D], F32)
        esink = sbt("esink", [P, 16], F32)
        cos_t = sbt("cos", [P, NT, 32], F32)
        sin_t = sbt("sin", [P, NT, 32], F32)
        w_in_b, w_out_b = B("w_in"), B("w_out")
        ident_b, identb_b, mask_b, gb_b, esink_b, cs_b = B("id"), B("idb"), B("mask"), B("gb"), B("esink"), B("cs")

        pmm = [pst(f"pmm{i}", [P, 512], F32) for i in range(2)]
        ptb = pst("ptb", [P, 8, P], BF16)
        psc = [pst(f"psc{i}", [P, 512], F32) for i in range(2)]
        pov = [pst(f"pov{i}", [P, 512], F32) for i in range(3)]
        pmm_b, psc_b, pov_b, ptb_b = S.pbufs("pmm", 2), S.pbufs("psc", 2), S.pbufs("pov", 3), S.pbufs("ptb", 1)[0]

        S.dma("sp", identf[:, :], ident_d[:, :], writes=[ident_b], stream=f"c0")
        S.dma("sp", g_t[:, :], g.partition_broadcast(P), writes=[gb_b], stream=f"c1")
        S.dma("sp", b_t[:, :], b.partition_broadcast(P), writes=[gb_b], stream=f"c2")
        S.dma("sp", esink[:, :], sink.partition_broadcast(P), writes=[esink_b], stream=f"c3")
        S.dma("pool", maskp[:, :], maskp_d[:, :], writes=[mask_b], stream=f"c4")
        S.dma("pool", maskn[:, :], maskn_d[:, :], writes=[mask_b], stream=f"c5")
        cl = CastLoader(S, nc, ctx, tag)
        w_inv = w_in.rearrange("(c p) f -> p c f", p=P)
        for c in range(8):
            for hh in range(2):
                cl.load(w_inb[:, c, hh * 768:(hh + 1) * 768], w_inv[:, c, hh * 768:(hh + 1) * 768], w_in_b)
        w_outv = w_out.rearrange("(c p) f -> p c f", p=P)
        for c in range(8):
            cl.load(w_outb[:, c, :], w_outv[:, c, :], w_out_b)
        S.op("dve", lambda e: e.tensor_copy(out=identb[:, :], in_=identf[:, :]), reads=[ident_b], writes=[identb_b])
        S.op("act", lambda e: e.activation(out=esink[:, :], in_=esink[:, :], func=AF.Exp),
             reads=[esink_b], writes=[esink_b])

        with ExitStack() as c2:
            sb2 = lambda name, shape, dt: c2.enter_context(nc.sbuf_tensor(f"{tag}_{name}", shape, dt))
            posi = sb2("posi", [NT, P], I32)
            posf = sb2("posf", [NT, P], F32)
            posT = sb2("posT", [P, NT], F32)
            invf = sb2("invf", [P, 32], F32)
            ang = sb2("ang", [P, NT, 32], F32)
            u = sb2("u", [P, NT, 32], F32)
            ki = sb2("ki", [P, NT, 32], I32)
            kf = sb2("kf", [P, NT, 32], F32)
            posi_b, posf_b, posT_b, invf_b, ang_b, u_b, ki_b, kf_b = S.bufs("rp", 8)
            S.dma("sp", posi[:, :], pos.rearrange("(t p) -> t p", p=P), writes=[posi_b], stream=f"c6")
            S.dma("sp", invf[:, :], invf_d.partition_broadcast(P), writes=[invf_b], stream=f"c7")
            S.op("dve", lambda e: e.tensor_copy(out=posf[:, :], in_=posi[:, :]), reads=[posi_b], writes=[posf_b])
            S.op("pe", lambda e: e.transpose(out=pmm[0][:, 0:NT], in_=posf[:, :], identity=identf[0:NT, 0:NT]),
                 reads=[posf_b, ident_b], writes=[pmm_b[0]])
            S.op("dve", lambda e: e.tensor_copy(out=posT[:, :], in_=pmm[0][:, 0:NT]), reads=[pmm_b[0]], writes=[posT_b])
            S.op("dve", lambda e: e.tensor_tensor(out=ang[:, :, :], in0=bcast(posT[:, :].unsqueeze(2), [P, NT, 32]),
                                                  in1=bcast(invf[:, :].unsqueeze(1), [P, NT, 32]), op=ALU.mult),
                 reads=[posT_b, invf_b], writes=[ang_b])
            for which, off, dst in (("sin", 0.0, sin_t), ("cos", 0.25, cos_t)):
                S.op("dve", lambda e, off=off: e.tensor_scalar(out=u[:, :, :], in0=ang[:, :, :], scalar1=float(1.0 / TWO_PI),
                                                               scalar2=off, op0=ALU.mult, op1=ALU.add),
                     reads=[ang_b], writes=[u_b])
                S.op("dve", lambda e: e.tensor_copy(out=ki[:, :, :], in_=u[:, :, :]), reads=[u_b], writes=[ki_b])
                S.op("dve", lambda e: e.tensor_copy(out=kf[:, :, :], in_=ki[:, :, :]), reads=[ki_b], writes=[kf_b])
                S.op("dve", lambda e: e.scalar_tensor_tensor(out=u[:, :, :], in0=kf[:, :, :], scalar=-CW1, in1=ang[:, :, :],
                                                             op0=ALU.mult, op1=ALU.add),
                     reads=[kf_b, ang_b], writes=[u_b])
                S.op("dve", lambda e: e.scalar_tensor_tensor(out=u[:, :, :], in0=kf[:, :, :], scalar=-CW2, in1=u[:, :, :],
                                                             op0=ALU.mult, op1=ALU.add),
                     reads=[kf_b, u_b], writes=[u_b])
                S.op("dve", lambda e, off=off: e.tensor_scalar(out=u[:, :, :], in0=u[:, :, :], scalar1=float(off * TWO_PI),
                                                               scalar2=PI_LO, op0=ALU.add, op1=ALU.min),
                     reads=[u_b], writes=[u_b])
                S.op("dve", lambda e: e.tensor_scalar(out=u[:, :, :], in0=u[:, :, :], scalar1=-PI_LO, scalar2=None, op0=ALU.max),
                     reads=[u_b], writes=[u_b])
                S.op("act", lambda e, dst=dst: e.activation(out=dst[:, :, :], in_=u[:, :, :], func=AF.Sin),
                     reads=[u_b], writes=[cs_b])
            S.barrier()

        NXS = 6
        NW = 3
        NQ = 4
        NK = 6
        xs = [sbt(f"x{i}", [P, D], F32) for i in range(NXS)]
        xT = [sbt(f"xT{i}", [P, 8, P], BF16) for i in range(NW)]
        ra = [sbt(f"ra{i}", [P, 8, 2, 32], F32) for i in range(2)]
        rb = [sbt(f"rb{i}", [P, 8, 2, 32], F32) for i in range(2)]
        qr = [sbt(f"qr{i}", [P, 16, 2, 32], BF16) for i in range(NW)]
        kr = [sbt(f"kr{i}", [P, 4, 2, 32], BF16) for i in range(NW)]
        qT = [sbt(f"qT{i}", [P, 8, P], BF16) for i in range(NQ)]
        kTw = [sbt(f"kT{i}", [P, 2, P], BF16) for i in range(NK)]
        Vw = [sbt(f"V{i}", [P, 4, 65], BF16) for i in range(NK)]
        pT = [sbt(f"pT{i}", [P, 512], BF16) for i in range(3)]
        den = [sbt(f"den{i}", [P, 16], F32) for i in range(NW)]
        on = [sbt(f"on{i}", [P, 16, 64], BF16) for i in range(NW)]
        oT = [sbt(f"oT{i}", [P, 8, P], BF16) for i in range(NW)]
        ot = [sbt(f"ot{i}", [P, D], F32) for i in range(NW)]
        stats = [sbt(f"st{i}", [P, 2, 6], F32) for i in range(NW)]
        mv = [sbt(f"mv{i}", [P, 2], F32) for i in range(NW)]
        sc = [sbt(f"sc{i}", [P, 2], F32) for i in range(NW)]
        xs_b, xT_b = S.bufs("xs", NXS), S.bufs("xT", NW)
        ra_b, rb_b, qr_b, kr_b, qT_b = S.bufs("ra", 2), S.bufs("rb", 2), S.bufs("qr", NW), S.bufs("kr", NW), S.bufs("qT", NQ)
        kT_b, V_b, pT_b = S.bufs("kT", NK), S.bufs("V", NK), S.bufs("pT", 3)
        den_b, on_b, oT_b, ot_b = S.bufs("den", NW), S.bufs("on", NW), S.bufs("oT", NW), S.bufs("ot", NW)
        st_b, mv_b, sc_b = S.bufs("st", NW), S.bufs("mv", NW), S.bufs("sc", NW)
        for i in range(NK):
            S.op("pool", lambda e, i=i: e.memset(Vw[i][:, :, :], 1.0), writes=[V_b[i]])

        cnt = {"mm": 0, "sc": 0, "pt": 0, "rr": 0}
        pov_free = [True]

        def nxt(key, n):
            v = cnt[key] % n
            cnt[key] += 1
            return v

        def load_x(t):
            sl = t % NXS
            S.dma("sp", xs[sl][:, :], xin[t * P:(t + 1) * P, :], reads=[xin_buf], writes=[xs_b[sl]],
                  stream=f"x{sl}")

        def rope(pm, pm_bf, nh, t, dst, dst_bf, h0):
            r = nxt("rr", 2)
            pm4 = pm[:, 0:nh * 64].rearrange("p (h two d) -> p h two d", two=2, d=32)
            cb = bcast(cos_t[:, t:t + 1, :].unsqueeze(1), [P, nh, 2, 32])
            sb_ = bcast(sin_t[:, t:t + 1, :].unsqueeze(1), [P, nh, 2, 32])
            S.op("dve", lambda e: e.tensor_tensor(out=ra[r][:, 0:nh, :, :], in0=pm4, in1=cb, op=ALU.mult),
                 reads=[pm_bf, cs_b], writes=[ra_b[r]])
            S.op("dve", lambda e: e.tensor_tensor(out=rb[r][:, 0:nh, :, :], in0=pm4, in1=sb_, op=ALU.mult),
                 reads=[pm_bf, cs_b], writes=[rb_b[r]])
            S.op("pool", lambda e: e.tensor_tensor(out=dst[:, h0:h0 + nh, 0, :], in0=ra[r][:, 0:nh, 0, :],
                                                   in1=rb[r][:, 0:nh, 1, :], op=ALU.subtract),
                 reads=[ra_b[r], rb_b[r]], writes=[dst_bf])
            S.op("pool", lambda e: e.tensor_tensor(out=dst[:, h0:h0 + nh, 1, :], in0=ra[r][:, 0:nh, 1, :],
                                                   in1=rb[r][:, 0:nh, 0, :], op=ALU.add),
                 reads=[ra_b[r], rb_b[r]], writes=[dst_bf])

        def stage1(t):
            sl, p2, p4, pq = t % NXS, t % NW, t % NK, t % NQ
            for hh in range(2):
                m = nxt("mm", 2)

                def f_tr(e, hh=hh, m=m):
                    for c in range(4):
                        ins = e.transpose(out=pmm[m][:, c * P:(c + 1) * P],
                                          in_=xs[sl][:, (hh * 4 + c) * P:(hh * 4 + c + 1) * P], identity=identf[:, :])
                    return ins
                S.op("pe", f_tr, reads=[xs_b[sl], ident_b], writes=[pmm_b[m]])
                src = pmm[m][:, :].rearrange("p (c t) -> p c t", c=4)
                if hh == 0:
                    S.op("act", lambda e, src=src: e.copy(out=xT[p2][:, 0:4, :], in_=src), reads=[pmm_b[m]], writes=[xT_b[p2]])
                else:
                    S.op("dve", lambda e, src=src: e.tensor_copy(out=xT[p2][:, 4:8, :], in_=src), reads=[pmm_b[m]], writes=[xT_b[p2]])
                yield
            for n in range(3):
                m = nxt("mm", 2)

                def f_in(e, n=n, m=m):
                    for c in range(8):
                        ins = e.matmul(pmm[m][:, :], lhsT=xT[p2][:, c, :], rhs=w_inb[:, c, n * 512:(n + 1) * 512],
                                       start=(c == 0), stop=(c == 7))
                    return ins
                S.op("pe", f_in, reads=[xT_b[p2], w_in_b], writes=[pmm_b[m]])
                if n < 2:
                    rope(pmm[m], pmm_b[m], 8, t, qr[p2], qr_b[p2], n * 8)
                else:
                    rope(pmm[m], pmm_b[m], 4, t, kr[p2], kr_b[p2], 0)
                    S.op("act", lambda e, m=m: e.copy(out=Vw[p4][:, :, 0:64],
                                                      in_=pmm[m][:, 256:512].rearrange("p (h d) -> p h d", d=64)),
                         reads=[pmm_b[m]], writes=[V_b[p4]])
                yield

            def f_qt(e):
                qflat = qr[p2][:, :, :, :].rearrange("p h two d -> p (h two d)")
                for s_ in range(8):
                    ins = e.transpose(out=ptb[:, s_, :], in_=qflat[:, s_ * P:(s_ + 1) * P], identity=identb[:, :])
                return ins
            S.op("pe", f_qt, reads=[qr_b[p2], identb_b], writes=[ptb_b])
            S.op("act", lambda e: e.copy(out=qT[pq][:, :, :], in_=ptb[:, :, :]), reads=[ptb_b], writes=[qT_b[pq]])
            yield

            def f_kt(e):
                kflat = kr[p2][:, :, :, :].rearrange("p h two d -> p (h two d)")
                for s_ in range(2):
                    ins = e.transpose(out=ptb[:, s_, :], in_=kflat[:, s_ * P:(s_ + 1) * P], identity=identb[:, :])
                return ins
            S.op("pe", f_kt, reads=[kr_b[p2], identb_b], writes=[ptb_b])
            S.op("dve", lambda e: e.tensor_copy(out=kTw[p4][:, :, :], in_=ptb[:, 0:2, :]), reads=[ptb_b], writes=[kT_b[p4]])
            yield

        def stage2(i):
            sl, p2, pq = i % NXS, i % NW, i % NQ
            kts = [kt for kt in (i - 1, i, i + 1) if 0 <= kt < NT]
            bank_started = [False, False, False]
            last_in_bank = {0: 5, 1: 11, 2: 15}
            assert pov_free[0]
            pov_free[0] = False
            work = [(kv, kt) for kv in range(4) for kt in kts]

            def emit_scores(kv, kt):
                base, ch = 64 * (kv % 2), kv // 2
                s_ = nxt("sc", 2)

                def f_sc(e):
                    if kt != i:
                        e.matmul(psc[s_][:, :], lhsT=identb[:, :], rhs=(maskp if kt < i else maskn)[:, :],
                                 start=True, stop=False)
                    return e.matmul(psc[s_][:, :], lhsT=kTw[kt % NK][base:base + 64, ch, :],
                                    rhs=qT[pq][base:base + 64, 4 * ch:4 * ch + 4, :],
                                    start=(kt == i), stop=True)
                S.op("pe", f_sc, reads=[kT_b[kt % NK], qT_b[pq], identb_b, mask_b], writes=[psc_b[s_]])
                return s_

            def emit_exp_pv(kv, kt, s_):
                pi_ = nxt("pt", 3)
                S.op("act", lambda e: e.activation(out=pT[pi_][:, :], in_=psc[s_][:, :], func=AF.Exp, scale=float(scale)),
                     reads=[psc_b[s_]], writes=[pT_b[pi_]])

                def f_pv(e):
                    for g_ in range(4):
                        h = kv * 4 + g_
                        bk, col = h // 6, (h % 6) * 65
                        st_ = not bank_started[bk]
                        bank_started[bk] = True
                        ins = e.matmul(pov[bk][:, col:col + 65], lhsT=pT[pi_][:, g_ * P:(g_ + 1) * P],
                                       rhs=Vw[kt % NK][:, kv, :], start=st_,
                                       stop=(h == last_in_bank[bk] and kt == kts[-1]), skip_group_check=True)
                    return ins
                banks = sorted({(kv * 4 + g_) // 6 for g_ in range(4)})
                S.op("pe", f_pv, reads=[pT_b[pi_], V_b[kt % NK]], writes=[pov_b[bk] for bk in banks])

            prev = None
            for (kv, kt) in work:
                s_ = emit_scores(kv, kt)
                if prev is not None:
                    emit_exp_pv(*prev)
                prev = (kv, kt, s_)
            emit_exp_pv(*prev)
            yield
            for bk in range(3):
                nh = 6 if bk < 2 else 4
                h0 = bk * 6
                pv3 = pov[bk][:, 0:nh * 65].rearrange("p (h d) -> p h d", d=65)
                S.op("dve", lambda e, pv3=pv3, h0=h0, nh=nh: e.tensor_tensor(
                    out=den[p2][:, h0:h0 + nh].unsqueeze(2), in0=pv3[:, :, 64:65],
                    in1=esink[:, h0:h0 + nh].unsqueeze(2), op=ALU.add),
                     reads=[pov_b[bk], esink_b], writes=[den_b[p2]])
                S.op("dve", lambda e, h0=h0, nh=nh: e.reciprocal(out=den[p2][:, h0:h0 + nh], in_=den[p2][:, h0:h0 + nh]),
                     reads=[den_b[p2]], writes=[den_b[p2]])
                S.op("dve", lambda e, pv3=pv3, h0=h0, nh=nh: e.tensor_tensor(
                    out=on[p2][:, h0:h0 + nh, :], in0=pv3[:, :, 0:64],
                    in1=bcast(den[p2][:, h0:h0 + nh].unsqueeze(2), [P, nh, 64]), op=ALU.mult),
                     reads=[pov_b[bk], den_b[p2]], writes=[on_b[p2]])
            pov_free[0] = True
            yield

            def f_ot(e):
                oflat = on[p2][:, :, :].rearrange("p h d -> p (h d)")
                for s_ in range(8):
                    ins = e.transpose(out=ptb[:, s_, :], in_=oflat[:, s_ * P:(s_ + 1) * P], identity=identb[:, :])
                return ins
            S.op("pe", f_ot, reads=[on_b[p2], identb_b], writes=[ptb_b])
            S.op("act", lambda e: e.copy(out=oT[p2][:, :, :], in_=ptb[:, :, :]), reads=[ptb_b], writes=[oT_b[p2]])
            yield
            for n in range(2):
                m = nxt("mm", 2)

                def f_op(e, n=n, m=m):
                    for c in range(8):
                        ins = e.matmul(pmm[m][:, :], lhsT=oT[p2][:, c, :], rhs=w_outb[:, c, n * 512:(n + 1) * 512],
                                       start=(c == 0), stop=(c == 7))
                    return ins
                S.op("pe", f_op, reads=[oT_b[p2], w_out_b], writes=[pmm_b[m]])
                S.op("dve", lambda e, n=n, m=m: e.scalar_tensor_tensor(
                    out=xs[sl][:, n * 512:(n + 1) * 512], in0=xs[sl][:, n * 512:(n + 1) * 512], scalar=DN_ALPHA,
                    in1=pmm[m][:, :], op0=ALU.mult, op1=ALU.add),
                     reads=[xs_b[sl], pmm_b[m]], writes=[xs_b[sl]])
                yield
            yield from ln_epilogue_gen(S, (stats[p2], st_b[p2], mv[p2], mv_b[p2], sc[p2], sc_b[p2]),
                                       xs[sl], xs_b[sl], g_t, b_t, gb_b, ot[p2], ot_b[p2])
            S.dma("sp", xout[i * P:(i + 1) * P, :], ot[p2][:, :], reads=[ot_b[p2]], writes=[xout_buf],
                  stream=f"o{p2}")
            if i + NXS < NT:
                load_x(i + NXS)

        def tile_gen(t):
            if t < NT:
                yield from stage1(t)
            else:
                for _ in range(7):
                    yield
            if t >= 1:
                yield from stage2(t - 1)

        for t in range(min(NXS, NT)):
            load_x(t)
        run_interleaved((tile_gen(t) for t in range(NT + 1)), depth=3, skew=6)
        S.barrier()


def attn_consts_host():
    j = np.arange(P)[:, None]
    i = np.arange(P)[None, :]
    mp = np.where(j >= i, 0.0, -30000.0).astype(np.float32)
    mn = np.where(j <= i, 0.0, -30000.0).astype(np.float32)
    invf = (10000.0 ** (-np.arange(0, HD, 2, dtype=np.float64) / HD)).astype(np.float32)
    return {"c_ident": np.eye(P, dtype=np.float32), "c_maskp": np.tile(mp, (1, 4)), "c_maskn": np.tile(mn, (1, 4)),
            "c_invf": invf}


def permute_attn_w_in(w_in):
    q = w_in[:, :1024].reshape(1024, 16, 64)[:, Q_HEAD_ORDER, :].reshape(1024, 1024)
    return np.ascontiguousarray(np.concatenate([q, w_in[:, 1024:]], axis=1))


def build_test_attn(NT):
    nc = bass.Bass("TRN2", target_bir_lowering=False)
    dt = lambda name, shape, dtp=F32, kind="ExternalInput": nc.dram_tensor(name, shape, dtp, kind=kind).ap()
    x = dt("x", [NT * P, D])
    pos = dt("pos", [NT * P], I32)
    w_in = dt("w_in", [D, A_IN])
    sink = dt("sink", [16])
    w_out = dt("w_out", [D, D])
    g = dt("g", [D])
    b = dt("b", [D])
    consts = (dt("c_ident", [P, P]), dt("c_maskp", [P, 512]), dt("c_maskn", [P, 512]), dt("c_invf", [32]))
    y = dt("y", [NT * P, D], kind="ExternalOutput")
    with ExitStack() as ctx:
        S = Sched(nc, ctx)
        attn_phase(S, nc, NT, x, y, pos, w_in, sink, w_out, g, b, consts, S.buf("xin"), S.buf("xout"), "a0")
        S.finish()
    return nc


G_H = 4
G_DK = 128
G_DV = 256
G_IN = 3104
G_TAU = 16.0


class PsumPool:
    def __init__(self, S, nc, ctx, tag, n=8):
        self.t = [ctx.enter_context(nc.psum_tensor(f"{tag}_pp{i}", [P, 512], F32)) for i in range(n)]
        self.b = S.pbufs(f"{tag}pp", n)
        self.i = 0
        self.n = n

    def get(self):
        k = self.i % self.n
        self.i += 1
        return self.t[k], self.b[k]


def gla_sweep1(S, nc, NT, xin, w_in, gw2f, gbf, gw2b, gbb, consts, scr, xin_buf, scr_buf, tag):
    ident_d, tri_d, negcol_d, gmask_d = consts
    with ExitStack() as ctx:
        sbt = lambda name, shape, dt: ctx.enter_context(nc.sbuf_tensor(f"{tag}_{name}", shape, dt))
        B = S.buf
        pp = PsumPool(S, nc, ctx, tag)
        w_inb = sbt("w_inb", [P, 8, 2080], BF16)
        w2a = sbt("w2a", [17, 2, 512], BF16)
        tri = sbt("tri", [P, 4, P], F32)
        negcol = sbt("negcol", [P, 1], F32)
        gmask = sbt("gmask", [P, 2, P], F32)
        identf = sbt("identf", [P, P], F32)
        identb = sbt("identb", [P, P], BF16)
        S32 = sbt("S32", [P, 4, G_DV], F32)
        Sb = sbt("Sb", [P, 4, G_DV], BF16)
        w_in_b, w2a_b, cst_b, identb_b, S32_b, Sb_b = B("w_in"), B("w2a"), B("cst"), B("idb"), B("S32"), B("Sb")
        S.dma("sp", identf[:, :], ident_d[:, :], writes=[cst_b], stream=f"c0")
        S.dma("sp", tri[:, :, :], tri_d.rearrange("k p t -> p k t"), writes=[cst_b], stream=f"c1")
        S.dma("sp", negcol[:, :], negcol_d[:, :], writes=[cst_b], stream=f"c2")
        S.dma("sp", gmask[:, :, :], gmask_d.rearrange("k p t -> p k t"), writes=[cst_b], stream=f"c3")
        w_inv = w_in.rearrange("(c p) f -> p c f", p=P)
        cl = CastLoader(S, nc, ctx, tag, width=512)
        for c in range(8):
            for hh in range(4):
                cl.load(w_inb[:, c, hh * 512:(hh + 1) * 512], w_inv[:, c, hh * 512:(hh + 1) * 512], w_in_b)
        for c in range(8):
            S.dma("pool", w_inb[:, c, 2048:2080], w_inv[:, c, 3072:3104], writes=[w_in_b], stream=f"w{c % 2}")
        for d_, (w2, gb_) in enumerate(((gw2f, gbf), (gw2b, gbb))):
            S.dma("pool", w2a[0:16, d_, :], w2[:, :], writes=[w2a_b], stream=f"w0")
            S.dma("pool", w2a[16:17, d_, :], gb_.rearrange("(o f) -> o f", o=1), writes=[w2a_b], stream=f"w1")
        S.op("dve", lambda e: e.tensor_copy(out=identb[:, :], in_=identf[:, :]), reads=[cst_b], writes=[identb_b])
        trib = sbt("trib", [P, 4, P], BF16)
        negcolb = sbt("negcolb", [P, 1], BF16)
        S.op("dve", lambda e: e.tensor_copy(out=trib[:, :, :], in_=tri[:, :, :]), reads=[cst_b], writes=[identb_b])
        S.op("dve", lambda e: e.tensor_copy(out=negcolb[:, :], in_=negcol[:, :]), reads=[cst_b], writes=[identb_b])
        S.op("pool", lambda e: e.memset(S32[:, :, :], 0.0), writes=[S32_b])
        S.op("pool", lambda e: e.memset(Sb[:, :, :], 0.0), writes=[Sb_b])

        NXS = 4
        NW = 3
        xs = [sbt(f"x{i}", [P, D], F32) for i in range(NXS)]
        xT = [sbt(f"xT{i}", [P, 8, P], BF16) for i in range(NW)]
        q_sb = [sbt(f"q{i}", [P, 512], F32) for i in range(NW)]
        k_sb = [sbt(f"k{i}", [P, 512], F32) for i in range(NW)]
        v_sb = [sbt(f"v{i}", [P, 1024], BF16) for i in range(NW)]
        lrT = [sbt(f"lrT{i}", [17, 2, P], BF16) for i in range(NW)]
        sp = [[sbt(f"sp{i}_{d_}", [P, 512], F32) for d_ in range(2)] for i in range(NW)]
        dec = [[sbt(f"dec{i}_{d_}", [P, 4], F32) for d_ in range(2)] for i in range(NW)]
        sph = [[sbt(f"sph{i}_{d_}", [P, 512], BF16) for d_ in range(2)] for i in range(NW)]
        spl = [[sbt(f"spl{i}_{d_}", [P, 512], BF16) for d_ in range(2)] for i in range(NW)]
        tb = [[[sbt(f"tb{i}_{d_}_{j}", [P, 512], F32) for j in range(3)] for d_ in range(2)] for i in range(NW)]
        qe = [[sbt(f"qe{i}_{d_}", [P, 512], BF16) for d_ in range(2)] for i in range(NW)]
        ke = [[sbt(f"ke{i}_{d_}", [P, 512], BF16) for d_ in range(2)] for i in range(NW)]
        kd = [[sbt(f"kd{i}_{d_}", [P, 512], BF16) for d_ in range(2)] for i in range(NW)]
        qkT = [[sbt(f"qkT{i}_{d_}", [P, 8, P], BF16) for d_ in range(2)] for i in range(NW)]
        attm = [[sbt(f"attm{i}_{d_}", [P, 4, P], BF16) for d_ in range(2)] for i in range(NW)]
        opart = [sbt(f"op{i}", [P, 1024], F32) for i in range(NW)]
        xs_b = S.bufs("xs", NXS)
        xT_b, q_b, k_b, v_b, lrT_b, op_b = (S.bufs(n_, NW) for n_ in ("xT", "q", "k", "v", "lrT", "op"))
        sp_b, dec_b, qe_b, ke_b, kd_b, qkT_b, attm_b = ([S.bufs(f"{n_}{i}", 2) for i in range(NW)]
                                                        for n_ in ("sp", "dec", "qe", "ke", "kd", "qkT", "attm"))
        tb_b = [[S.bufs(f"tb{i}_{d_}", 3) for d_ in range(2)] for i in range(NW)]
        sphl_b = [S.bufs(f"sphl{i}", 2) for i in range(NW)]
        for i in range(NW):
            S.op("pool", lambda e, i=i: e.memset(lrT[i][:, :, :], 1.0), writes=[lrT_b[i]])

        def load_x(t):
            sl = t % NXS
            S.dma("sp", xs[sl][:, :], xin[t * P:(t + 1) * P, :], reads=[xin_buf], writes=[xs_b[sl]], stream=f"x{sl}")

        state_ver = [0]

        def tile_gen(t):
            sl, p2 = t % NXS, t % NW

            def mm8(out_ap, c0, c1, wr_b):
                def f(e):
                    for c in range(8):
                        ins = e.matmul(out_ap, lhsT=xT[p2][:, c, :], rhs=w_inb[:, c, c0:c1], start=(c == 0), stop=(c == 7))
                    return ins
                S.op("pe", f, reads=[xT_b[p2], w_in_b], writes=[wr_b])

            for hh in range(2):
                pm, pm_b = pp.get()

                def f_tr(e, hh=hh, pm=pm):
                    for c in range(4):
                        ins = e.transpose(out=pm[:, c * P:(c + 1) * P], in_=xs[sl][:, (hh * 4 + c) * P:(hh * 4 + c + 1) * P],
                                          identity=identf[:, :])
                    return ins
                S.op("pe", f_tr, reads=[xs_b[sl], cst_b], writes=[pm_b])
                src = pm[:, :].rearrange("p (c t) -> p c t", c=4)
                if hh == 0:
                    S.op("act", lambda e, src=src: e.copy(out=xT[p2][:, 0:4, :], in_=src), reads=[pm_b], writes=[xT_b[p2]])
                else:
                    S.op("dve", lambda e, src=src: e.tensor_copy(out=xT[p2][:, 4:8, :], in_=src), reads=[pm_b], writes=[xT_b[p2]])
                yield
            pm, pm_b = pp.get()

            def f_lr(e, pm=pm):
                for d_ in range(2):
                    for c in range(8):
                        ins = e.matmul(pm[0:16, d_ * P:(d_ + 1) * P], lhsT=w_inb[:, c, 2048 + 16 * d_:2064 + 16 * d_],
                                       rhs=xT[p2][:, c, :], start=(c == 0), stop=(c == 7))
                return ins
            S.op("pe", f_lr, reads=[xT_b[p2], w_in_b], writes=[pm_b])
            S.op("dve", lambda e, pm=pm: e.tensor_copy(out=lrT[p2][0:16, :, :], in_=pm[0:16, 0:256].rearrange("p (a t) -> p a t", a=2)),
                 reads=[pm_b], writes=[lrT_b[p2]])
            yield
            pm, pm_b = pp.get()
            mm8(pm[:, :], 0, 512, pm_b)
            S.op("act", lambda e, pm=pm: e.activation(out=q_sb[p2][:, :], in_=pm[:, :], func=AF.Copy, scale=float(G_DK ** -0.5)),
                 reads=[pm_b], writes=[q_b[p2]])
            yield
            pm, pm_b = pp.get()
            mm8(pm[:, :], 512, 1024, pm_b)
            S.op("dve", lambda e, pm=pm: e.tensor_copy(out=k_sb[p2][:, :], in_=pm[:, :]), reads=[pm_b], writes=[k_b[p2]])
            yield
            for n in range(2):
                pm, pm_b = pp.get()
                mm8(pm[:, :], 1024 + n * 512, 1536 + n * 512, pm_b)
                S.op("act", lambda e, pm=pm, n=n: e.copy(out=v_sb[p2][:, n * 512:(n + 1) * 512], in_=pm[:, :]),
                     reads=[pm_b], writes=[v_b[p2]])
                yield
            for d_ in range(2):
                spd, spd_b, tbd, tbd_b = sp[p2][d_], sp_b[p2][d_], tb[p2][d_], tb_b[p2][d_]
                pm, pm_b = pp.get()
                S.op("pe", lambda e, pm=pm, d_=d_: e.matmul(pm[:, :], lhsT=lrT[p2][0:17, d_, :], rhs=w2a[0:17, d_, :],
                                                           start=True, stop=True),
                     reads=[lrT_b[p2], w2a_b], writes=[pm_b])
                S.op("act", lambda e, pm=pm, spd=spd: e.activation(out=spd[:, :], in_=pm[:, :], func=AF.Exp, scale=-1.0),
                     reads=[pm_b], writes=[spd_b])
                S.op("act", lambda e, spd=spd: e.activation(out=spd[:, :], in_=spd[:, :], func=AF.Ln, bias=1.0, scale=1.0),
                     reads=[spd_b], writes=[spd_b])
                yield
                hi, lo, hl_b = sph[p2][d_], spl[p2][d_], sphl_b[p2][d_]
                S.op("dve", lambda e, spd=spd, hi=hi: e.tensor_copy(out=hi[:, :], in_=spd[:, :]), reads=[spd_b], writes=[hl_b])
                S.op("dve", lambda e, spd=spd, hi=hi, lo=lo: e.tensor_tensor(out=lo[:, :], in0=spd[:, :], in1=hi[:, :], op=ALU.subtract),
                     reads=[spd_b, hl_b], writes=[hl_b])
                yield
                pm, pm_b = pp.get()

                def f_dec(e, pm=pm, hi=hi, lo=lo):
                    for h in range(4):
                        e.matmul(pm[:, 2 * h:2 * h + 1], lhsT=hi[:, h * P:(h + 1) * P], rhs=negcolb[:, 0:1], start=True, stop=False)
                        ins = e.matmul(pm[:, 2 * h:2 * h + 1], lhsT=lo[:, h * P:(h + 1) * P], rhs=negcolb[:, 0:1], start=False, stop=True)
                    return ins
                S.op("pe", f_dec, reads=[hl_b, identb_b], writes=[pm_b])
                S.op("act", lambda e, pm=pm, d_=d_: e.activation(
                    out=dec[p2][d_][:, :].unsqueeze(2), in_=pm[:, 0:8].rearrange("p (h two) -> p h two", two=2)[:, :, 0:1],
                    func=AF.Exp), reads=[pm_b], writes=[dec_b[p2][d_]])
                pmb, pmb_b = pp.get()

                def f_b(e, pmb=pmb, d_=d_, hi=hi, lo=lo):
                    e.matmul(pmb[:, :], lhsT=trib[:, 2 * d_, :], rhs=hi[:, :], start=True, stop=False)
                    return e.matmul(pmb[:, :], lhsT=trib[:, 2 * d_, :], rhs=lo[:, :], start=False, stop=True)
                S.op("pe", f_b, reads=[hl_b, identb_b], writes=[pmb_b])
                pmc, pmc_b = pp.get()

                def f_c(e, pmc=pmc, d_=d_, hi=hi, lo=lo):
                    e.matmul(pmc[:, :], lhsT=trib[:, 2 * d_ + 1, :], rhs=hi[:, :], start=True, stop=False)
                    return e.matmul(pmc[:, :], lhsT=trib[:, 2 * d_ + 1, :], rhs=lo[:, :], start=False, stop=True)
                S.op("pe", f_c, reads=[hl_b, identb_b], writes=[pmc_b])
                yield
                S.op("act", lambda e, pmb=pmb, tbd=tbd: e.activation(out=tbd[0][:, :], in_=pmb[:, :], func=AF.Exp),
                     reads=[pmb_b], writes=[tbd_b[0]])
                S.op("act", lambda e, pmb=pmb, tbd=tbd: e.activation(out=tbd[1][:, :], in_=pmb[:, :], func=AF.Exp, scale=-1.0),
                     reads=[pmb_b], writes=[tbd_b[1]])
                S.op("act", lambda e, pmc=pmc, tbd=tbd: e.activation(out=tbd[2][:, :], in_=pmc[:, :], func=AF.Exp),
                     reads=[pmc_b], writes=[tbd_b[2]])
                yield
                S.op("dve", lambda e, d_=d_, tbd=tbd: e.tensor_tensor(out=qe[p2][d_][:, :], in0=q_sb[p2][:, :], in1=tbd[0][:, :], op=ALU.mult),
                     reads=[q_b[p2], tbd_b[0]], writes=[qe_b[p2][d_]])
                S.op("dve", lambda e, d_=d_, tbd=tbd: e.tensor_tensor(out=ke[p2][d_][:, :], in0=k_sb[p2][:, :], in1=tbd[1][:, :], op=ALU.mult),
                     reads=[k_b[p2], tbd_b[1]], writes=[ke_b[p2][d_]])
                S.op("pool", lambda e, d_=d_, tbd=tbd: e.tensor_tensor(out=kd[p2][d_][:, :], in0=k_sb[p2][:, :], in1=tbd[2][:, :], op=ALU.mult),
                     reads=[k_b[p2], tbd_b[2]], writes=[kd_b[p2][d_]])
                yield
                pm, pm_b = pp.get()
                pmv = pm[:, :].bitcast(BF16).rearrange("p (c t) -> p c t", c=8)

                def f_t(e, pmv=pmv, d_=d_):
                    for j, src in enumerate((qe[p2][d_], ke[p2][d_])):
                        for h in range(4):
                            ins = e.transpose(out=pmv[:, j * 4 + h, :], in_=src[:, h * P:(h + 1) * P], identity=identb[:, :])
                    return ins
                S.op("pe", f_t, reads=[qe_b[p2][d_], ke_b[p2][d_], identb_b], writes=[pm_b])
                S.op("act" if d_ == 0 else "dve",
                     (lambda e, pmv=pmv, d_=d_: e.copy(out=qkT[p2][d_][:, :, :], in_=pmv)) if d_ == 0 else
                     (lambda e, pmv=pmv, d_=d_: e.tensor_copy(out=qkT[p2][d_][:, :, :], in_=pmv)),
                     reads=[pm_b], writes=[qkT_b[p2][d_]])
                yield
                pm, pm_b = pp.get()

                def f_att(e, pm=pm, d_=d_):
                    for h in range(4):
                        ins = e.matmul(pm[:, h * P:(h + 1) * P], lhsT=qkT[p2][d_][:, 4 + h, :], rhs=qkT[p2][d_][:, h, :],
                                       start=True, stop=True)
                    return ins
                S.op("pe", f_att, reads=[qkT_b[p2][d_]], writes=[pm_b])
                S.op("dve", lambda e, pm=pm, d_=d_: e.tensor_tensor(
                    out=attm[p2][d_][:, :, :], in0=pm[:, :].rearrange("p (h t) -> p h t", h=4),
                    in1=bcast(gmask[:, d_:d_ + 1, :], [P, 4, P]), op=ALU.mult),
                     reads=[pm_b, cst_b], writes=[attm_b[p2][d_]])
                yield
            assert state_ver[0] == t
            for hp in range(2):
                pm, pm_b = pp.get()

                def f_o(e, pm=pm, hp=hp):
                    for hh in range(2):
                        h = 2 * hp + hh
                        o_ap = pm[:, hh * G_DV:(hh + 1) * G_DV]
                        vv = v_sb[p2][:, h * G_DV:(h + 1) * G_DV]
                        e.matmul(o_ap, lhsT=attm[p2][0][:, h, :], rhs=vv, start=True, stop=False)
                        e.matmul(o_ap, lhsT=attm[p2][1][:, h, :], rhs=vv, start=False, stop=False)
                        ins = e.matmul(o_ap, lhsT=qkT[p2][0][:, h, :], rhs=Sb[:, h, :], start=False, stop=True)
                    return ins
                S.op("pe", f_o, reads=[attm_b[p2][0], attm_b[p2][1], v_b[p2], qkT_b[p2][0], Sb_b], writes=[pm_b])
                S.op("act", lambda e, pm=pm, hp=hp: e.copy(out=opart[p2][:, hp * 512:(hp + 1) * 512], in_=pm[:, :]),
                     reads=[pm_b], writes=[op_b[p2]])
            for hp in range(2):
                pm, pm_b = pp.get()

                def f_s(e, pm=pm, hp=hp):
                    for hh in range(2):
                        h = 2 * hp + hh
                        ins = e.matmul(pm[:, hh * G_DV:(hh + 1) * G_DV], lhsT=kd[p2][0][:, h * P:(h + 1) * P],
                                       rhs=v_sb[p2][:, h * G_DV:(h + 1) * G_DV], start=True, stop=True)
                    return ins
                S.op("pe", f_s, reads=[kd_b[p2][0], v_b[p2]], writes=[pm_b])
                for hh in range(2):
                    h = 2 * hp + hh
                    S.op("dve", lambda e, pm=pm, h=h, hh=hh: e.scalar_tensor_tensor(
                        out=S32[:, h, :], in0=S32[:, h, :], scalar=dec[p2][0][:, h:h + 1], in1=pm[:, hh * G_DV:(hh + 1) * G_DV],
                        op0=ALU.mult, op1=ALU.add), reads=[S32_b, dec_b[p2][0], pm_b], writes=[S32_b])
            S.op("act", lambda e: e.copy(out=Sb[:, :, :], in_=S32[:, :, :]), reads=[S32_b], writes=[Sb_b])
            state_ver[0] = t + 1
            yield
            S.dma("sp", scr["opart"][t * P:(t + 1) * P, :], opart[p2][:, :], reads=[op_b[p2]], writes=[scr_buf], stream=f"so{p2}")
            S.dma("sp", scr["qeTb"][t], qkT[p2][1][:, 0:4, :].rearrange("p h t -> p (h t)"), reads=[qkT_b[p2][1]],
                  writes=[scr_buf], stream=f"sq{p2}")
            S.dma("sp", scr["kdb"][t * P:(t + 1) * P, :], kd[p2][1][:, :], reads=[kd_b[p2][1]], writes=[scr_buf], stream=f"sk{p2}")
            S.dma("sp", scr["vsb"][t * P:(t + 1) * P, :], v_sb[p2][:, :], reads=[v_b[p2]], writes=[scr_buf], stream=f"sv{p2}")
            S.dma("sp", scr["decb"][t], dec[p2][1][:, :], reads=[dec_b[p2][1]], writes=[scr_buf], stream=f"sd{p2}")
            if t + NXS < NT:
                load_x(t + NXS)

        for t in range(min(NXS, NT)):
            load_x(t)
        run_interleaved((tile_gen(t) for t in range(NT)), depth=3, skew=7)
        S.barrier()


def gla_sweep2(S, nc, NT, xin, xout, w_in, norm_g, w_out, g, b, consts, scr, xin_buf, scr_buf, xout_buf, tag):
    ident_d = consts[0]
    with ExitStack() as ctx:
        sbt = lambda name, shape, dt: ctx.enter_context(nc.sbuf_tensor(f"{tag}_{name}", shape, dt))
        B = S.buf
        pp = PsumPool(S, nc, ctx, tag)
        w_rb = sbt("w_rb", [P, 8, 1024], BF16)
        w_outb = sbt("w_outb", [P, 8, D], BF16)
        identf = sbt("identf", [P, P], F32)
        identb = sbt("identb", [P, P], BF16)
        ng_t = sbt("ng", [P, G_DV], F32)
        g_t = sbt("g", [P, D], F32)
        b_t = sbt("b", [P, D], F32)
        S32 = sbt("S32", [P, 4, G_DV], F32)
        Sb = sbt("Sb", [P, 4, G_DV], BF16)
        w_b, cst_b, identb_b, gb_b, S32_b, Sb_b = B("w"), B("cst"), B("idb"), B("gb"), B("S32"), B("Sb")
        S.dma("sp", identf[:, :], ident_d[:, :], writes=[cst_b], stream=f"c0")
        S.dma("sp", ng_t[:, :], norm_g.partition_broadcast(P), writes=[gb_b], stream=f"c1")
        S.dma("sp", g_t[:, :], g.partition_broadcast(P), writes=[gb_b], stream=f"c2")
        S.dma("sp", b_t[:, :], b.partition_broadcast(P), writes=[gb_b], stream=f"c3")
        w_inv = w_in.rearrange("(c p) f -> p c f", p=P)
        w_outv = w_out.rearrange("(c p) f -> p c f", p=P)
        cl = CastLoader(S, nc, ctx, tag)
        for c in range(8):
            cl.load(w_rb[:, c, :], w_inv[:, c, 2048:3072], w_b)
        for c in range(8):
            cl.load(w_outb[:, c, :], w_outv[:, c, :], w_b)
        S.op("dve", lambda e: e.tensor_copy(out=identb[:, :], in_=identf[:, :]), reads=[cst_b], writes=[identb_b])
        S.op("pool", lambda e: e.memset(S32[:, :, :], 0.0), writes=[S32_b])
        S.op("pool", lambda e: e.memset(Sb[:, :, :], 0.0), writes=[Sb_b])

        NXS = 4
        NW = 4
        xs = [sbt(f"x{i}", [P, D], F32) for i in range(NXS)]
        opa = [sbt(f"opa{i}", [P, D], F32) for i in range(NXS)]
        qeT = [sbt(f"qeT{i}", [P, 4, P], BF16) for i in range(NXS)]
        kdb = [sbt(f"kdb{i}", [P, 512], BF16) for i in range(NXS)]
        vsb = [sbt(f"vsb{i}", [P, 1024], BF16) for i in range(NXS)]
        decb = [sbt(f"decb{i}", [P, 4], F32) for i in range(NXS)]
        xT = [sbt(f"xT{i}", [P, 8, P], BF16) for i in range(NW)]
        er = [sbt(f"er{i}", [P, 1024], F32) for i in range(NW)]
        sr = [sbt(f"sr{i}", [P, 4, G_DV], F32) for i in range(NW)]
        of = [sbt(f"of{i}", [P, 4, G_DV], F32) for i in range(NW)]
        sq = sbt("sq", [P, G_DV], F32)
        ss = [sbt(f"ss{i}", [P, 4], F32) for i in range(NW)]
        og = [sbt(f"og{i}", [P, 1024], BF16) for i in range(NW)]
        oT = [sbt(f"oT{i}", [P, 8, P], BF16) for i in range(NW)]
        ot = [sbt(f"ot{i}", [P, D], F32) for i in range(NW)]
        stats = [sbt(f"st{i}", [P, 2, 6], F32) for i in range(NW)]
        mv = [sbt(f"mv{i}", [P, 2], F32) for i in range(NW)]
        sc = [sbt(f"sc{i}", [P, 2], F32) for i in range(NW)]
        xs_b, opa_b, qeT_b, kdb_b, vsb_b, decb_b = (S.bufs(n_, NXS) for n_ in ("xs", "opa", "qeT", "kdb", "vsb", "decb"))
        xT_b, er_b, sr_b, of_b, ss_b, og_b, oT_b, ot_b, st_b, mv_b, sc_b = (
            S.bufs(n_, NW) for n_ in ("xT", "er", "sr", "of", "ss", "og", "oT", "ot", "st", "mv", "sc"))

        order = list(range(NT - 1, -1, -1))

        def load(idx):
            t = order[idx]
            sl = idx % NXS
            S.dma("sp", qeT[sl][:, :, :].rearrange("p h t -> p (h t)"), scr["qeTb"][t], reads=[scr_buf], writes=[qeT_b[sl]],
                  stream=f"lq{sl}")
            S.dma("sp", kdb[sl][:, :], scr["kdb"][t * P:(t + 1) * P, :], reads=[scr_buf], writes=[kdb_b[sl]], stream=f"lk{sl}")
            S.dma("sp", vsb[sl][:, :], scr["vsb"][t * P:(t + 1) * P, :], reads=[scr_buf], writes=[vsb_b[sl]], stream=f"lv{sl}")
            S.dma("sp", decb[sl][:, :], scr["decb"][t], reads=[scr_buf], writes=[decb_b[sl]], stream=f"ld{sl}")
            S.dma("sp", opa[sl][:, :], scr["opart"][t * P:(t + 1) * P, :], reads=[scr_buf], writes=[opa_b[sl]], stream=f"lo{sl}")
            S.dma("sp", xs[sl][:, :], xin[t * P:(t + 1) * P, :], reads=[xin_buf], writes=[xs_b[sl]], stream=f"x{sl}")

        state_ver = [0]

        def tile_gen(idx):
            t = order[idx]
            sl, p2 = idx % NXS, idx % NW
            assert state_ver[0] == idx
            pmo = []
            for hp in range(2):
                pm, pm_b = pp.get()
                pmo.append((pm, pm_b))

                def f_o(e, pm=pm, hp=hp):
                    for hh in range(2):
                        h = 2 * hp + hh
                        ins = e.matmul(pm[:, hh * G_DV:(hh + 1) * G_DV], lhsT=qeT[sl][:, h, :], rhs=Sb[:, h, :], start=True, stop=True)
                    return ins
                S.op("pe", f_o, reads=[qeT_b[sl], Sb_b], writes=[pm_b])
            pms = []
            for hp in range(2):
                pm, pm_b = pp.get()
                pms.append((pm, pm_b))

                def f_s(e, pm=pm, hp=hp):
                    for hh in range(2):
                        h = 2 * hp + hh
                        ins = e.matmul(pm[:, hh * G_DV:(hh + 1) * G_DV], lhsT=kdb[sl][:, h * P:(h + 1) * P],
                                       rhs=vsb[sl][:, h * G_DV:(h + 1) * G_DV], start=True, stop=True)
                    return ins
                S.op("pe", f_s, reads=[kdb_b[sl], vsb_b[sl]], writes=[pm_b])
            for hp in range(2):
                pm, pm_b = pms[hp]
                for hh in range(2):
                    h = 2 * hp + hh
                    S.op("dve", lambda e, pm=pm, h=h, hh=hh: e.scalar_tensor_tensor(
                        out=S32[:, h, :], in0=S32[:, h, :], scalar=decb[sl][:, h:h + 1], in1=pm[:, hh * G_DV:(hh + 1) * G_DV],
                        op0=ALU.mult, op1=ALU.add), reads=[S32_b, decb_b[sl], pm_b], writes=[S32_b])
            S.op("act", lambda e: e.copy(out=Sb[:, :, :], in_=S32[:, :, :]), reads=[S32_b], writes=[Sb_b])
            state_ver[0] = idx + 1
            for hp in range(2):
                pm, pm_b = pmo[hp]
                S.op("dve", lambda e, pm=pm, hp=hp: e.tensor_tensor(
                    out=of[p2][:, 2 * hp:2 * hp + 2, :].rearrange("p h d -> p (h d)"), in0=pm[:, :],
                    in1=opa[sl][:, hp * 512:(hp + 1) * 512], op=ALU.add),
                     reads=[pm_b, opa_b[sl]], writes=[of_b[p2]])
            yield
            def f_sq(e):
                for h in range(4):
                    ins = e.activation(out=sq[:, :], in_=of[p2][:, h, :], func=AF.Square, accum_out=ss[p2][:, h:h + 1])
                return ins
            S.op("act", f_sq, reads=[of_b[p2]], writes=[ss_b[p2]])
            S.op("act", lambda e: e.activation(out=ss[p2][:, :], in_=ss[p2][:, :], func=AF.Ln, bias=1e-6, scale=1.0 / G_DV),
                 reads=[ss_b[p2]], writes=[ss_b[p2]])
            S.op("act", lambda e: e.activation(out=ss[p2][:, :], in_=ss[p2][:, :], func=AF.Exp, scale=-0.5),
                 reads=[ss_b[p2]], writes=[ss_b[p2]])

            def f_hn(e):
                for h in range(4):
                    ins = e.activation(out=of[p2][:, h, :], in_=of[p2][:, h, :], func=AF.Copy, scale=ss[p2][:, h:h + 1])
                return ins
            S.op("act", f_hn, reads=[of_b[p2], ss_b[p2]], writes=[of_b[p2]])
            yield
            for hh in range(2):
                pm, pm_b = pp.get()

                def f_tr(e, hh=hh, pm=pm):
                    for c in range(4):
                        ins = e.transpose(out=pm[:, c * P:(c + 1) * P], in_=xs[sl][:, (hh * 4 + c) * P:(hh * 4 + c + 1) * P],
                                          identity=identf[:, :])
                    return ins
                S.op("pe", f_tr, reads=[xs_b[sl], cst_b], writes=[pm_b])
                src = pm[:, :].rearrange("p (c t) -> p c t", c=4)
                if hh == 0:
                    S.op("act", lambda e, src=src: e.copy(out=xT[p2][:, 0:4, :], in_=src), reads=[pm_b], writes=[xT_b[p2]])
                else:
                    S.op("dve", lambda e, src=src: e.tensor_copy(out=xT[p2][:, 4:8, :], in_=src), reads=[pm_b], writes=[xT_b[p2]])
                yield
            for n in range(2):
                pm, pm_b = pp.get()

                def f_r(e, pm=pm, n=n):
                    for c in range(8):
                        ins = e.matmul(pm[:, :], lhsT=xT[p2][:, c, :], rhs=w_rb[:, c, n * 512:(n + 1) * 512], start=(c == 0), stop=(c == 7))
                    return ins
                S.op("pe", f_r, reads=[xT_b[p2], w_b], writes=[pm_b])
                ern = er[p2][:, n * 512:(n + 1) * 512]
                srn = sr[p2][:, :, :].rearrange("p h d -> p (h d)")[:, n * 512:(n + 1) * 512]
                S.op("act", lambda e, pm=pm, ern=ern: e.activation(out=ern, in_=pm[:, :], func=AF.Exp, scale=-1.0),
                     reads=[pm_b], writes=[er_b[p2]])
                S.op("act", lambda e, ern=ern: e.activation(out=ern, in_=ern, func=AF.Ln, bias=1.0, scale=1.0),
                     reads=[er_b[p2]], writes=[er_b[p2]])
                S.op("act", lambda e, ern=ern: e.activation(out=ern, in_=ern, func=AF.Exp, scale=-1.0),
                     reads=[er_b[p2]], writes=[er_b[p2]])
                yield
                S.op("dve", lambda e, pm=pm, ern=ern, srn=srn: e.tensor_tensor(out=srn, in0=pm[:, :], in1=ern, op=ALU.mult),
                     reads=[pm_b, er_b[p2]], writes=[sr_b[p2]])
                S.op("dve", lambda e, srn=srn, n=n: e.tensor_tensor(
                    out=srn.rearrange("p (h d) -> p h d", d=G_DV), in0=srn.rearrange("p (h d) -> p h d", d=G_DV),
                    in1=bcast(ng_t[:, :].unsqueeze(1), [P, 2, G_DV]), op=ALU.mult),
                     reads=[sr_b[p2], gb_b], writes=[sr_b[p2]])
                yield
            S.op("dve", lambda e: e.tensor_tensor(out=og[p2][:, :], in0=of[p2][:, :, :].rearrange("p h d -> p (h d)"),
                                                  in1=sr[p2][:, :, :].rearrange("p h d -> p (h d)"), op=ALU.mult),
                 reads=[of_b[p2], sr_b[p2]], writes=[og_b[p2]])
            yield
            pm, pm_b = pp.get()
            pmv = pm[:, :].bitcast(BF16).rearrange("p (c t) -> p c t", c=8)

            def f_ot(e, pmv=pmv):
                for s_ in range(8):
                    ins = e.transpose(out=pmv[:, s_, :], in_=og[p2][:, s_ * P:(s_ + 1) * P], identity=identb[:, :])
                return ins
            S.op("pe", f_ot, reads=[og_b[p2], identb_b], writes=[pm_b])
            S.op("act", lambda e, pmv=pmv: e.copy(out=oT[p2][:, :, :], in_=pmv), reads=[pm_b], writes=[oT_b[p2]])
            yield
            for n in range(2):
                pm, pm_b = pp.get()

                def f_op(e, pm=pm, n=n):
                    for c in range(8):
                        ins = e.matmul(pm[:, :], lhsT=oT[p2][:, c, :], rhs=w_outb[:, c, n * 512:(n + 1) * 512], start=(c == 0), stop=(c == 7))
                    return ins
                S.op("pe", f_op, reads=[oT_b[p2], w_b], writes=[pm_b])
                S.op("dve", lambda e, pm=pm, n=n: e.scalar_tensor_tensor(
                    out=xs[sl][:, n * 512:(n + 1) * 512], in0=xs[sl][:, n * 512:(n + 1) * 512], scalar=DN_ALPHA,
                    in1=pm[:, :], op0=ALU.mult, op1=ALU.add), reads=[xs_b[sl], pm_b], writes=[xs_b[sl]])
                yield
            yield from ln_epilogue_gen(S, (stats[p2], st_b[p2], mv[p2], mv_b[p2], sc[p2], sc_b[p2]),
                                       xs[sl], xs_b[sl], g_t, b_t, gb_b, ot[p2], ot_b[p2])
            S.dma("sp", xout[t * P:(t + 1) * P, :], ot[p2][:, :], reads=[ot_b[p2]], writes=[xout_buf], stream=f"o{p2}")
            if idx + NXS < NT:
                load(idx + NXS)

        for idx in range(min(NXS, NT)):
            load(idx)
        run_interleaved((tile_gen(i) for i in range(NT)), depth=4, skew=5)
        S.barrier()


def gla_consts_host():
    u = np.arange(P)[:, None]
    t = np.arange(P)[None, :]
    c = np.float32(-1.0 / G_TAU)
    tri = np.stack([np.where(u <= t, c, 0), np.where(u > t, c, 0),
                    np.where(u >= t, c, 0), np.where(u < t, c, 0)]).astype(np.float32)
    gmask = np.stack([np.where(t >= u, 1.0, 0.0), np.where(t <= u, 1.0, 0.0)]).astype(np.float32)
    return {"c_tri": tri, "c_negcol": np.full((P, 1), c, np.float32), "c_gmask": gmask}


def gla_scratch(nc, NT, tag):
    mk = lambda name, shape, dtp: nc.dram_tensor(f"{tag}_{name}", shape, dtp, kind="Internal").ap()
    return {"opart": mk("opart", [NT * P, 1024], F32), "qeTb": mk("qeTb", [NT, P, 512], BF16),
            "kdb": mk("kdb", [NT * P, 512], BF16), "vsb": mk("vsb", [NT * P, 1024], BF16),
            "decb": mk("decb", [NT, P, 4], F32)}


def build_test_gla(NT):
    nc = bass.Bass("TRN2", target_bir_lowering=False)
    dt = lambda name, shape, dtp=F32, kind="ExternalInput": nc.dram_tensor(name, shape, dtp, kind=kind).ap()
    x = dt("x", [NT * P, D])
    w_in = dt("w_in", [D, G_IN])
    gw2f, gbf, gw2b, gbb = dt("gw2f", [16, 512]), dt("gbf", [512]), dt("gw2b", [16, 512]), dt("gbb", [512])
    norm_g = dt("norm_g", [G_DV])
    w_out = dt("w_out", [D, D])
    g, b = dt("g", [D]), dt("b", [D])
    consts = (dt("c_ident", [P, P]), dt("c_tri", [4, P, P]), dt("c_negcol", [P, 1]), dt("c_gmask", [2, P, P]))
    y = dt("y", [NT * P, D], kind="ExternalOutput")
    scr = gla_scratch(nc, NT, "g1")
    with ExitStack() as ctx:
        S = Sched(nc, ctx)
        xin_buf, scr_buf, xout_buf = S.buf("xin"), S.buf("scr"), S.buf("xout")
        gla_sweep1(S, nc, NT, x, w_in, gw2f, gbf, gw2b, gbb, consts, scr, xin_buf, scr_buf, "g1a")
        gla_sweep2(S, nc, NT, x, y, w_in, norm_g, w_out, g, b, consts, scr, xin_buf, scr_buf, xout_buf, "g1b")
        S.finish()
    return nc


def build_full(NT):
    nc = bass.Bass("TRN2", target_bir_lowering=False)
    dt = lambda name, shape, dtp=F32, kind="ExternalInput": nc.dram_tensor(name, shape, dtp, kind=kind).ap()
    N = NT * P
    x = dt("x", [N, D])
    pos = dt("pos", [N], I32)
    a_w_in, a_sink, a_w_out = dt("a_w_in", [D, A_IN]), dt("a_sink", [16]), dt("a_w_out", [D, D])
    g_w_in = dt("g_w_in", [D, G_IN])
    gw2f, gbf, gw2b, gbb = dt("gw2f", [16, 512]), dt("gbf", [512]), dt("gw2b", [16, 512]), dt("gbb", [512])
    norm_g = dt("norm_g", [G_DV])
    g_w_out = dt("g_w_out", [D, D])
    mix_g, mix_b = [dt(f"mix_g{i}", [D]) for i in range(2)], [dt(f"mix_b{i}", [D]) for i in range(2)]
    w1 = [dt(f"w1_{i}", [D, DFF]) for i in range(2)]
    w2 = [dt(f"w2_{i}", [DFF, D]) for i in range(2)]
    mlp_g, mlp_b = [dt(f"mlp_g{i}", [D]) for i in range(2)], [dt(f"mlp_b{i}", [D]) for i in range(2)]
    c_ident = dt("c_ident", [P, P])
    a_consts = (c_ident, dt("c_maskp", [P, 512]), dt("c_maskn", [P, 512]), dt("c_invf", [32]))
    g_consts = (c_ident, dt("c_tri", [4, P, P]), dt("c_negcol", [P, 1]), dt("c_gmask", [2, P, P]))
    out = dt("out", [N, D], kind="ExternalOutput")
    s = [nc.dram_tensor(f"act{i}", [N, D], F32, kind="Internal").ap() for i in range(3)]
    scr = gla_scratch(nc, NT, "g1")
    with ExitStack() as ctx:
        S = Sched(nc, ctx)
        bx, bout, bscr = S.buf("x"), S.buf("out"), S.buf("scr")
        bs = S.bufs("act", 3)
        attn_phase(S, nc, NT, x, s[0], pos, a_w_in, a_sink, a_w_out, mix_g[0], mix_b[0], a_consts, bx, bs[0], "a0")
        mlp_phase(S, nc, NT, s[0], s[1], w1[0], w2[0], mlp_g[0], mlp_b[0], c_ident, bs[0], bs[1], "m0")
        gla_sweep1(S, nc, NT, s[1], g_w_in, gw2f, gbf, gw2b, gbb, g_consts, scr, bs[1], bscr, "ga")
        gla_sweep2(S, nc, NT, s[1], s[2], g_w_in, norm_g, g_w_out, mix_g[1], mix_b[1], g_consts, scr, bs[1], bscr, bs[2], "gb")
        mlp_phase(S, nc, NT, s[2], out, w1[1], w2[1], mlp_g[1], mlp_b[1], c_ident, bs[2], bout, "m1")
        S.finish()
    return nc


def host_inputs(inp, core, NT=SEQ // P):
    f = lambda a: np.ascontiguousarray(np.asarray(a, dtype=np.float32))
    N = NT * P
    m = {
        "x": f(inp["x"][core, :N]),
        "pos": np.ascontiguousarray(np.asarray(inp["positions"][core, :N], dtype=np.int32)),
        "a_w_in": permute_attn_w_in(f(inp["attn_w_in"][0])),
        "a_sink": f(inp["attn_sink"][0]),
        "a_w_out": f(inp["attn_w_out"][0]),
        "g_w_in": f(inp["gla_w_in"][0]),
        "gw2f": f(inp["gla_gate_w2_fwd"][0]), "gbf": f(inp["gla_gate_b_fwd"][0]),
        "gw2b": f(inp["gla_gate_w2_bwd"][0]), "gbb": f(inp["gla_gate_b_bwd"][0]),
        "norm_g": f(inp["gla_norm_g"][0]),
        "g_w_out": f(inp["gla_w_out"][0]),
    }
    for i in range(2):
        m[f"mix_g{i}"] = f(inp["mix_ln_g"][i])
        m[f"mix_b{i}"] = f(inp["mix_ln_b"][i])
        m[f"w1_{i}"] = f(inp["mlp_w1"][i])
        m[f"w2_{i}"] = f(inp["mlp_w2"][i])
        m[f"mlp_g{i}"] = f(inp["mlp_ln_g"][i])
        m[f"mlp_b{i}"] = f(inp["mlp_ln_b"][i])
    m.update(attn_consts_host())
    m.update(gla_consts_host())
    return m


def kernel(**inputs):
    NT = SEQ // P
    nc = build_full(NT)
    shared = host_inputs(inputs, 0)
    in_maps = []
    for c in range(NCORES):
        m = dict(shared)
        m["x"] = np.ascontiguousarray(np.asarray(inputs["x"][c], dtype=np.float32))
        m["pos"] = np.ascontiguousarray(np.asarray(inputs["positions"][c], dtype=np.int32))
        in_maps.append(m)
    res = run_bass_kernel_spmd(nc, in_maps, core_ids=list(range(NCORES)))
    return np.stack([np.asarray(r["out"], dtype=np.float32) for r in res.results], axis=0)
```

```python
import os
import numpy as np
from contextlib import ExitStack
import ml_dtypes
import concourse.bass as bass
import concourse.mybir as mybir
from concourse.bass_utils import run_bass_kernel_spmd

F32 = mybir.dt.float32
BF16 = mybir.dt.bfloat16
I32 = mybir.dt.int32
AF = mybir.ActivationFunctionType
ALU = mybir.AluOpType

P = 128
D = 1024
DFF = 4096
SEQ = 8192
NCORES = 8
DEPTH = 2
DN_ALPHA = float((2 * DEPTH) ** 0.25)
LN_EPS = 1e-5

SEM_EPOCH = 30000


class Buf:
    __slots__ = ("name", "lw", "rd", "rd_dma", "excl")

    def __init__(self, name, excl=False):
        self.name = name
        self.excl = excl
        self.lw = None
        self.rd = {}
        self.rd_dma = []


class Sched:
    def __init__(self, nc, ctx):
        self.nc = nc
        self.ctx = ctx
        self.E = {"pe": nc.tensor, "act": nc.scalar, "dve": nc.vector,
                  "pool": nc.gpsimd, "sp": nc.sync}
        self.cnt = {e: 0 for e in self.E}
        self.sems = {e: [] for e in self.E}
        self.waited = {e: {} for e in self.E}
        self.streams = {}
        self.nwaits = 0

    def buf(self, name):
        return Buf(name)

    def bufs(self, name, n):
        return [Buf(f"{name}{i}") for i in range(n)]

    def pbufs(self, name, n):
        return [Buf(f"{name}{i}", excl=True) for i in range(n)]

    def _eng_sem(self, eng, k):
        ep = (k - 1) // SEM_EPOCH
        while len(self.sems[eng]) <= ep:
            self.sems[eng].append(
                self.ctx.enter_context(self.nc.semaphore(f"s_{eng}{len(self.sems[eng])}")))
        return self.sems[eng][ep], (k - 1) % SEM_EPOCH + 1

    def _stream(self, name):
        st = self.streams.get(name)
        if st is None:
            st = {"n": 0, "sem": self.ctx.enter_context(self.nc.semaphore(f"d_{name}"))}
            self.streams[name] = st
        return st

    def _wait(self, eng, tok):
        w = self.waited[eng]
        if tok[0] == "c":
            _, peng, k = tok
            if peng == "pe" and eng == "pe":
                return
            if w.get(peng, 0) >= k:
                return
            w[peng] = k
            sem, val = self._eng_sem(peng, k)
        else:
            _, sname, n = tok
            key = "d:" + sname
            if w.get(key, 0) >= n:
                return
            w[key] = n
            sem, val = self.streams[sname]["sem"], 16 * n
        self.E[eng].wait_ge(sem, val)
        self.nwaits += 1

    def _deps(self, reads, writes, eng=None):
        deps = []
        for b in reads:
            if b.lw is not None:
                deps.append(b.lw)
            if b.excl:
                deps.extend(tok for e2, tok in b.rd.items() if e2 != eng)
        for b in writes:
            if b.lw is not None:
                deps.append(b.lw)
            deps.extend(b.rd.values())
            deps.extend(b.rd_dma)
        return deps

    def op(self, eng, fn, reads=(), writes=()):
        for tok in self._deps(reads, writes, eng):
            self._wait(eng, tok)
        ins = fn(self.E[eng])
        self.cnt[eng] += 1
        k = self.cnt[eng]
        sem, _ = self._eng_sem(eng, k)
        ins.then_inc(sem, 1)
        tok = ("c", eng, k)
        for b in writes:
            b.lw = tok
            b.rd = {}
            b.rd_dma = []
        for b in reads:
            if b.lw is not tok:
                b.rd[eng] = tok
        return tok

    def dma(self, q, out, in_, reads=(), writes=(), stream=None, **kw):
        stream = f"{q}_{stream}"
        st = self._stream(stream)
        deps = self._deps(reads, writes)
        if st["n"] > 0:
            deps.append(("d", stream, st["n"]))
        for tok in deps:
            self._wait(q, tok)
        ins = self.E[q].dma_start(out=out, in_=in_, **kw)
        ins.then_inc(st["sem"], 16)
        st["n"] += 1
        tok = ("d", stream, st["n"])
        for b in writes:
            b.lw = tok
            b.rd = {}
            b.rd_dma = []
        for b in reads:
            b.rd_dma.append(tok)
        return tok

    def barrier(self):
        for eng in self.E:
            for peng in self.E:
                if peng != eng and self.cnt[peng] > 0:
                    self._wait(eng, ("c", peng, self.cnt[peng]))
            for sname, st in self.streams.items():
                if st["n"] > 0:
                    self._wait(eng, ("d", sname, st["n"]))

    def finish(self):
        for sname, st in self.streams.items():
            if st["n"] > 0:
                self._wait("sp", ("d", sname, st["n"]))


def run_interleaved(gens, depth=2, skew=1):
    it = iter(gens)
    pending = next(it, None)
    active = []
    while active or pending is not None:
        if pending is not None and len(active) < depth and (not active or active[-1][1] >= skew):
            active.append([pending, 0])
            pending = next(it, None)
        for a in list(active):
            try:
                next(a[0])
                a[1] += 1
            except StopIteration:
                active.remove(a)


def load_rowvec_bcast(S, q, dst, src_1d, n, stream, wbuf):
    S.dma(q, dst, src_1d.partition_broadcast(P), writes=[wbuf], stream=stream)


def ln_epilogue(S, sb, z, zb, g_t, b_t, gb_buf, out_t, out_b, eps=LN_EPS):
    stats, stats_b, mv, mv_b, sc, sc_b = sb
    S.op("dve", lambda e: (e.bn_stats(out=stats[:, 0, :], in_=z[:, 0:512]),
                           e.bn_stats(out=stats[:, 1, :], in_=z[:, 512:1024]))[-1],
         reads=[zb], writes=[stats_b])
    S.op("dve", lambda e: e.bn_aggr(out=mv[:, :], in_=stats[:, :, :].rearrange("p a b -> p (a b)")),
         reads=[stats_b], writes=[mv_b])
    S.op("act", lambda e: e.activation(out=sc[:, 0:1], in_=mv[:, 1:2], func=AF.Ln, bias=eps, scale=1.0),
         reads=[mv_b], writes=[sc_b])
    S.op("act", lambda e: e.activation(out=sc[:, 0:1], in_=sc[:, 0:1], func=AF.Exp, scale=-0.5),
         reads=[sc_b], writes=[sc_b])
    S.op("dve", lambda e: e.scalar_tensor_tensor(out=sc[:, 1:2], in0=mv[:, 0:1], scalar=-1.0,
                                                 in1=sc[:, 0:1], op0=ALU.mult, op1=ALU.mult),
         reads=[mv_b, sc_b], writes=[sc_b])
    S.op("act", lambda e: e.activation(out=out_t[:, :], in_=z[:, :], func=AF.Identity,
                                       bias=sc[:, 1:2], scale=sc[:, 0:1]),
         reads=[zb, sc_b], writes=[out_b])
    S.op("dve", lambda e: e.tensor_tensor(out=out_t[:, :], in0=out_t[:, :], in1=g_t[:, :], op=ALU.mult),
         reads=[out_b, gb_buf], writes=[out_b])
    S.op("dve", lambda e: e.tensor_tensor(out=out_t[:, :], in0=out_t[:, :], in1=b_t[:, :], op=ALU.add),
         reads=[out_b, gb_buf], writes=[out_b])


class CastLoader:
    def __init__(self, S, nc, ctx, tag, width=1024, n=4):
        self.S = S
        self.stg = [ctx.enter_context(nc.sbuf_tensor(f"{tag}_stg{i}", [P, width], F32)) for i in range(n)]
        self.stg_b = S.bufs("stg", n)
        self.k = 0

    def load(self, dst, src, dbuf):
        S = self.S
        i = self.k % len(self.stg)
        self.k += 1
        w = dst.shape[-1]
        st = self.stg[i]
        S.dma("sp", st[:, 0:w], src, writes=[self.stg_b[i]], stream=f"stg{i}")
        if self.k % 2:
            S.op("act", lambda e: e.copy(out=dst, in_=st[:, 0:w]), reads=[self.stg_b[i]], writes=[dbuf])
        else:
            S.op("dve", lambda e: e.tensor_copy(out=dst, in_=st[:, 0:w]), reads=[self.stg_b[i]], writes=[dbuf])


def ln_epilogue_gen(S, sb, z, zb, g_t, b_t, gb_buf, out_t, out_b, eps=LN_EPS):
    stats, stats_b, mv, mv_b, sc, sc_b = sb
    S.op("dve", lambda e: (e.bn_stats(out=stats[:, 0, :], in_=z[:, 0:512]),
                           e.bn_stats(out=stats[:, 1, :], in_=z[:, 512:1024]))[-1],
         reads=[zb], writes=[stats_b])
    S.op("dve", lambda e: e.bn_aggr(out=mv[:, :], in_=stats[:, :, :].rearrange("p a b -> p (a b)")),
         reads=[stats_b], writes=[mv_b])
    yield
    S.op("act", lambda e: e.activation(out=sc[:, 0:1], in_=mv[:, 1:2], func=AF.Ln, bias=eps, scale=1.0),
         reads=[mv_b], writes=[sc_b])
    S.op("act", lambda e: e.activation(out=sc[:, 0:1], in_=sc[:, 0:1], func=AF.Exp, scale=-0.5),
         reads=[sc_b], writes=[sc_b])
    yield
    S.op("dve", lambda e: e.scalar_tensor_tensor(out=sc[:, 1:2], in0=mv[:, 0:1], scalar=-1.0,
                                                 in1=sc[:, 0:1], op0=ALU.mult, op1=ALU.mult),
         reads=[mv_b, sc_b], writes=[sc_b])
    yield
    S.op("act", lambda e: e.activation(out=out_t[:, :], in_=z[:, :], func=AF.Identity,
                                       bias=sc[:, 1:2], scale=sc[:, 0:1]),
         reads=[zb, sc_b], writes=[out_b])
    yield
    S.op("dve", lambda e: e.tensor_tensor(out=out_t[:, :], in0=out_t[:, :], in1=g_t[:, :], op=ALU.mult),
         reads=[out_b, gb_buf], writes=[out_b])
    S.op("dve", lambda e: e.tensor_tensor(out=out_t[:, :], in0=out_t[:, :], in1=b_t[:, :], op=ALU.add),
         reads=[out_b, gb_buf], writes=[out_b])
    yield


def mlp_phase(S, nc, NT, xin, xout, w1, w2, g, b, ident_d, xin_buf, xout_buf, tag):
    ST = 2
    NS = NT // ST
    TOK = ST * P
    with ExitStack() as ctx:
        sbt = lambda name, shape, dt: ctx.enter_context(nc.sbuf_tensor(f"{tag}_{name}", shape, dt))
        pst = lambda name, shape, dt: ctx.enter_context(nc.psum_tensor(f"{tag}_{name}", shape, dt))
        w1b = sbt("w1b", [P, 8, DFF], BF16)
        w2b = sbt("w2b", [P, 32, D], BF16)
        identf = sbt("identf", [P, P], F32)
        g_t = sbt("g", [P, D], F32)
        b_t = sbt("b", [P, D], F32)
        NXS = 4
        xs = [sbt(f"x{i}", [P, D], F32) for i in range(NXS)]
        xT = [sbt(f"xT{i}", [P, 8, TOK], BF16) for i in range(2)]
        hT = sbt("hT", [P, 32, TOK], BF16)
        rt = [sbt(f"rt{i}", [P, 512], F32) for i in range(2)]
        ot = [sbt(f"ot{i}", [P, D], F32) for i in range(2)]
        stats = [sbt(f"st{i}", [P, 2, 6], F32) for i in range(2)]
        mv = [sbt(f"mv{i}", [P, 2], F32) for i in range(2)]
        sc = [sbt(f"sc{i}", [P, 2], F32) for i in range(2)]
        ptr = [pst(f"ptr{i}", [P, 4, P], F32) for i in range(2)]
        pm1 = [pst(f"pm1{i}", [P, 512], F32) for i in range(2)]
        pm2 = [pst(f"pm2{i}", [P, 2, 512], F32) for i in range(2)]

        B = S.buf
        w1_b = [B(f"w1b{c}") for c in range(8)]
        w2_b = [B(f"w2b{c}") for c in range(8)]
        ident_b, gb_b = B("ident"), B("gb")
        xs_b = S.bufs("xs", NXS)
        xT_b = S.bufs("xT", 2)
        hT_b = [B(f"hT{j}") for j in range(16)]
        rt_b = S.bufs("rt", 2)
        ot_b = S.bufs("ot", 2)
        st_b, mv_b, sc_b = S.bufs("st", 2), S.bufs("mv", 2), S.bufs("sc", 2)
        ptr_b, pm1_b, pm2_b = S.pbufs("ptr", 2), S.pbufs("pm1", 2), S.pbufs("pm2", 2)

        S.dma("sp", identf[:, :], ident_d[:, :], writes=[ident_b], stream=f"c0")
        S.dma("sp", g_t[:, :], g.partition_broadcast(P), writes=[gb_b], stream=f"c1")
        S.dma("sp", b_t[:, :], b.partition_broadcast(P), writes=[gb_b], stream=f"c2")
        cl = CastLoader(S, nc, ctx, tag)
        w1v = w1.rearrange("(c p) f -> p c f", p=P)
        for c in range(8):
            for q4 in range(4):
                cl.load(w1b[:, c, q4 * 1024:(q4 + 1) * 1024], w1v[:, c, q4 * 1024:(q4 + 1) * 1024], w1_b[c])
        w2v = w2.rearrange("(c p) f -> p c f", p=P)
        for c in range(32):
            cl.load(w2b[:, c, :], w2v[:, c, :], w2_b[c // 4])

        def load_x(t):
            sl = t % NXS
            S.dma("sp", xs[sl][:, :], xin[t * P:(t + 1) * P, :], reads=[xin_buf], writes=[xs_b[sl]],
                  stream=f"x{sl}")

        for t in range(min(NXS, NT)):
            load_x(t)

        def transposes(s):
            xTs, xTs_b = xT[s % 2], xT_b[s % 2]
            for m in range(ST):
                t = s * ST + m
                sl = t % NXS
                for hh in range(2):
                    pt, pt_b = ptr[hh], ptr_b[hh]

                    def f_tr(e, sl=sl, hh=hh, pt=pt):
                        for c in range(4):
                            ins = e.transpose(out=pt[:, c, :], in_=xs[sl][:, (hh * 4 + c) * P:(hh * 4 + c + 1) * P],
                                              identity=identf[:, :])
                        return ins
                    S.op("pe", f_tr, reads=[xs_b[sl], ident_b], writes=[pt_b])
                    S.op("act" if hh == 0 else "dve",
                         (lambda e, pt=pt, hh=hh, m=m, xTs=xTs: e.copy(out=xTs[:, hh * 4:(hh + 1) * 4, m * P:(m + 1) * P], in_=pt[:, :, :]))
                         if hh == 0 else
                         (lambda e, pt=pt, hh=hh, m=m, xTs=xTs: e.tensor_copy(out=xTs[:, hh * 4:(hh + 1) * 4, m * P:(m + 1) * P], in_=pt[:, :, :])),
                         reads=[pt_b], writes=[xTs_b])
        transposes(0)
        for s in range(NS):
            xTs, xTs_b = xT[s % 2], xT_b[s % 2]
            for jj in range(16):
                pm, pm_b = pm1[jj % 2], pm1_b[jj % 2]

                def f_mm1(e, jj=jj, pm=pm, xTs=xTs):
                    for u in range(2):
                        j = jj * 2 + u
                        for c in range(8):
                            ins = e.matmul(pm[:, u * TOK:(u + 1) * TOK], lhsT=w1b[:, c, j * P:(j + 1) * P],
                                           rhs=xTs[:, c, :], start=(c == 0), stop=(c == 7))
                    return ins
                S.op("pe", f_mm1, reads=[xTs_b] + w1_b, writes=[pm_b])
                r, r_b = rt[jj % 2], rt_b[jj % 2]
                S.op("act", lambda e, pm=pm, r=r: e.activation(out=r[:, :], in_=pm[:, :], func=AF.Relu),
                     reads=[pm_b], writes=[r_b])
                S.op("dve", lambda e, r=r, jj=jj: e.tensor_tensor(
                    out=hT[:, 2 * jj:2 * jj + 2, :].rearrange("p a t -> p (a t)"), in0=r[:, :], in1=r[:, :], op=ALU.mult),
                     reads=[r_b], writes=[hT_b[jj]])
            if s + 1 < NS:
                transposes(s + 1)
            for m in range(ST):
                t = s * ST + m
                sl = t % NXS
                k2 = t % 2
                py, py_b = pm2[k2], pm2_b[k2]

                def f_mm2(e, m=m, py=py):
                    for n in range(2):
                        for j in range(32):
                            ins = e.matmul(py[:, n, :], lhsT=hT[:, j, m * P:(m + 1) * P],
                                           rhs=w2b[:, j, n * 512:(n + 1) * 512], start=(j == 0), stop=(j == 31))
                    return ins
                S.op("pe", f_mm2, reads=hT_b + w2_b, writes=[py_b])
                S.op("dve", lambda e, sl=sl, py=py: e.scalar_tensor_tensor(
                    out=xs[sl][:, :], in0=xs[sl][:, :], scalar=DN_ALPHA,
                    in1=py[:, :, :].rearrange("p a b -> p (a b)"), op0=ALU.mult, op1=ALU.add),
                     reads=[xs_b[sl], py_b], writes=[xs_b[sl]])
                ln_epilogue(S, (stats[k2], st_b[k2], mv[k2], mv_b[k2], sc[k2], sc_b[k2]),
                            xs[sl], xs_b[sl], g_t, b_t, gb_b, ot[k2], ot_b[k2])
                S.dma("sp", xout[t * P:(t + 1) * P, :], ot[k2][:, :], reads=[ot_b[k2]], writes=[xout_buf],
                      stream=f"o{k2}")
                if t + NXS < NT:
                    load_x(t + NXS)
        S.barrier()


def build_test_mlp(NT):
    nc = bass.Bass("TRN2", target_bir_lowering=False)
    x = nc.dram_tensor("x", [NT * P, D], F32, kind="ExternalInput").ap()
    w1 = nc.dram_tensor("w1", [D, DFF], F32, kind="ExternalInput").ap()
    w2 = nc.dram_tensor("w2", [DFF, D], F32, kind="ExternalInput").ap()
    g = nc.dram_tensor("g", [D], F32, kind="ExternalInput").ap()
    b = nc.dram_tensor("b", [D], F32, kind="ExternalInput").ap()
    ident = nc.dram_tensor("ident", [P, P], F32, kind="ExternalInput").ap()
    y = nc.dram_tensor("y", [NT * P, D], F32, kind="ExternalOutput").ap()
    with ExitStack() as ctx:
        S = Sched(nc, ctx)
        mlp_phase(S, nc, NT, x, y, w1, w2, g, b, ident, S.buf("xin"), S.buf("xout"), "m0")
        S.finish()
    return nc


A_HEADS = 16
A_KV = 4
HD = 64
A_IN = 1536
TWO_PI = 2.0 * np.pi
CW1 = 6.28125
CW2 = float(TWO_PI - 6.28125)
PI_LO = 3.1415925
Q_HEAD_ORDER = [0, 4, 1, 5, 2, 6, 3, 7, 8, 12, 9, 13, 10, 14, 11, 15]


def bcast(ap, shape):
    return ap.broadcast_to(list(shape))


def attn_phase(S, nc, NT, xin, xout, pos, w_in, sink, w_out, g, b, consts, xin_buf, xout_buf, tag):
    ident_d, maskp_d, maskn_d, invf_d = consts
    scale = HD ** -0.5
    with ExitStack() as ctx:
        sbt = lambda name, shape, dt: ctx.enter_context(nc.sbuf_tensor(f"{tag}_{name}", shape, dt))
        pst = lambda name, shape, dt: ctx.enter_context(nc.psum_tensor(f"{tag}_{name}", shape, dt))
        B = S.buf
        w_inb = sbt("w_inb", [P, 8, A_IN], BF16)
        w_outb = sbt("w_outb", [P, 8, D], BF16)
        identf = sbt("identf", [P, P], F32)
        identb = sbt("identb", [P, P], BF16)
        maskp = sbt("maskp", [P, 512], BF16)
        maskn = sbt("maskn", [P, 512], BF16)
        g_t = sbt("g", [P, D], F32)
        b_t = sbt("b", [P, D], F32)
        esink = sbt("esink", [P, 16], F32)
        cos_t = sbt("cos", [P, NT, 32], F32)
        sin_t = sbt("sin", [P, NT, 32], F32)
        w_in_b, w_out_b = B("w_in"), B("w_out")
        ident_b, identb_b, mask_b, gb_b, esink_b, cs_b = B("id"), B("idb"), B("mask"), B("gb"), B("esink"), B("cs")

        pmm = [pst(f"pmm{i}", [P, 512], F32) for i in range(2)]
        ptb = pst("ptb", [P, 8, P], BF16)
        psc = [pst(f"psc{i}", [P, 512], F32) for i in range(2)]
        pov = [pst(f"pov{i}", [P, 512], F32) for i in range(3)]
        pmm_b, psc_b, pov_b, ptb_b = S.pbufs("pmm", 2), S.pbufs("psc", 2), S.pbufs("pov", 3), S.pbufs("ptb", 1)[0]

        S.dma("sp", identf[:, :], ident_d[:, :], writes=[ident_b], stream=f"c0")
        S.dma("sp", g_t[:, :], g.partition_broadcast(P), writes=[gb_b], stream=f"c1")
        S.dma("sp", b_t[:, :], b.partition_broadcast(P), writes=[gb_b], stream=f"c2")
        S.dma("sp", esink[:, :], sink.partition_broadcast(P), writes=[esink_b], stream=f"c3")
        S.dma("pool", maskp[:, :], maskp_d[:, :], writes=[mask_b], stream=f"c4")
        S.dma("pool", maskn[:, :], maskn_d[:, :], writes=[mask_b], stream=f"c5")
        cl = CastLoader(S, nc, ctx, tag)
        w_inv = w_in.rearrange("(c p) f -> p c f", p=P)
        for c in range(8):
            for hh in range(2):
                cl.load(w_inb[:, c, hh * 768:(hh + 1) * 768], w_inv[:, c, hh * 768:(hh + 1) * 768], w_in_b)
        w_outv = w_out.rearrange("(c p) f -> p c f", p=P)
        for c in range(8):
            cl.load(w_outb[:, c, :], w_outv[:, c, :], w_out_b)
        S.op("dve", lambda e: e.tensor_copy(out=identb[:, :], in_=identf[:, :]), reads=[ident_b], writes=[identb_b])
        S.op("act", lambda e: e.activation(out=esink[:, :], in_=esink[:, :], func=AF.Exp),
             reads=[esink_b], writes=[esink_b])

        with ExitStack() as c2:
            sb2 = lambda name, shape, dt: c2.enter_context(nc.sbuf_tensor(f"{tag}_{name}", shape, dt))
            posi = sb2("posi", [NT, P], I32)
            posf = sb2("posf", [NT, P], F32)
            posT = sb2("posT", [P, NT], F32)
            invf = sb2("invf", [P, 32], F32)
            ang = sb2("ang", [P, NT, 32], F32)
            u = sb2("u", [P, NT, 32], F32)
            ki = sb2("ki", [P, NT, 32], I32)
            kf = sb2("kf", [P, NT, 32], F32)
            posi_b, posf_b, posT_b, invf_b, ang_b, u_b, ki_b, kf_b = S.bufs("rp", 8)
            S.dma("sp", posi[:, :], pos.rearrange("(t p) -> t p", p=P), writes=[posi_b], stream=f"c6")
            S.dma("sp", invf[:, :], invf_d.partition_broadcast(P), writes=[invf_b], stream=f"c7")
            S.op("dve", lambda e: e.tensor_copy(out=posf[:, :], in_=posi[:, :]), reads=[posi_b], writes=[posf_b])
            S.op("pe", lambda e: e.transpose(out=pmm[0][:, 0:NT], in_=posf[:, :], identity=identf[0:NT, 0:NT]),
                 reads=[posf_b, ident_b], writes=[pmm_b[0]])
            S.op("dve", lambda e: e.tensor_copy(out=posT[:, :], in_=pmm[0][:, 0:NT]), reads=[pmm_b[0]], writes=[posT_b])
            S.op("dve", lambda e: e.tensor_tensor(out=ang[:, :, :], in0=bcast(posT[:, :].unsqueeze(2), [P, NT, 32]),
                                                  in1=bcast(invf[:, :].unsqueeze(1), [P, NT, 32]), op=ALU.mult),
                 reads=[posT_b, invf_b], writes=[ang_b])
            for which, off, dst in (("sin", 0.0, sin_t), ("cos", 0.25, cos_t)):
                S.op("dve", lambda e, off=off: e.tensor_scalar(out=u[:, :, :], in0=ang[:, :, :], scalar1=float(1.0 / TWO_PI),
                                                               scalar2=off, op0=ALU.mult, op1=ALU.add),
                     reads=[ang_b], writes=[u_b])
                S.op("dve", lambda e: e.tensor_copy(out=ki[:, :, :], in_=u[:, :, :]), reads=[u_b], writes=[ki_b])
                S.op("dve", lambda e: e.tensor_copy(out=kf[:, :, :], in_=ki[:, :, :]), reads=[ki_b], writes=[kf_b])
                S.op("dve", lambda e: e.scalar_tensor_tensor(out=u[:, :, :], in0=kf[:, :, :], scalar=-CW1, in1=ang[:, :, :],
                                                             op0=ALU.mult, op1=ALU.add),
                     reads=[kf_b, ang_b], writes=[u_b])
                S.op("dve", lambda e: e.scalar_tensor_tensor(out=u[:, :, :], in0=kf[:, :, :], scalar=-CW2, in1=u[:, :, :],
                                                             op0=ALU.mult, op1=ALU.add),
                     reads=[kf_b, u_b], writes=[u_b])
                S.op("dve", lambda e, off=off: e.tensor_scalar(out=u[:, :, :], in0=u[:, :, :], scalar1=float(off * TWO_PI),
                                                               scalar2=PI_LO, op0=ALU.add, op1=ALU.min),
                     reads=[u_b], writes=[u_b])
                S.op("dve", lambda e: e.tensor_scalar(out=u[:, :, :], in0=u[:, :, :], scalar1=-PI_LO, scalar2=None, op0=ALU.max),
                     reads=[u_b], writes=[u_b])
                S.op("act", lambda e, dst=dst: e.activation(out=dst[:, :, :], in_=u[:, :, :], func=AF.Sin),
                     reads=[u_b], writes=[cs_b])
            S.barrier()

        NXS = 7
        NW = 4
        NQ = 5
        NK = 7
        xs = [sbt(f"x{i}", [P, D], F32) for i in range(NXS)]
        xT = [sbt(f"xT{i}", [P, 8, P], BF16) for i in range(NW)]
        ra = [sbt(f"ra{i}", [P, 8, 2, 32], F32) for i in range(2)]
        rb = [sbt(f"rb{i}", [P, 8, 2, 32], F32) for i in range(2)]
        qr = [sbt(f"qr{i}", [P, 16, 2, 32], BF16) for i in range(NW)]
        kr = [sbt(f"kr{i}", [P, 4, 2, 32], BF16) for i in range(NW)]
        qT = [sbt(f"qT{i}", [P, 8, P], BF16) for i in range(NQ)]
        kTw = [sbt(f"kT{i}", [P, 2, P], BF16) for i in range(NK)]
        Vw = [sbt(f"V{i}", [P, 4, 65], BF16) for i in range(NK)]
        pT = [sbt(f"pT{i}", [P, 512], BF16) for i in range(3)]
        den = [sbt(f"den{i}", [P, 16], F32) for i in range(NW)]
        on = [sbt(f"on{i}", [P, 16, 64], BF16) for i in range(NW)]
        oT = [sbt(f"oT{i}", [P, 8, P], BF16) for i in range(NW)]
        ot = [sbt(f"ot{i}", [P, D], F32) for i in range(NW)]
        stats = [sbt(f"st{i}", [P, 2, 6], F32) for i in range(NW)]
        mv = [sbt(f"mv{i}", [P, 2], F32) for i in range(NW)]
        sc = [sbt(f"sc{i}", [P, 2], F32) for i in range(NW)]
        xs_b, xT_b = S.bufs("xs", NXS), S.bufs("xT", NW)
        ra_b, rb_b, qr_b, kr_b, qT_b = S.bufs("ra", 2), S.bufs("rb", 2), S.bufs("qr", NW), S.bufs("kr", NW), S.bufs("qT", NQ)
        kT_b, V_b, pT_b = S.bufs("kT", NK), S.bufs("V", NK), S.bufs("pT", 3)
        den_b, on_b, oT_b, ot_b = S.bufs("den", NW), S.bufs("on", NW), S.bufs("oT", NW), S.bufs("ot", NW)
        st_b, mv_b, sc_b = S.bufs("st", NW), S.bufs("mv", NW), S.bufs("sc", NW)
        for i in range(NK):
            S.op("pool", lambda e, i=i: e.memset(Vw[i][:, :, :], 1.0), writes=[V_b[i]])

        cnt = {"mm": 0, "sc": 0, "pt": 0, "rr": 0}
        pov_free = [True]

        def nxt(key, n):
            v = cnt[key] % n
            cnt[key] += 1
            return v

        def load_x(t):
            sl = t % NXS
            S.dma("sp", xs[sl][:, :], xin[t * P:(t + 1) * P, :], reads=[xin_buf], writes=[xs_b[sl]],
                  stream=f"x{sl}")

        def rope(pm, pm_bf, nh, t, dst, dst_bf, h0):
            r = nxt("rr", 2)
            pm4 = pm[:, 0:nh * 64].rearrange("p (h two d) -> p h two d", two=2, d=32)
            cb = bcast(cos_t[:, t:t + 1, :].unsqueeze(1), [P, nh, 2, 32])
            sb_ = bcast(sin_t[:, t:t + 1, :].unsqueeze(1), [P, nh, 2, 32])
            S.op("dve", lambda e: e.tensor_tensor(out=ra[r][:, 0:nh, :, :], in0=pm4, in1=cb, op=ALU.mult),
                 reads=[pm_bf, cs_b], writes=[ra_b[r]])
            S.op("dve", lambda e: e.tensor_tensor(out=rb[r][:, 0:nh, :, :], in0=pm4, in1=sb_, op=ALU.mult),
                 reads=[pm_bf, cs_b], writes=[rb_b[r]])
            S.op("pool", lambda e: e.tensor_tensor(out=dst[:, h0:h0 + nh, 0, :], in0=ra[r][:, 0:nh, 0, :],
                                                   in1=rb[r][:, 0:nh, 1, :], op=ALU.subtract),
                 reads=[ra_b[r], rb_b[r]], writes=[dst_bf])
            S.op("pool", lambda e: e.tensor_tensor(out=dst[:, h0:h0 + nh, 1, :], in0=ra[r][:, 0:nh, 1, :],
                                                   in1=rb[r][:, 0:nh, 0, :], op=ALU.add),
                 reads=[ra_b[r], rb_b[r]], writes=[dst_bf])

        def stage1(t):
            sl, p2, p4, pq = t % NXS, t % NW, t % NK, t % NQ
            for hh in range(2):
                m = nxt("mm", 2)

                def f_tr(e, hh=hh, m=m):
                    for c in range(4):
                        ins = e.transpose(out=pmm[m][:, c * P:(c + 1) * P],
                                          in_=xs[sl][:, (hh * 4 + c) * P:(hh * 4 + c + 1) * P], identity=identf[:, :])
                    return ins
                S.op("pe", f_tr, reads=[xs_b[sl], ident_b], writes=[pmm_b[m]])
                src = pmm[m][:, :].rearrange("p (c t) -> p c t", c=4)
                if hh == 0:
                    S.op("act", lambda e, src=src: e.copy(out=xT[p2][:, 0:4, :], in_=src), reads=[pmm_b[m]], writes=[xT_b[p2]])
                else:
                    S.op("dve", lambda e, src=src: e.tensor_copy(out=xT[p2][:, 4:8, :], in_=src), reads=[pmm_b[m]], writes=[xT_b[p2]])
                yield
            for n in range(3):
                m = nxt("mm", 2)

                def f_in(e, n=n, m=m):
                    for c in range(8):
                        ins = e.matmul(pmm[m][:, :], lhsT=xT[p2][:, c, :], rhs=w_inb[:, c, n * 512:(n + 1) * 512],
                                       start=(c == 0), stop=(c == 7))
                    return ins
                S.op("pe", f_in, reads=[xT_b[p2], w_in_b], writes=[pmm_b[m]])
                if n < 2:
                    rope(pmm[m], pmm_b[m], 8, t, qr[p2], qr_b[p2], n * 8)
                else:
                    rope(pmm[m], pmm_b[m], 4, t, kr[p2], kr_b[p2], 0)
                    S.op("act", lambda e, m=m: e.copy(out=Vw[p4][:, :, 0:64],
                                                      in_=pmm[m][:, 256:512].rearrange("p (h d) -> p h d", d=64)),
                         reads=[pmm_b[m]], writes=[V_b[p4]])
                yield

            def f_qt(e):
                qflat = qr[p2][:, :, :, :].rearrange("p h two d -> p (h two d)")
                for s_ in range(8):
                    ins = e.transpose(out=ptb[:, s_, :], in_=qflat[:, s_ * P:(s_ + 1) * P], identity=identb[:, :])
                return ins
            S.op("pe", f_qt, reads=[qr_b[p2], identb_b], writes=[ptb_b])
            S.op("act", lambda e: e.copy(out=qT[pq][:, :, :], in_=ptb[:, :, :]), reads=[ptb_b], writes=[qT_b[pq]])
            yield

            def f_kt(e):
                kflat = kr[p2][:, :, :, :].rearrange("p h two d -> p (h two d)")
                for s_ in range(2):
                    ins = e.transpose(out=ptb[:, s_, :], in_=kflat[:, s_ * P:(s_ + 1) * P], identity=identb[:, :])
                return ins
            S.op("pe", f_kt, reads=[kr_b[p2], identb_b], writes=[ptb_b])
            S.op("dve", lambda e: e.tensor_copy(out=kTw[p4][:, :, :], in_=ptb[:, 0:2, :]), reads=[ptb_b], writes=[kT_b[p4]])
            yield

        def stage2(i):
            sl, p2, pq = i % NXS, i % NW, i % NQ
            kts = [kt for kt in (i - 1, i, i + 1) if 0 <= kt < NT]
            bank_started = [False, False, False]
            last_in_bank = {0: 5, 1: 11, 2: 15}
            assert pov_free[0]
            pov_free[0] = False
            work = [(kv, kt) for kv in range(4) for kt in kts]

            def emit_scores(kv, kt):
                base, ch = 64 * (kv % 2), kv // 2
                s_ = nxt("sc", 2)

                def f_sc(e):
                    if kt != i:
                        e.matmul(psc[s_][:, :], lhsT=identb[:, :], rhs=(maskp if kt < i else maskn)[:, :],
                                 start=True, stop=False)
                    return e.matmul(psc[s_][:, :], lhsT=kTw[kt % NK][base:base + 64, ch, :],
                                    rhs=qT[pq][base:base + 64, 4 * ch:4 * ch + 4, :],
                                    start=(kt == i), stop=True)
                S.op("pe", f_sc, reads=[kT_b[kt % NK], qT_b[pq], identb_b, mask_b], writes=[psc_b[s_]])
                return s_

            def emit_exp_pv(kv, kt, s_):
                pi_ = nxt("pt", 3)
                S.op("act", lambda e: e.activation(out=pT[pi_][:, :], in_=psc[s_][:, :], func=AF.Exp, scale=float(scale)),
                     reads=[psc_b[s_]], writes=[pT_b[pi_]])

                def f_pv(e):
                    for g_ in range(4):
                        h = kv * 4 + g_
                        bk, col = h // 6, (h % 6) * 65
                        st_ = not bank_started[bk]
                        bank_started[bk] = True
                        ins = e.matmul(pov[bk][:, col:col + 65], lhsT=pT[pi_][:, g_ * P:(g_ + 1) * P],
                                       rhs=Vw[kt % NK][:, kv, :], start=st_,
                                       stop=(h == last_in_bank[bk] and kt == kts[-1]), skip_group_check=True)
                    return ins
                banks = sorted({(kv * 4 + g_) // 6 for g_ in range(4)})
                S.op("pe", f_pv, reads=[pT_b[pi_], V_b[kt % NK]], writes=[pov_b[bk] for bk in banks])

            prev = None
            for (kv, kt) in work:
                s_ = emit_scores(kv, kt)
                if prev is not None:
                    emit_exp_pv(*prev)
                prev = (kv, kt, s_)
            emit_exp_pv(*prev)
            yield
            for bk in range(3):
                nh = 6 if bk < 2 else 4
                h0 = bk * 6
                pv3 = pov[bk][:, 0:nh * 65].rearrange("p (h d) -> p h d", d=65)
                S.op("dve", lambda e, pv3=pv3, h0=h0, nh=nh: e.tensor_tensor(
                    out=den[p2][:, h0:h0 + nh].unsqueeze(2), in0=pv3[:, :, 64:65],
                    in1=esink[:, h0:h0 + nh].unsqueeze(2), op=ALU.add),
                     reads=[pov_b[bk], esink_b], writes=[den_b[p2]])
                S.op("dve", lambda e, h0=h0, nh=nh: e.reciprocal(out=den[p2][:, h0:h0 + nh], in_=den[p2][:, h0:h0 + nh]),
                     reads=[den_b[p2]], writes=[den_b[p2]])
                S.op("dve", lambda e, pv3=pv3, h0=h0, nh=nh: e.tensor_tensor(
                    out=on[p2][:, h0:h0 + nh, :], in0=pv3[:, :, 0:64],
                    in1=bcast(den[p2][:, h0:h0 + nh].unsqueeze(2), [P, nh, 64]), op=ALU.mult),
                     reads=[pov_b[bk], den_b[p2]], writes=[on_b[p2]])
            pov_free[0] = True
            yield

            def f_ot(e):
                oflat = on[p2][:, :, :].rearrange("p h d -> p (h d)")
                for s_ in range(8):
                    ins = e.transpose(out=ptb[:, s_, :], in_=oflat[:, s_ * P:(s_ + 1) * P], identity=identb[:, :])
                return ins
            S.op("pe", f_ot, reads=[on_b[p2], identb_b], writes=[ptb_b])
            S.op("act", lambda e: e.copy(out=oT[p2][:, :, :], in_=ptb[:, :, :]), reads=[ptb_b], writes=[oT_b[p2]])
            yield
            for n in range(2):
                m = nxt("mm", 2)

                def f_op(e, n=n, m=m):
                    for c in range(8):
                        ins = e.matmul(pmm[m][:, :], lhsT=oT[p2][:, c, :], rhs=w_outb[:, c, n * 512:(n + 1) * 512],
                                       start=(c == 0), stop=(c == 7))
                    return ins
                S.op("pe", f_op, reads=[oT_b[p2], w_out_b], writes=[pmm_b[m]])
                S.op("dve", lambda e, n=n, m=m: e.scalar_tensor_tensor(
                    out=xs[sl][:, n * 512:(n + 1) * 512], in0=xs[sl][:, n * 512:(n + 1) * 512], scalar=DN_ALPHA,
                    in1=pmm[m][:, :], op0=ALU.mult, op1=ALU.add),
                     reads=[xs_b[sl], pmm_b[m]], writes=[xs_b[sl]])
                yield
            yield from ln_epilogue_gen(S, (stats[p2], st_b[p2], mv[p2], mv_b[p2], sc[p2], sc_b[p2]),
                                       xs[sl], xs_b[sl], g_t, b_t, gb_b, ot[p2], ot_b[p2])
            S.dma("sp", xout[i * P:(i + 1) * P, :], ot[p2][:, :], reads=[ot_b[p2]], writes=[xout_buf],
                  stream=f"o{p2}")
            if i + NXS < NT:
                load_x(i + NXS)

        def tile_gen(t):
            if t < NT:
                yield from stage1(t)
            else:
                for _ in range(7):
                    yield
            if t >= 1:
                yield from stage2(t - 1)

        for t in range(min(NXS, NT)):
            load_x(t)
        run_interleaved((tile_gen(t) for t in range(NT + 1)), depth=4, skew=5)
        S.barrier()


def attn_consts_host():
    j = np.arange(P)[:, None]
    i = np.arange(P)[None, :]
    mp = np.where(j >= i, 0.0, -30000.0).astype(np.float32)
    mn = np.where(j <= i, 0.0, -30000.0).astype(np.float32)
    invf = (10000.0 ** (-np.arange(0, HD, 2, dtype=np.float64) / HD)).astype(np.float32)
    return {"c_ident": np.eye(P, dtype=np.float32), "c_maskp": np.tile(mp, (1, 4)), "c_maskn": np.tile(mn, (1, 4)),
            "c_invf": invf}


def permute_attn_w_in(w_in):
    q = w_in[:, :1024].reshape(1024, 16, 64)[:, Q_HEAD_ORDER, :].reshape(1024, 1024)
    return np.ascontiguousarray(np.concatenate([q, w_in[:, 1024:]], axis=1))


def build_test_attn(NT):
    nc = bass.Bass("TRN2", target_bir_lowering=False)
    dt = lambda name, shape, dtp=F32, kind="ExternalInput": nc.dram_tensor(name, shape, dtp, kind=kind).ap()
    x = dt("x", [NT * P, D])
    pos = dt("pos", [NT * P], I32)
    w_in = dt("w_in", [D, A_IN])
    sink = dt("sink", [16])
    w_out = dt("w_out", [D, D])
    g = dt("g", [D])
    b = dt("b", [D])
    consts = (dt("c_ident", [P, P]), dt("c_maskp", [P, 512]), dt("c_maskn", [P, 512]), dt("c_invf", [32]))
    y = dt("y", [NT * P, D], kind="ExternalOutput")
    with ExitStack() as ctx:
        S = Sched(nc, ctx)
        attn_phase(S, nc, NT, x, y, pos, w_in, sink, w_out, g, b, consts, S.buf("xin"), S.buf("xout"), "a0")
        S.finish()
    return nc


G_H = 4
G_DK = 128
G_DV = 256
G_IN = 3104
G_TAU = 16.0


class PsumPool:
    def __init__(self, S, nc, ctx, tag, n=8):
        self.t = [ctx.enter_context(nc.psum_tensor(f"{tag}_pp{i}", [P, 512], F32)) for i in range(n)]
        self.b = S.pbufs(f"{tag}pp", n)
        self.i = 0
        self.n = n

    def get(self):
        k = self.i % self.n
        self.i += 1
        return self.t[k], self.b[k]


def gla_sweep1(S, nc, NT, xin, w_in, gw2f, gbf, gw2b, gbb, consts, scr, xin_buf, scr_buf, tag):
    ident_d, tri_d, negcol_d, gmask_d = consts
    with ExitStack() as ctx:
        sbt = lambda name, shape, dt: ctx.enter_context(nc.sbuf_tensor(f"{tag}_{name}", shape, dt))
        B = S.buf
        pp = PsumPool(S, nc, ctx, tag)
        w_inb = sbt("w_inb", [P, 8, 2080], BF16)
        w2a = sbt("w2a", [17, 2, 512], BF16)
        tri = sbt("tri", [P, 4, P], F32)
        negcol = sbt("negcol", [P, 1], F32)
        gmask = sbt("gmask", [P, 2, P], F32)
        identf = sbt("identf", [P, P], F32)
        identb = sbt("identb", [P, P], BF16)
        S32 = sbt("S32", [P, 4, G_DV], F32)
        Sb = sbt("Sb", [P, 4, G_DV], BF16)
        w_in_b, w2a_b, cst_b, identb_b, S32_b, Sb_b = B("w_in"), B("w2a"), B("cst"), B("idb"), B("S32"), B("Sb")
        S.dma("sp", identf[:, :], ident_d[:, :], writes=[cst_b], stream=f"c0")
        S.dma("sp", tri[:, :, :], tri_d.rearrange("k p t -> p k t"), writes=[cst_b], stream=f"c1")
        S.dma("sp", negcol[:, :], negcol_d[:, :], writes=[cst_b], stream=f"c2")
        S.dma("sp", gmask[:, :, :], gmask_d.rearrange("k p t -> p k t"), writes=[cst_b], stream=f"c3")
        w_inv = w_in.rearrange("(c p) f -> p c f", p=P)
        cl = CastLoader(S, nc, ctx, tag, width=512)
        for c in range(8):
            for hh in range(4):
                cl.load(w_inb[:, c, hh * 512:(hh + 1) * 512], w_inv[:, c, hh * 512:(hh + 1) * 512], w_in_b)
        for c in range(8):
            S.dma("pool", w_inb[:, c, 2048:2080], w_inv[:, c, 3072:3104], writes=[w_in_b], stream=f"w{c % 2}")
        for d_, (w2, gb_) in enumerate(((gw2f, gbf), (gw2b, gbb))):
            S.dma("pool", w2a[0:16, d_, :], w2[:, :], writes=[w2a_b], stream=f"w0")
            S.dma("pool", w2a[16:17, d_, :], gb_.rearrange("(o f) -> o f", o=1), writes=[w2a_b], stream=f"w1")
        S.op("dve", lambda e: e.tensor_copy(out=identb[:, :], in_=identf[:, :]), reads=[cst_b], writes=[identb_b])
        trib = sbt("trib", [P, 4, P], BF16)
        negcolb = sbt("negcolb", [P, 1], BF16)
        S.op("dve", lambda e: e.tensor_copy(out=trib[:, :, :], in_=tri[:, :, :]), reads=[cst_b], writes=[identb_b])
        S.op("dve", lambda e: e.tensor_copy(out=negcolb[:, :], in_=negcol[:, :]), reads=[cst_b], writes=[identb_b])
        S.op("pool", lambda e: e.memset(S32[:, :, :], 0.0), writes=[S32_b])
        S.op("pool", lambda e: e.memset(Sb[:, :, :], 0.0), writes=[Sb_b])

        NXS = 4
        NW = 3
        xs = [sbt(f"x{i}", [P, D], F32) for i in range(NXS)]
        xT = [sbt(f"xT{i}", [P, 8, P], BF16) for i in range(NW)]
        q_sb = [sbt(f"q{i}", [P, 512], F32) for i in range(NW)]
        k_sb = [sbt(f"k{i}", [P, 512], F32) for i in range(NW)]
        v_sb = [sbt(f"v{i}", [P, 1024], BF16) for i in range(NW)]
        lrT = [sbt(f"lrT{i}", [17, 2, P], BF16) for i in range(NW)]
        sp = [[sbt(f"sp{i}_{d_}", [P, 512], F32) for d_ in range(2)] for i in range(NW)]
        dec = [[sbt(f"dec{i}_{d_}", [P, 4], F32) for d_ in range(2)] for i in range(NW)]
        sph = [[sbt(f"sph{i}_{d_}", [P, 512], BF16) for d_ in range(2)] for i in range(NW)]
        spl = [[sbt(f"spl{i}_{d_}", [P, 512], BF16) for d_ in range(2)] for i in range(NW)]
        tb = [[[sbt(f"tb{i}_{d_}_{j}", [P, 512], F32) for j in range(3)] for d_ in range(2)] for i in range(NW)]
        qe = [[sbt(f"qe{i}_{d_}", [P, 512], BF16) for d_ in range(2)] for i in range(NW)]
        ke = [[sbt(f"ke{i}_{d_}", [P, 512], BF16) for d_ in range(2)] for i in range(NW)]
        kd = [[sbt(f"kd{i}_{d_}", [P, 512], BF16) for d_ in range(2)] for i in range(NW)]
        qkT = [[sbt(f"qkT{i}_{d_}", [P, 8, P], BF16) for d_ in range(2)] for i in range(NW)]
        attm = [[sbt(f"attm{i}_{d_}", [P, 4, P], BF16) for d_ in range(2)] for i in range(NW)]
        opart = [sbt(f"op{i}", [P, 1024], F32) for i in range(NW)]
        xs_b = S.bufs("xs", NXS)
        xT_b, q_b, k_b, v_b, lrT_b, op_b = (S.bufs(n_, NW) for n_ in ("xT", "q", "k", "v", "lrT", "op"))
        sp_b, dec_b, qe_b, ke_b, kd_b, qkT_b, attm_b = ([S.bufs(f"{n_}{i}", 2) for i in range(NW)]
                                                        for n_ in ("sp", "dec", "qe", "ke", "kd", "qkT", "attm"))
        tb_b = [[S.bufs(f"tb{i}_{d_}", 3) for d_ in range(2)] for i in range(NW)]
        sphl_b = [S.bufs(f"sphl{i}", 2) for i in range(NW)]
        for i in range(NW):
            S.op("pool", lambda e, i=i: e.memset(lrT[i][:, :, :], 1.0), writes=[lrT_b[i]])

        def load_x(t):
            sl = t % NXS
            S.dma("sp", xs[sl][:, :], xin[t * P:(t + 1) * P, :], reads=[xin_buf], writes=[xs_b[sl]], stream=f"x{sl}")

        state_ver = [0]

        def tile_gen(t):
            sl, p2 = t % NXS, t % NW

            def mm8(out_ap, c0, c1, wr_b):
                def f(e):
                    for c in range(8):
                        ins = e.matmul(out_ap, lhsT=xT[p2][:, c, :], rhs=w_inb[:, c, c0:c1], start=(c == 0), stop=(c == 7))
                    return ins
                S.op("pe", f, reads=[xT_b[p2], w_in_b], writes=[wr_b])

            for hh in range(2):
                pm, pm_b = pp.get()

                def f_tr(e, hh=hh, pm=pm):
                    for c in range(4):
                        ins = e.transpose(out=pm[:, c * P:(c + 1) * P], in_=xs[sl][:, (hh * 4 + c) * P:(hh * 4 + c + 1) * P],
                                          identity=identf[:, :])
                    return ins
                S.op("pe", f_tr, reads=[xs_b[sl], cst_b], writes=[pm_b])
                src = pm[:, :].rearrange("p (c t) -> p c t", c=4)
                if hh == 0:
                    S.op("act", lambda e, src=src: e.copy(out=xT[p2][:, 0:4, :], in_=src), reads=[pm_b], writes=[xT_b[p2]])
                else:
                    S.op("dve", lambda e, src=src: e.tensor_copy(out=xT[p2][:, 4:8, :], in_=src), reads=[pm_b], writes=[xT_b[p2]])
                yield
            pm, pm_b = pp.get()

            def f_lr(e, pm=pm):
                for d_ in range(2):
                    for c in range(8):
                        ins = e.matmul(pm[0:16, d_ * P:(d_ + 1) * P], lhsT=w_inb[:, c, 2048 + 16 * d_:2064 + 16 * d_],
                                       rhs=xT[p2][:, c, :], start=(c == 0), stop=(c == 7))
                return ins
            S.op("pe", f_lr, reads=[xT_b[p2], w_in_b], writes=[pm_b])
            S.op("dve", lambda e, pm=pm: e.tensor_copy(out=lrT[p2][0:16, :, :], in_=pm[0:16, 0:256].rearrange("p (a t) -> p a t", a=2)),
                 reads=[pm_b], writes=[lrT_b[p2]])
            yield
            pm, pm_b = pp.get()
            mm8(pm[:, :], 0, 512, pm_b)
            S.op("act", lambda e, pm=pm: e.activation(out=q_sb[p2][:, :], in_=pm[:, :], func=AF.Copy, scale=float(G_DK ** -0.5)),
                 reads=[pm_b], writes=[q_b[p2]])
            yield
            pm, pm_b = pp.get()
            mm8(pm[:, :], 512, 1024, pm_b)
            S.op("dve", lambda e, pm=pm: e.tensor_copy(out=k_sb[p2][:, :], in_=pm[:, :]), reads=[pm_b], writes=[k_b[p2]])
            yield
            for n in range(2):
                pm, pm_b = pp.get()
                mm8(pm[:, :], 1024 + n * 512, 1536 + n * 512, pm_b)
                S.op("act", lambda e, pm=pm, n=n: e.copy(out=v_sb[p2][:, n * 512:(n + 1) * 512], in_=pm[:, :]),
                     reads=[pm_b], writes=[v_b[p2]])
                yield
            for d_ in range(2):
                spd, spd_b, tbd, tbd_b = sp[p2][d_], sp_b[p2][d_], tb[p2][d_], tb_b[p2][d_]
                pm, pm_b = pp.get()
                S.op("pe", lambda e, pm=pm, d_=d_: e.matmul(pm[:, :], lhsT=lrT[p2][0:17, d_, :], rhs=w2a[0:17, d_, :],
                                                           start=True, stop=True),
                     reads=[lrT_b[p2], w2a_b], writes=[pm_b])
                S.op("act", lambda e, pm=pm, spd=spd: e.activation(out=spd[:, :], in_=pm[:, :], func=AF.Exp, scale=-1.0),
                     reads=[pm_b], writes=[spd_b])
                S.op("act", lambda e, spd=spd: e.activation(out=spd[:, :], in_=spd[:, :], func=AF.Ln, bias=1.0, scale=1.0),
                     reads=[spd_b], writes=[spd_b])
                yield
                hi, lo, hl_b = sph[p2][d_], spl[p2][d_], sphl_b[p2][d_]
                S.op("dve", lambda e, spd=spd, hi=hi: e.tensor_copy(out=hi[:, :], in_=spd[:, :]), reads=[spd_b], writes=[hl_b])
                S.op("dve", lambda e, spd=spd, hi=hi, lo=lo: e.tensor_tensor(out=lo[:, :], in0=spd[:, :], in1=hi[:, :], op=ALU.subtract),
                     reads=[spd_b, hl_b], writes=[hl_b])
                yield
                pm, pm_b = pp.get()

                def f_dec(e, pm=pm, hi=hi, lo=lo):
                    for h in range(4):
                        e.matmul(pm[:, 2 * h:2 * h + 1], lhsT=hi[:, h * P:(h + 1) * P], rhs=negcolb[:, 0:1], start=True, stop=False)
                        ins = e.matmul(pm[:, 2 * h:2 * h + 1], lhsT=lo[:, h * P:(h + 1) * P], rhs=negcolb[:, 0:1], start=False, stop=True)
                    return ins
                S.op("pe", f_dec, reads=[hl_b, identb_b], writes=[pm_b])
                S.op("act", lambda e, pm=pm, d_=d_: e.activation(
                    out=dec[p2][d_][:, :].unsqueeze(2), in_=pm[:, 0:8].rearrange("p (h two) -> p h two", two=2)[:, :, 0:1],
                    func=AF.Exp), reads=[pm_b], writes=[dec_b[p2][d_]])
                pmb, pmb_b = pp.get()

                def f_b(e, pmb=pmb, d_=d_, hi=hi, lo=lo):
                    e.matmul(pmb[:, :], lhsT=trib[:, 2 * d_, :], rhs=hi[:, :], start=True, stop=False)
                    return e.matmul(pmb[:, :], lhsT=trib[:, 2 * d_, :], rhs=lo[:, :], start=False, stop=True)
                S.op("pe", f_b, reads=[hl_b, identb_b], writes=[pmb_b])
                pmc, pmc_b = pp.get()

                def f_c(e, pmc=pmc, d_=d_, hi=hi, lo=lo):
                    e.matmul(pmc[:, :], lhsT=trib[:, 2 * d_ + 1, :], rhs=hi[:, :], start=True, stop=False)
                    return e.matmul(pmc[:, :], lhsT=trib[:, 2 * d_ + 1, :], rhs=lo[:, :], start=False, stop=True)
                S.op("pe", f_c, reads=[hl_b, identb_b], writes=[pmc_b])
                yield
                S.op("act", lambda e, pmb=pmb, tbd=tbd: e.activation(out=tbd[0][:, :], in_=pmb[:, :], func=AF.Exp),
                     reads=[pmb_b], writes=[tbd_b[0]])
                S.op("act", lambda e, pmb=pmb, tbd=tbd: e.activation(out=tbd[1][:, :], in_=pmb[:, :], func=AF.Exp, scale=-1.0),
                     reads=[pmb_b], writes=[tbd_b[1]])
                S.op("act", lambda e, pmc=pmc, tbd=tbd: e.activation(out=tbd[2][:, :], in_=pmc[:, :], func=AF.Exp),
                     reads=[pmc_b], writes=[tbd_b[2]])
                yield
                S.op("dve", lambda e, d_=d_, tbd=tbd: e.tensor_tensor(out=qe[p2][d_][:, :], in0=q_sb[p2][:, :], in1=tbd[0][:, :], op=ALU.mult),
                     reads=[q_b[p2], tbd_b[0]], writes=[qe_b[p2][d_]])
                S.op("dve", lambda e, d_=d_, tbd=tbd: e.tensor_tensor(out=ke[p2][d_][:, :], in0=k_sb[p2][:, :], in1=tbd[1][:, :], op=ALU.mult),
                     reads=[k_b[p2], tbd_b[1]], writes=[ke_b[p2][d_]])
                S.op("pool", lambda e, d_=d_, tbd=tbd: e.tensor_tensor(out=kd[p2][d_][:, :], in0=k_sb[p2][:, :], in1=tbd[2][:, :], op=ALU.mult),
                     reads=[k_b[p2], tbd_b[2]], writes=[kd_b[p2][d_]])
                yield
                pm, pm_b = pp.get()
                pmv = pm[:, :].bitcast(BF16).rearrange("p (c t) -> p c t", c=8)

                def f_t(e, pmv=pmv, d_=d_):
                    for j, src in enumerate((qe[p2][d_], ke[p2][d_])):
                        for h in range(4):
                            ins = e.transpose(out=pmv[:, j * 4 + h, :], in_=src[:, h * P:(h + 1) * P], identity=identb[:, :])
                    return ins
                S.op("pe", f_t, reads=[qe_b[p2][d_], ke_b[p2][d_], identb_b], writes=[pm_b])
                S.op("act" if d_ == 0 else "dve",
                     (lambda e, pmv=pmv, d_=d_: e.copy(out=qkT[p2][d_][:, :, :], in_=pmv)) if d_ == 0 else
                     (lambda e, pmv=pmv, d_=d_: e.tensor_copy(out=qkT[p2][d_][:, :, :], in_=pmv)),
                     reads=[pm_b], writes=[qkT_b[p2][d_]])
                yield
                pm, pm_b = pp.get()

                def f_att(e, pm=pm, d_=d_):
                    for h in range(4):
                        ins = e.matmul(pm[:, h * P:(h + 1) * P], lhsT=qkT[p2][d_][:, 4 + h, :], rhs=qkT[p2][d_][:, h, :],
                                       start=True, stop=True)
                    return ins
                S.op("pe", f_att, reads=[qkT_b[p2][d_]], writes=[pm_b])
                S.op("dve", lambda e, pm=pm, d_=d_: e.tensor_tensor(
                    out=attm[p2][d_][:, :, :], in0=pm[:, :].rearrange("p (h t) -> p h t", h=4),
                    in1=bcast(gmask[:, d_:d_ + 1, :], [P, 4, P]), op=ALU.mult),
                     reads=[pm_b, cst_b], writes=[attm_b[p2][d_]])
                yield
            assert state_ver[0] == t
            for hp in range(2):
                pm, pm_b = pp.get()

                def f_o(e, pm=pm, hp=hp):
                    for hh in range(2):
                        h = 2 * hp + hh
                        o_ap = pm[:, hh * G_DV:(hh + 1) * G_DV]
                        vv = v_sb[p2][:, h * G_DV:(h + 1) * G_DV]
                        e.matmul(o_ap, lhsT=attm[p2][0][:, h, :], rhs=vv, start=True, stop=False)
                        e.matmul(o_ap, lhsT=attm[p2][1][:, h, :], rhs=vv, start=False, stop=False)
                        ins = e.matmul(o_ap, lhsT=qkT[p2][0][:, h, :], rhs=Sb[:, h, :], start=False, stop=True)
                    return ins
                S.op("pe", f_o, reads=[attm_b[p2][0], attm_b[p2][1], v_b[p2], qkT_b[p2][0], Sb_b], writes=[pm_b])
                S.op("act", lambda e, pm=pm, hp=hp: e.copy(out=opart[p2][:, hp * 512:(hp + 1) * 512], in_=pm[:, :]),
                     reads=[pm_b], writes=[op_b[p2]])
            for hp in range(2):
                pm, pm_b = pp.get()

                def f_s(e, pm=pm, hp=hp):
                    for hh in range(2):
                        h = 2 * hp + hh
                        ins = e.matmul(pm[:, hh * G_DV:(hh + 1) * G_DV], lhsT=kd[p2][0][:, h * P:(h + 1) * P],
                                       rhs=v_sb[p2][:, h * G_DV:(h + 1) * G_DV], start=True, stop=True)
                    return ins
                S.op("pe", f_s, reads=[kd_b[p2][0], v_b[p2]], writes=[pm_b])
                for hh in range(2):
                    h = 2 * hp + hh
                    S.op("dve", lambda e, pm=pm, h=h, hh=hh: e.scalar_tensor_tensor(
                        out=S32[:, h, :], in0=S32[:, h, :], scalar=dec[p2][0][:, h:h + 1], in1=pm[:, hh * G_DV:(hh + 1) * G_DV],
                        op0=ALU.mult, op1=ALU.add), reads=[S32_b, dec_b[p2][0], pm_b], writes=[S32_b])
            S.op("act", lambda e: e.copy(out=Sb[:, :, :], in_=S32[:, :, :]), reads=[S32_b], writes=[Sb_b])
            state_ver[0] = t + 1
            yield
            S.dma("sp", scr["opart"][t * P:(t + 1) * P, :], opart[p2][:, :], reads=[op_b[p2]], writes=[scr_buf], stream=f"so{p2}")
            S.dma("sp", scr["qeTb"][t], qkT[p2][1][:, 0:4, :].rearrange("p h t -> p (h t)"), reads=[qkT_b[p2][1]],
                  writes=[scr_buf], stream=f"sq{p2}")
            S.dma("sp", scr["kdb"][t * P:(t + 1) * P, :], kd[p2][1][:, :], reads=[kd_b[p2][1]], writes=[scr_buf], stream=f"sk{p2}")
            S.dma("sp", scr["vsb"][t * P:(t + 1) * P, :], v_sb[p2][:, :], reads=[v_b[p2]], writes=[scr_buf], stream=f"sv{p2}")
            S.dma("sp", scr["decb"][t], dec[p2][1][:, :], reads=[dec_b[p2][1]], writes=[scr_buf], stream=f"sd{p2}")
            if t + NXS < NT:
                load_x(t + NXS)

        for t in range(min(NXS, NT)):
            load_x(t)
        run_interleaved((tile_gen(t) for t in range(NT)), depth=3, skew=7)
        S.barrier()


def gla_sweep2(S, nc, NT, xin, xout, w_in, norm_g, w_out, g, b, consts, scr, xin_buf, scr_buf, xout_buf, tag):
    ident_d = consts[0]
    with ExitStack() as ctx:
        sbt = lambda name, shape, dt: ctx.enter_context(nc.sbuf_tensor(f"{tag}_{name}", shape, dt))
        B = S.buf
        pp = PsumPool(S, nc, ctx, tag)
        w_rb = sbt("w_rb", [P, 8, 1024], BF16)
        w_outb = sbt("w_outb", [P, 8, D], BF16)
        identf = sbt("identf", [P, P], F32)
        identb = sbt("identb", [P, P], BF16)
        ng_t = sbt("ng", [P, G_DV], F32)
        g_t = sbt("g", [P, D], F32)
        b_t = sbt("b", [P, D], F32)
        S32 = sbt("S32", [P, 4, G_DV], F32)
        Sb = sbt("Sb", [P, 4, G_DV], BF16)
        w_b, cst_b, identb_b, gb_b, S32_b, Sb_b = B("w"), B("cst"), B("idb"), B("gb"), B("S32"), B("Sb")
        S.dma("sp", identf[:, :], ident_d[:, :], writes=[cst_b], stream=f"c0")
        S.dma("sp", ng_t[:, :], norm_g.partition_broadcast(P), writes=[gb_b], stream=f"c1")
        S.dma("sp", g_t[:, :], g.partition_broadcast(P), writes=[gb_b], stream=f"c2")
        S.dma("sp", b_t[:, :], b.partition_broadcast(P), writes=[gb_b], stream=f"c3")
        w_inv = w_in.rearrange("(c p) f -> p c f", p=P)
        w_outv = w_out.rearrange("(c p) f -> p c f", p=P)
        cl = CastLoader(S, nc, ctx, tag)
        for c in range(8):
            cl.load(w_rb[:, c, :], w_inv[:, c, 2048:3072], w_b)
        for c in range(8):
            cl.load(w_outb[:, c, :], w_outv[:, c, :], w_b)
        S.op("dve", lambda e: e.tensor_copy(out=identb[:, :], in_=identf[:, :]), reads=[cst_b], writes=[identb_b])
        S.op("pool", lambda e: e.memset(S32[:, :, :], 0.0), writes=[S32_b])
        S.op("pool", lambda e: e.memset(Sb[:, :, :], 0.0), writes=[Sb_b])

        NXS = 4
        NW = 4
        xs = [sbt(f"x{i}", [P, D], F32) for i in range(NXS)]
        opa = [sbt(f"opa{i}", [P, D], F32) for i in range(NXS)]
        qeT = [sbt(f"qeT{i}", [P, 4, P], BF16) for i in range(NXS)]
        kdb = [sbt(f"kdb{i}", [P, 512], BF16) for i in range(NXS)]
        vsb = [sbt(f"vsb{i}", [P, 1024], BF16) for i in range(NXS)]
        decb = [sbt(f"decb{i}", [P, 4], F32) for i in range(NXS)]
        xT = [sbt(f"xT{i}", [P, 8, P], BF16) for i in range(NW)]
        er = [sbt(f"er{i}", [P, 1024], F32) for i in range(NW)]
        sr = [sbt(f"sr{i}", [P, 4, G_DV], F32) for i in range(NW)]
        of = [sbt(f"of{i}", [P, 4, G_DV], F32) for i in range(NW)]
        sq = sbt("sq", [P, G_DV], F32)
        ss = [sbt(f"ss{i}", [P, 4], F32) for i in range(NW)]
        og = [sbt(f"og{i}", [P, 1024], BF16) for i in range(NW)]
        oT = [sbt(f"oT{i}", [P, 8, P], BF16) for i in range(NW)]
        ot = [sbt(f"ot{i}", [P, D], F32) for i in range(NW)]
        stats = [sbt(f"st{i}", [P, 2, 6], F32) for i in range(NW)]
        mv = [sbt(f"mv{i}", [P, 2], F32) for i in range(NW)]
        sc = [sbt(f"sc{i}", [P, 2], F32) for i in range(NW)]
        xs_b, opa_b, qeT_b, kdb_b, vsb_b, decb_b = (S.bufs(n_, NXS) for n_ in ("xs", "opa", "qeT", "kdb", "vsb", "decb"))
        xT_b, er_b, sr_b, of_b, ss_b, og_b, oT_b, ot_b, st_b, mv_b, sc_b = (
            S.bufs(n_, NW) for n_ in ("xT", "er", "sr", "of", "ss", "og", "oT", "ot", "st", "mv", "sc"))

        order = list(range(NT - 1, -1, -1))

        def load(idx):
            t = order[idx]
            sl = idx % NXS
            S.dma("sp", qeT[sl][:, :, :].rearrange("p h t -> p (h t)"), scr["qeTb"][t], reads=[scr_buf], writes=[qeT_b[sl]],
                  stream=f"lq{sl}")
            S.dma("sp", kdb[sl][:, :], scr["kdb"][t * P:(t + 1) * P, :], reads=[scr_buf], writes=[kdb_b[sl]], stream=f"lk{sl}")
            S.dma("sp", vsb[sl][:, :], scr["vsb"][t * P:(t + 1) * P, :], reads=[scr_buf], writes=[vsb_b[sl]], stream=f"lv{sl}")
            S.dma("sp", decb[sl][:, :], scr["decb"][t], reads=[scr_buf], writes=[decb_b[sl]], stream=f"ld{sl}")
            S.dma("sp", opa[sl][:, :], scr["opart"][t * P:(t + 1) * P, :], reads=[scr_buf], writes=[opa_b[sl]], stream=f"lo{sl}")
            S.dma("sp", xs[sl][:, :], xin[t * P:(t + 1) * P, :], reads=[xin_buf], writes=[xs_b[sl]], stream=f"x{sl}")

        state_ver = [0]

        def tile_gen(idx):
            t = order[idx]
            sl, p2 = idx % NXS, idx % NW
            assert state_ver[0] == idx
            pmo = []
            for hp in range(2):
                pm, pm_b = pp.get()
                pmo.append((pm, pm_b))

                def f_o(e, pm=pm, hp=hp):
                    for hh in range(2):
                        h = 2 * hp + hh
                        ins = e.matmul(pm[:, hh * G_DV:(hh + 1) * G_DV], lhsT=qeT[sl][:, h, :], rhs=Sb[:, h, :], start=True, stop=True)
                    return ins
                S.op("pe", f_o, reads=[qeT_b[sl], Sb_b], writes=[pm_b])
            pms = []
            for hp in range(2):
                pm, pm_b = pp.get()
                pms.append((pm, pm_b))

                def f_s(e, pm=pm, hp=hp):
                    for hh in range(2):
                        h = 2 * hp + hh
                        ins = e.matmul(pm[:, hh * G_DV:(hh + 1) * G_DV], lhsT=kdb[sl][:, h * P:(h + 1) * P],
                                       rhs=vsb[sl][:, h * G_DV:(h + 1) * G_DV], start=True, stop=True)
                    return ins
                S.op("pe", f_s, reads=[kdb_b[sl], vsb_b[sl]], writes=[pm_b])
            for hp in range(2):
                pm, pm_b = pms[hp]
                for hh in range(2):
                    h = 2 * hp + hh
                    S.op("dve", lambda e, pm=pm, h=h, hh=hh: e.scalar_tensor_tensor(
                        out=S32[:, h, :], in0=S32[:, h, :], scalar=decb[sl][:, h:h + 1], in1=pm[:, hh * G_DV:(hh + 1) * G_DV],
                        op0=ALU.mult, op1=ALU.add), reads=[S32_b, decb_b[sl], pm_b], writes=[S32_b])
            S.op("act", lambda e: e.copy(out=Sb[:, :, :], in_=S32[:, :, :]), reads=[S32_b], writes=[Sb_b])
            state_ver[0] = idx + 1
            for hp in range(2):
                pm, pm_b = pmo[hp]
                S.op("dve", lambda e, pm=pm, hp=hp: e.tensor_tensor(
                    out=of[p2][:, 2 * hp:2 * hp + 2, :].rearrange("p h d -> p (h d)"), in0=pm[:, :],
                    in1=opa[sl][:, hp * 512:(hp + 1) * 512], op=ALU.add),
                     reads=[pm_b, opa_b[sl]], writes=[of_b[p2]])
            yield
            def f_sq(e):
                for h in range(4):
                    ins = e.activation(out=sq[:, :], in_=of[p2][:, h, :], func=AF.Square, accum_out=ss[p2][:, h:h + 1])
                return ins
            S.op("act", f_sq, reads=[of_b[p2]], writes=[ss_b[p2]])
            S.op("act", lambda e: e.activation(out=ss[p2][:, :], in_=ss[p2][:, :], func=AF.Ln, bias=1e-6, scale=1.0 / G_DV),
                 reads=[ss_b[p2]], writes=[ss_b[p2]])
            S.op("act", lambda e: e.activation(out=ss[p2][:, :], in_=ss[p2][:, :], func=AF.Exp, scale=-0.5),
                 reads=[ss_b[p2]], writes=[ss_b[p2]])

            def f_hn(e):
                for h in range(4):
                    ins = e.activation(out=of[p2][:, h, :], in_=of[p2][:, h, :], func=AF.Copy, scale=ss[p2][:, h:h + 1])
                return ins
            S.op("act", f_hn, reads=[of_b[p2], ss_b[p2]], writes=[of_b[p2]])
            yield
            for hh in range(2):
                pm, pm_b = pp.get()

                def f_tr(e, hh=hh, pm=pm):
                    for c in range(4):
                        ins = e.transpose(out=pm[:, c * P:(c + 1) * P], in_=xs[sl][:, (hh * 4 + c) * P:(hh * 4 + c + 1) * P],
                                          identity=identf[:, :])
                    return ins
                S.op("pe", f_tr, reads=[xs_b[sl], cst_b], writes=[pm_b])
                src = pm[:, :].rearrange("p (c t) -> p c t", c=4)
                if hh == 0:
                    S.op("act", lambda e, src=src: e.copy(out=xT[p2][:, 0:4, :], in_=src), reads=[pm_b], writes=[xT_b[p2]])
                else:
                    S.op("dve", lambda e, src=src: e.tensor_copy(out=xT[p2][:, 4:8, :], in_=src), reads=[pm_b], writes=[xT_b[p2]])
                yield
            for n in range(2):
                pm, pm_b = pp.get()

                def f_r(e, pm=pm, n=n):
                    for c in range(8):
                        ins = e.matmul(pm[:, :], lhsT=xT[p2][:, c, :], rhs=w_rb[:, c, n * 512:(n + 1) * 512], start=(c == 0), stop=(c == 7))
                    return ins
                S.op("pe", f_r, reads=[xT_b[p2], w_b], writes=[pm_b])
                ern = er[p2][:, n * 512:(n + 1) * 512]
                srn = sr[p2][:, :, :].rearrange("p h d -> p (h d)")[:, n * 512:(n + 1) * 512]
                S.op("act", lambda e, pm=pm, ern=ern: e.activation(out=ern, in_=pm[:, :], func=AF.Exp, scale=-1.0),
                     reads=[pm_b], writes=[er_b[p2]])
                S.op("act", lambda e, ern=ern: e.activation(out=ern, in_=ern, func=AF.Ln, bias=1.0, scale=1.0),
                     reads=[er_b[p2]], writes=[er_b[p2]])
                S.op("act", lambda e, ern=ern: e.activation(out=ern, in_=ern, func=AF.Exp, scale=-1.0),
                     reads=[er_b[p2]], writes=[er_b[p2]])
                yield
                S.op("dve", lambda e, pm=pm, ern=ern, srn=srn: e.tensor_tensor(out=srn, in0=pm[:, :], in1=ern, op=ALU.mult),
                     reads=[pm_b, er_b[p2]], writes=[sr_b[p2]])
                S.op("dve", lambda e, srn=srn, n=n: e.tensor_tensor(
                    out=srn.rearrange("p (h d) -> p h d", d=G_DV), in0=srn.rearrange("p (h d) -> p h d", d=G_DV),
                    in1=bcast(ng_t[:, :].unsqueeze(1), [P, 2, G_DV]), op=ALU.mult),
                     reads=[sr_b[p2], gb_b], writes=[sr_b[p2]])
                yield
            S.op("dve", lambda e: e.tensor_tensor(out=og[p2][:, :], in0=of[p2][:, :, :].rearrange("p h d -> p (h d)"),
                                                  in1=sr[p2][:, :, :].rearrange("p h d -> p (h d)"), op=ALU.mult),
                 reads=[of_b[p2], sr_b[p2]], writes=[og_b[p2]])
            yield
            pm, pm_b = pp.get()
            pmv = pm[:, :].bitcast(BF16).rearrange("p (c t) -> p c t", c=8)

            def f_ot(e, pmv=pmv):
                for s_ in range(8):
                    ins = e.transpose(out=pmv[:, s_, :], in_=og[p2][:, s_ * P:(s_ + 1) * P], identity=identb[:, :])
                return ins
            S.op("pe", f_ot, reads=[og_b[p2], identb_b], writes=[pm_b])
            S.op("act", lambda e, pmv=pmv: e.copy(out=oT[p2][:, :, :], in_=pmv), reads=[pm_b], writes=[oT_b[p2]])
            yield
            for n in range(2):
                pm, pm_b = pp.get()

                def f_op(e, pm=pm, n=n):
                    for c in range(8):
                        ins = e.matmul(pm[:, :], lhsT=oT[p2][:, c, :], rhs=w_outb[:, c, n * 512:(n + 1) * 512], start=(c == 0), stop=(c == 7))
                    return ins
                S.op("pe", f_op, reads=[oT_b[p2], w_b], writes=[pm_b])
                S.op("dve", lambda e, pm=pm, n=n: e.scalar_tensor_tensor(
                    out=xs[sl][:, n * 512:(n + 1) * 512], in0=xs[sl][:, n * 512:(n + 1) * 512], scalar=DN_ALPHA,
                    in1=pm[:, :], op0=ALU.mult, op1=ALU.add), reads=[xs_b[sl], pm_b], writes=[xs_b[sl]])
                yield
            yield from ln_epilogue_gen(S, (stats[p2], st_b[p2], mv[p2], mv_b[p2], sc[p2], sc_b[p2]),
                                       xs[sl], xs_b[sl], g_t, b_t, gb_b, ot[p2], ot_b[p2])
            S.dma("sp", xout[t * P:(t + 1) * P, :], ot[p2][:, :], reads=[ot_b[p2]], writes=[xout_buf], stream=f"o{p2}")
            if idx + NXS < NT:
                load(idx + NXS)

        for idx in range(min(NXS, NT)):
            load(idx)
        run_interleaved((tile_gen(i) for i in range(NT)), depth=4, skew=5)
        S.barrier()


def gla_consts_host():
    u = np.arange(P)[:, None]
    t = np.arange(P)[None, :]
    c = np.float32(-1.0 / G_TAU)
    tri = np.stack([np.where(u <= t, c, 0), np.where(u > t, c, 0),
                    np.where(u >= t, c, 0), np.where(u < t, c, 0)]).astype(np.float32)
    gmask = np.stack([np.where(t >= u, 1.0, 0.0), np.where(t <= u, 1.0, 0.0)]).astype(np.float32)
    return {"c_tri": tri, "c_negcol": np.full((P, 1), c, np.float32), "c_gmask": gmask}


def gla_scratch(nc, NT, tag):
    mk = lambda name, shape, dtp: nc.dram_tensor(f"{tag}_{name}", shape, dtp, kind="Internal").ap()
    return {"opart": mk("opart", [NT * P, 1024], F32), "qeTb": mk("qeTb", [NT, P, 512], BF16),
            "kdb": mk("kdb", [NT * P, 512], BF16), "vsb": mk("vsb", [NT * P, 1024], BF16),
            "decb": mk("decb", [NT, P, 4], F32)}


def build_test_gla(NT):
    nc = bass.Bass("TRN2", target_bir_lowering=False)
    dt = lambda name, shape, dtp=F32, kind="ExternalInput": nc.dram_tensor(name, shape, dtp, kind=kind).ap()
    x = dt("x", [NT * P, D])
    w_in = dt("w_in", [D, G_IN])
    gw2f, gbf, gw2b, gbb = dt("gw2f", [16, 512]), dt("gbf", [512]), dt("gw2b", [16, 512]), dt("gbb", [512])
    norm_g = dt("norm_g", [G_DV])
    w_out = dt("w_out", [D, D])
    g, b = dt("g", [D]), dt("b", [D])
    consts = (dt("c_ident", [P, P]), dt("c_tri", [4, P, P]), dt("c_negcol", [P, 1]), dt("c_gmask", [2, P, P]))
    y = dt("y", [NT * P, D], kind="ExternalOutput")
    scr = gla_scratch(nc, NT, "g1")
    with ExitStack() as ctx:
        S = Sched(nc, ctx)
        xin_buf, scr_buf, xout_buf = S.buf("xin"), S.buf("scr"), S.buf("xout")
        gla_sweep1(S, nc, NT, x, w_in, gw2f, gbf, gw2b, gbb, consts, scr, xin_buf, scr_buf, "g1a")
        gla_sweep2(S, nc, NT, x, y, w_in, norm_g, w_out, g, b, consts, scr, xin_buf, scr_buf, xout_buf, "g1b")
        S.finish()
    return nc


def build_full(NT):
    nc = bass.Bass("TRN2", target_bir_lowering=False)
    dt = lambda name, shape, dtp=F32, kind="ExternalInput": nc.dram_tensor(name, shape, dtp, kind=kind).ap()
    N = NT * P
    x = dt("x", [N, D])
    pos = dt("pos", [N], I32)
    a_w_in, a_sink, a_w_out = dt("a_w_in", [D, A_IN]), dt("a_sink", [16]), dt("a_w_out", [D, D])
    g_w_in = dt("g_w_in", [D, G_IN])
    gw2f, gbf, gw2b, gbb = dt("gw2f", [16, 512]), dt("gbf", [512]), dt("gw2b", [16, 512]), dt("gbb", [512])
    norm_g = dt("norm_g", [G_DV])
    g_w_out = dt("g_w_out", [D, D])
    mix_g, mix_b = [dt(f"mix_g{i}", [D]) for i in range(2)], [dt(f"mix_b{i}", [D]) for i in range(2)]
    w1 = [dt(f"w1_{i}", [D, DFF]) for i in range(2)]
    w2 = [dt(f"w2_{i}", [DFF, D]) for i in range(2)]
    mlp_g, mlp_b = [dt(f"mlp_g{i}", [D]) for i in range(2)], [dt(f"mlp_b{i}", [D]) for i in range(2)]
    c_ident = dt("c_ident", [P, P])
    a_consts = (c_ident, dt("c_maskp", [P, 512]), dt("c_maskn", [P, 512]), dt("c_invf", [32]))
    g_consts = (c_ident, dt("c_tri", [4, P, P]), dt("c_negcol", [P, 1]), dt("c_gmask", [2, P, P]))
    out = dt("out", [N, D], kind="ExternalOutput")
    s = [nc.dram_tensor(f"act{i}", [N, D], F32, kind="Internal").ap() for i in range(3)]
    scr = gla_scratch(nc, NT, "g1")
    with ExitStack() as ctx:
        S = Sched(nc, ctx)
        bx, bout, bscr = S.buf("x"), S.buf("out"), S.buf("scr")
        bs = S.bufs("act", 3)
        attn_phase(S, nc, NT, x, s[0], pos, a_w_in, a_sink, a_w_out, mix_g[0], mix_b[0], a_consts, bx, bs[0], "a0")
        mlp_phase(S, nc, NT, s[0], s[1], w1[0], w2[0], mlp_g[0], mlp_b[0], c_ident, bs[0], bs[1], "m0")
        gla_sweep1(S, nc, NT, s[1], g_w_in, gw2f, gbf, gw2b, gbb, g_consts, scr, bs[1], bscr, "ga")
        gla_sweep2(S, nc, NT, s[1], s[2], g_w_in, norm_g, g_w_out, mix_g[1], mix_b[1], g_consts, scr, bs[1], bscr, bs[2], "gb")
        mlp_phase(S, nc, NT, s[2], out, w1[1], w2[1], mlp_g[1], mlp_b[1], c_ident, bs[2], bout, "m1")
        S.finish()
    return nc


def host_inputs(inp, core, NT=SEQ // P):
    f = lambda a: np.ascontiguousarray(np.asarray(a, dtype=np.float32))
    N = NT * P
    m = {
        "x": f(inp["x"][core, :N]),
        "pos": np.ascontiguousarray(np.asarray(inp["positions"][core, :N], dtype=np.int32)),
        "a_w_in": permute_attn_w_in(f(inp["attn_w_in"][0])),
        "a_sink": f(inp["attn_sink"][0]),
        "a_w_out": f(inp["attn_w_out"][0]),
        "g_w_in": f(inp["gla_w_in"][0]),
        "gw2f": f(inp["gla_gate_w2_fwd"][0]), "gbf": f(inp["gla_gate_b_fwd"][0]),
        "gw2b": f(inp["gla_gate_w2_bwd"][0]), "gbb": f(inp["gla_gate_b_bwd"][0]),
        "norm_g": f(inp["gla_norm_g"][0]),
        "g_w_out": f(inp["gla_w_out"][0]),
    }
    for i in range(2):
        m[f"mix_g{i}"] = f(inp["mix_ln_g"][i])
        m[f"mix_b{i}"] = f(inp["mix_ln_b"][i])
        m[f"w1_{i}"] = f(inp["mlp_w1"][i])
        m[f"w2_{i}"] = f(inp["mlp_w2"][i])
        m[f"mlp_g{i}"] = f(inp["mlp_ln_g"][i])
        m[f"mlp_b{i}"] = f(inp["mlp_ln_b"][i])
    m.update(attn_consts_host())
    m.update(gla_consts_host())
    return m


def kernel(**inputs):
    NT = SEQ // P
    nc = build_full(NT)
    shared = host_inputs(inputs, 0)
    in_maps = []
    for c in range(NCORES):
        m = dict(shared)
        m["x"] = np.ascontiguousarray(np.asarray(inputs["x"][c], dtype=np.float32))
        m["pos"] = np.ascontiguousarray(np.asarray(inputs["positions"][c], dtype=np.int32))
        in_maps.append(m)
    res = run_bass_kernel_spmd(nc, in_maps, core_ids=list(range(NCORES)))
    return np.stack([np.asarray(r["out"], dtype=np.float32) for r in res.results], axis=0)
```
